# Optimizing a Trainium2 kernel written in Bass

```python
import jax, jax.numpy as jnp
from jax import lax
import numpy as np

D_MODEL = 1024
BATCH = 32
SEQ = 256
DEPTH = 2
DEC_BATCH = 8
DEC_SEQ = 2048
PAST_LEN = 256

GRID_W = 64
EPS = 1e-6
F32 = jnp.float32
A_HEADS = 4
A_DK = 128
A_DV = 128
A_W = A_HEADS * A_DK
CHUNK = 32
LB_FLOOR = 1e-30
NEG_BIG = -1e30
B_W = 512
B_GROUPS = 4
CONV_W = 3
C_HEADS = 8
C_NOPE = 64
C_ROPE = 32
C_V = 64
C_Q_LORA = 384
C_KV_LORA = 256
ROPE_THETA = 10000.0
Q_BLOCK = 128
D_FF = 2816
N_BRANCH = 3
IN_SIZES = (A_W, A_W, A_W, A_W, A_W, B_W, B_W, B_W, C_Q_LORA, C_KV_LORA + C_ROPE, N_BRANCH * D_MODEL)
IN_COLS = 5 * A_W + 3 * B_W + C_Q_LORA + C_KV_LORA + C_ROPE + N_BRANCH * D_MODEL

kernel_name = 'hybrid_flow_backbone_step'


def rmsnorm(x, g):
    xf = x.astype(F32)
    y = xf * lax.rsqrt(jnp.mean(xf * xf, axis=-1, keepdims=True) + EPS)
    return (y * g.astype(F32)).astype(x.dtype)


def split_cols(z, sizes):
    out, o = [], 0
    for s in sizes:
        out.append(z[..., o:o + s])
        o += s
    return out


def dwconv3(u, w):
    up = jnp.pad(u, ((0, 0), (1, 1), (0, 0)))
    return up[:, :-2] * w[0] + up[:, 1:-1] * w[1] + up[:, 2:] * w[2]


def axial_rope_tables(T):
    rows = T // GRID_W
    r = jnp.repeat(jnp.arange(rows, dtype=F32), GRID_W)
    col = jnp.tile(jnp.arange(GRID_W, dtype=F32), rows)
    n_freq = C_ROPE // 4
    freq = ROPE_THETA ** (-jnp.arange(n_freq, dtype=F32) / n_freq)
    ang = jnp.stack([r[:, None] * freq, col[:, None] * freq], axis=1)
    return jnp.cos(ang), jnp.sin(ang)


def apply_axial_rope(x, cos, sin):
    n_freq = C_ROPE // 4
    xf = x.astype(F32).reshape(*x.shape[:-1], 2, 2, n_freq)
    x1, x2 = xf[..., 0, :], xf[..., 1, :]
    out = jnp.stack([x1 * cos - x2 * sin, x2 * cos + x1 * sin], axis=-2)
    return out.reshape(x.shape).astype(x.dtype)


def log_forget(z, lb):
    return jnp.logaddexp(jnp.log(jnp.maximum(lb, LB_FLOOR)),
                         jnp.log1p(-lb) + jax.nn.log_sigmoid(z.astype(F32)))


def chunk_gated_recurrence(q, k, v, logf, s0):
    B, T, H, DK = q.shape
    DV = v.shape[-1]
    n = T // CHUNK

    def to_chunks(a):
        return a.reshape(B, n, CHUNK, H, a.shape[-1]).transpose(1, 0, 3, 2, 4)

    mask = jnp.tril(jnp.ones((CHUNK, CHUNK), dtype=bool))

    def step(S, inp):
        qc, kc, vc, gc = inp
        b = jnp.cumsum(gc, axis=2)
        diff = jnp.where(mask[:, :, None], b[:, :, :, None, :] - b[:, :, None, :, :], NEG_BIG)
        scores = jnp.einsum('bhtd,bhsd,bhtsd->bhts', qc, kc, jnp.exp(diff))
        o = jnp.einsum('bhts,bhsv->bhtv', scores, vc) + jnp.einsum('bhtd,bhdv->bhtv', qc * jnp.exp(b), S)
        b_last = b[:, :, -1, :]
        S_new = jnp.exp(b_last)[..., None] * S + jnp.einsum('bhsd,bhsv->bhdv', kc * jnp.exp(b_last[:, :, None, :] - b), vc)
        return S_new, o

    S_fin, o = lax.scan(step, s0, (to_chunks(q), to_chunks(k), to_chunks(v), to_chunks(logf)))
    return o.transpose(1, 0, 3, 2, 4).reshape(B, T, H, DV), S_fin


def hgrn2_mixer(q_raw, f_raw_fwd, f_raw_bwd, i_raw, g_raw, lb, gnorm, s_init):
    B, T, _ = q_raw.shape

    def heads(a, d):
        return a.reshape(B, T, A_HEADS, d)

    q = heads(jax.nn.silu(q_raw.astype(F32)) * A_DK ** -0.5, A_DK)
    v = heads(i_raw.astype(F32), A_DV)
    outs, states = [], []
    for d, f_raw in enumerate((f_raw_fwd, f_raw_bwd)):
        logf = heads(log_forget(f_raw, lb[d]), A_DK)
        k = -jnp.expm1(logf)
        if s_init is None:
            s0 = jnp.zeros((B, A_HEADS, A_DK, A_DV), F32)
        else:
            s0 = s_init[:, d].astype(F32)
        if d == 0:
            o, s = chunk_gated_recurrence(q, k, v, logf, s0)
        else:
            fl = lambda a: jnp.flip(a, axis=1)
            o, s = chunk_gated_recurrence(fl(q), fl(k), fl(v), fl(logf), s0)
            o = fl(o)
        outs.append(o)
        states.append(s)
    o = (outs[0] + outs[1]).astype(q_raw.dtype)
    o = rmsnorm(o, gnorm) * jax.nn.silu(heads(g_raw, A_DV))
    return o.reshape(B, T, A_W), jnp.stack(states, axis=1)


def expand_kv(c_kv, w_kv_up):
    B, T, _ = c_kv.shape
    kv = (c_kv @ w_kv_up).reshape(B, T, C_HEADS, C_NOPE + C_V)
    return kv[..., :C_NOPE], kv[..., C_NOPE:]


def mla_attention(q_nope, q_rope, k_nope, k_rope, v):
    B, Tq, H, _ = q_nope.shape
    nb = Tq // Q_BLOCK
    scale = (C_NOPE + C_ROPE) ** -0.5

    def blk(a):
        return a.reshape(B, nb, Q_BLOCK, *a.shape[2:]).swapaxes(0, 1)

    def one(args):
        qn, qr = args
        s = jnp.einsum('bqhd,bkhd->bhqk', qn, k_nope) + jnp.einsum('bqhr,bkr->bhqk', qr, k_rope)
        p = jax.nn.softmax(s.astype(F32) * scale, axis=-1).astype(v.dtype)
        return jnp.einsum('bhqk,bkhv->bqhv', p, v)

    o = lax.map(one, (blk(q_nope), blk(q_rope)))
    return o.swapaxes(0, 1).reshape(B, Tq, H * C_V)


def trunk_layer(x, mod, lw, lb, hgrn_init, ctx_cache, rope):
    (norm1, w_in, hgrn_gnorm, w_o_hgrn, conv_w, w_o_conv, q_norm, w_q_up, kv_norm, w_kv_up,
     w_o_mla, w_out, norm2, w_up, ffn_conv_w, w_down) = lw
    B, T, _ = x.shape
    shift1, scale1, gate1, shift2, scale2, gate2 = jnp.split(mod, 6, axis=-1)
    xn = rmsnorm(x, norm1) * (1 + scale1) + shift1
    (qA, fA_f, fA_b, iA, gA, bB, cB, hB, cq, ckv, gates) = split_cols(xn @ w_in, IN_SIZES)

    yA, states = hgrn2_mixer(qA, fA_f, fA_b, iA, gA, lb, hgrn_gnorm, hgrn_init)
    yA = yA @ w_o_hgrn

    yB = (bB * dwconv3(cB * hB, conv_w)) @ w_o_conv

    qh = (rmsnorm(cq, q_norm) @ w_q_up).reshape(B, T, C_HEADS, C_NOPE + C_ROPE)
    q_nope, q_rope = qh[..., :C_NOPE], qh[..., C_NOPE:]
    c_kv = rmsnorm(ckv[..., :C_KV_LORA], kv_norm)
    k_rope = ckv[..., C_KV_LORA:]
    own_cache = jnp.concatenate([c_kv, k_rope], axis=-1)
    if rope is not None:
        cos, sin = rope
        q_rope = apply_axial_rope(q_rope, cos[:, None], sin[:, None])
        k_rope = apply_axial_rope(k_rope, cos, sin)
    k_nope, v = expand_kv(c_kv, w_kv_up)
    if ctx_cache is not None:
        kn_c, v_c = expand_kv(ctx_cache[..., :C_KV_LORA], w_kv_up)
        k_nope = jnp.concatenate([k_nope, kn_c], axis=1)
        k_rope = jnp.concatenate([k_rope, ctx_cache[..., C_KV_LORA:]], axis=1)
        v = jnp.concatenate([v, v_c], axis=1)
    yC = mla_attention(q_nope, q_rope, k_nope, k_rope, v) @ w_o_mla

    g = jax.nn.sigmoid(gates.reshape(B, T, N_BRANCH, D_MODEL))
    h = g[:, :, 0] * yA + g[:, :, 1] * yB + g[:, :, 2] * yC
    x = x + gate1 * (h @ w_out)

    xn2 = rmsnorm(x, norm2) * (1 + scale2) + shift2
    u = dwconv3(xn2 @ w_up, ffn_conv_w)
    a, bval = u[..., :D_FF], u[..., D_FF:]
    x = x + gate2 * ((jax.nn.silu(a) * bval) @ w_down)
    return x, states, own_cache


def setup_inputs(seed: int = 0) -> dict:
    key = jax.random.key(seed)
    ks = jax.random.split(key, 32)

    def nrm(k, shape, scale):
        return jax.random.normal(k, shape, F32) * scale

    def gain(k, shape):
        return 1.0 + 0.05 * jax.random.normal(k, shape, F32)

    return {
        'x_prompt': nrm(ks[0], (BATCH, SEQ, D_MODEL), 1.0),
        'x_sample': nrm(ks[1], (DEC_BATCH, DEC_SEQ, D_MODEL), 1.0),
        'state_hgrn': nrm(ks[2], (DEC_BATCH, DEPTH, 2, A_HEADS, A_DK, A_DV), 0.5),
        'cache_mla': nrm(ks[3], (DEC_BATCH, DEPTH, PAST_LEN, C_KV_LORA + C_ROPE), 1.0),
        'c': nrm(ks[4], (DEC_BATCH, D_MODEL), 1.0),
        'c_ctx': nrm(ks[5], (D_MODEL,), 1.0),
        'w_ada': nrm(ks[6], (DEPTH, D_MODEL, 6 * D_MODEL), 0.3 * D_MODEL ** -0.5),
        'b_ada': nrm(ks[7], (DEPTH, 6 * D_MODEL), 0.01),
        'norm1': gain(ks[8], (DEPTH, D_MODEL)),
        'w_in': nrm(ks[9], (DEPTH, D_MODEL, IN_COLS), D_MODEL ** -0.5),
        'hgrn_lb_logits': nrm(ks[10], (DEPTH, 2, A_W), 0.5),
        'hgrn_gnorm': gain(ks[11], (DEPTH, A_DV)),
        'w_o_hgrn': nrm(ks[12], (DEPTH, A_W, D_MODEL), A_W ** -0.5),
        'conv_w': nrm(ks[13], (DEPTH, CONV_W, B_W), CONV_W ** -0.5),
        'w_o_conv': nrm(ks[14], (DEPTH, B_W, D_MODEL), B_W ** -0.5),
        'mla_q_norm': gain(ks[15], (DEPTH, C_Q_LORA)),
        'w_q_up': nrm(ks[16], (DEPTH, C_Q_LORA, C_HEADS * (C_NOPE + C_ROPE)), C_Q_LORA ** -0.5),
        'mla_kv_norm': gain(ks[17], (DEPTH, C_KV_LORA)),
        'w_kv_up': nrm(ks[18], (DEPTH, C_KV_LORA, C_HEADS * (C_NOPE + C_V)), C_KV_LORA ** -0.5),
        'w_o_mla': nrm(ks[19], (DEPTH, C_HEADS * C_V, D_MODEL), (C_HEADS * C_V) ** -0.5),
        'w_out': nrm(ks[20], (DEPTH, D_MODEL, D_MODEL), D_MODEL ** -0.5),
        'norm2': gain(ks[21], (DEPTH, D_MODEL)),
        'w_up': nrm(ks[22], (DEPTH, D_MODEL, 2 * D_FF), D_MODEL ** -0.5),
        'ffn_conv_w': nrm(ks[23], (DEPTH, CONV_W, 2 * D_FF), CONV_W ** -0.5),
        'w_down': nrm(ks[24], (DEPTH, D_FF, D_MODEL), D_FF ** -0.5),
        'final_norm': gain(ks[25], (D_MODEL,)),
    }


def reference(x_prompt, x_sample, state_hgrn, cache_mla, c, c_ctx, w_ada, b_ada, norm1, w_in,
              hgrn_lb_logits, hgrn_gnorm, w_o_hgrn, conv_w, w_o_conv, mla_q_norm, w_q_up,
              mla_kv_norm, w_kv_up, w_o_mla, w_out, norm2, w_up, ffn_conv_w, w_down, final_norm):
    p = jax.nn.softmax(hgrn_lb_logits.astype(F32), axis=0)
    lb_all = jnp.cumsum(p, axis=0) - p[0]
    rope = axial_rope_tables(x_sample.shape[1])
    xp, xs = x_prompt, x_sample
    new_states, new_caches = [], []
    for l in range(DEPTH):
        lw = (norm1[l], w_in[l], hgrn_gnorm[l], w_o_hgrn[l], conv_w[l], w_o_conv[l], mla_q_norm[l],
              w_q_up[l], mla_kv_norm[l], w_kv_up[l], w_o_mla[l], w_out[l], norm2[l], w_up[l],
              ffn_conv_w[l], w_down[l])
        mod_ctx = (jax.nn.silu(c_ctx) @ w_ada[l] + b_ada[l])[None, None, :]
        xp, st, kvc = trunk_layer(xp, mod_ctx, lw, lb_all[l], None, None, None)
        new_states.append(st.astype(x_prompt.dtype))
        new_caches.append(kvc)
        mod_lat = (jax.nn.silu(c) @ w_ada[l] + b_ada[l])[:, None, :]
        xs, _, _ = trunk_layer(xs, mod_lat, lw, lb_all[l], state_hgrn[:, l], cache_mla[:, l], rope)
    y_prompt = rmsnorm(xp, final_norm)
    y_sample = rmsnorm(xs, final_norm)
    new_state_hgrn = jnp.stack(new_states, axis=1)
    new_cache_mla = jnp.stack(new_caches, axis=1)
    return (y_prompt, y_sample, new_state_hgrn, new_cache_mla)
```

```python
import types
import numpy as np
import ml_dtypes
from contextlib import ExitStack
import concourse.bass as bass
import concourse.mybir as mybir
from concourse.bass_utils import run_bass_kernel_spmd

F32 = mybir.dt.float32
BF16 = mybir.dt.bfloat16
AF = mybir.ActivationFunctionType
ALU = mybir.AluOpType

D = 1024
DEPTH = 2
A_W = 512
B_W = 512
C_Q = 384
C_KV = 256
C_ROPE = 32
D_FF = 2816
IN_COLS = 7840
OFF_Q, OFF_FF, OFF_FB, OFF_I, OFF_G = 0, 512, 1024, 1536, 2048
OFF_BB, OFF_BC, OFF_BH = 2560, 3072, 3584
OFF_CQ = 4096
OFF_CKV = 4480
OFF_GATE = 4768
EPS = 1e-6
NP_SEQ, L_P = 4, 256
L_S = 2048
PAST = 256
WSLOT = 2816


class Buf:
    __slots__ = ("name", "w", "r", "excl")

    def __init__(self, name, excl=False):
        self.name = name
        self.w = None
        self.r = []
        self.excl = excl


def _snap(fn):
    if fn.__closure__ is None:
        return fn
    cells = []
    for c in fn.__closure__:
        try:
            cells.append(types.CellType(c.cell_contents))
        except ValueError:
            cells.append(c)
    return types.FunctionType(fn.__code__, fn.__globals__, fn.__name__, fn.__defaults__, tuple(cells))


class Node:
    __slots__ = ("eng", "emit", "deps", "cost", "lat", "idx", "sig", "start", "finish", "prev", "isdma", "pri")


class Eng:
    def __init__(self, name, h, sems, is_pe=False):
        self.name = name
        self.h = h
        self.sems = sems
        self.si = 0
        self.cnt = 0
        self.seen = {}
        self.is_pe = is_pe
        self.dslots = []
        self.di = 0


class KB:
    ROLL = 16000
    W = 48
    LAT = 0.2

    def __init__(self, nc, es):
        self.nc = nc
        self.es = es
        self.E = {}
        for name, h, n in (("pe", nc.tensor, 6), ("act", nc.scalar, 6), ("dve", nc.vector, 8),
                           ("pool", nc.gpsimd, 3), ("sp", nc.sync, 1)):
            sems = [es.enter_context(nc.semaphore(f"s_{name}{i}")) for i in range(n)]
            self.E[name] = Eng(name, h, sems, is_pe=(name == "pe"))
        for qn, n in (("sp", 12), ("pool", 8)):
            q = self.E[qn]
            q.dslots = [[es.enter_context(nc.semaphore(f"d_{qn}{i}")), 0] for i in range(n)]
        self.pe_nums = set(s.num for s in self.E["pe"].sems)
        self.pe_ins = 0
        self.marks = []
        self.nodes = []
        self.nidx = 0
        self.reorder = True

    def _mk(self, en, emit, r, w, cost, lat=None):
        nd = Node()
        nd.eng = en
        nd.emit = emit
        nd.cost = cost
        nd.lat = cost if lat is None else lat
        nd.idx = self.nidx
        self.nidx += 1
        nd.sig = None
        nd.start = None
        nd.finish = None
        nd.prev = 0
        nd.isdma = False
        nd.pri = 1.0 if (en != "pe" and any(b.excl for b in r)) else 0.0
        deps = set()
        for b in r:
            if b.w is not None:
                deps.add(b.w)
            if b.excl:
                deps.update(b.r)
        for b in w:
            if b.w is not None:
                deps.add(b.w)
            deps.update(b.r)
        nd.deps = deps
        for b in r:
            b.r.append(nd)
        for b in w:
            b.w = nd
            b.r = []
        self.nodes.append(nd)
        return nd

    def op(self, en, fn, r=(), w=(), n=512):
        if en == "act":
            cost = 0.22 + n / 1400.0
        elif en == "dve":
            cost = 0.10 + max(n, 64) / 960.0
        else:
            cost = 0.30 + n / 450.0

        fn2 = _snap(fn)

        def emit(h):
            return fn2(h)
        return self._mk(en, emit, r, w, cost)

    def mm(self, steps, r=(), w=(), start=True, stop=True):
        n = len(steps)
        self.pe_ins += n
        cost = 0.0
        for st in steps:
            fr = 1
            for d_ in st[2].shape[1:]:
                fr *= d_
            c = max(fr, 64) / 2400.0 + 0.05
            if st[1].dtype == F32:
                c *= 4
            cost += c

        def emit(h):
            ins = None
            for i, st in enumerate(steps):
                kw = st[3] if len(st) > 3 else {}
                ins = h.matmul(st[0], st[1], st[2], start=(start and i == 0), stop=(stop and i == n - 1), **kw)
            return ins
        return self._mk("pe", emit, r, w, cost, lat=cost + 0.15)

    def tr(self, outs_ins, ident, r=(), w=()):
        self.pe_ins += len(outs_ins)

        def emit(h):
            ins = None
            for out, in_ in outs_ins:
                ins = h.transpose(out, in_, ident)
            return ins
        return self._mk("pe", emit, r, w, 0.12 * len(outs_ins), lat=0.12 * len(outs_ins) + 0.15)

    def dma(self, qn, out, in_, r=(), w=()):
        nbytes = 1
        for d_ in in_.shape:
            nbytes *= d_
        nbytes *= 4 if in_.dtype == F32 else 2

        def emit(h):
            return h.dma_start(out=out, in_=in_)
        occ = 1.1 if qn == "pool" else 0.1
        nd = self._mk(qn, emit, r, w, occ, lat=2.2 + nbytes / 150e3)
        nd.isdma = True
        return nd

    def flush(self):
        nodes = self.nodes
        self.nodes = []
        if not nodes:
            return
        per = {en: [] for en in self.E}
        for nd in nodes:
            per[nd.eng].append(nd)
        order = {en: [] for en in self.E}
        if not self.reorder:
            for en in per:
                order[en] = per[en]
        else:
            head = {en: 0 for en in self.E}
            t_eng = {en: 0.0 for en in self.E}
            nleft = len(nodes)
            W, LAT = self.W, self.LAT
            while nleft:
                best = None
                for en, lst in per.items():
                    hpos = head[en]
                    L_ = len(lst)
                    while hpos < L_ and lst[hpos].start is not None:
                        hpos += 1
                    head[en] = hpos
                    cnt = 0
                    i = hpos
                    te = t_eng[en]
                    while i < L_ and cnt < W:
                        nd = lst[i]
                        i += 1
                        if nd.start is not None:
                            continue
                        cnt += 1
                        rt = te
                        ok = True
                        for d_ in nd.deps:
                            if d_.start is None:
                                ok = False
                                break
                            f = d_.finish if d_.eng == en else d_.finish + LAT
                            if f > rt:
                                rt = f
                        if not ok:
                            continue
                        key = rt - nd.pri
                        if best is None or key < best[3] or (key == best[3] and nd.idx < best[1].idx):
                            best = (rt, nd, en, key)
                        if rt <= te and (en == "pe" or nd.pri > 0):
                            break
                assert best is not None, "scheduler stuck"
                rt, nd, en = best[0], best[1], best[2]
                nd.start = rt
                nd.finish = rt + nd.lat
                t_eng[en] = rt + nd.cost
                order[en].append(nd)
                nleft -= 1
        for en, lst in order.items():
            e = self.E[en]
            for nd in lst:
                if nd.isdma:
                    slot = e.dslots[e.di % len(e.dslots)]
                    e.di += 1
                    nd.prev = (slot[0], slot[1])
                    slot[1] += 16
                    nd.sig = (slot[0], slot[1], 16)
                else:
                    if e.cnt >= self.ROLL:
                        e.si += 1
                        e.cnt = 0
                    e.cnt += 1
                    nd.sig = (e.sems[e.si], e.cnt, 1)
        for en, lst in order.items():
            e = self.E[en]
            for nd in lst:
                best = {}
                for d_ in nd.deps:
                    sem, val = d_.sig[0], d_.sig[1]
                    if e.is_pe and sem.num in self.pe_nums:
                        continue
                    if val > best.get(sem.num, (None, 0))[1]:
                        best[sem.num] = (sem, val)
                if nd.sig[2] == 16 and nd.prev[1] > 0:
                    sem, val = nd.prev
                    if val > best.get(sem.num, (None, 0))[1]:
                        best[sem.num] = (sem, val)
                for num, (sem, val) in best.items():
                    if e.seen.get(num, 0) < val:
                        e.h.wait_ge(sem, val)
                        e.seen[num] = val
                ins = nd.emit(e.h)
                ins.then_inc(nd.sig[0], nd.sig[2])
                nd.emit = None
        for nd in nodes:
            nd.start = 0.0
            nd.finish = 0.0
            nd.deps = ()

    def barrier(self, engines=("pe", "act", "dve", "sp"), include_pool=False):
        self.flush()
        evs = []
        for en, e in self.E.items():
            if en == "pool" and not include_pool:
                continue
            if e.cnt > 0:
                evs.append((e.sems[e.si], e.cnt))
            for sl in e.dslots:
                if sl[1] > 0:
                    evs.append((sl[0], sl[1]))
        for en in engines:
            e = self.E[en]
            for sem, val in evs:
                if e.is_pe and sem.num in self.pe_nums:
                    continue
                if e.seen.get(sem.num, 0) < val:
                    e.h.wait_ge(sem, val)
                    e.seen[sem.num] = val


def host_consts():
    c = {}
    c["identF"] = np.eye(128, dtype=np.float32)
    s = np.arange(128)[:, None]
    t = np.arange(128)[None, :]
    same = (s // 32) == (t // 32)
    c["mLE"] = (same & (s <= t)).astype(np.float32)
    c["mGE"] = (same & (s >= t)).astype(np.float32)
    c["mLT"] = (same & (s < t)).astype(np.float32)
    c["mGT"] = (same & (s > t)).astype(np.float32)
    T = L_S
    rows = np.repeat(np.arange(T // 64, dtype=np.float32), 64)
    col = np.tile(np.arange(64, dtype=np.float32), T // 64)
    nf = 8
    freq = (np.float32(10000.0) ** (-np.arange(nf, dtype=np.float32) / np.float32(nf))).astype(np.float32)
    ang = np.stack([rows[:, None] * freq, col[:, None] * freq], axis=1).astype(np.float32)
    cosv = np.cos(ang).astype(np.float32)
    sinv = np.sin(ang).astype(np.float32)
    C = np.zeros((96, T), np.float32)
    S = np.zeros((96, T), np.float32)
    for a in range(2):
        for hf in range(2):
            for f in range(nf):
                C[64 + a * 16 + hf * 8 + f] = cosv[:, a, f]
                S[64 + a * 16 + hf * 8 + f] = sinv[:, a, f]
    c["ropeC"] = C
    c["ropeS"] = S
    P = np.zeros((96, 96), np.float32)
    for a in range(2):
        for f in range(nf):
            m0 = 64 + a * 16 + f
            m1 = 64 + a * 16 + 8 + f
            P[m1, m0] = -1.0
            P[m0, m1] = 1.0
    c["P96"] = P
    sw = np.zeros((128, 128), np.float32)
    for m_ in range(128):
        sw[(m_ + 64) % 128, m_] = 1.0
    c["swapM"] = sw
    return c


WEIGHTS = ["w_ada", "b_ada", "norm1", "w_in", "hgrn_lb_logits", "hgrn_gnorm", "w_o_hgrn", "conv_w", "w_o_conv",
           "mla_q_norm", "w_q_up", "mla_kv_norm", "w_kv_up", "w_o_mla", "w_out", "norm2", "w_up", "ffn_conv_w",
           "w_down", "final_norm"]
W_SHAPES = {
    "w_ada": [2, 1024, 6144], "b_ada": [2, 6144], "norm1": [2, 1024], "w_in": [2, 1024, IN_COLS],
    "hgrn_lb_logits": [2, 2, 512], "hgrn_gnorm": [2, 128], "w_o_hgrn": [2, 512, 1024], "conv_w": [2, 3, 512],
    "w_o_conv": [2, 512, 1024], "mla_q_norm": [2, 384], "w_q_up": [2, 384, 768], "mla_kv_norm": [2, 256],
    "w_kv_up": [2, 256, 1024], "w_o_mla": [2, 512, 1024], "w_out": [2, 1024, 1024], "norm2": [2, 1024],
    "w_up": [2, 1024, 5632], "ffn_conv_w": [2, 3, 5632], "w_down": [2, 2816, 1024], "final_norm": [1024],
}


_CACHE = {}


class _Stop(Exception):
    pass


def build(debug=None, stop=None):
    nc = bass.Bass("TRN2", target_bir_lowering=False)
    dr = {}

    def din(name, shape, dt=F32):
        dr[name] = nc.dram_tensor(name, list(shape), dt, kind="ExternalInput").ap()
        return dr[name]

    def dout(name, shape):
        dr[name] = nc.dram_tensor(name, list(shape), F32, kind="ExternalOutput").ap()
        return dr[name]

    din("xp", [NP_SEQ * L_P, D])
    din("xs", [L_S, D])
    din("st_in", [2, 2, 4, 128, 128])
    din("cache_in", [2, PAST, C_KV + C_ROPE])
    din("cvec", [2, D])
    for wn in WEIGHTS:
        din(wn, W_SHAPES[wn])
    hc = host_consts()
    for k, v in hc.items():
        din(k, v.shape)
    dout("yp", [NP_SEQ * L_P, D])
    dout("ys", [L_S, D])
    dout("st_out", [NP_SEQ, 2, 2, 4, 128, 128])
    dout("cache_out", [NP_SEQ, 2, L_P, C_KV + C_ROPE])
    if debug:
        dout("dbg", [128, debug])
    ofw_d = nc.dram_tensor("ofw_d", [128, 4, L_S], BF16).ap()
    oA_d = nc.dram_tensor("oA_d", [128, 4, L_S], BF16).ap()

    with ExitStack() as es:
        kb = KB(nc, es)
        _CACHE['kb'] = kb

        def sb(name, shape, dt):
            return es.enter_context(nc.sbuf_tensor("sb_" + name, list(shape), dt))

        xT = sb("xT", [128, 8, L_S], F32)
        xbuf = [Buf(f"x{j}") for j in range(L_S // 512)]
        wsl = [sb(f"wsl{i}", [128, WSLOT], BF16) for i in range(4)]
        wslb = [Buf(f"wsl{i}") for i in range(4)]
        wctr = [0]
        identF = sb("identF", [128, 128], F32)
        identB = sb("identB", [128, 128], BF16)
        onesB = sb("onesB", [128, 128], BF16)
        mLE = sb("mLE", [128, 128], F32)
        mGE = sb("mGE", [128, 128], F32)
        mLT = sb("mLT", [128, 128], F32)
        mGT = sb("mGT", [128, 128], F32)
        swapM = sb("swapM", [128, 128], F32)
        P96 = sb("P96", [96, 96], BF16)
        P96f = sb("P96f", [96, 96], F32)
        epsT = sb("epsT", [128, 1], F32)
        cbuf = Buf("consts")
        fv = sb("fv", [128, 2, 224], F32)
        fvb = Buf("fv")
        modT = sb("modT", [128, 2, 48, 2], F32)
        modb = Buf("mod")
        dmod = sb("dmod", [128, 6, 8], F32)
        dmodb = Buf("dmod")
        lbT = sb("lbT", [128, 2, 2, 512], F32)
        lbb = Buf("lb")
        kvnB = sb("kvnB", [128, 2, 256], F32)
        kvnb = Buf("kvnB")
        ARENA = 53600
        arena = sb("arena", [128, ARENA], BF16)
        psum = [es.enter_context(nc.psum_tensor(f"ps{i}", [128, 512], F32)) for i in range(8)]
        psb = [Buf(f"ps{i}", excl=True) for i in range(8)]

        class Carver:
            def __init__(self):
                self.off = 0

            def reset(self):
                self.off = 0

            def get(self, shape, dt, nbuf=1):
                n = int(np.prod(shape[1:]))
                nb = n * (4 if dt == F32 else 2)
                nb = (nb + 3) // 4 * 4
                o = self.off
                self.off += nb
                assert self.off <= ARENA * 2, f"arena overflow {self.off}"
                ap = arena[0:shape[0], o // 2:(o + nb) // 2]
                if dt == F32:
                    ap = ap.bitcast(F32)
                ap = ap[:, 0:n]
                if len(shape) == 3:
                    ap = ap.rearrange("p (a b) -> p a b", a=shape[1])
                elif len(shape) == 4:
                    ap = ap.rearrange("p (a b c) -> p a b c", a=shape[1], b=shape[2])
                return ap

        cv = Carver()

        def phase():
            kb.barrier()
            cv.reset()

        class Rot:
            def __init__(self, name, shape, dt, n):
                self.t = [cv.get(shape, dt) for _ in range(n)]
                self.b = [Buf(f"{name}{i}") for i in range(n)]
                self.i = 0

            def nxt(self):
                i = self.i % len(self.t)
                self.i += 1
                return self.t[i], self.b[i]

        def wload(src, kc, ncols, extra=None):
            i = wctr[0] % 4
            wctr[0] += 1
            n = kc * ncols * (extra or 1)
            assert n <= WSLOT
            if extra:
                view = wsl[i][:, 0:n].rearrange("p (k e n) -> p k e n", k=kc, e=extra)
            else:
                view = wsl[i][:, 0:n].rearrange("p (k n) -> p k n", k=kc)
            if extra:
                for e_ in range(extra):
                    kb.dma("pool", view[:, :, e_, :], src[:, :, e_, :], w=[wslb[i]])
            else:
                kb.dma("pool", view, src, w=[wslb[i]])
            return view, wslb[i]

        def w3(name, l):
            return dr[name][l].rearrange("(kc p) n -> p kc n", p=128)

        for nm, t_ in (("identF", identF), ("mLE", mLE), ("mGE", mGE), ("mLT", mLT), ("mGT", mGT), ("swapM", swapM)):
            kb.dma("sp", t_[:], dr[nm], w=[cbuf])
        kb.dma("sp", P96f[:], dr["P96"], w=[cbuf])
        kb.op("dve", lambda h: h.tensor_copy(out=identB[:], in_=identF[:]), r=[cbuf], w=[cbuf])
        kb.op("dve", lambda h: h.tensor_copy(out=P96[:], in_=P96f[:]), r=[cbuf], w=[cbuf])
        kb.op("dve", lambda h: h.memset(onesB[:], 1.0), w=[cbuf])
        kb.op("dve", lambda h: h.memset(epsT[:], EPS), w=[cbuf])

        cv.reset()
        stg = cv.get([128, 128], F32)
        stgb = Buf("stg")
        cin = sb("cin", [128, 2, 8], F32)

        def featvec(rows_ap, nrows, dst_ap, dstbuf):
            kb.dma("sp", stg[0:nrows, :], rows_ap, w=[stgb])
            kb.tr([(psum[7][:, 0:nrows], stg[0:nrows, :])], identF[0:nrows, 0:nrows], r=[stgb, cbuf], w=[psb[7]])
            kb.op("dve", lambda h: h.tensor_copy(out=dst_ap, in_=psum[7][:, 0:nrows]), r=[psb[7]], w=[dstbuf])

        for l in range(DEPTH):
            featvec(dr["b_ada"][l].rearrange("(j p) -> j p", p=128), 48, fv[:, l, 0:48], fvb)
            featvec(dr["norm1"][l].rearrange("(j p) -> j p", p=128), 8, fv[:, l, 48:56], fvb)
            featvec(dr["norm2"][l].rearrange("(j p) -> j p", p=128), 8, fv[:, l, 56:64], fvb)
            featvec(dr["hgrn_gnorm"][l].rearrange("(j p) -> j p", p=128), 1, fv[:, l, 64:65], fvb)
            featvec(dr["conv_w"][l].rearrange("j (g p) -> (j g) p", p=128), 12, fv[:, l, 65:77], fvb)
            featvec(dr["mla_q_norm"][l].rearrange("(j p) -> j p", p=128), 3, fv[:, l, 77:80], fvb)
            featvec(dr["mla_kv_norm"][l].rearrange("(j p) -> j p", p=128), 2, fv[:, l, 80:82], fvb)
            fc = dr["ffn_conv_w"][l].rearrange("j (m p) -> (j m) p", p=128)
            featvec(fc[0:88], 88, fv[:, l, 82:170], fvb)
            featvec(fc[88:132], 44, fv[:, l, 170:214], fvb)
        featvec(dr["final_norm"].rearrange("(j p) -> j p", p=128), 8, fv[:, 0, 214:222], fvb)
        featvec(dr["cvec"].rearrange("c (j p) -> (c j) p", p=128), 16, cin[:].rearrange("p c j -> p (c j)"), fvb)

        lg = cv.get([128, 2, 2, 512], F32)
        lgb = Buf("lg")
        for l in range(2):
            for d_ in range(2):
                kb.dma("sp", lg[:, l, d_, :], dr["hgrn_lb_logits"][l, d_].partition_broadcast(128), w=[lgb])
        for d_ in range(2):
            kb.op("dve", lambda h, d_=d_: h.tensor_tensor(out=lbT[:, d_, 0, :], in0=lg[:, 1, d_, :], in1=lg[:, 0, d_, :],
                                                          op=ALU.subtract), r=[lgb], w=[lbb])
            kb.op("act", lambda h, d_=d_: h.activation(out=lbT[:, d_, 0, :], in_=lbT[:, d_, 0, :], func=AF.Sigmoid),
                  r=[lbb], w=[lbb])
            kb.op("dve", lambda h, d_=d_: h.tensor_scalar(out=lbT[:, d_, 1, :], in0=lbT[:, d_, 0, :], scalar1=-1.0,
                                                          scalar2=1.0, op0=ALU.mult, op1=ALU.add), r=[lbb], w=[lbb])
        for l in range(2):
            kb.dma("sp", kvnB[:, l, :], dr["mla_kv_norm"][l].partition_broadcast(128), w=[kvnb])

        silc = cv.get([128, 8, 2], BF16)
        silb = Buf("silc")
        kb.op("act", lambda h: h.activation(out=silc[:].rearrange("p k c -> p c k"), in_=cin[:], func=AF.Silu),
              r=[fvb], w=[silb])
        for l in range(DEPTH):
            wa = w3("w_ada", l)
            for js in range(0, 48, 2):
                wv, wb = wload(wa[:, :, js * 128:(js + 2) * 128], 8, 256)
                for j in range(js, js + 2):
                    kb.mm([(psum[6][:, 2 * j:2 * j + 2], wv[:, k, (j - js) * 128:(j - js + 1) * 128], silc[:, k, :])
                           for k in range(8)], r=[wb, silb], w=[psb[6]])
            kb.op("dve", lambda h, l=l: h.tensor_tensor(
                out=modT[:, l, :, :], in0=psum[6][:, 0:96].rearrange("p (j c) -> p j c", c=2),
                in1=fv[:, l, 0:48].unsqueeze(2).to_broadcast([128, 48, 2]), op=ALU.add), r=[psb[6], fvb], w=[modb])

        def chk(name):
            kb.marks.append((name, kb.pe_ins))
            if stop == name:
                raise _Stop()

        def run_job(job):
            T, L, nseq = job["T"], job["L"], job["nseq"]
            ji = job["idx"]
            nb = T // 512
            n128 = T // 128
            QN = min(512, L)
            Tk = L + (PAST if job["cache"] else 0)
            nkb = Tk // 128

            phase()
            st0 = Rot("xst", [128, 1024], F32, 2)
            for i in range(n128):
                s_t, s_b = st0.nxt()
                kb.dma("sp", s_t, job["xin"][i * 128:(i + 1) * 128, :], w=[s_b])
                for half in range(2):
                    pi = (2 * i + half) % 4
                    kb.tr([(psum[pi][:, q * 128:(q + 1) * 128], s_t[:, (half * 4 + q) * 128:(half * 4 + q + 1) * 128])
                           for q in range(4)], identF[:], r=[s_b, cbuf], w=[psb[pi]])
                    eng = "dve" if half == 0 else "act"
                    if eng == "dve":
                        kb.op("dve", lambda h, pi=pi, half=half, i=i: h.tensor_copy(
                            out=xT[:, half * 4:half * 4 + 4, i * 128:(i + 1) * 128],
                            in_=psum[pi][:].rearrange("p (q t) -> p q t", q=4)), r=[psb[pi]], w=[xbuf[i // 4]])
                    else:
                        kb.op("act", lambda h, pi=pi, half=half, i=i: h.copy(
                            out=xT[:, half * 4:half * 4 + 4, i * 128:(i + 1) * 128],
                            in_=psum[pi][:].rearrange("p (q t) -> p q t", q=4)), r=[psb[pi]], w=[xbuf[i // 4]])

            chk("p0")

            def rstd_block(j, ntok=512):
                sq = Rot_sq
                for k in range(8):
                    s_t, s_b = sq.nxt()
                    kb.op("act", lambda h, k=k, s_t=s_t: h.activation(out=s_t, in_=xT[:, k, j * 512:(j + 1) * 512],
                                                                      func=AF.Square), r=[xbuf[j]], w=[s_b])
                    kb.mm([(psum[5][:], onesB[:], s_t)], r=[s_b, cbuf], w=[psb[5]], start=(k == 0), stop=(k == 7))
                r_t, r_b = Rot_rstd.nxt()
                kb.op("act", lambda h: h.activation(out=r_t, in_=psum[5][:], func=AF.Ln, bias=epsT[:], scale=1.0 / D),
                      r=[psb[5], cbuf], w=[r_b])
                kb.op("act", lambda h: h.activation(out=r_t, in_=r_t, func=AF.Exp, scale=-0.5), r=[r_b], w=[r_b])
                return r_t, r_b

            class VirtXN:
                def __init__(self):
                    self.tiles = {}

                def __getitem__(self, key):
                    p_, k_, ts_ = key
                    j_ = ts_.start // 512
                    return self.tiles[j_][p_, k_, ts_.start - j_ * 512:ts_.stop - j_ * 512]

            def prep_xn(j, gi, si, xnv, xnb, R_xn):
                t_, b_ = R_xn.nxt()
                xnv.tiles[j] = t_
                xnb[j] = b_
                r_t, r_b = rstd_block(j)
                for k in range(8):
                    t_t, t_b = Rot_tmp.nxt()
                    kb.op("dve", lambda h, k=k, t_t=t_t: h.scalar_tensor_tensor(
                        out=t_t, in0=xT[:, k, j * 512:(j + 1) * 512], scalar=dmod[:, gi, k:k + 1], in1=r_t,
                        op0=ALU.mult, op1=ALU.mult), r=[xbuf[j], dmodb, r_b], w=[t_b])
                    kb.op("act", lambda h, k=k, t_t=t_t: h.activation(
                        out=t_[:, k, :], in_=t_t, func=AF.Identity,
                        bias=dmod[:, si, k:k + 1], scale=1.0), r=[t_b, dmodb], w=[b_])

            def norm_all(gi, si, xn, xnb):
                for j in range(nb):
                    r_t, r_b = rstd_block(j)
                    for k in range(8):
                        t_t, t_b = Rot_tmp.nxt()
                        kb.op("dve", lambda h, k=k, t_t=t_t: h.scalar_tensor_tensor(
                            out=t_t, in0=xT[:, k, j * 512:(j + 1) * 512], scalar=dmod[:, gi, k:k + 1], in1=r_t,
                            op0=ALU.mult, op1=ALU.mult), r=[xbuf[j], dmodb, r_b], w=[t_b])
                        kb.op("act", lambda h, k=k, t_t=t_t: h.activation(
                            out=xn[:, k, j * 512:(j + 1) * 512], in_=t_t, func=AF.Identity,
                            bias=dmod[:, si, k:k + 1], scale=1.0), r=[t_b, dmodb], w=[xnb[j]])

            for l in range(DEPTH):
                fvl = lambda a, b: fv[:, l, a:b]
                phase()
                mo = lambda c0: modT[:, l, c0:c0 + 8, ji]
                kb.op("dve", lambda h: h.scalar_tensor_tensor(out=dmod[:, 0, :], in0=mo(8), scalar=1.0, in1=fvl(48, 56),
                                                              op0=ALU.add, op1=ALU.mult), r=[modb, fvb], w=[dmodb])
                kb.op("dve", lambda h: h.tensor_copy(out=dmod[:, 1, :], in_=mo(0)), r=[modb], w=[dmodb])
                kb.op("dve", lambda h: h.tensor_copy(out=dmod[:, 2, :], in_=mo(16)), r=[modb], w=[dmodb])
                kb.op("dve", lambda h: h.scalar_tensor_tensor(out=dmod[:, 3, :], in0=mo(32), scalar=1.0, in1=fvl(56, 64),
                                                              op0=ALU.add, op1=ALU.mult), r=[modb, fvb], w=[dmodb])
                kb.op("dve", lambda h: h.tensor_copy(out=dmod[:, 4, :], in_=mo(24)), r=[modb], w=[dmodb])
                kb.op("dve", lambda h: h.tensor_copy(out=dmod[:, 5, :], in_=mo(40)), r=[modb], w=[dmodb])
                win = w3("w_in", l)

                R_xn = Rot("xnt", [128, 8, 512], BF16, 2)
                xn = VirtXN()
                xnb = [None] * nb
                mark_x = cv.off
                qT = cv.get([128, 4, T], BF16)
                qTb = [Buf(f"qT{j}") for j in range(nb)]
                mark_h = cv.off
                ofwb = [Buf(f"ofw{j}") for j in range(n128)]
                oAb = [Buf(f"oA{j}") for j in range(nb)]
                R_of = Rot("of", [128, 4, 128], BF16, 2)
                R_oA = Rot("oAt", [128, 4, 512], BF16, 1)
                Rot_sq = Rot("sq", [128, 512], BF16, 2)
                Rot_rstd = Rot("rstd", [128, 512], F32, 2)
                Rot_tmp = Rot("tmp", [128, 512], F32, 2)
                S32 = [cv.get([128, 4, 128], F32) for _ in range(2)]
                S32b = [Buf("S32a"), Buf("S32b")]
                Sp = [0]
                Sbf = cv.get([128, 5, 4, 128], BF16)
                Sbfb = Buf("Sbf")
                R_s = Rot("hs_", [128, 512], F32, 2)
                R_lf = Rot("hlf_", [128, 512], F32, 2)
                R_k = Rot("hk_", [128, 512], BF16, 2)
                R_kh = Rot("hkh_", [128, 512], BF16, 2)
                R_v = Rot("hv_", [128, 512], BF16, 2)
                R_eb = Rot("heb_", [128, 4, 128], F32, 2)
                R_enb = Rot("henb_", [128, 4, 128], F32, 2)
                R_es = Rot("hes_", [128, 512], F32, 2)
                R_qt = Rot("hqt_", [128, 4, 128], BF16, 2)
                R_kt = Rot("hkt_", [128, 4, 128], BF16, 2)
                R_A = Rot("hA_", [128, 4, 128], BF16, 2)
                R_osum = Rot("hos", [128, 4, 512], F32, 1)
                R_sg = Rot("hsg", [128, 4, 512], BF16, 1)

                def hgrn_step(dr_, tb, wf, wfb, wi, wib):
                    tok = slice(tb * 128, (tb + 1) * 128)
                    j = tb // 4
                    TRI = mLE if dr_ == 0 else mGE
                    XM = mGT if dr_ == 0 else mLT
                    MSK = mLE if dr_ == 0 else mGE
                    kb.mm([(psum[0][:, 0:256], xn[:, k, tok], wf[0][:, k, :]) for k in range(8)],
                          r=[xnb[j], wfb[0]], w=[psb[0]])
                    kb.mm([(psum[0][:, 256:512], xn[:, k, tok], wf[1][:, k, :]) for k in range(8)],
                          r=[xnb[j], wfb[1]], w=[psb[0]])
                    s_t, s_b = R_s.nxt()
                    kb.op("act", lambda h: h.activation(out=s_t, in_=psum[0][:], func=AF.Sigmoid), r=[psb[0]], w=[s_b])
                    kb.mm([(psum[0][:, 0:256], xn[:, k, tok], wi[0][:, k, :]) for k in range(8)],
                          r=[xnb[j], wib[0]], w=[psb[0]])
                    kb.mm([(psum[0][:, 256:512], xn[:, k, tok], wi[1][:, k, :]) for k in range(8)],
                          r=[xnb[j], wib[1]], w=[psb[0]])
                    v_t, v_b = R_v.nxt()
                    kb.op("act", lambda h: h.copy(out=v_t, in_=psum[0][:]), r=[psb[0]], w=[v_b])
                    if l > 0:
                        kb.op("dve", lambda h: h.tensor_tensor(out=s_t, in0=s_t, in1=lbT[:, dr_, 1, :], op=ALU.mult),
                              r=[s_b, lbb], w=[s_b])
                        kb.op("dve", lambda h: h.tensor_tensor(out=s_t, in0=s_t, in1=lbT[:, dr_, 0, :], op=ALU.add),
                              r=[s_b, lbb], w=[s_b])
                    lf_t, lf_b = R_lf.nxt()
                    kb.op("act", lambda h: h.activation(out=lf_t, in_=s_t, func=AF.Ln), r=[s_b], w=[lf_b])
                    k_t, k_b = R_k.nxt()
                    kb.op("dve", lambda h: h.tensor_scalar(out=k_t, in0=s_t, scalar1=-1.0, scalar2=1.0, op0=ALU.mult,
                                                           op1=ALU.add), r=[s_b], w=[k_b])
                    for hh in range(4):
                        kb.mm([(psum[2][:, hh * 128:(hh + 1) * 128], lf_t[:, hh * 128:(hh + 1) * 128], TRI[:])],
                              r=[lf_b, cbuf], w=[psb[2]])
                    kb.mm([(psum[3][:], XM[:], lf_t)], r=[lf_b, cbuf], w=[psb[3]])
                    eb_t, eb_b = R_eb.nxt()
                    enb_t, enb_b = R_enb.nxt()
                    es_t, es_b = R_es.nxt()
                    p2v = psum[2][:].rearrange("p (h t) -> p h t", h=4)
                    kb.op("act", lambda h: h.activation(out=eb_t, in_=p2v, func=AF.Exp), r=[psb[2]], w=[eb_b])
                    kb.op("act", lambda h: h.activation(out=enb_t, in_=p2v, func=AF.Exp, scale=-1.0), r=[psb[2]],
                          w=[enb_b])
                    kb.op("act", lambda h: h.activation(out=es_t, in_=psum[3][:], func=AF.Exp), r=[psb[3]], w=[es_b])
                    kh_t, kh_b = R_kh.nxt()
                    kb.op("dve", lambda h: h.tensor_tensor(out=kh_t, in0=k_t, in1=es_t, op=ALU.mult), r=[k_b, es_b],
                          w=[kh_b])
                    p4b = psum[4][:].bitcast(BF16)
                    kb.tr([(p4b[:, hh * 128:(hh + 1) * 128], k_t[:, hh * 128:(hh + 1) * 128]) for hh in range(4)],
                          identB[:], r=[k_b, cbuf], w=[psb[4]])
                    kt_t, kt_b = R_kt.nxt()
                    kb.op("dve", lambda h: h.tensor_tensor(out=kt_t, in0=p4b[:, 0:512].rearrange("p (h t) -> p h t", h=4),
                                                           in1=enb_t, op=ALU.mult), r=[psb[4], enb_b], w=[kt_b])
                    qt_t, qt_b = R_qt.nxt()
                    kb.op("dve", lambda h: h.tensor_tensor(out=qt_t, in0=qT[:, :, tok], in1=eb_t, op=ALU.mult),
                          r=[qTb[j], eb_b], w=[qt_b])
                    for hh in range(4):
                        kb.mm([(psum[5][:, hh * 128:(hh + 1) * 128], kt_t[:, hh, :], qt_t[:, hh, :])],
                              r=[kt_b, qt_b], w=[psb[5]])
                    A_t, A_b = R_A.nxt()
                    kb.op("dve", lambda h: h.tensor_tensor(
                        out=A_t, in0=psum[5][:].rearrange("p (h t) -> p h t", h=4),
                        in1=MSK[:].unsqueeze(1).to_broadcast([128, 4, 128]), op=ALU.mult), r=[psb[5], cbuf], w=[A_b])
                    corder = [0, 1, 2, 3] if dr_ == 0 else [3, 2, 1, 0]
                    p_ = Sp[0]
                    kb.op("act", lambda h: h.copy(out=Sbf[:, 0, :, :], in_=S32[p_][:]), r=[S32b[p_]], w=[Sbfb])
                    for ci, c in enumerate(corder):
                        pu = psum[6 + (ci % 2)]
                        pub = psb[6 + (ci % 2)]
                        for hh in range(4):
                            kw = {"tile_position": (96, 0)} if c == 3 else {}
                            kb.mm([(pu[:, hh * 128:(hh + 1) * 128], kh_t[c * 32:(c + 1) * 32, hh * 128:(hh + 1) * 128],
                                    v_t[c * 32:(c + 1) * 32, hh * 128:(hh + 1) * 128], kw)], r=[kh_b, v_b], w=[pub])
                        tcol = c * 32 + (31 if dr_ == 0 else 0)
                        src, dst = Sp[0], 1 - Sp[0]
                        for hh in range(4):
                            kb.op("dve", lambda h, hh=hh, pu=pu, tcol=tcol, src=src, dst=dst: h.scalar_tensor_tensor(
                                out=S32[dst][:, hh, :], in0=S32[src][:, hh, :], scalar=eb_t[:, hh, tcol:tcol + 1],
                                in1=pu[:, hh * 128:(hh + 1) * 128], op0=ALU.mult, op1=ALU.add),
                                r=[S32b[src], eb_b, pub], w=[S32b[dst]], n=128)
                        Sp[0] = dst
                        if ci < 3:
                            kb.op("act", lambda h, ci=ci, dst=dst: h.copy(out=Sbf[:, ci + 1, :, :], in_=S32[dst][:]),
                                  r=[S32b[dst]], w=[Sbfb])
                    for hh in range(4):
                        steps = [(psum[1][:, hh * 128:(hh + 1) * 128], v_t[:, hh * 128:(hh + 1) * 128], A_t[:, hh, :])]
                        for ci, c in enumerate(corder):
                            steps.append((psum[1][:, hh * 128 + c * 32:hh * 128 + (c + 1) * 32], Sbf[:, ci, hh, :],
                                          qt_t[:, hh, c * 32:(c + 1) * 32]))
                        kb.mm(steps, r=[v_b, A_b, Sbfb, qt_b], w=[psb[1]])
                    return psum[1][:].rearrange("p (h t) -> p h t", h=4), psb[1]

                def state_init(dr_, s):
                    p_ = Sp[0]
                    if job["cache"]:
                        kb.dma("sp", S32[p_][:], dr["st_in"][l, dr_].rearrange("h d v -> d h v"), w=[S32b[p_]])
                    else:
                        kb.op("dve", lambda h: h.memset(S32[p_][:], 0.0), w=[S32b[p_]])

                def state_out(dr_, s):
                    p_ = Sp[0]
                    if not job["cache"]:
                        kb.dma("sp", dr["st_out"][s, l, dr_].rearrange("h d v -> d h v"), S32[p_][:], r=[S32b[p_]])

                for j in range(nb):
                    blk = slice(j * 512, (j + 1) * 512)
                    prep_xn(j, 0, 1, xn, xnb, R_xn)
                    for half in range(2):
                        wv, wb = wload(win[:, :, OFF_Q + half * 256:OFF_Q + (half + 1) * 256], 8, 256)
                        for cc in range(2):
                            hh = half * 2 + cc
                            pq = psum[6 + cc]
                            kb.mm([(pq[:], wv[:, k, cc * 128:(cc + 1) * 128], xn[:, k, blk]) for k in range(8)],
                                  r=[wb, xnb[j]], w=[psb[6 + cc]])
                            t_t, t_b = Rot_tmp.nxt()
                            kb.op("act", lambda h, pq=pq, t_t=t_t: h.activation(out=t_t, in_=pq[:], func=AF.Silu),
                                  r=[psb[6 + cc]], w=[t_b])
                            kb.op("dve", lambda h, hh=hh, t_t=t_t: h.tensor_scalar(
                                out=qT[:, hh, blk], in0=t_t, scalar1=128.0 ** -0.5, scalar2=None, op0=ALU.mult),
                                r=[t_b], w=[qTb[j]])
                    wf, wfb, wi, wib = [], [], [], []
                    for half in range(2):
                        a, b_ = wload(win[:, :, OFF_FF + half * 256:OFF_FF + (half + 1) * 256], 8, 256)
                        wf.append(a)
                        wfb.append(b_)
                    for half in range(2):
                        a, b_ = wload(win[:, :, OFF_I + half * 256:OFF_I + (half + 1) * 256], 8, 256)
                        wi.append(a)
                        wib.append(b_)
                    for sub in range(4):
                        tb = j * 4 + sub
                        if (tb * 128) % L == 0:
                            state_init(0, (tb * 128) // L)
                        po, pob = hgrn_step(0, tb, wf, wfb, wi, wib)
                        of_t, of_b = R_of.nxt()
                        kb.op("act", lambda h, po=po, of_t=of_t: h.copy(out=of_t, in_=po), r=[pob], w=[of_b])
                        kb.dma("sp", ofw_d[:, :, tb * 128:(tb + 1) * 128], of_t, r=[of_b], w=[ofwb[tb]])
                        if ((tb + 1) * 128) % L == 0:
                            state_out(0, (tb * 128) // L)
                chk("fwd")
                for j in reversed(range(nb)):
                    blk = slice(j * 512, (j + 1) * 512)
                    prep_xn(j, 0, 1, xn, xnb, R_xn)
                    wf, wfb, wi, wib = [], [], [], []
                    for half in range(2):
                        a, b_ = wload(win[:, :, OFF_FB + half * 256:OFF_FB + (half + 1) * 256], 8, 256)
                        wf.append(a)
                        wfb.append(b_)
                    for half in range(2):
                        a, b_ = wload(win[:, :, OFF_I + half * 256:OFF_I + (half + 1) * 256], 8, 256)
                        wi.append(a)
                        wib.append(b_)
                    os_t, os_b = R_osum.nxt()
                    for sub in reversed(range(4)):
                        tb = j * 4 + sub
                        if ((tb + 1) * 128) % L == 0:
                            state_init(1, (tb * 128) // L)
                        of_t, of_b = R_of.nxt()
                        kb.dma("sp", of_t, ofw_d[:, :, tb * 128:(tb + 1) * 128], r=[ofwb[tb]], w=[of_b])
                        po, pob = hgrn_step(1, tb, wf, wfb, wi, wib)
                        kb.op("dve", lambda h, po=po, sub=sub, of_t=of_t: h.tensor_tensor(
                            out=os_t[:, :, sub * 128:(sub + 1) * 128], in0=po, in1=of_t,
                            op=ALU.add), r=[pob, of_b], w=[os_b])
                        if (tb * 128) % L == 0:
                            state_out(1, (tb * 128) // L)
                    sg_t, sg_b = R_sg.nxt()
                    for half in range(2):
                        wv, wb = wload(win[:, :, OFF_G + half * 256:OFF_G + (half + 1) * 256], 8, 256)
                        for cc in range(2):
                            hh = half * 2 + cc
                            pq = psum[6 + cc]
                            kb.mm([(pq[:], wv[:, k, cc * 128:(cc + 1) * 128], xn[:, k, blk]) for k in range(8)],
                                  r=[wb, xnb[j]], w=[psb[6 + cc]])
                            kb.op("act", lambda h, pq=pq, hh=hh: h.activation(out=sg_t[:, hh, :], in_=pq[:], func=AF.Silu),
                                  r=[psb[6 + cc]], w=[sg_b])
                    oA_t, oA_b = R_oA.nxt()
                    for hh in range(4):
                        q_t, q_b = Rot_sq.nxt()
                        kb.op("act", lambda h, hh=hh, q_t=q_t: h.activation(out=q_t, in_=os_t[:, hh, :], func=AF.Square),
                              r=[os_b], w=[q_b])
                        pn = psum[4 + (hh % 2)]
                        pnb = psb[4 + (hh % 2)]
                        kb.mm([(pn[:], onesB[:], q_t)], r=[q_b, cbuf], w=[pnb])
                        r_t, r_b = Rot_rstd.nxt()
                        kb.op("act", lambda h, pn=pn, r_t=r_t: h.activation(out=r_t, in_=pn[:], func=AF.Ln, bias=epsT[:],
                                                                            scale=1.0 / 128), r=[pnb, cbuf], w=[r_b])
                        kb.op("act", lambda h, r_t=r_t: h.activation(out=r_t, in_=r_t, func=AF.Exp, scale=-0.5), r=[r_b], w=[r_b])
                        kb.op("dve", lambda h, hh=hh, r_t=r_t: h.scalar_tensor_tensor(
                            out=r_t, in0=os_t[:, hh, :], scalar=fv[:, l, 64:65], in1=r_t, op0=ALU.mult, op1=ALU.mult),
                            r=[os_b, fvb, r_b], w=[r_b])
                        kb.op("dve", lambda h, hh=hh, r_t=r_t, oA_t=oA_t: h.tensor_tensor(out=oA_t[:, hh, :], in0=r_t,
                                                                                          in1=sg_t[:, hh, :], op=ALU.mult),
                              r=[r_b, sg_b], w=[oA_b])
                    kb.dma("sp", oA_d[:, :, blk], oA_t, r=[oA_b], w=[oAb[j]])

                chk("bwd")
                kb.barrier()
                cv.off = mark_x
                Rot_sq = Rot("sq", [128, 512], BF16, 2)
                Rot_rstd = Rot("rstd", [128, 512], F32, 2)
                Rot_tmp = Rot("tmp", [128, 512], F32, 2)
                cqn = cv.get([128, 3, T], BF16)
                cqnb = [Buf(f"cqn{j}") for j in range(nb)]
                ckvT = cv.get([128, 2, nseq * Tk], BF16)
                ckvb = Buf("ckvT")
                krT = cv.get([96, nseq * Tk], BF16)
                krb = Buf("krT")
                mark_p3 = cv.off
                R_c = Rot("cqf", [128, 3, 512], F32, 1)
                R_rp = Rot("rp", [96, 512], F32, 2)
                R_xb = Rot("xb", [96, 512], BF16, 2)
                R_co = Rot("co", [128, 288], F32, 2)
                R_ss = Rot("ss", [128, 2], F32, 2)
                R_cs = Rot("cs", [96, 2, 512], F32, 1)
                for j in range(nb):
                    blk = slice(j * 512, (j + 1) * 512)
                    prep_xn(j, 0, 1, xn, xnb, R_xn)
                    s0 = (j * 512) // L
                    nsb = max(1, 512 // L)
                    c_t, c_b = R_c.nxt()
                    wva0, wba0 = wload(win[:, :, OFF_CQ:OFF_CQ + 256], 8, 256)
                    chk("p3w")
                    wva1, wba1 = wload(win[:, :, OFF_CQ + 256:OFF_CQ + 384], 8, 128)
                    chk("p3x")
                    for cc in range(3):
                        if cc == 1:
                            chk("p3y")
                        pq = psum[cc % 2]
                        pqb = psb[cc % 2]
                        wva, wba, co_ = (wva0, wba0, cc) if cc < 2 else (wva1, wba1, 0)
                        kb.mm([(pq[:], wva[:, k, co_ * 128:(co_ + 1) * 128], xn[:, k, blk]) for k in range(8)],
                              r=[wba, xnb[j]], w=[pqb])
                        q_t, q_b = Rot_sq.nxt()
                        kb.op("act", lambda h, pq=pq, q_t=q_t: h.activation(out=q_t, in_=pq[:], func=AF.Square), r=[pqb],
                              w=[q_b])
                        kb.op("dve", lambda h, pq=pq, cc=cc: h.tensor_copy(out=c_t[:, cc, :], in_=pq[:]), r=[pqb], w=[c_b])
                        kb.mm([(psum[2][:], onesB[:], q_t)], r=[q_b, cbuf], w=[psb[2]], start=(cc == 0), stop=(cc == 2))
                    r_t, r_b = Rot_rstd.nxt()
                    kb.op("act", lambda h: h.activation(out=r_t, in_=psum[2][:], func=AF.Ln, bias=epsT[:],
                                                        scale=1.0 / C_Q), r=[psb[2], cbuf], w=[r_b])
                    kb.op("act", lambda h: h.activation(out=r_t, in_=r_t, func=AF.Exp, scale=-0.5), r=[r_b], w=[r_b])
                    for cc in range(3):
                        kb.op("dve", lambda h, cc=cc: h.scalar_tensor_tensor(
                            out=cqn[:, cc, blk], in0=c_t[:, cc, :], scalar=fv[:, l, 77 + cc:78 + cc], in1=r_t,
                            op0=ALU.mult, op1=ALU.mult), r=[c_b, fvb, r_b], w=[cqnb[j]])
                    chk("p3a")
                    c_t, c_b = R_c.nxt()
                    wvk, wbk = wload(win[:, :, OFF_CKV:OFF_CKV + 256], 8, 256)
                    for cc in range(2):
                        pq = psum[cc % 2]
                        pqb = psb[cc % 2]
                        kb.mm([(pq[:], wvk[:, k, cc * 128:(cc + 1) * 128], xn[:, k, blk]) for k in range(8)],
                              r=[wbk, xnb[j]], w=[pqb])
                        q_t, q_b = Rot_sq.nxt()
                        kb.op("act", lambda h, pq=pq, q_t=q_t: h.activation(out=q_t, in_=pq[:], func=AF.Square), r=[pqb],
                              w=[q_b])
                        kb.op("dve", lambda h, pq=pq, cc=cc: h.tensor_copy(out=c_t[:, cc, :], in_=pq[:]), r=[pqb], w=[c_b])
                        kb.mm([(psum[2][:], onesB[:], q_t)], r=[q_b, cbuf], w=[psb[2]], start=(cc == 0), stop=(cc == 1))
                    r_t, r_b = Rot_rstd.nxt()
                    kb.op("act", lambda h: h.activation(out=r_t, in_=psum[2][:], func=AF.Ln, bias=epsT[:],
                                                        scale=1.0 / C_KV), r=[psb[2], cbuf], w=[r_b])
                    kb.op("act", lambda h: h.activation(out=r_t, in_=r_t, func=AF.Exp, scale=-0.5), r=[r_b], w=[r_b])
                    for cc in range(2):
                        for sb_ in range(nsb):
                            ln = 512 // nsb
                            ko = (s0 + sb_) * Tk + ((j * 512 + sb_ * ln) % L)
                            kb.op("dve", lambda h, cc=cc, sb_=sb_, ln=ln, ko=ko: h.scalar_tensor_tensor(
                                out=ckvT[:, cc, ko:ko + ln], in0=c_t[:, cc, sb_ * ln:(sb_ + 1) * ln],
                                scalar=fv[:, l, 80 + cc:81 + cc], in1=r_t[:, sb_ * ln:(sb_ + 1) * ln],
                                op0=ALU.mult, op1=ALU.mult), r=[c_b, fvb, r_b], w=[ckvb])
                    chk("p3b")
                    wvr, wbr = wload(win[:, :, OFF_CKV + 192:OFF_CKV + 288], 8, 96)
                    kb.mm([(psum[3][0:96, :], wvr[:, k, :], xn[:, k, blk]) for k in range(8)], r=[wbr, xnb[j]], w=[psb[3]])
                    if job["rope"]:
                        x_t, x_b = R_xb.nxt()
                        kb.op("act", lambda h: h.copy(out=x_t[64:96, :], in_=psum[3][64:96, :]), r=[psb[3]], w=[x_b])
                        kb.op("dve", lambda h: h.memset(x_t[0:64, :], 0.0), w=[x_b])
                        kb.mm([(psum[4][0:96, :], P96[:], x_t[:])], r=[x_b, cbuf], w=[psb[4]])
                        cs_t, cs_b = R_cs.nxt()
                        kb.dma("sp", cs_t[64:96, 0, :], dr["ropeC"][64:96, blk], w=[cs_b])
                        kb.dma("sp", cs_t[64:96, 1, :], dr["ropeS"][64:96, blk], w=[cs_b])
                        p_t, p_b = R_rp.nxt()
                        kb.op("dve", lambda h: h.tensor_tensor(out=p_t[64:96, :], in0=psum[4][64:96, :],
                                                               in1=cs_t[64:96, 1, :], op=ALU.mult), r=[psb[4], cs_b],
                              w=[p_b])
                        p2_t, p2_b = R_rp.nxt()
                        kb.op("dve", lambda h: h.tensor_tensor(out=p2_t[64:96, :], in0=psum[3][64:96, :],
                                                               in1=cs_t[64:96, 0, :], op=ALU.mult), r=[psb[3], cs_b],
                              w=[p2_b])
                        kb.op("dve", lambda h: h.tensor_tensor(out=krT[64:96, j * 512:(j + 1) * 512], in0=p_t[64:96, :],
                                                               in1=p2_t[64:96, :], op=ALU.add), r=[p_b, p2_b], w=[krb])
                    else:
                        for sb_ in range(nsb):
                            ln = 512 // nsb
                            ko = (s0 + sb_) * Tk + ((j * 512 + sb_ * ln) % L)
                            kb.op("act", lambda h, sb_=sb_, ln=ln, ko=ko: h.copy(
                                out=krT[64:96, ko:ko + ln], in_=psum[3][64:96, sb_ * ln:(sb_ + 1) * ln]),
                                r=[psb[3]], w=[krb])
                    chk("p3c")
                    if not job["cache"]:
                        for sub in range(4):
                            tb = j * 4 + sub
                            tok = slice(tb * 128, (tb + 1) * 128)
                            pc = psum[5 + (sub % 2)]
                            pcb = psb[5 + (sub % 2)]
                            kb.mm([(pc[:, 0:256], xn[:, k, tok], wvk[:, k, :]) for k in range(8)], r=[xnb[j], wbk],
                                  w=[pcb])
                            kb.mm([(pc[:, 256:288], xn[:, k, tok], wvr[:, k, 64:96]) for k in range(8)], r=[xnb[j], wbr],
                                  w=[pcb])
                            co_t, co_b = R_co.nxt()
                            ss_t, ss_b = R_ss.nxt()
                            kb.op("act", lambda h, pc=pc, co_t=co_t, ss_t=ss_t: h.activation(
                                out=co_t[:, 0:256], in_=pc[:, 0:256], func=AF.Square, accum_out=ss_t[:, 0:1]),
                                r=[pcb], w=[co_b, ss_b])
                            kb.op("act", lambda h, ss_t=ss_t: h.activation(out=ss_t[:, 1:2], in_=ss_t[:, 0:1], func=AF.Ln,
                                                                           bias=epsT[:], scale=1.0 / C_KV),
                                  r=[ss_b, cbuf], w=[ss_b])
                            kb.op("act", lambda h, ss_t=ss_t: h.activation(out=ss_t[:, 1:2], in_=ss_t[:, 1:2], func=AF.Exp,
                                                                           scale=-0.5), r=[ss_b], w=[ss_b], n=1)
                            kb.op("dve", lambda h, pc=pc, co_t=co_t, ss_t=ss_t: h.scalar_tensor_tensor(
                                out=co_t[:, 0:256], in0=pc[:, 0:256], scalar=ss_t[:, 1:2], in1=kvnB[:, l, :],
                                op0=ALU.mult, op1=ALU.mult), r=[pcb, ss_b, kvnb, co_b], w=[co_b])
                            kb.op("act", lambda h, pc=pc, co_t=co_t: h.copy(out=co_t[:, 256:288], in_=pc[:, 256:288]),
                                  r=[pcb], w=[co_b])
                            s_i = (tb * 128) // L
                            to = (tb * 128) % L
                            kb.dma("sp", dr["cache_out"][s_i, l, to:to + 128, :], co_t[:], r=[co_b])
                if job["cache"]:
                    cst = cv.get([128, 2, 288], F32)
                    cstb = Buf("cst")
                    kb.dma("sp", cst[:], dr["cache_in"][l].rearrange("(tb p) f -> p tb f", p=128), w=[cstb])
                    for tb in range(2):
                        kb.tr([(psum[0][:, cc * 128:(cc + 1) * 128], cst[:, tb, cc * 128:(cc + 1) * 128]) for cc in range(2)],
                              identF[:], r=[cstb, cbuf], w=[psb[0]])
                        kb.op("dve", lambda h, tb=tb: h.tensor_copy(
                            out=ckvT[:, :, L + tb * 128:L + (tb + 1) * 128],
                            in_=psum[0][:, 0:256].rearrange("p (c t) -> p c t", c=2)), r=[psb[0]], w=[ckvb])
                        kb.tr([(psum[1][0:96, 0:128], cst[:, tb, 192:288])], identF[:], r=[cstb, cbuf], w=[psb[1]])
                        kb.op("dve", lambda h, tb=tb: h.tensor_copy(out=krT[64:96, L + tb * 128:L + (tb + 1) * 128],
                                                                    in_=psum[1][64:96, 0:128]), r=[psb[1]], w=[krb])

                chk("p3")
                kb.barrier(engines=("pe", "act", "dve", "sp", "pool"))
                mark_a = cv.off
                cv.off = 0
                oC = cv.get([128, 4, T], BF16)
                assert cv.off <= mark_x
                oCb = [Buf(f"oC{j}") for j in range(nb)]
                cv.off = mark_p3
                Wkv = cv.get([128, 2, 1024], BF16)
                Wq = cv.get([128, 3, 768], BF16)
                wab = Buf("Wattn")
                kb.dma("pool", Wkv, w3("w_kv_up", l), w=[wab])
                kb.dma("pool", Wq, w3("w_q_up", l), w=[wab])
                vaug = [cv.get([128, nkb, 128], BF16) for _ in range(2)]
                vaugb = [Buf("vaug0"), Buf("vaug1")]
                rden = [cv.get([128, 512], F32) for _ in range(2)]
                rdenb = [Buf("rden0"), Buf("rden1")]
                kb.op("dve", lambda h: h.memset(vaug[0][:, :, 64:128], 1.0), w=[vaugb[0]])
                kb.op("dve", lambda h: h.memset(vaug[1][:, :, 0:64], 1.0), w=[vaugb[1]])
                kb.op("dve", lambda h: h.memset(rden[0][:], 0.0), w=[rdenb[0]])
                kb.op("dve", lambda h: h.memset(rden[1][:], 0.0), w=[rdenb[1]])
                R_kT = Rot("kT", [96, Tk], BF16, 2)
                R_q = Rot("qh", [96, 512], BF16, 2)
                R_pT = Rot("pT", [128, 512], BF16, 3)
                R_rb = Rot("rb", [128, 512], F32, 1)
                R_rp = Rot("rp2", [96, 512], F32, 2)
                R_cs = Rot("cs2", [96, 2, 512], F32, 1)
                scale_qk = 96.0 ** -0.5
                tasks = [(s, hh, qb) for s in range(nseq) for hh in range(8) for qb in range(L // QN)]
                headc, qc = {}, {}

                def prep_head(s, hh):
                    par = hh % 2
                    kbase = s * Tk
                    kT_t, kT_b = R_kT.nxt()
                    kb.op("dve", lambda h: h.tensor_copy(out=kT_t[64:96, :], in_=krT[64:96, kbase:kbase + Tk]),
                          r=[krb], w=[kT_b])
                    for k5 in range(0, Tk, 512):
                        n5 = min(512, Tk - k5)
                        kb.mm([(psum[7][0:64, 0:n5], Wkv[:, c, hh * 128:hh * 128 + 64],
                                ckvT[:, c, kbase + k5:kbase + k5 + n5]) for c in range(2)], r=[ckvb, wab], w=[psb[7]])
                        kb.op("act", lambda h, k5=k5, n5=n5: h.copy(out=kT_t[0:64, k5:k5 + n5], in_=psum[7][0:64, 0:n5]),
                              r=[psb[7]], w=[kT_b])
                    voff = 0 if par == 0 else 64
                    for kb8 in range(0, nkb, 8):
                        n8 = min(8, nkb - kb8)
                        for q8 in range(n8):
                            kblk = kb8 + q8
                            kb.mm([(psum[7][:, q8 * 64:(q8 + 1) * 64],
                                    ckvT[:, c, kbase + kblk * 128:kbase + (kblk + 1) * 128],
                                    Wkv[:, c, hh * 128 + 64:hh * 128 + 128]) for c in range(2)], r=[ckvb, wab],
                                  w=[psb[7]])
                        kb.op("dve", lambda h, kb8=kb8, n8=n8: h.tensor_copy(
                            out=vaug[par][:, kb8:kb8 + n8, voff:voff + 64],
                            in_=psum[7][:, 0:n8 * 64].rearrange("p (a b) -> p a b", a=n8)), r=[psb[7]], w=[vaugb[par]])
                    return kT_t, kT_b

                def prep_q(s, hh, qb):
                    q0 = s * L + qb * QN
                    jq = q0 // 512
                    q_t, q_b = R_q.nxt()
                    kb.mm([(psum[7][0:96, 0:QN], Wq[:, c, hh * 96:(hh + 1) * 96], cqn[:, c, q0:q0 + QN])
                           for c in range(3)], r=[wab, cqnb[jq]], w=[psb[7]])
                    kb.op("act", lambda h: h.activation(out=q_t[:, 0:QN], in_=psum[7][0:96, 0:QN], func=AF.Identity,
                                                        scale=scale_qk), r=[psb[7]], w=[q_b])
                    if job["rope"]:
                        kb.mm([(psum[3][0:96, 0:QN], P96[:], q_t[:, 0:QN])], r=[q_b, cbuf], w=[psb[3]])
                        cs_t, cs_b = R_cs.nxt()
                        kb.dma("sp", cs_t[64:96, 0, 0:QN], dr["ropeC"][64:96, qb * QN:(qb + 1) * QN], w=[cs_b])
                        kb.dma("sp", cs_t[64:96, 1, 0:QN], dr["ropeS"][64:96, qb * QN:(qb + 1) * QN], w=[cs_b])
                        p_t, p_b = R_rp.nxt()
                        kb.op("dve", lambda h: h.tensor_tensor(out=p_t[64:96, 0:QN], in0=psum[3][64:96, 0:QN],
                                                               in1=cs_t[64:96, 1, 0:QN], op=ALU.mult),
                              r=[psb[3], cs_b], w=[p_b])
                        p2_t, p2_b = R_rp.nxt()
                        kb.op("dve", lambda h: h.scalar_tensor_tensor(
                            out=p2_t[64:96, 0:QN], in0=psum[7][64:96, 0:QN], scalar=scale_qk, in1=cs_t[64:96, 0, 0:QN],
                            op0=ALU.mult, op1=ALU.mult), r=[psb[7], cs_b], w=[p2_b])
                        kb.op("dve", lambda h: h.tensor_tensor(out=q_t[64:96, 0:QN], in0=p_t[64:96, 0:QN],
                                                               in1=p2_t[64:96, 0:QN], op=ALU.add), r=[p_b, p2_b], w=[q_b])
                    return q_t, q_b

                def ensure(i):
                    s, hh, qb = tasks[i]
                    if (s, hh) not in headc:
                        headc[(s, hh)] = prep_head(s, hh)
                    if i not in qc:
                        qc[i] = prep_q(s, hh, qb)

                def attn_main(i):
                    s, hh, qb = tasks[i]
                    par = hh % 2
                    hp = hh // 2
                    q0 = s * L + qb * QN
                    jq = q0 // 512
                    kT_t, kT_b = headc[(s, hh)]
                    q_t, q_b = qc.pop(i)
                    acc, accb = psum[i % 2], psb[i % 2]

                    def qk(kblk):
                        pi = 4 + (kblk % 3)
                        kb.mm([(psum[pi][:, 0:QN], kT_t[:, kblk * 128:(kblk + 1) * 128], q_t[:, 0:QN])], r=[kT_b, q_b],
                              w=[psb[pi]])

                    qk(0)
                    for kblk in range(nkb):
                        if kblk + 1 < nkb:
                            qk(kblk + 1)
                        pi = 4 + (kblk % 3)
                        pT_t, pT_b = R_pT.nxt()
                        kb.op("act", lambda h, pi=pi, pT_t=pT_t: h.activation(out=pT_t[:, 0:QN], in_=psum[pi][:, 0:QN],
                                                                             func=AF.Exp), r=[psb[pi]], w=[pT_b])
                        kb.mm([(acc[:, 0:QN], vaug[par][:, kblk, :], pT_t[:, 0:QN])], r=[vaugb[par], pT_b], w=[accb],
                              start=(kblk == 0), stop=(kblk == nkb - 1))
                    nrows = slice(0, 64) if par == 0 else slice(64, 128)
                    drows = slice(64, 128) if par == 0 else slice(0, 64)
                    kb.op("act", lambda h: h.activation(out=rden[par][drows, 0:QN], in_=acc[drows, 0:QN], func=AF.Ln),
                          r=[accb], w=[rdenb[par]])
                    kb.op("act", lambda h: h.activation(out=rden[par][drows, 0:QN], in_=rden[par][drows, 0:QN],
                                                        func=AF.Exp, scale=-1.0), r=[rdenb[par]], w=[rdenb[par]])
                    kb.mm([(psum[2][:, 0:QN], swapM[:], rden[par][:, 0:QN])], r=[cbuf, rdenb[par]], w=[psb[2]])
                    rb_t, rb_b = R_rb.nxt()
                    kb.op("act", lambda h: h.copy(out=rb_t[nrows, 0:QN], in_=psum[2][nrows, 0:QN]), r=[psb[2]], w=[rb_b])
                    kb.op("dve", lambda h: h.tensor_tensor(out=oC[nrows, hp, q0:q0 + QN], in0=acc[nrows, 0:QN],
                                                           in1=rb_t[nrows, 0:QN], op=ALU.mult), r=[accb, rb_b],
                          w=[oCb[jq]])

                ensure(0)
                for i in range(len(tasks)):
                    if i + 1 < len(tasks):
                        ensure(i + 1)
                    attn_main(i)

                chk("p4")
                kb.barrier()
                cv.off = 0
                oC2 = cv.get([128, 4, T], BF16)
                xn = cv.get([128, 8, T], BF16)
                xnb = [Buf(f"xnm{j}") for j in range(nb)]
                Rot_sq = Rot("sq", [128, 512], BF16, 2)
                Rot_rstd = Rot("rstd", [128, 512], F32, 2)
                Rot_tmp = Rot("tmp", [128, 512], F32, 2)
                xh = cv.get([128, 8, 2], BF16)
                xhb = Buf("xh")
                R_e = Rot("e", [128, 514], F32, 2)
                R_acc = Rot("acc", [128, 512], F32, 2)
                oBt = cv.get([128, 4, 512], BF16)
                oBb = Buf("oB")
                R_a2 = Rot("a2", [128, 2, 512], F32, 1)
                R_oAl = Rot("oAl", [128, 4, 512], BF16, 1)
                hB = cv.get([128, 8, 512], BF16)
                hBb = Buf("hB")
                R_g = Rot("g", [128, 512], F32, 2)

                def halo_cols(xsrc, xsb, j):
                    t0 = j * 512
                    if t0 % L == 0:
                        kb.op("dve", lambda h: h.memset(xh[:, :, 0:1], 0.0), w=[xhb])
                    else:
                        kb.op("dve", lambda h: h.tensor_copy(out=xh[:, :, 0:1], in_=xsrc[:, :, t0 - 1:t0]),
                              r=[xsb[j - 1]], w=[xhb])
                    if (t0 + 512) % L == 0:
                        kb.op("dve", lambda h: h.memset(xh[:, :, 1:2], 0.0), w=[xhb])
                    else:
                        kb.op("dve", lambda h: h.tensor_copy(out=xh[:, :, 1:2], in_=xsrc[:, :, t0 + 512:t0 + 513]),
                              r=[xsb[j + 1]], w=[xhb])

                def conv3(e_t, e_b, acc_t, acc_b, w0, w1, w2, wbuf):
                    kb.op("act", lambda h: h.activation(out=acc_t, in_=e_t[:, 1:513], func=AF.Identity, scale=w1),
                          r=[e_b, wbuf], w=[acc_b])
                    seg = min(L, 512)
                    for a in range(0, 512, seg):
                        lo = a if a == 0 else a + 1
                        kb.op("dve", lambda h, lo=lo, a=a: h.scalar_tensor_tensor(
                            out=acc_t[:, lo:a + seg], in0=e_t[:, lo:a + seg], scalar=w0, in1=acc_t[:, lo:a + seg],
                            op0=ALU.mult, op1=ALU.add), r=[e_b, wbuf, acc_b], w=[acc_b])
                        hi = a + seg if a + seg == 512 else a + seg - 1
                        kb.op("dve", lambda h, hi=hi, a=a: h.scalar_tensor_tensor(
                            out=acc_t[:, a:hi], in0=e_t[:, a + 2:hi + 2], scalar=w2, in1=acc_t[:, a:hi],
                            op0=ALU.mult, op1=ALU.add), r=[e_b, wbuf, acc_b], w=[acc_b])

                norm_all(0, 1, xn, xnb)
                for j in range(nb):
                    blk = slice(j * 512, (j + 1) * 512)
                    halo_cols(xn, xnb, j)
                    for half in range(2):
                        wvc, wbc = wload(win[:, :, OFF_BC + half * 256:OFF_BC + (half + 1) * 256], 8, 256)
                        wvh, wbh = wload(win[:, :, OFF_BH + half * 256:OFF_BH + (half + 1) * 256], 8, 256)
                        wvb, wbb = wload(win[:, :, OFF_BB + half * 256:OFF_BB + (half + 1) * 256], 8, 256)
                        for cc in range(2):
                            g = half * 2 + cc
                            cs_ = slice(cc * 128, (cc + 1) * 128)
                            kb.mm([(psum[0][:], wvc[:, k, cs_], xn[:, k, blk]) for k in range(8)], r=[wbc, xnb[j]],
                                  w=[psb[0]])
                            kb.mm([(psum[1][:], wvh[:, k, cs_], xn[:, k, blk]) for k in range(8)], r=[wbh, xnb[j]],
                                  w=[psb[1]])
                            kb.mm([(psum[2][:, 0:2], wvc[:, k, cs_], xh[:, k, :]) for k in range(8)], r=[wbc, xhb],
                                  w=[psb[2]])
                            kb.mm([(psum[2][:, 2:4], wvh[:, k, cs_], xh[:, k, :]) for k in range(8)], r=[wbh, xhb],
                                  w=[psb[2]])
                            kb.mm([(psum[3][:], wvb[:, k, cs_], xn[:, k, blk]) for k in range(8)], r=[wbb, xnb[j]],
                                  w=[psb[3]])
                            t_t, t_b = Rot_tmp.nxt()
                            kb.op("act", lambda h, t_t=t_t: h.copy(out=t_t, in_=psum[0][:]), r=[psb[0]], w=[t_b])
                            e_t, e_b = R_e.nxt()
                            kb.op("dve", lambda h, t_t=t_t, e_t=e_t: h.tensor_tensor(out=e_t[:, 1:513], in0=psum[1][:],
                                                                                    in1=t_t, op=ALU.mult),
                                  r=[psb[1], t_b], w=[e_b])
                            t2_t, t2_b = Rot_tmp.nxt()
                            kb.op("act", lambda h, t2_t=t2_t: h.copy(out=t2_t[:, 0:2], in_=psum[2][:, 0:2]), r=[psb[2]],
                                  w=[t2_b])
                            kb.op("dve", lambda h, t2_t=t2_t, e_t=e_t: h.tensor_tensor(
                                out=e_t[:, 0:514:513], in0=psum[2][:, 2:4], in1=t2_t[:, 0:2], op=ALU.mult),
                                r=[psb[2], t2_b], w=[e_b])
                            acc_t, acc_b = R_acc.nxt()
                            conv3(e_t, e_b, acc_t, acc_b, fv[:, l, 65 + g:66 + g], fv[:, l, 69 + g:70 + g],
                                  fv[:, l, 73 + g:74 + g], fvb)
                            kb.op("dve", lambda h, g=g, acc_t=acc_t: h.tensor_tensor(out=oBt[:, g, :], in0=psum[3][:],
                                                                                    in1=acc_t, op=ALU.mult),
                                  r=[psb[3], acc_b], w=[oBb])
                    oAl_t, oAl_b = R_oAl.nxt()
                    kb.dma("sp", oAl_t, oA_d[:, :, blk], r=[oAb[j]], w=[oAl_b])
                    srcs = (("w_o_hgrn", oAl_t, oAl_b), ("w_o_conv", oBt, oBb), ("w_o_mla", oC2[:, :, blk], oCb[j]))
                    for o2 in range(0, 8, 2):
                        a2_t, a2_b = R_a2.nxt()
                        for br, (wname, osrc, osb) in enumerate(srcs):
                            wvo, wbo = wload(w3(wname, l)[:, :, o2 * 128:(o2 + 2) * 128], 4, 256)
                            gc0 = OFF_GATE + br * 1024 + o2 * 128
                            wvg, wbg = wload(win[:, :, gc0:gc0 + 256], 8, 256)
                            for o1 in range(2):
                                oc = o2 + o1
                                py, pyb = psum[4 + o1], psb[4 + o1]
                                pg, pgb = psum[6 + o1], psb[6 + o1]
                                kb.mm([(py[:], wvo[:, k, o1 * 128:(o1 + 1) * 128], osrc[:, k, :]) for k in range(4)],
                                      r=[wbo, osb], w=[pyb])
                                kb.mm([(pg[:], wvg[:, k, o1 * 128:(o1 + 1) * 128], xn[:, k, blk]) for k in range(8)],
                                      r=[wbg, xnb[j]], w=[pgb])
                                g_t, g_b = R_g.nxt()
                                kb.op("act", lambda h, pg=pg, g_t=g_t: h.activation(out=g_t, in_=pg[:], func=AF.Sigmoid),
                                      r=[pgb], w=[g_b])
                                if br == 0:
                                    kb.op("dve", lambda h, o1=o1, py=py, g_t=g_t, a2_t=a2_t: h.tensor_tensor(
                                        out=a2_t[:, o1, :], in0=py[:], in1=g_t, op=ALU.mult), r=[pyb, g_b], w=[a2_b])
                                else:
                                    kb.op("dve", lambda h, py=py, g_t=g_t: h.tensor_tensor(
                                        out=g_t, in0=py[:], in1=g_t, op=ALU.mult), r=[pyb, g_b], w=[g_b])
                                    if br == 1:
                                        kb.op("dve", lambda h, o1=o1, g_t=g_t, a2_t=a2_t: h.tensor_tensor(
                                            out=a2_t[:, o1, :], in0=a2_t[:, o1, :], in1=g_t, op=ALU.add),
                                            r=[a2_b, g_b], w=[a2_b])
                                    else:
                                        kb.op("dve", lambda h, oc=oc, o1=o1, g_t=g_t, a2_t=a2_t: h.tensor_tensor(
                                            out=hB[:, oc, :], in0=a2_t[:, o1, :], in1=g_t, op=ALU.add),
                                            r=[a2_b, g_b], w=[hBb])
                    wo3 = w3("w_out", l)
                    for o2 in range(0, 8, 2):
                        wvo, wbo = wload(wo3[:, :, o2 * 128:(o2 + 2) * 128], 8, 256)
                        for o1 in range(2):
                            oc = o2 + o1
                            py, pyb = psum[oc % 2], psb[oc % 2]
                            kb.mm([(py[:], wvo[:, k, o1 * 128:(o1 + 1) * 128], hB[:, k, :]) for k in range(8)],
                                  r=[wbo, hBb], w=[pyb])
                            kb.op("dve", lambda h, oc=oc, py=py: h.scalar_tensor_tensor(
                                out=xT[:, oc, blk], in0=py[:], scalar=dmod[:, 2, oc:oc + 1], in1=xT[:, oc, blk],
                                op0=ALU.mult, op1=ALU.add), r=[pyb, dmodb, xbuf[j]], w=[xbuf[j]])

                chk("p5")
                phase()
                xn2 = cv.get([128, 8, T], BF16)
                xn2b = [Buf(f"xn2{j}") for j in range(nb)]
                Rot_sq = Rot("sq", [128, 512], BF16, 2)
                Rot_rstd = Rot("rstd", [128, 512], F32, 2)
                Rot_tmp = Rot("tmp", [128, 512], F32, 2)
                xh = cv.get([128, 8, 2], BF16)
                xhb = Buf("xh")
                R_e = Rot("e", [128, 514], F32, 3)
                R_acc = Rot("acc", [128, 512], F32, 3)
                hid = cv.get([128, 22, 512], BF16)
                hidb = Buf("hid")
                norm_all(3, 4, xn2, xn2b)
                wup = dr["w_up"][l].rearrange("(kc p) (two n) -> p kc two n", p=128, two=2)
                wd3 = w3("w_down", l)
                fo = 82
                for j in range(nb):
                    blk = slice(j * 512, (j + 1) * 512)
                    halo_cols(xn2, xn2b, j)
                    for m in range(22):
                        wv, wb = wload(wup[:, :, :, m * 128:(m + 1) * 128], 8, 128, extra=2)
                        accs = []
                        for ab in range(2):
                            pm, pmb = psum[2 * ab], psb[2 * ab]
                            ph, phb = psum[2 * ab + 1], psb[2 * ab + 1]
                            kb.mm([(pm[:], wv[:, k, ab, :], xn2[:, k, blk]) for k in range(8)], r=[wb, xn2b[j]], w=[pmb])
                            kb.mm([(ph[:, 0:2], wv[:, k, ab, :], xh[:, k, :]) for k in range(8)], r=[wb, xhb], w=[phb])
                            e_t, e_b = R_e.nxt()
                            kb.op("act", lambda h, pm=pm, e_t=e_t: h.copy(out=e_t[:, 1:513], in_=pm[:]), r=[pmb], w=[e_b])
                            kb.op("dve", lambda h, ph=ph, e_t=e_t: h.tensor_copy(out=e_t[:, 0:514:513], in_=ph[:, 0:2]),
                                  r=[phb, e_b], w=[e_b])
                            acc_t, acc_b = R_acc.nxt()
                            mm_ = ab * 22 + m
                            conv3(e_t, e_b, acc_t, acc_b, fv[:, l, fo + mm_:fo + mm_ + 1],
                                  fv[:, l, fo + 44 + mm_:fo + 44 + mm_ + 1], fv[:, l, fo + 88 + mm_:fo + 88 + mm_ + 1], fvb)
                            accs.append((acc_t, acc_b))
                        (a_t, a_b), (b_t, b_b) = accs
                        t_t, t_b = Rot_tmp.nxt()
                        kb.op("act", lambda h, a_t=a_t, t_t=t_t: h.activation(out=t_t, in_=a_t, func=AF.Silu), r=[a_b],
                              w=[t_b])
                        kb.op("dve", lambda h, m=m, t_t=t_t, b_t=b_t: h.tensor_tensor(out=hid[:, m, :], in0=t_t, in1=b_t,
                                                                                     op=ALU.mult), r=[t_b, b_b], w=[hidb])
                    for oc in range(8):
                        wv, wb = wload(wd3[:, :, oc * 128:(oc + 1) * 128], 22, 128)
                        py, pyb = psum[4 + (oc % 2)], psb[4 + (oc % 2)]
                        kb.mm([(py[:], wv[:, k, :], hid[:, k, :]) for k in range(22)], r=[wb, hidb], w=[pyb])
                        kb.op("dve", lambda h, oc=oc, py=py: h.scalar_tensor_tensor(
                            out=xT[:, oc, blk], in0=py[:], scalar=dmod[:, 5, oc:oc + 1], in1=xT[:, oc, blk],
                            op0=ALU.mult, op1=ALU.add), r=[pyb, dmodb, xbuf[j]], w=[xbuf[j]])

            chk("p6")
            phase()
            Rot_sq = Rot("sq", [128, 512], BF16, 2)
            Rot_rstd = Rot("rstd", [128, 512], F32, 2)
            yt = cv.get([128, 8, 512], F32)
            ytb = Buf("yt")
            ost = Rot("ost", [128, 1024], F32, 2)
            for j in range(nb):
                r_t, r_b = rstd_block(j)
                for k in range(8):
                    kb.op("dve", lambda h, k=k: h.scalar_tensor_tensor(
                        out=yt[:, k, :], in0=xT[:, k, j * 512:(j + 1) * 512], scalar=fv[:, 0, 214 + k:215 + k], in1=r_t,
                        op0=ALU.mult, op1=ALU.mult), r=[xbuf[j], fvb, r_b], w=[ytb])
                for sub in range(4):
                    o_t, o_b = ost.nxt()
                    for half in range(2):
                        pi = half
                        kb.tr([(psum[pi][:, q * 128:(q + 1) * 128], yt[:, half * 4 + q, sub * 128:(sub + 1) * 128])
                               for q in range(4)], identF[:], r=[ytb, cbuf], w=[psb[pi]])
                        if half == 0:
                            kb.op("dve", lambda h, o_t=o_t: h.tensor_copy(out=o_t[:, 0:512], in_=psum[0][:]), r=[psb[0]],
                                  w=[o_b])
                        else:
                            kb.op("act", lambda h, o_t=o_t: h.copy(out=o_t[:, 512:1024], in_=psum[1][:]), r=[psb[1]],
                                  w=[o_b])
                    r0 = j * 512 + sub * 128
                    kb.dma("sp", job["yout"][r0:r0 + 128, :], o_t, r=[o_b])

        jobs = [
            dict(idx=0, T=NP_SEQ * L_P, L=L_P, nseq=NP_SEQ, rope=False, cache=False, xin=dr["xp"], yout=dr["yp"]),
            dict(idx=1, T=L_S, L=L_S, nseq=1, rope=True, cache=True, xin=dr["xs"], yout=dr["ys"]),
        ]
        try:
            chk("mod")
            for job in jobs:
                run_job(job)
        except _Stop:
            pass
        kb.barrier(engines=("pe", "act", "dve", "sp", "pool"), include_pool=True)
    return nc, hc


def kernel(**inputs):
    n = 8
    if "nc" not in _CACHE:
        _CACHE["nc"] = build()
    nc, hc = _CACHE["nc"]
    f = lambda a: np.ascontiguousarray(np.asarray(a, dtype=np.float32))
    xp = f(inputs["x_prompt"])
    xs = f(inputs["x_sample"])
    st = f(inputs["state_hgrn"])
    cm = f(inputs["cache_mla"])
    c = f(inputs["c"])
    cctx = f(inputs["c_ctx"])
    in_maps = []
    for i in range(n):
        m = {
            "xp": xp[4 * i:4 * i + 4].reshape(NP_SEQ * L_P, D),
            "xs": xs[i],
            "st_in": st[i],
            "cache_in": cm[i],
            "cvec": np.stack([cctx, c[i]], axis=0),
        }
        for wn in WEIGHTS:
            m[wn] = f(inputs[wn])
        for k, v in hc.items():
            m[k] = v
        in_maps.append({k: np.ascontiguousarray(v) for k, v in m.items()})
    res = run_bass_kernel_spmd(nc, in_maps, core_ids=list(range(n)))
    R = res.results
    y_p = np.concatenate([r["yp"].reshape(NP_SEQ, L_P, D) for r in R], axis=0)
    y_s = np.stack([r["ys"] for r in R], axis=0)
    st_o = np.concatenate([r["st_out"] for r in R], axis=0)
    ch_o = np.concatenate([r["cache_out"] for r in R], axis=0)
    return (y_p.astype(np.float32), y_s.astype(np.float32), st_o.astype(np.float32), ch_o.astype(np.float32))
```

```python
import types
import numpy as np
import ml_dtypes
from contextlib import ExitStack
import concourse.bass as bass
import concourse.mybir as mybir
from concourse.bass_utils import run_bass_kernel_spmd

F32 = mybir.dt.float32
BF16 = mybir.dt.bfloat16
AF = mybir.ActivationFunctionType
ALU = mybir.AluOpType

D = 1024
DEPTH = 2
A_W = 512
B_W = 512
C_Q = 384
C_KV = 256
C_ROPE = 32
D_FF = 2816
IN_COLS = 7840
OFF_Q, OFF_FF, OFF_FB, OFF_I, OFF_G = 0, 512, 1024, 1536, 2048
OFF_BB, OFF_BC, OFF_BH = 2560, 3072, 3584
OFF_CQ = 4096
OFF_CKV = 4480
OFF_GATE = 4768
EPS = 1e-6
NP_SEQ, L_P = 4, 256
L_S = 2048
PAST = 256
WSLOT = 2816


class Buf:
    __slots__ = ("name", "w", "r", "excl")

    def __init__(self, name, excl=False):
        self.name = name
        self.w = None
        self.r = []
        self.excl = excl


def _snap(fn):
    if fn.__closure__ is None:
        return fn
    cells = []
    for c in fn.__closure__:
        try:
            cells.append(types.CellType(c.cell_contents))
        except ValueError:
            cells.append(c)
    return types.FunctionType(fn.__code__, fn.__globals__, fn.__name__, fn.__defaults__, tuple(cells))


class Node:
    __slots__ = ("eng", "emit", "deps", "cost", "lat", "idx", "sig", "start", "finish", "prev", "isdma", "pri")


class Eng:
    def __init__(self, name, h, sems, is_pe=False):
        self.name = name
        self.h = h
        self.sems = sems
        self.si = 0
        self.cnt = 0
        self.seen = {}
        self.is_pe = is_pe
        self.dslots = []
        self.di = 0


class KB:
    ROLL = 16000
    W = 96
    LAT = 0.2

    def __init__(self, nc, es):
        self.nc = nc
        self.es = es
        self.E = {}
        for name, h, n in (("pe", nc.tensor, 6), ("act", nc.scalar, 6), ("dve", nc.vector, 8),
                           ("pool", nc.gpsimd, 3), ("sp", nc.sync, 1)):
            sems = [es.enter_context(nc.semaphore(f"s_{name}{i}")) for i in range(n)]
            self.E[name] = Eng(name, h, sems, is_pe=(name == "pe"))
        for qn, n in (("sp", 12), ("pool", 8)):
            q = self.E[qn]
            q.dslots = [[es.enter_context(nc.semaphore(f"d_{qn}{i}")), 0] for i in range(n)]
        self.pe_nums = set(s.num for s in self.E["pe"].sems)
        self.pe_ins = 0
        self.marks = []
        self.nodes = []
        self.nidx = 0
        self.reorder = True

    def _mk(self, en, emit, r, w, cost, lat=None):
        nd = Node()
        nd.eng = en
        nd.emit = emit
        nd.cost = cost
        nd.lat = cost if lat is None else lat
        nd.idx = self.nidx
        self.nidx += 1
        nd.sig = None
        nd.start = None
        nd.finish = None
        nd.prev = 0
        nd.isdma = False
        nd.pri = 0.4 if (en != "pe" and any(b.excl for b in r)) else 0.0
        deps = set()
        for b in r:
            if b.w is not None:
                deps.add(b.w)
            if b.excl:
                deps.update(b.r)
        for b in w:
            if b.w is not None:
                deps.add(b.w)
            deps.update(b.r)
        nd.deps = deps
        for b in r:
            b.r.append(nd)
        for b in w:
            b.w = nd
            b.r = []
        self.nodes.append(nd)
        return nd

    def op(self, en, fn, r=(), w=(), n=512):
        if en == "act":
            cost = 0.22 + n / 1400.0
        elif en == "dve":
            cost = 0.10 + max(n, 64) / 960.0
        else:
            cost = 0.30 + n / 450.0

        fn2 = _snap(fn)

        def emit(h):
            return fn2(h)
        return self._mk(en, emit, r, w, cost)

    def mm(self, steps, r=(), w=(), start=True, stop=True):
        n = len(steps)
        self.pe_ins += n
        cost = 0.0
        for st in steps:
            fr = 1
            for d_ in st[2].shape[1:]:
                fr *= d_
            c = max(fr, 64) / 2400.0 + 0.05
            if st[1].dtype == F32:
                c *= 4
            cost += c

        def emit(h):
            ins = None
            for i, st in enumerate(steps):
                kw = st[3] if len(st) > 3 else {}
                ins = h.matmul(st[0], st[1], st[2], start=(start and i == 0), stop=(stop and i == n - 1), **kw)
            return ins
        return self._mk("pe", emit, r, w, cost, lat=cost + 0.15)

    def tr(self, outs_ins, ident, r=(), w=()):
        self.pe_ins += len(outs_ins)

        def emit(h):
            ins = None
            for out, in_ in outs_ins:
                ins = h.transpose(out, in_, ident)
            return ins
        return self._mk("pe", emit, r, w, 0.12 * len(outs_ins), lat=0.12 * len(outs_ins) + 0.15)

    def dma(self, qn, out, in_, r=(), w=()):
        nbytes = 1
        for d_ in in_.shape:
            nbytes *= d_
        nbytes *= 4 if in_.dtype == F32 else 2

        def emit(h):
            return h.dma_start(out=out, in_=in_)
        occ = 1.1 if qn == "pool" else 0.1
        nd = self._mk(qn, emit, r, w, occ, lat=2.2 + nbytes / 150e3)
        nd.isdma = True
        return nd

    def flush(self):
        nodes = self.nodes
        self.nodes = []
        if not nodes:
            return
        per = {en: [] for en in self.E}
        for nd in nodes:
            per[nd.eng].append(nd)
        order = {en: [] for en in self.E}
        if not self.reorder:
            for en in per:
                order[en] = per[en]
        else:
            head = {en: 0 for en in self.E}
            t_eng = {en: 0.0 for en in self.E}
            nleft = len(nodes)
            W, LAT = self.W, self.LAT
            while nleft:
                best = None
                for en, lst in per.items():
                    hpos = head[en]
                    L_ = len(lst)
                    while hpos < L_ and lst[hpos].start is not None:
                        hpos += 1
                    head[en] = hpos
                    cnt = 0
                    i = hpos
                    te = t_eng[en]
                    while i < L_ and cnt < W:
                        nd = lst[i]
                        i += 1
                        if nd.start is not None:
                            continue
                        cnt += 1
                        rt = te
                        ok = True
                        for d_ in nd.deps:
                            if d_.start is None:
                                ok = False
                                break
                            f = d_.finish if d_.eng == en else d_.finish + LAT
                            if f > rt:
                                rt = f
                        if not ok:
                            continue
                        key = rt - nd.pri
                        if best is None or key < best[3] or (key == best[3] and nd.idx < best[1].idx):
                            best = (rt, nd, en, key)
                        if rt <= te and (en == "pe" or nd.pri > 0):
                            break
                assert best is not None, "scheduler stuck"
                rt, nd, en = best[0], best[1], best[2]
                nd.start = rt
                nd.finish = rt + nd.lat
                t_eng[en] = rt + nd.cost
                order[en].append(nd)
                nleft -= 1
        for en, lst in order.items():
            e = self.E[en]
            for nd in lst:
                if nd.isdma:
                    slot = e.dslots[e.di % len(e.dslots)]
                    e.di += 1
                    nd.prev = (slot[0], slot[1])
                    slot[1] += 16
                    nd.sig = (slot[0], slot[1], 16)
                else:
                    if e.cnt >= self.ROLL:
                        e.si += 1
                        e.cnt = 0
                    e.cnt += 1
                    nd.sig = (e.sems[e.si], e.cnt, 1)
        for en, lst in order.items():
            e = self.E[en]
            for nd in lst:
                best = {}
                for d_ in nd.deps:
                    sem, val = d_.sig[0], d_.sig[1]
                    if e.is_pe and sem.num in self.pe_nums:
                        continue
                    if val > best.get(sem.num, (None, 0))[1]:
                        best[sem.num] = (sem, val)
                if nd.sig[2] == 16 and nd.prev[1] > 0:
                    sem, val = nd.prev
                    if val > best.get(sem.num, (None, 0))[1]:
                        best[sem.num] = (sem, val)
                for num, (sem, val) in best.items():
                    if e.seen.get(num, 0) < val:
                        e.h.wait_ge(sem, val)
                        e.seen[num] = val
                ins = nd.emit(e.h)
                ins.then_inc(nd.sig[0], nd.sig[2])
                nd.emit = None
        for nd in nodes:
            nd.start = 0.0
            nd.finish = 0.0
            nd.deps = ()

    def barrier(self, engines=("pe", "act", "dve", "sp"), include_pool=False):
        self.flush()
        evs = []
        for en, e in self.E.items():
            if en == "pool" and not include_pool:
                continue
            if e.cnt > 0:
                evs.append((e.sems[e.si], e.cnt))
            for sl in e.dslots:
                if sl[1] > 0:
                    evs.append((sl[0], sl[1]))
        for en in engines:
            e = self.E[en]
            for sem, val in evs:
                if e.is_pe and sem.num in self.pe_nums:
                    continue
                if e.seen.get(sem.num, 0) < val:
                    e.h.wait_ge(sem, val)
                    e.seen[sem.num] = val


def host_consts():
    c = {}
    c["identF"] = np.eye(128, dtype=np.float32)
    s = np.arange(128)[:, None]
    t = np.arange(128)[None, :]
    same = (s // 32) == (t // 32)
    c["mLE"] = (same & (s <= t)).astype(np.float32)
    c["mGE"] = (same & (s >= t)).astype(np.float32)
    c["mLT"] = (same & (s < t)).astype(np.float32)
    c["mGT"] = (same & (s > t)).astype(np.float32)
    T = L_S
    rows = np.repeat(np.arange(T // 64, dtype=np.float32), 64)
    col = np.tile(np.arange(64, dtype=np.float32), T // 64)
    nf = 8
    freq = (np.float32(10000.0) ** (-np.arange(nf, dtype=np.float32) / np.float32(nf))).astype(np.float32)
    ang = np.stack([rows[:, None] * freq, col[:, None] * freq], axis=1).astype(np.float32)
    cosv = np.cos(ang).astype(np.float32)
    sinv = np.sin(ang).astype(np.float32)
    C = np.zeros((96, T), np.float32)
    S = np.zeros((96, T), np.float32)
    for a in range(2):
        for hf in range(2):
            for f in range(nf):
                C[64 + a * 16 + hf * 8 + f] = cosv[:, a, f]
                S[64 + a * 16 + hf * 8 + f] = sinv[:, a, f]
    c["ropeC"] = C
    c["ropeS"] = S
    P = np.zeros((96, 96), np.float32)
    for a in range(2):
        for f in range(nf):
            m0 = 64 + a * 16 + f
            m1 = 64 + a * 16 + 8 + f
            P[m1, m0] = -1.0
            P[m0, m1] = 1.0
    c["P96"] = P
    sw = np.zeros((128, 128), np.float32)
    for m_ in range(128):
        sw[(m_ + 64) % 128, m_] = 1.0
    c["swapM"] = sw
    return c


WEIGHTS = ["w_ada", "b_ada", "norm1", "w_in", "hgrn_lb_logits", "hgrn_gnorm", "w_o_hgrn", "conv_w", "w_o_conv",
           "mla_q_norm", "w_q_up", "mla_kv_norm", "w_kv_up", "w_o_mla", "w_out", "norm2", "w_up", "ffn_conv_w",
           "w_down", "final_norm"]
W_SHAPES = {
    "w_ada": [2, 1024, 6144], "b_ada": [2, 6144], "norm1": [2, 1024], "w_in": [2, 1024, IN_COLS],
    "hgrn_lb_logits": [2, 2, 512], "hgrn_gnorm": [2, 128], "w_o_hgrn": [2, 512, 1024], "conv_w": [2, 3, 512],
    "w_o_conv": [2, 512, 1024], "mla_q_norm": [2, 384], "w_q_up": [2, 384, 768], "mla_kv_norm": [2, 256],
    "w_kv_up": [2, 256, 1024], "w_o_mla": [2, 512, 1024], "w_out": [2, 1024, 1024], "norm2": [2, 1024],
    "w_up": [2, 1024, 5632], "ffn_conv_w": [2, 3, 5632], "w_down": [2, 2816, 1024], "final_norm": [1024],
}


_CACHE = {}


class _Stop(Exception):
    pass


def build(debug=None, stop=None):
    nc = bass.Bass("TRN2", target_bir_lowering=False)
    dr = {}

    def din(name, shape, dt=F32):
        dr[name] = nc.dram_tensor(name, list(shape), dt, kind="ExternalInput").ap()
        return dr[name]

    def dout(name, shape):
        dr[name] = nc.dram_tensor(name, list(shape), F32, kind="ExternalOutput").ap()
        return dr[name]

    din("xp", [NP_SEQ * L_P, D])
    din("xs", [L_S, D])
    din("st_in", [2, 2, 4, 128, 128])
    din("cache_in", [2, PAST, C_KV + C_ROPE])
    din("cvec", [2, D])
    for wn in WEIGHTS:
        din(wn, W_SHAPES[wn])
    hc = host_consts()
    for k, v in hc.items():
        din(k, v.shape)
    dout("yp", [NP_SEQ * L_P, D])
    dout("ys", [L_S, D])
    dout("st_out", [NP_SEQ, 2, 2, 4, 128, 128])
    dout("cache_out", [NP_SEQ, 2, L_P, C_KV + C_ROPE])
    if debug:
        dout("dbg", [128, debug])
    ofw_d = nc.dram_tensor("ofw_d", [128, 4, L_S], BF16).ap()
    oA_d = nc.dram_tensor("oA_d", [128, 4, L_S], BF16).ap()

    with ExitStack() as es:
        kb = KB(nc, es)
        _CACHE['kb'] = kb

        def sb(name, shape, dt):
            return es.enter_context(nc.sbuf_tensor("sb_" + name, list(shape), dt))

        xT = sb("xT", [128, 8, L_S], F32)
        xbuf = [Buf(f"x{j}") for j in range(L_S // 512)]
        wsl = [sb(f"wsl{i}", [128, WSLOT], BF16) for i in range(4)]
        wslb = [Buf(f"wsl{i}") for i in range(4)]
        wctr = [0]
        identF = sb("identF", [128, 128], F32)
        identB = sb("identB", [128, 128], BF16)
        onesB = sb("onesB", [128, 128], BF16)
        mLE = sb("mLE", [128, 128], F32)
        mGE = sb("mGE", [128, 128], F32)
        mLT = sb("mLT", [128, 128], F32)
        mGT = sb("mGT", [128, 128], F32)
        swapM = sb("swapM", [128, 128], F32)
        P96 = sb("P96", [96, 96], BF16)
        P96f = sb("P96f", [96, 96], F32)
        epsT = sb("epsT", [128, 1], F32)
        cbuf = Buf("consts")
        fv = sb("fv", [128, 2, 224], F32)
        fvb = Buf("fv")
        modT = sb("modT", [128, 2, 48, 2], F32)
        modb = Buf("mod")
        dmod = sb("dmod", [128, 6, 8], F32)
        dmodb = Buf("dmod")
        lbT = sb("lbT", [128, 2, 2, 512], F32)
        lbb = Buf("lb")
        kvnB = sb("kvnB", [128, 2, 256], F32)
        kvnb = Buf("kvnB")
        ARENA = 53600
        arena = sb("arena", [128, ARENA], BF16)
        psum = [es.enter_context(nc.psum_tensor(f"ps{i}", [128, 512], F32)) for i in range(8)]
        psb = [Buf(f"ps{i}", excl=True) for i in range(8)]

        class Carver:
            def __init__(self):
                self.off = 0

            def reset(self):
                self.off = 0

            def get(self, shape, dt, nbuf=1):
                n = int(np.prod(shape[1:]))
                nb = n * (4 if dt == F32 else 2)
                nb = (nb + 3) // 4 * 4
                o = self.off
                self.off += nb
                assert self.off <= ARENA * 2, f"arena overflow {self.off}"
                ap = arena[0:shape[0], o // 2:(o + nb) // 2]
                if dt == F32:
                    ap = ap.bitcast(F32)
                ap = ap[:, 0:n]
                if len(shape) == 3:
                    ap = ap.rearrange("p (a b) -> p a b", a=shape[1])
                elif len(shape) == 4:
                    ap = ap.rearrange("p (a b c) -> p a b c", a=shape[1], b=shape[2])
                return ap

        cv = Carver()

        def phase():
            kb.barrier()
            cv.reset()

        class Rot:
            def __init__(self, name, shape, dt, n):
                self.t = [cv.get(shape, dt) for _ in range(n)]
                self.b = [Buf(f"{name}{i}") for i in range(n)]
                self.i = 0

            def nxt(self):
                i = self.i % len(self.t)
                self.i += 1
                return self.t[i], self.b[i]

        def wload(src, kc, ncols, extra=None):
            i = wctr[0] % 4
            wctr[0] += 1
            n = kc * ncols * (extra or 1)
            assert n <= WSLOT
            if extra:
                view = wsl[i][:, 0:n].rearrange("p (k e n) -> p k e n", k=kc, e=extra)
            else:
                view = wsl[i][:, 0:n].rearrange("p (k n) -> p k n", k=kc)
            if extra:
                for e_ in range(extra):
                    kb.dma("pool", view[:, :, e_, :], src[:, :, e_, :], w=[wslb[i]])
            else:
                kb.dma("pool", view, src, w=[wslb[i]])
            return view, wslb[i]

        def w3(name, l):
            return dr[name][l].rearrange("(kc p) n -> p kc n", p=128)

        for nm, t_ in (("identF", identF), ("mLE", mLE), ("mGE", mGE), ("mLT", mLT), ("mGT", mGT), ("swapM", swapM)):
            kb.dma("sp", t_[:], dr[nm], w=[cbuf])
        kb.dma("sp", P96f[:], dr["P96"], w=[cbuf])
        kb.op("dve", lambda h: h.tensor_copy(out=identB[:], in_=identF[:]), r=[cbuf], w=[cbuf])
        kb.op("dve", lambda h: h.tensor_copy(out=P96[:], in_=P96f[:]), r=[cbuf], w=[cbuf])
        kb.op("dve", lambda h: h.memset(onesB[:], 1.0), w=[cbuf])
        kb.op("dve", lambda h: h.memset(epsT[:], EPS), w=[cbuf])

        cv.reset()
        stg = cv.get([128, 128], F32)
        stgb = Buf("stg")
        cin = sb("cin", [128, 2, 8], F32)

        def featvec(rows_ap, nrows, dst_ap, dstbuf):
            kb.dma("sp", stg[0:nrows, :], rows_ap, w=[stgb])
            kb.tr([(psum[7][:, 0:nrows], stg[0:nrows, :])], identF[0:nrows, 0:nrows], r=[stgb, cbuf], w=[psb[7]])
            kb.op("dve", lambda h: h.tensor_copy(out=dst_ap, in_=psum[7][:, 0:nrows]), r=[psb[7]], w=[dstbuf])

        for l in range(DEPTH):
            featvec(dr["b_ada"][l].rearrange("(j p) -> j p", p=128), 48, fv[:, l, 0:48], fvb)
            featvec(dr["norm1"][l].rearrange("(j p) -> j p", p=128), 8, fv[:, l, 48:56], fvb)
            featvec(dr["norm2"][l].rearrange("(j p) -> j p", p=128), 8, fv[:, l, 56:64], fvb)
            featvec(dr["hgrn_gnorm"][l].rearrange("(j p) -> j p", p=128), 1, fv[:, l, 64:65], fvb)
            featvec(dr["conv_w"][l].rearrange("j (g p) -> (j g) p", p=128), 12, fv[:, l, 65:77], fvb)
            featvec(dr["mla_q_norm"][l].rearrange("(j p) -> j p", p=128), 3, fv[:, l, 77:80], fvb)
            featvec(dr["mla_kv_norm"][l].rearrange("(j p) -> j p", p=128), 2, fv[:, l, 80:82], fvb)
            fc = dr["ffn_conv_w"][l].rearrange("j (m p) -> (j m) p", p=128)
            featvec(fc[0:88], 88, fv[:, l, 82:170], fvb)
            featvec(fc[88:132], 44, fv[:, l, 170:214], fvb)
        featvec(dr["final_norm"].rearrange("(j p) -> j p", p=128), 8, fv[:, 0, 214:222], fvb)
        featvec(dr["cvec"].rearrange("c (j p) -> (c j) p", p=128), 16, cin[:].rearrange("p c j -> p (c j)"), fvb)

        lg = cv.get([128, 2, 2, 512], F32)
        lgb = Buf("lg")
        for l in range(2):
            for d_ in range(2):
                kb.dma("sp", lg[:, l, d_, :], dr["hgrn_lb_logits"][l, d_].partition_broadcast(128), w=[lgb])
        for d_ in range(2):
            kb.op("dve", lambda h, d_=d_: h.tensor_tensor(out=lbT[:, d_, 0, :], in0=lg[:, 1, d_, :], in1=lg[:, 0, d_, :],
                                                          op=ALU.subtract), r=[lgb], w=[lbb])
            kb.op("act", lambda h, d_=d_: h.activation(out=lbT[:, d_, 0, :], in_=lbT[:, d_, 0, :], func=AF.Sigmoid),
                  r=[lbb], w=[lbb])
            kb.op("dve", lambda h, d_=d_: h.tensor_scalar(out=lbT[:, d_, 1, :], in0=lbT[:, d_, 0, :], scalar1=-1.0,
                                                          scalar2=1.0, op0=ALU.mult, op1=ALU.add), r=[lbb], w=[lbb])
        for l in range(2):
            kb.dma("sp", kvnB[:, l, :], dr["mla_kv_norm"][l].partition_broadcast(128), w=[kvnb])

        silc = cv.get([128, 8, 2], BF16)
        silb = Buf("silc")
        kb.op("act", lambda h: h.activation(out=silc[:].rearrange("p k c -> p c k"), in_=cin[:], func=AF.Silu),
              r=[fvb], w=[silb])
        for l in range(DEPTH):
            wa = w3("w_ada", l)
            for js in range(0, 48, 2):
                wv, wb = wload(wa[:, :, js * 128:(js + 2) * 128], 8, 256)
                for j in range(js, js + 2):
                    kb.mm([(psum[6][:, 2 * j:2 * j + 2], wv[:, k, (j - js) * 128:(j - js + 1) * 128], silc[:, k, :])
                           for k in range(8)], r=[wb, silb], w=[psb[6]])
            kb.op("dve", lambda h, l=l: h.tensor_tensor(
                out=modT[:, l, :, :], in0=psum[6][:, 0:96].rearrange("p (j c) -> p j c", c=2),
                in1=fv[:, l, 0:48].unsqueeze(2).to_broadcast([128, 48, 2]), op=ALU.add), r=[psb[6], fvb], w=[modb])

        def chk(name):
            kb.marks.append((name, kb.pe_ins))
            if stop == name:
                raise _Stop()

        def run_job(job):
            T, L, nseq = job["T"], job["L"], job["nseq"]
            ji = job["idx"]
            nb = T // 512
            n128 = T // 128
            QN = min(512, L)
            Tk = L + (PAST if job["cache"] else 0)
            nkb = Tk // 128

            phase()
            st0 = Rot("xst", [128, 1024], F32, 2)
            for i in range(n128):
                s_t, s_b = st0.nxt()
                kb.dma("sp", s_t, job["xin"][i * 128:(i + 1) * 128, :], w=[s_b])
                for half in range(2):
                    pi = (2 * i + half) % 4
                    kb.tr([(psum[pi][:, q * 128:(q + 1) * 128], s_t[:, (half * 4 + q) * 128:(half * 4 + q + 1) * 128])
                           for q in range(4)], identF[:], r=[s_b, cbuf], w=[psb[pi]])
                    eng = "dve" if half == 0 else "act"
                    if eng == "dve":
                        kb.op("dve", lambda h, pi=pi, half=half, i=i: h.tensor_copy(
                            out=xT[:, half * 4:half * 4 + 4, i * 128:(i + 1) * 128],
                            in_=psum[pi][:].rearrange("p (q t) -> p q t", q=4)), r=[psb[pi]], w=[xbuf[i // 4]])
                    else:
                        kb.op("act", lambda h, pi=pi, half=half, i=i: h.copy(
                            out=xT[:, half * 4:half * 4 + 4, i * 128:(i + 1) * 128],
                            in_=psum[pi][:].rearrange("p (q t) -> p q t", q=4)), r=[psb[pi]], w=[xbuf[i // 4]])

            chk("p0")

            def rstd_block(j, ntok=512):
                sq = Rot_sq
                for k in range(8):
                    s_t, s_b = sq.nxt()
                    kb.op("act", lambda h, k=k, s_t=s_t: h.activation(out=s_t, in_=xT[:, k, j * 512:(j + 1) * 512],
                                                                      func=AF.Square), r=[xbuf[j]], w=[s_b])
                    kb.mm([(psum[5][:], onesB[:], s_t)], r=[s_b, cbuf], w=[psb[5]], start=(k == 0), stop=(k == 7))
                r_t, r_b = Rot_rstd.nxt()
                kb.op("act", lambda h: h.activation(out=r_t, in_=psum[5][:], func=AF.Ln, bias=epsT[:], scale=1.0 / D),
                      r=[psb[5], cbuf], w=[r_b])
                kb.op("act", lambda h: h.activation(out=r_t, in_=r_t, func=AF.Exp, scale=-0.5), r=[r_b], w=[r_b])
                return r_t, r_b

            class VirtXN:
                def __init__(self):
                    self.tiles = {}

                def __getitem__(self, key):
                    p_, k_, ts_ = key
                    j_ = ts_.start // 512
                    return self.tiles[j_][p_, k_, ts_.start - j_ * 512:ts_.stop - j_ * 512]

            def prep_xn(j, gi, si, xnv, xnb, R_xn):
                t_, b_ = R_xn.nxt()
                xnv.tiles[j] = t_
                xnb[j] = b_
                r_t, r_b = rstd_block(j)
                for k in range(8):
                    t_t, t_b = Rot_tmp.nxt()
                    kb.op("dve", lambda h, k=k, t_t=t_t: h.scalar_tensor_tensor(
                        out=t_t, in0=xT[:, k, j * 512:(j + 1) * 512], scalar=dmod[:, gi, k:k + 1], in1=r_t,
                        op0=ALU.mult, op1=ALU.mult), r=[xbuf[j], dmodb, r_b], w=[t_b])
                    kb.op("act", lambda h, k=k, t_t=t_t: h.activation(
                        out=t_[:, k, :], in_=t_t, func=AF.Identity,
                        bias=dmod[:, si, k:k + 1], scale=1.0), r=[t_b, dmodb], w=[b_])

            def norm_all(gi, si, xn, xnb):
                for j in range(nb):
                    r_t, r_b = rstd_block(j)
                    for k in range(8):
                        t_t, t_b = Rot_tmp.nxt()
                        kb.op("dve", lambda h, k=k, t_t=t_t: h.scalar_tensor_tensor(
                            out=t_t, in0=xT[:, k, j * 512:(j + 1) * 512], scalar=dmod[:, gi, k:k + 1], in1=r_t,
                            op0=ALU.mult, op1=ALU.mult), r=[xbuf[j], dmodb, r_b], w=[t_b])
                        kb.op("act", lambda h, k=k, t_t=t_t: h.activation(
                            out=xn[:, k, j * 512:(j + 1) * 512], in_=t_t, func=AF.Identity,
                            bias=dmod[:, si, k:k + 1], scale=1.0), r=[t_b, dmodb], w=[xnb[j]])

            for l in range(DEPTH):
                fvl = lambda a, b: fv[:, l, a:b]
                phase()
                mo = lambda c0: modT[:, l, c0:c0 + 8, ji]
                kb.op("dve", lambda h: h.scalar_tensor_tensor(out=dmod[:, 0, :], in0=mo(8), scalar=1.0, in1=fvl(48, 56),
                                                              op0=ALU.add, op1=ALU.mult), r=[modb, fvb], w=[dmodb])
                kb.op("dve", lambda h: h.tensor_copy(out=dmod[:, 1, :], in_=mo(0)), r=[modb], w=[dmodb])
                kb.op("dve", lambda h: h.tensor_copy(out=dmod[:, 2, :], in_=mo(16)), r=[modb], w=[dmodb])
                kb.op("dve", lambda h: h.scalar_tensor_tensor(out=dmod[:, 3, :], in0=mo(32), scalar=1.0, in1=fvl(56, 64),
                                                              op0=ALU.add, op1=ALU.mult), r=[modb, fvb], w=[dmodb])
                kb.op("dve", lambda h: h.tensor_copy(out=dmod[:, 4, :], in_=mo(24)), r=[modb], w=[dmodb])
                kb.op("dve", lambda h: h.tensor_copy(out=dmod[:, 5, :], in_=mo(40)), r=[modb], w=[dmodb])
                win = w3("w_in", l)

                R_xn = Rot("xnt", [128, 8, 512], BF16, 2)
                xn = VirtXN()
                xnb = [None] * nb
                mark_x = cv.off
                qT = cv.get([128, 4, T], BF16)
                qTb = [Buf(f"qT{j}") for j in range(nb)]
                mark_h = cv.off
                ofwb = [Buf(f"ofw{j}") for j in range(n128)]
                oAb = [Buf(f"oA{j}") for j in range(nb)]
                R_of = Rot("of", [128, 4, 128], BF16, 2)
                R_oA = Rot("oAt", [128, 4, 512], BF16, 1)
                Rot_sq = Rot("sq", [128, 512], BF16, 2)
                Rot_rstd = Rot("rstd", [128, 512], F32, 2)
                Rot_tmp = Rot("tmp", [128, 512], F32, 2)
                S32 = [cv.get([128, 4, 128], F32) for _ in range(2)]
                S32b = [Buf("S32a"), Buf("S32b")]
                Sp = [0]
                Sbf = cv.get([128, 5, 4, 128], BF16)
                Sbfb = Buf("Sbf")
                R_s = Rot("hs_", [128, 512], F32, 2)
                R_lf = Rot("hlf_", [128, 512], F32, 2)
                R_k = Rot("hk_", [128, 512], BF16, 2)
                R_kh = Rot("hkh_", [128, 512], BF16, 2)
                R_v = Rot("hv_", [128, 512], BF16, 2)
                R_eb = Rot("heb_", [128, 4, 128], F32, 2)
                R_enb = Rot("henb_", [128, 4, 128], F32, 2)
                R_es = Rot("hes_", [128, 512], F32, 2)
                R_qt = Rot("hqt_", [128, 4, 128], BF16, 2)
                R_kt = Rot("hkt_", [128, 4, 128], BF16, 2)
                R_A = Rot("hA_", [128, 4, 128], BF16, 2)
                R_osum = Rot("hos", [128, 4, 512], F32, 1)
                R_sg = Rot("hsg", [128, 4, 512], BF16, 1)

                def hgrn_step(dr_, tb, wf, wfb, wi, wib):
                    tok = slice(tb * 128, (tb + 1) * 128)
                    j = tb // 4
                    TRI = mLE if dr_ == 0 else mGE
                    XM = mGT if dr_ == 0 else mLT
                    MSK = mLE if dr_ == 0 else mGE
                    kb.mm([(psum[0][:, 0:256], xn[:, k, tok], wf[0][:, k, :]) for k in range(8)],
                          r=[xnb[j], wfb[0]], w=[psb[0]])
                    kb.mm([(psum[0][:, 256:512], xn[:, k, tok], wf[1][:, k, :]) for k in range(8)],
                          r=[xnb[j], wfb[1]], w=[psb[0]])
                    s_t, s_b = R_s.nxt()
                    kb.op("act", lambda h: h.activation(out=s_t, in_=psum[0][:], func=AF.Sigmoid), r=[psb[0]], w=[s_b])
                    kb.mm([(psum[0][:, 0:256], xn[:, k, tok], wi[0][:, k, :]) for k in range(8)],
                          r=[xnb[j], wib[0]], w=[psb[0]])
                    kb.mm([(psum[0][:, 256:512], xn[:, k, tok], wi[1][:, k, :]) for k in range(8)],
                          r=[xnb[j], wib[1]], w=[psb[0]])
                    v_t, v_b = R_v.nxt()
                    kb.op("act", lambda h: h.copy(out=v_t, in_=psum[0][:]), r=[psb[0]], w=[v_b])
                    if l > 0:
                        kb.op("dve", lambda h: h.tensor_tensor(out=s_t, in0=s_t, in1=lbT[:, dr_, 1, :], op=ALU.mult),
                              r=[s_b, lbb], w=[s_b])
                        kb.op("dve", lambda h: h.tensor_tensor(out=s_t, in0=s_t, in1=lbT[:, dr_, 0, :], op=ALU.add),
                              r=[s_b, lbb], w=[s_b])
                    lf_t, lf_b = R_lf.nxt()
                    kb.op("act", lambda h: h.activation(out=lf_t, in_=s_t, func=AF.Ln), r=[s_b], w=[lf_b])
                    k_t, k_b = R_k.nxt()
                    kb.op("dve", lambda h: h.tensor_scalar(out=k_t, in0=s_t, scalar1=-1.0, scalar2=1.0, op0=ALU.mult,
                                                           op1=ALU.add), r=[s_b], w=[k_b])
                    for hh in range(4):
                        kb.mm([(psum[2][:, hh * 128:(hh + 1) * 128], lf_t[:, hh * 128:(hh + 1) * 128], TRI[:])],
                              r=[lf_b, cbuf], w=[psb[2]])
                    kb.mm([(psum[3][:], XM[:], lf_t)], r=[lf_b, cbuf], w=[psb[3]])
                    eb_t, eb_b = R_eb.nxt()
                    enb_t, enb_b = R_enb.nxt()
                    es_t, es_b = R_es.nxt()
                    p2v = psum[2][:].rearrange("p (h t) -> p h t", h=4)
                    kb.op("act", lambda h: h.activation(out=eb_t, in_=p2v, func=AF.Exp), r=[psb[2]], w=[eb_b])
                    kb.op("act", lambda h: h.activation(out=enb_t, in_=p2v, func=AF.Exp, scale=-1.0), r=[psb[2]],
                          w=[enb_b])
                    kb.op("act", lambda h: h.activation(out=es_t, in_=psum[3][:], func=AF.Exp), r=[psb[3]], w=[es_b])
                    kh_t, kh_b = R_kh.nxt()
                    kb.op("dve", lambda h: h.tensor_tensor(out=kh_t, in0=k_t, in1=es_t, op=ALU.mult), r=[k_b, es_b],
                          w=[kh_b])
                    p4b = psum[4][:].bitcast(BF16)
                    kb.tr([(p4b[:, hh * 128:(hh + 1) * 128], k_t[:, hh * 128:(hh + 1) * 128]) for hh in range(4)],
                          identB[:], r=[k_b, cbuf], w=[psb[4]])
                    kt_t, kt_b = R_kt.nxt()
                    kb.op("dve", lambda h: h.tensor_tensor(out=kt_t, in0=p4b[:, 0:512].rearrange("p (h t) -> p h t", h=4),
                                                           in1=enb_t, op=ALU.mult), r=[psb[4], enb_b], w=[kt_b])
                    qt_t, qt_b = R_qt.nxt()
                    kb.op("dve", lambda h: h.tensor_tensor(out=qt_t, in0=qT[:, :, tok], in1=eb_t, op=ALU.mult),
                          r=[qTb[j], eb_b], w=[qt_b])
                    for hh in range(4):
                        kb.mm([(psum[5][:, hh * 128:(hh + 1) * 128], kt_t[:, hh, :], qt_t[:, hh, :])],
                              r=[kt_b, qt_b], w=[psb[5]])
                    A_t, A_b = R_A.nxt()
                    kb.op("dve", lambda h: h.tensor_tensor(
                        out=A_t, in0=psum[5][:].rearrange("p (h t) -> p h t", h=4),
                        in1=MSK[:].unsqueeze(1).to_broadcast([128, 4, 128]), op=ALU.mult), r=[psb[5], cbuf], w=[A_b])
                    corder = [0, 1, 2, 3] if dr_ == 0 else [3, 2, 1, 0]
                    p_ = Sp[0]
                    kb.op("act", lambda h: h.copy(out=Sbf[:, 0, :, :], in_=S32[p_][:]), r=[S32b[p_]], w=[Sbfb])
                    for ci, c in enumerate(corder):
                        pu = psum[6 + (ci % 2)]
                        pub = psb[6 + (ci % 2)]
                        for hh in range(4):
                            kw = {"tile_position": (96, 0)} if c == 3 else {}
                            kb.mm([(pu[:, hh * 128:(hh + 1) * 128], kh_t[c * 32:(c + 1) * 32, hh * 128:(hh + 1) * 128],
                                    v_t[c * 32:(c + 1) * 32, hh * 128:(hh + 1) * 128], kw)], r=[kh_b, v_b], w=[pub])
                        tcol = c * 32 + (31 if dr_ == 0 else 0)
                        src, dst = Sp[0], 1 - Sp[0]
                        for hh in range(4):
                            kb.op("dve", lambda h, hh=hh, pu=pu, tcol=tcol, src=src, dst=dst: h.scalar_tensor_tensor(
                                out=S32[dst][:, hh, :], in0=S32[src][:, hh, :], scalar=eb_t[:, hh, tcol:tcol + 1],
                                in1=pu[:, hh * 128:(hh + 1) * 128], op0=ALU.mult, op1=ALU.add),
                                r=[S32b[src], eb_b, pub], w=[S32b[dst]], n=128)
                        Sp[0] = dst
                        if ci < 3:
                            kb.op("act", lambda h, ci=ci, dst=dst: h.copy(out=Sbf[:, ci + 1, :, :], in_=S32[dst][:]),
                                  r=[S32b[dst]], w=[Sbfb])
                    for hh in range(4):
                        steps = [(psum[1][:, hh * 128:(hh + 1) * 128], v_t[:, hh * 128:(hh + 1) * 128], A_t[:, hh, :])]
                        for ci, c in enumerate(corder):
                            steps.append((psum[1][:, hh * 128 + c * 32:hh * 128 + (c + 1) * 32], Sbf[:, ci, hh, :],
                                          qt_t[:, hh, c * 32:(c + 1) * 32]))
                        kb.mm(steps, r=[v_b, A_b, Sbfb, qt_b], w=[psb[1]])
                    return psum[1][:].rearrange("p (h t) -> p h t", h=4), psb[1]

                def state_init(dr_, s):
                    p_ = Sp[0]
                    if job["cache"]:
                        kb.dma("sp", S32[p_][:], dr["st_in"][l, dr_].rearrange("h d v -> d h v"), w=[S32b[p_]])
                    else:
                        kb.op("dve", lambda h: h.memset(S32[p_][:], 0.0), w=[S32b[p_]])

                def state_out(dr_, s):
                    p_ = Sp[0]
                    if not job["cache"]:
                        kb.dma("sp", dr["st_out"][s, l, dr_].rearrange("h d v -> d h v"), S32[p_][:], r=[S32b[p_]])

                for j in range(nb):
                    blk = slice(j * 512, (j + 1) * 512)
                    prep_xn(j, 0, 1, xn, xnb, R_xn)
                    for half in range(2):
                        wv, wb = wload(win[:, :, OFF_Q + half * 256:OFF_Q + (half + 1) * 256], 8, 256)
                        for cc in range(2):
                            hh = half * 2 + cc
                            pq = psum[6 + cc]
                            kb.mm([(pq[:], wv[:, k, cc * 128:(cc + 1) * 128], xn[:, k, blk]) for k in range(8)],
                                  r=[wb, xnb[j]], w=[psb[6 + cc]])
                            t_t, t_b = Rot_tmp.nxt()
                            kb.op("act", lambda h, pq=pq, t_t=t_t: h.activation(out=t_t, in_=pq[:], func=AF.Silu),
                                  r=[psb[6 + cc]], w=[t_b])
                            kb.op("dve", lambda h, hh=hh, t_t=t_t: h.tensor_scalar(
                                out=qT[:, hh, blk], in0=t_t, scalar1=128.0 ** -0.5, scalar2=None, op0=ALU.mult),
                                r=[t_b], w=[qTb[j]])
                    wf, wfb, wi, wib = [], [], [], []
                    for half in range(2):
                        a, b_ = wload(win[:, :, OFF_FF + half * 256:OFF_FF + (half + 1) * 256], 8, 256)
                        wf.append(a)
                        wfb.append(b_)
                    for half in range(2):
                        a, b_ = wload(win[:, :, OFF_I + half * 256:OFF_I + (half + 1) * 256], 8, 256)
                        wi.append(a)
                        wib.append(b_)
                    for sub in range(4):
                        tb = j * 4 + sub
                        if (tb * 128) % L == 0:
                            state_init(0, (tb * 128) // L)
                        po, pob = hgrn_step(0, tb, wf, wfb, wi, wib)
                        of_t, of_b = R_of.nxt()
                        kb.op("act", lambda h, po=po, of_t=of_t: h.copy(out=of_t, in_=po), r=[pob], w=[of_b])
                        kb.dma("sp", ofw_d[:, :, tb * 128:(tb + 1) * 128], of_t, r=[of_b], w=[ofwb[tb]])
                        if ((tb + 1) * 128) % L == 0:
                            state_out(0, (tb * 128) // L)
                chk("fwd")
                for j in reversed(range(nb)):
                    blk = slice(j * 512, (j + 1) * 512)
                    prep_xn(j, 0, 1, xn, xnb, R_xn)
                    wf, wfb, wi, wib = [], [], [], []
                    for half in range(2):
                        a, b_ = wload(win[:, :, OFF_FB + half * 256:OFF_FB + (half + 1) * 256], 8, 256)
                        wf.append(a)
                        wfb.append(b_)
                    for half in range(2):
                        a, b_ = wload(win[:, :, OFF_I + half * 256:OFF_I + (half + 1) * 256], 8, 256)
                        wi.append(a)
                        wib.append(b_)
                    os_t, os_b = R_osum.nxt()
                    for sub in reversed(range(4)):
                        tb = j * 4 + sub
                        if ((tb + 1) * 128) % L == 0:
                            state_init(1, (tb * 128) // L)
                        of_t, of_b = R_of.nxt()
                        kb.dma("sp", of_t, ofw_d[:, :, tb * 128:(tb + 1) * 128], r=[ofwb[tb]], w=[of_b])
                        po, pob = hgrn_step(1, tb, wf, wfb, wi, wib)
                        kb.op("dve", lambda h, po=po, sub=sub, of_t=of_t: h.tensor_tensor(
                            out=os_t[:, :, sub * 128:(sub + 1) * 128], in0=po, in1=of_t,
                            op=ALU.add), r=[pob, of_b], w=[os_b])
                        if (tb * 128) % L == 0:
                            state_out(1, (tb * 128) // L)
                    sg_t, sg_b = R_sg.nxt()
                    for half in range(2):
                        wv, wb = wload(win[:, :, OFF_G + half * 256:OFF_G + (half + 1) * 256], 8, 256)
                        for cc in range(2):
                            hh = half * 2 + cc
                            pq = psum[6 + cc]
                            kb.mm([(pq[:], wv[:, k, cc * 128:(cc + 1) * 128], xn[:, k, blk]) for k in range(8)],
                                  r=[wb, xnb[j]], w=[psb[6 + cc]])
                            kb.op("act", lambda h, pq=pq, hh=hh: h.activation(out=sg_t[:, hh, :], in_=pq[:], func=AF.Silu),
                                  r=[psb[6 + cc]], w=[sg_b])
                    oA_t, oA_b = R_oA.nxt()
                    for hh in range(4):
                        q_t, q_b = Rot_sq.nxt()
                        kb.op("act", lambda h, hh=hh, q_t=q_t: h.activation(out=q_t, in_=os_t[:, hh, :], func=AF.Square),
                              r=[os_b], w=[q_b])
                        pn = psum[4 + (hh % 2)]
                        pnb = psb[4 + (hh % 2)]
                        kb.mm([(pn[:], onesB[:], q_t)], r=[q_b, cbuf], w=[pnb])
                        r_t, r_b = Rot_rstd.nxt()
                        kb.op("act", lambda h, pn=pn, r_t=r_t: h.activation(out=r_t, in_=pn[:], func=AF.Ln, bias=epsT[:],
                                                                            scale=1.0 / 128), r=[pnb, cbuf], w=[r_b])
                        kb.op("act", lambda h, r_t=r_t: h.activation(out=r_t, in_=r_t, func=AF.Exp, scale=-0.5), r=[r_b], w=[r_b])
                        kb.op("dve", lambda h, hh=hh, r_t=r_t: h.scalar_tensor_tensor(
                            out=r_t, in0=os_t[:, hh, :], scalar=fv[:, l, 64:65], in1=r_t, op0=ALU.mult, op1=ALU.mult),
                            r=[os_b, fvb, r_b], w=[r_b])
                        kb.op("dve", lambda h, hh=hh, r_t=r_t, oA_t=oA_t: h.tensor_tensor(out=oA_t[:, hh, :], in0=r_t,
                                                                                          in1=sg_t[:, hh, :], op=ALU.mult),
                              r=[r_b, sg_b], w=[oA_b])
                    kb.dma("sp", oA_d[:, :, blk], oA_t, r=[oA_b], w=[oAb[j]])

                chk("bwd")
                kb.barrier()
                cv.off = mark_x
                Rot_sq = Rot("sq", [128, 512], BF16, 2)
                Rot_rstd = Rot("rstd", [128, 512], F32, 2)
                Rot_tmp = Rot("tmp", [128, 512], F32, 2)
                cqn = cv.get([128, 3, T], BF16)
                cqnb = [Buf(f"cqn{j}") for j in range(nb)]
                ckvT = cv.get([128, 2, nseq * Tk], BF16)
                ckvb = Buf("ckvT")
                krT = cv.get([96, nseq * Tk], BF16)
                krb = Buf("krT")
                mark_p3 = cv.off
                R_c = Rot("cqf", [128, 3, 512], F32, 1)
                R_rp = Rot("rp", [96, 512], F32, 2)
                R_xb = Rot("xb", [96, 512], BF16, 2)
                R_co = Rot("co", [128, 288], F32, 2)
                R_ss = Rot("ss", [128, 2], F32, 2)
                R_cs = Rot("cs", [96, 2, 512], F32, 1)
                for j in range(nb):
                    blk = slice(j * 512, (j + 1) * 512)
                    prep_xn(j, 0, 1, xn, xnb, R_xn)
                    s0 = (j * 512) // L
                    nsb = max(1, 512 // L)
                    c_t, c_b = R_c.nxt()
                    wva0, wba0 = wload(win[:, :, OFF_CQ:OFF_CQ + 256], 8, 256)
                    chk("p3w")
                    wva1, wba1 = wload(win[:, :, OFF_CQ + 256:OFF_CQ + 384], 8, 128)
                    chk("p3x")
                    for cc in range(3):
                        if cc == 1:
                            chk("p3y")
                        pq = psum[cc % 2]
                        pqb = psb[cc % 2]
                        wva, wba, co_ = (wva0, wba0, cc) if cc < 2 else (wva1, wba1, 0)
                        kb.mm([(pq[:], wva[:, k, co_ * 128:(co_ + 1) * 128], xn[:, k, blk]) for k in range(8)],
                              r=[wba, xnb[j]], w=[pqb])
                        q_t, q_b = Rot_sq.nxt()
                        kb.op("act", lambda h, pq=pq, q_t=q_t: h.activation(out=q_t, in_=pq[:], func=AF.Square), r=[pqb],
                              w=[q_b])
                        kb.op("dve", lambda h, pq=pq, cc=cc: h.tensor_copy(out=c_t[:, cc, :], in_=pq[:]), r=[pqb], w=[c_b])
                        kb.mm([(psum[2][:], onesB[:], q_t)], r=[q_b, cbuf], w=[psb[2]], start=(cc == 0), stop=(cc == 2))
                    r_t, r_b = Rot_rstd.nxt()
                    kb.op("act", lambda h: h.activation(out=r_t, in_=psum[2][:], func=AF.Ln, bias=epsT[:],
                                                        scale=1.0 / C_Q), r=[psb[2], cbuf], w=[r_b])
                    kb.op("act", lambda h: h.activation(out=r_t, in_=r_t, func=AF.Exp, scale=-0.5), r=[r_b], w=[r_b])
                    for cc in range(3):
                        kb.op("dve", lambda h, cc=cc: h.scalar_tensor_tensor(
                            out=cqn[:, cc, blk], in0=c_t[:, cc, :], scalar=fv[:, l, 77 + cc:78 + cc], in1=r_t,
                            op0=ALU.mult, op1=ALU.mult), r=[c_b, fvb, r_b], w=[cqnb[j]])
                    chk("p3a")
                    c_t, c_b = R_c.nxt()
                    wvk, wbk = wload(win[:, :, OFF_CKV:OFF_CKV + 256], 8, 256)
                    for cc in range(2):
                        pq = psum[cc % 2]
                        pqb = psb[cc % 2]
                        kb.mm([(pq[:], wvk[:, k, cc * 128:(cc + 1) * 128], xn[:, k, blk]) for k in range(8)],
                              r=[wbk, xnb[j]], w=[pqb])
                        q_t, q_b = Rot_sq.nxt()
                        kb.op("act", lambda h, pq=pq, q_t=q_t: h.activation(out=q_t, in_=pq[:], func=AF.Square), r=[pqb],
                              w=[q_b])
                        kb.op("dve", lambda h, pq=pq, cc=cc: h.tensor_copy(out=c_t[:, cc, :], in_=pq[:]), r=[pqb], w=[c_b])
                        kb.mm([(psum[2][:], onesB[:], q_t)], r=[q_b, cbuf], w=[psb[2]], start=(cc == 0), stop=(cc == 1))
                    r_t, r_b = Rot_rstd.nxt()
                    kb.op("act", lambda h: h.activation(out=r_t, in_=psum[2][:], func=AF.Ln, bias=epsT[:],
                                                        scale=1.0 / C_KV), r=[psb[2], cbuf], w=[r_b])
                    kb.op("act", lambda h: h.activation(out=r_t, in_=r_t, func=AF.Exp, scale=-0.5), r=[r_b], w=[r_b])
                    for cc in range(2):
                        for sb_ in range(nsb):
                            ln = 512 // nsb
                            ko = (s0 + sb_) * Tk + ((j * 512 + sb_ * ln) % L)
                            kb.op("dve", lambda h, cc=cc, sb_=sb_, ln=ln, ko=ko: h.scalar_tensor_tensor(
                                out=ckvT[:, cc, ko:ko + ln], in0=c_t[:, cc, sb_ * ln:(sb_ + 1) * ln],
                                scalar=fv[:, l, 80 + cc:81 + cc], in1=r_t[:, sb_ * ln:(sb_ + 1) * ln],
                                op0=ALU.mult, op1=ALU.mult), r=[c_b, fvb, r_b], w=[ckvb])
                    chk("p3b")
                    wvr, wbr = wload(win[:, :, OFF_CKV + 192:OFF_CKV + 288], 8, 96)
                    kb.mm([(psum[3][0:96, :], wvr[:, k, :], xn[:, k, blk]) for k in range(8)], r=[wbr, xnb[j]], w=[psb[3]])
                    if job["rope"]:
                        x_t, x_b = R_xb.nxt()
                        kb.op("act", lambda h: h.copy(out=x_t[64:96, :], in_=psum[3][64:96, :]), r=[psb[3]], w=[x_b])
                        kb.op("dve", lambda h: h.memset(x_t[0:64, :], 0.0), w=[x_b])
                        kb.mm([(psum[4][0:96, :], P96[:], x_t[:])], r=[x_b, cbuf], w=[psb[4]])
                        cs_t, cs_b = R_cs.nxt()
                        kb.dma("sp", cs_t[64:96, 0, :], dr["ropeC"][64:96, blk], w=[cs_b])
                        kb.dma("sp", cs_t[64:96, 1, :], dr["ropeS"][64:96, blk], w=[cs_b])
                        p_t, p_b = R_rp.nxt()
                        kb.op("dve", lambda h: h.tensor_tensor(out=p_t[64:96, :], in0=psum[4][64:96, :],
                                                               in1=cs_t[64:96, 1, :], op=ALU.mult), r=[psb[4], cs_b],
                              w=[p_b])
                        p2_t, p2_b = R_rp.nxt()
                        kb.op("dve", lambda h: h.tensor_tensor(out=p2_t[64:96, :], in0=psum[3][64:96, :],
                                                               in1=cs_t[64:96, 0, :], op=ALU.mult), r=[psb[3], cs_b],
                              w=[p2_b])
                        kb.op("dve", lambda h: h.tensor_tensor(out=krT[64:96, j * 512:(j + 1) * 512], in0=p_t[64:96, :],
                                                               in1=p2_t[64:96, :], op=ALU.add), r=[p_b, p2_b], w=[krb])
                    else:
                        for sb_ in range(nsb):
                            ln = 512 // nsb
                            ko = (s0 + sb_) * Tk + ((j * 512 + sb_ * ln) % L)
                            kb.op("act", lambda h, sb_=sb_, ln=ln, ko=ko: h.copy(
                                out=krT[64:96, ko:ko + ln], in_=psum[3][64:96, sb_ * ln:(sb_ + 1) * ln]),
                                r=[psb[3]], w=[krb])
                    chk("p3c")
                    if not job["cache"]:
                        for sub in range(4):
                            tb = j * 4 + sub
                            tok = slice(tb * 128, (tb + 1) * 128)
                            pc = psum[5 + (sub % 2)]
                            pcb = psb[5 + (sub % 2)]
                            kb.mm([(pc[:, 0:256], xn[:, k, tok], wvk[:, k, :]) for k in range(8)], r=[xnb[j], wbk],
                                  w=[pcb])
                            kb.mm([(pc[:, 256:288], xn[:, k, tok], wvr[:, k, 64:96]) for k in range(8)], r=[xnb[j], wbr],
                                  w=[pcb])
                            co_t, co_b = R_co.nxt()
                            ss_t, ss_b = R_ss.nxt()
                            kb.op("act", lambda h, pc=pc, co_t=co_t, ss_t=ss_t: h.activation(
                                out=co_t[:, 0:256], in_=pc[:, 0:256], func=AF.Square, accum_out=ss_t[:, 0:1]),
                                r=[pcb], w=[co_b, ss_b])
                            kb.op("act", lambda h, ss_t=ss_t: h.activation(out=ss_t[:, 1:2], in_=ss_t[:, 0:1], func=AF.Ln,
                                                                           bias=epsT[:], scale=1.0 / C_KV),
                                  r=[ss_b, cbuf], w=[ss_b])
                            kb.op("act", lambda h, ss_t=ss_t: h.activation(out=ss_t[:, 1:2], in_=ss_t[:, 1:2], func=AF.Exp,
                                                                           scale=-0.5), r=[ss_b], w=[ss_b], n=1)
                            kb.op("dve", lambda h, pc=pc, co_t=co_t, ss_t=ss_t: h.scalar_tensor_tensor(
                                out=co_t[:, 0:256], in0=pc[:, 0:256], scalar=ss_t[:, 1:2], in1=kvnB[:, l, :],
                                op0=ALU.mult, op1=ALU.mult), r=[pcb, ss_b, kvnb, co_b], w=[co_b])
                            kb.op("act", lambda h, pc=pc, co_t=co_t: h.copy(out=co_t[:, 256:288], in_=pc[:, 256:288]),
                                  r=[pcb], w=[co_b])
                            s_i = (tb * 128) // L
                            to = (tb * 128) % L
                            kb.dma("sp", dr["cache_out"][s_i, l, to:to + 128, :], co_t[:], r=[co_b])
                if job["cache"]:
                    cst = cv.get([128, 2, 288], F32)
                    cstb = Buf("cst")
                    kb.dma("sp", cst[:], dr["cache_in"][l].rearrange("(tb p) f -> p tb f", p=128), w=[cstb])
                    for tb in range(2):
                        kb.tr([(psum[0][:, cc * 128:(cc + 1) * 128], cst[:, tb, cc * 128:(cc + 1) * 128]) for cc in range(2)],
                              identF[:], r=[cstb, cbuf], w=[psb[0]])
                        kb.op("dve", lambda h, tb=tb: h.tensor_copy(
                            out=ckvT[:, :, L + tb * 128:L + (tb + 1) * 128],
                            in_=psum[0][:, 0:256].rearrange("p (c t) -> p c t", c=2)), r=[psb[0]], w=[ckvb])
                        kb.tr([(psum[1][0:96, 0:128], cst[:, tb, 192:288])], identF[:], r=[cstb, cbuf], w=[psb[1]])
                        kb.op("dve", lambda h, tb=tb: h.tensor_copy(out=krT[64:96, L + tb * 128:L + (tb + 1) * 128],
                                                                    in_=psum[1][64:96, 0:128]), r=[psb[1]], w=[krb])

                chk("p3")
                kb.barrier(engines=("pe", "act", "dve", "sp", "pool"))
                mark_a = cv.off
                cv.off = 0
                oC = cv.get([128, 4, T], BF16)
                assert cv.off <= mark_x
                oCb = [Buf(f"oC{j}") for j in range(nb)]
                cv.off = mark_p3
                Wkv = cv.get([128, 2, 1024], BF16)
                Wq = cv.get([128, 3, 768], BF16)
                wab = Buf("Wattn")
                kb.dma("pool", Wkv, w3("w_kv_up", l), w=[wab])
                kb.dma("pool", Wq, w3("w_q_up", l), w=[wab])
                vaug = [cv.get([128, nkb, 128], BF16) for _ in range(2)]
                vaugb = [Buf("vaug0"), Buf("vaug1")]
                rden = [cv.get([128, 512], F32) for _ in range(2)]
                rdenb = [Buf("rden0"), Buf("rden1")]
                kb.op("dve", lambda h: h.memset(vaug[0][:, :, 64:128], 1.0), w=[vaugb[0]])
                kb.op("dve", lambda h: h.memset(vaug[1][:, :, 0:64], 1.0), w=[vaugb[1]])
                kb.op("dve", lambda h: h.memset(rden[0][:], 0.0), w=[rdenb[0]])
                kb.op("dve", lambda h: h.memset(rden[1][:], 0.0), w=[rdenb[1]])
                R_kT = Rot("kT", [96, Tk], BF16, 2)
                R_q = Rot("qh", [96, 512], BF16, 2)
                R_pT = Rot("pT", [128, 512], BF16, 3)
                R_rb = Rot("rb", [128, 512], F32, 1)
                R_rp = Rot("rp2", [96, 512], F32, 2)
                R_cs = Rot("cs2", [96, 2, 512], F32, 1)
                scale_qk = 96.0 ** -0.5
                tasks = [(s, hh, qb) for s in range(nseq) for hh in range(8) for qb in range(L // QN)]
                headc, qc = {}, {}

                def prep_head(s, hh):
                    par = hh % 2
                    kbase = s * Tk
                    kT_t, kT_b = R_kT.nxt()
                    kb.op("dve", lambda h: h.tensor_copy(out=kT_t[64:96, :], in_=krT[64:96, kbase:kbase + Tk]),
                          r=[krb], w=[kT_b])
                    for k5 in range(0, Tk, 512):
                        n5 = min(512, Tk - k5)
                        kb.mm([(psum[7][0:64, 0:n5], Wkv[:, c, hh * 128:hh * 128 + 64],
                                ckvT[:, c, kbase + k5:kbase + k5 + n5]) for c in range(2)], r=[ckvb, wab], w=[psb[7]])
                        kb.op("act", lambda h, k5=k5, n5=n5: h.copy(out=kT_t[0:64, k5:k5 + n5], in_=psum[7][0:64, 0:n5]),
                              r=[psb[7]], w=[kT_b])
                    voff = 0 if par == 0 else 64
                    for kb8 in range(0, nkb, 8):
                        n8 = min(8, nkb - kb8)
                        for q8 in range(n8):
                            kblk = kb8 + q8
                            kb.mm([(psum[7][:, q8 * 64:(q8 + 1) * 64],
                                    ckvT[:, c, kbase + kblk * 128:kbase + (kblk + 1) * 128],
                                    Wkv[:, c, hh * 128 + 64:hh * 128 + 128]) for c in range(2)], r=[ckvb, wab],
                                  w=[psb[7]])
                        kb.op("dve", lambda h, kb8=kb8, n8=n8: h.tensor_copy(
                            out=vaug[par][:, kb8:kb8 + n8, voff:voff + 64],
                            in_=psum[7][:, 0:n8 * 64].rearrange("p (a b) -> p a b", a=n8)), r=[psb[7]], w=[vaugb[par]])
                    return kT_t, kT_b

                def prep_q(s, hh, qb):
                    q0 = s * L + qb * QN
                    jq = q0 // 512
                    q_t, q_b = R_q.nxt()
                    kb.mm([(psum[7][0:96, 0:QN], Wq[:, c, hh * 96:(hh + 1) * 96], cqn[:, c, q0:q0 + QN])
                           for c in range(3)], r=[wab, cqnb[jq]], w=[psb[7]])
                    kb.op("act", lambda h: h.activation(out=q_t[:, 0:QN], in_=psum[7][0:96, 0:QN], func=AF.Identity,
                                                        scale=scale_qk), r=[psb[7]], w=[q_b])
                    if job["rope"]:
                        kb.mm([(psum[3][0:96, 0:QN], P96[:], q_t[:, 0:QN])], r=[q_b, cbuf], w=[psb[3]])
                        cs_t, cs_b = R_cs.nxt()
                        kb.dma("sp", cs_t[64:96, 0, 0:QN], dr["ropeC"][64:96, qb * QN:(qb + 1) * QN], w=[cs_b])
                        kb.dma("sp", cs_t[64:96, 1, 0:QN], dr["ropeS"][64:96, qb * QN:(qb + 1) * QN], w=[cs_b])
                        p_t, p_b = R_rp.nxt()
                        kb.op("dve", lambda h: h.tensor_tensor(out=p_t[64:96, 0:QN], in0=psum[3][64:96, 0:QN],
                                                               in1=cs_t[64:96, 1, 0:QN], op=ALU.mult),
                              r=[psb[3], cs_b], w=[p_b])
                        p2_t, p2_b = R_rp.nxt()
                        kb.op("dve", lambda h: h.scalar_tensor_tensor(
                            out=p2_t[64:96, 0:QN], in0=psum[7][64:96, 0:QN], scalar=scale_qk, in1=cs_t[64:96, 0, 0:QN],
                            op0=ALU.mult, op1=ALU.mult), r=[psb[7], cs_b], w=[p2_b])
                        kb.op("dve", lambda h: h.tensor_tensor(out=q_t[64:96, 0:QN], in0=p_t[64:96, 0:QN],
                                                               in1=p2_t[64:96, 0:QN], op=ALU.add), r=[p_b, p2_b], w=[q_b])
                    return q_t, q_b

                def ensure(i):
                    s, hh, qb = tasks[i]
                    if (s, hh) not in headc:
                        headc[(s, hh)] = prep_head(s, hh)
                    if i not in qc:
                        qc[i] = prep_q(s, hh, qb)

                def attn_main(i):
                    s, hh, qb = tasks[i]
                    par = hh % 2
                    hp = hh // 2
                    q0 = s * L + qb * QN
                    jq = q0 // 512
                    kT_t, kT_b = headc[(s, hh)]
                    q_t, q_b = qc.pop(i)
                    acc, accb = psum[i % 2], psb[i % 2]

                    def qk(kblk):
                        pi = 4 + (kblk % 3)
                        kb.mm([(psum[pi][:, 0:QN], kT_t[:, kblk * 128:(kblk + 1) * 128], q_t[:, 0:QN])], r=[kT_b, q_b],
                              w=[psb[pi]])

                    qk(0)
                    for kblk in range(nkb):
                        if kblk + 1 < nkb:
                            qk(kblk + 1)
                        pi = 4 + (kblk % 3)
                        pT_t, pT_b = R_pT.nxt()
                        kb.op("act", lambda h, pi=pi, pT_t=pT_t: h.activation(out=pT_t[:, 0:QN], in_=psum[pi][:, 0:QN],
                                                                             func=AF.Exp), r=[psb[pi]], w=[pT_b])
                        kb.mm([(acc[:, 0:QN], vaug[par][:, kblk, :], pT_t[:, 0:QN])], r=[vaugb[par], pT_b], w=[accb],
                              start=(kblk == 0), stop=(kblk == nkb - 1))
                    nrows = slice(0, 64) if par == 0 else slice(64, 128)
                    drows = slice(64, 128) if par == 0 else slice(0, 64)
                    kb.op("act", lambda h: h.activation(out=rden[par][drows, 0:QN], in_=acc[drows, 0:QN], func=AF.Ln),
                          r=[accb], w=[rdenb[par]])
                    kb.op("act", lambda h: h.activation(out=rden[par][drows, 0:QN], in_=rden[par][drows, 0:QN],
                                                        func=AF.Exp, scale=-1.0), r=[rdenb[par]], w=[rdenb[par]])
                    kb.mm([(psum[2][:, 0:QN], swapM[:], rden[par][:, 0:QN])], r=[cbuf, rdenb[par]], w=[psb[2]])
                    rb_t, rb_b = R_rb.nxt()
                    kb.op("act", lambda h: h.copy(out=rb_t[nrows, 0:QN], in_=psum[2][nrows, 0:QN]), r=[psb[2]], w=[rb_b])
                    kb.op("dve", lambda h: h.tensor_tensor(out=oC[nrows, hp, q0:q0 + QN], in0=acc[nrows, 0:QN],
                                                           in1=rb_t[nrows, 0:QN], op=ALU.mult), r=[accb, rb_b],
                          w=[oCb[jq]])

                ensure(0)
                for i in range(len(tasks)):
                    if i + 1 < len(tasks):
                        ensure(i + 1)
                    attn_main(i)

                chk("p4")
                kb.barrier()
                cv.off = 0
                oC2 = cv.get([128, 4, T], BF16)
                xn = cv.get([128, 8, T], BF16)
                xnb = [Buf(f"xnm{j}") for j in range(nb)]
                Rot_sq = Rot("sq", [128, 512], BF16, 2)
                Rot_rstd = Rot("rstd", [128, 512], F32, 2)
                Rot_tmp = Rot("tmp", [128, 512], F32, 2)
                xh = cv.get([128, 8, 2], BF16)
                xhb = Buf("xh")
                R_e = Rot("e", [128, 514], F32, 2)
                R_acc = Rot("acc", [128, 512], F32, 2)
                oBt = cv.get([128, 4, 512], BF16)
                oBb = Buf("oB")
                R_a2 = Rot("a2", [128, 2, 512], F32, 1)
                R_oAl = Rot("oAl", [128, 4, 512], BF16, 1)
                hB = cv.get([128, 8, 512], BF16)
                hBb = Buf("hB")
                R_g = Rot("g", [128, 512], F32, 2)

                def halo_cols(xsrc, xsb, j):
                    t0 = j * 512
                    if t0 % L == 0:
                        kb.op("dve", lambda h: h.memset(xh[:, :, 0:1], 0.0), w=[xhb])
                    else:
                        kb.op("dve", lambda h: h.tensor_copy(out=xh[:, :, 0:1], in_=xsrc[:, :, t0 - 1:t0]),
                              r=[xsb[j - 1]], w=[xhb])
                    if (t0 + 512) % L == 0:
                        kb.op("dve", lambda h: h.memset(xh[:, :, 1:2], 0.0), w=[xhb])
                    else:
                        kb.op("dve", lambda h: h.tensor_copy(out=xh[:, :, 1:2], in_=xsrc[:, :, t0 + 512:t0 + 513]),
                              r=[xsb[j + 1]], w=[xhb])

                def conv3(e_t, e_b, acc_t, acc_b, w0, w1, w2, wbuf):
                    kb.op("act", lambda h: h.activation(out=acc_t, in_=e_t[:, 1:513], func=AF.Identity, scale=w1),
                          r=[e_b, wbuf], w=[acc_b])
                    seg = min(L, 512)
                    for a in range(0, 512, seg):
                        lo = a if a == 0 else a + 1
                        kb.op("dve", lambda h, lo=lo, a=a: h.scalar_tensor_tensor(
                            out=acc_t[:, lo:a + seg], in0=e_t[:, lo:a + seg], scalar=w0, in1=acc_t[:, lo:a + seg],
                            op0=ALU.mult, op1=ALU.add), r=[e_b, wbuf, acc_b], w=[acc_b])
                        hi = a + seg if a + seg == 512 else a + seg - 1
                        kb.op("dve", lambda h, hi=hi, a=a: h.scalar_tensor_tensor(
                            out=acc_t[:, a:hi], in0=e_t[:, a + 2:hi + 2], scalar=w2, in1=acc_t[:, a:hi],
                            op0=ALU.mult, op1=ALU.add), r=[e_b, wbuf, acc_b], w=[acc_b])

                norm_all(0, 1, xn, xnb)
                for j in range(nb):
                    blk = slice(j * 512, (j + 1) * 512)
                    halo_cols(xn, xnb, j)
                    for half in range(2):
                        wvc, wbc = wload(win[:, :, OFF_BC + half * 256:OFF_BC + (half + 1) * 256], 8, 256)
                        wvh, wbh = wload(win[:, :, OFF_BH + half * 256:OFF_BH + (half + 1) * 256], 8, 256)
                        wvb, wbb = wload(win[:, :, OFF_BB + half * 256:OFF_BB + (half + 1) * 256], 8, 256)
                        for cc in range(2):
                            g = half * 2 + cc
                            cs_ = slice(cc * 128, (cc + 1) * 128)
                            kb.mm([(psum[0][:], wvc[:, k, cs_], xn[:, k, blk]) for k in range(8)], r=[wbc, xnb[j]],
                                  w=[psb[0]])
                            kb.mm([(psum[1][:], wvh[:, k, cs_], xn[:, k, blk]) for k in range(8)], r=[wbh, xnb[j]],
                                  w=[psb[1]])
                            kb.mm([(psum[2][:, 0:2], wvc[:, k, cs_], xh[:, k, :]) for k in range(8)], r=[wbc, xhb],
                                  w=[psb[2]])
                            kb.mm([(psum[2][:, 2:4], wvh[:, k, cs_], xh[:, k, :]) for k in range(8)], r=[wbh, xhb],
                                  w=[psb[2]])
                            kb.mm([(psum[3][:], wvb[:, k, cs_], xn[:, k, blk]) for k in range(8)], r=[wbb, xnb[j]],
                                  w=[psb[3]])
                            t_t, t_b = Rot_tmp.nxt()
                            kb.op("act", lambda h, t_t=t_t: h.copy(out=t_t, in_=psum[0][:]), r=[psb[0]], w=[t_b])
                            e_t, e_b = R_e.nxt()
                            kb.op("dve", lambda h, t_t=t_t, e_t=e_t: h.tensor_tensor(out=e_t[:, 1:513], in0=psum[1][:],
                                                                                    in1=t_t, op=ALU.mult),
                                  r=[psb[1], t_b], w=[e_b])
                            t2_t, t2_b = Rot_tmp.nxt()
                            kb.op("act", lambda h, t2_t=t2_t: h.copy(out=t2_t[:, 0:2], in_=psum[2][:, 0:2]), r=[psb[2]],
                                  w=[t2_b])
                            kb.op("dve", lambda h, t2_t=t2_t, e_t=e_t: h.tensor_tensor(
                                out=e_t[:, 0:514:513], in0=psum[2][:, 2:4], in1=t2_t[:, 0:2], op=ALU.mult),
                                r=[psb[2], t2_b], w=[e_b])
                            acc_t, acc_b = R_acc.nxt()
                            conv3(e_t, e_b, acc_t, acc_b, fv[:, l, 65 + g:66 + g], fv[:, l, 69 + g:70 + g],
                                  fv[:, l, 73 + g:74 + g], fvb)
                            kb.op("dve", lambda h, g=g, acc_t=acc_t: h.tensor_tensor(out=oBt[:, g, :], in0=psum[3][:],
                                                                                    in1=acc_t, op=ALU.mult),
                                  r=[psb[3], acc_b], w=[oBb])
                    oAl_t, oAl_b = R_oAl.nxt()
                    kb.dma("sp", oAl_t, oA_d[:, :, blk], r=[oAb[j]], w=[oAl_b])
                    srcs = (("w_o_hgrn", oAl_t, oAl_b), ("w_o_conv", oBt, oBb), ("w_o_mla", oC2[:, :, blk], oCb[j]))
                    for o2 in range(0, 8, 2):
                        a2_t, a2_b = R_a2.nxt()
                        for br, (wname, osrc, osb) in enumerate(srcs):
                            wvo, wbo = wload(w3(wname, l)[:, :, o2 * 128:(o2 + 2) * 128], 4, 256)
                            gc0 = OFF_GATE + br * 1024 + o2 * 128
                            wvg, wbg = wload(win[:, :, gc0:gc0 + 256], 8, 256)
                            for o1 in range(2):
                                oc = o2 + o1
                                py, pyb = psum[4 + o1], psb[4 + o1]
                                pg, pgb = psum[6 + o1], psb[6 + o1]
                                kb.mm([(py[:], wvo[:, k, o1 * 128:(o1 + 1) * 128], osrc[:, k, :]) for k in range(4)],
                                      r=[wbo, osb], w=[pyb])
                                kb.mm([(pg[:], wvg[:, k, o1 * 128:(o1 + 1) * 128], xn[:, k, blk]) for k in range(8)],
                                      r=[wbg, xnb[j]], w=[pgb])
                                g_t, g_b = R_g.nxt()
                                kb.op("act", lambda h, pg=pg, g_t=g_t: h.activation(out=g_t, in_=pg[:], func=AF.Sigmoid),
                                      r=[pgb], w=[g_b])
                                if br == 0:
                                    kb.op("dve", lambda h, o1=o1, py=py, g_t=g_t, a2_t=a2_t: h.tensor_tensor(
                                        out=a2_t[:, o1, :], in0=py[:], in1=g_t, op=ALU.mult), r=[pyb, g_b], w=[a2_b])
                                else:
                                    kb.op("dve", lambda h, py=py, g_t=g_t: h.tensor_tensor(
                                        out=g_t, in0=py[:], in1=g_t, op=ALU.mult), r=[pyb, g_b], w=[g_b])
                                    if br == 1:
                                        kb.op("dve", lambda h, o1=o1, g_t=g_t, a2_t=a2_t: h.tensor_tensor(
                                            out=a2_t[:, o1, :], in0=a2_t[:, o1, :], in1=g_t, op=ALU.add),
                                            r=[a2_b, g_b], w=[a2_b])
                                    else:
                                        kb.op("dve", lambda h, oc=oc, o1=o1, g_t=g_t, a2_t=a2_t: h.tensor_tensor(
                                            out=hB[:, oc, :], in0=a2_t[:, o1, :], in1=g_t, op=ALU.add),
                                            r=[a2_b, g_b], w=[hBb])
                    wo3 = w3("w_out", l)
                    for o2 in range(0, 8, 2):
                        wvo, wbo = wload(wo3[:, :, o2 * 128:(o2 + 2) * 128], 8, 256)
                        for o1 in range(2):
                            oc = o2 + o1
                            py, pyb = psum[oc % 2], psb[oc % 2]
                            kb.mm([(py[:], wvo[:, k, o1 * 128:(o1 + 1) * 128], hB[:, k, :]) for k in range(8)],
                                  r=[wbo, hBb], w=[pyb])
                            kb.op("dve", lambda h, oc=oc, py=py: h.scalar_tensor_tensor(
                                out=xT[:, oc, blk], in0=py[:], scalar=dmod[:, 2, oc:oc + 1], in1=xT[:, oc, blk],
                                op0=ALU.mult, op1=ALU.add), r=[pyb, dmodb, xbuf[j]], w=[xbuf[j]])

                chk("p5")
                phase()
                xn2 = cv.get([128, 8, T], BF16)
                xn2b = [Buf(f"xn2{j}") for j in range(nb)]
                Rot_sq = Rot("sq", [128, 512], BF16, 2)
                Rot_rstd = Rot("rstd", [128, 512], F32, 2)
                Rot_tmp = Rot("tmp", [128, 512], F32, 2)
                xh = cv.get([128, 8, 2], BF16)
                xhb = Buf("xh")
                R_e = Rot("e", [128, 514], F32, 3)
                R_acc = Rot("acc", [128, 512], F32, 3)
                hid = cv.get([128, 22, 512], BF16)
                hidb = Buf("hid")
                norm_all(3, 4, xn2, xn2b)
                wup = dr["w_up"][l].rearrange("(kc p) (two n) -> p kc two n", p=128, two=2)
                wd3 = w3("w_down", l)
                fo = 82
                for j in range(nb):
                    blk = slice(j * 512, (j + 1) * 512)
                    halo_cols(xn2, xn2b, j)
                    for m in range(22):
                        wv, wb = wload(wup[:, :, :, m * 128:(m + 1) * 128], 8, 128, extra=2)
                        accs = []
                        for ab in range(2):
                            pm, pmb = psum[2 * ab], psb[2 * ab]
                            ph, phb = psum[2 * ab + 1], psb[2 * ab + 1]
                            kb.mm([(pm[:], wv[:, k, ab, :], xn2[:, k, blk]) for k in range(8)], r=[wb, xn2b[j]], w=[pmb])
                            kb.mm([(ph[:, 0:2], wv[:, k, ab, :], xh[:, k, :]) for k in range(8)], r=[wb, xhb], w=[phb])
                            e_t, e_b = R_e.nxt()
                            kb.op("act", lambda h, pm=pm, e_t=e_t: h.copy(out=e_t[:, 1:513], in_=pm[:]), r=[pmb], w=[e_b])
                            kb.op("dve", lambda h, ph=ph, e_t=e_t: h.tensor_copy(out=e_t[:, 0:514:513], in_=ph[:, 0:2]),
                                  r=[phb, e_b], w=[e_b])
                            acc_t, acc_b = R_acc.nxt()
                            mm_ = ab * 22 + m
                            conv3(e_t, e_b, acc_t, acc_b, fv[:, l, fo + mm_:fo + mm_ + 1],
                                  fv[:, l, fo + 44 + mm_:fo + 44 + mm_ + 1], fv[:, l, fo + 88 + mm_:fo + 88 + mm_ + 1], fvb)
                            accs.append((acc_t, acc_b))
                        (a_t, a_b), (b_t, b_b) = accs
                        t_t, t_b = Rot_tmp.nxt()
                        kb.op("act", lambda h, a_t=a_t, t_t=t_t: h.activation(out=t_t, in_=a_t, func=AF.Silu), r=[a_b],
                              w=[t_b])
                        kb.op("dve", lambda h, m=m, t_t=t_t, b_t=b_t: h.tensor_tensor(out=hid[:, m, :], in0=t_t, in1=b_t,
                                                                                     op=ALU.mult), r=[t_b, b_b], w=[hidb])
                    for oc in range(8):
                        wv, wb = wload(wd3[:, :, oc * 128:(oc + 1) * 128], 22, 128)
                        py, pyb = psum[4 + (oc % 2)], psb[4 + (oc % 2)]
                        kb.mm([(py[:], wv[:, k, :], hid[:, k, :]) for k in range(22)], r=[wb, hidb], w=[pyb])
                        kb.op("dve", lambda h, oc=oc, py=py: h.scalar_tensor_tensor(
                            out=xT[:, oc, blk], in0=py[:], scalar=dmod[:, 5, oc:oc + 1], in1=xT[:, oc, blk],
                            op0=ALU.mult, op1=ALU.add), r=[pyb, dmodb, xbuf[j]], w=[xbuf[j]])

            chk("p6")
            phase()
            Rot_sq = Rot("sq", [128, 512], BF16, 2)
            Rot_rstd = Rot("rstd", [128, 512], F32, 2)
            yt = cv.get([128, 8, 512], F32)
            ytb = Buf("yt")
            ost = Rot("ost", [128, 1024], F32, 2)
            for j in range(nb):
                r_t, r_b = rstd_block(j)
                for k in range(8):
                    kb.op("dve", lambda h, k=k: h.scalar_tensor_tensor(
                        out=yt[:, k, :], in0=xT[:, k, j * 512:(j + 1) * 512], scalar=fv[:, 0, 214 + k:215 + k], in1=r_t,
                        op0=ALU.mult, op1=ALU.mult), r=[xbuf[j], fvb, r_b], w=[ytb])
                for sub in range(4):
                    o_t, o_b = ost.nxt()
                    for half in range(2):
                        pi = half
                        kb.tr([(psum[pi][:, q * 128:(q + 1) * 128], yt[:, half * 4 + q, sub * 128:(sub + 1) * 128])
                               for q in range(4)], identF[:], r=[ytb, cbuf], w=[psb[pi]])
                        if half == 0:
                            kb.op("dve", lambda h, o_t=o_t: h.tensor_copy(out=o_t[:, 0:512], in_=psum[0][:]), r=[psb[0]],
                                  w=[o_b])
                        else:
                            kb.op("act", lambda h, o_t=o_t: h.copy(out=o_t[:, 512:1024], in_=psum[1][:]), r=[psb[1]],
                                  w=[o_b])
                    r0 = j * 512 + sub * 128
                    kb.dma("sp", job["yout"][r0:r0 + 128, :], o_t, r=[o_b])

        jobs = [
            dict(idx=0, T=NP_SEQ * L_P, L=L_P, nseq=NP_SEQ, rope=False, cache=False, xin=dr["xp"], yout=dr["yp"]),
            dict(idx=1, T=L_S, L=L_S, nseq=1, rope=True, cache=True, xin=dr["xs"], yout=dr["ys"]),
        ]
        try:
            chk("mod")
            for job in jobs:
                run_job(job)
        except _Stop:
            pass
        kb.barrier(engines=("pe", "act", "dve", "sp", "pool"), include_pool=True)
    return nc, hc


def kernel(**inputs):
    n = 8
    if "nc" not in _CACHE:
        _CACHE["nc"] = build()
    nc, hc = _CACHE["nc"]
    f = lambda a: np.ascontiguousarray(np.asarray(a, dtype=np.float32))
    xp = f(inputs["x_prompt"])
    xs = f(inputs["x_sample"])
    st = f(inputs["state_hgrn"])
    cm = f(inputs["cache_mla"])
    c = f(inputs["c"])
    cctx = f(inputs["c_ctx"])
    in_maps = []
    for i in range(n):
        m = {
            "xp": xp[4 * i:4 * i + 4].reshape(NP_SEQ * L_P, D),
            "xs": xs[i],
            "st_in": st[i],
            "cache_in": cm[i],
            "cvec": np.stack([cctx, c[i]], axis=0),
        }
        for wn in WEIGHTS:
            m[wn] = f(inputs[wn])
        for k, v in hc.items():
            m[k] = v
        in_maps.append({k: np.ascontiguousarray(v) for k, v in m.items()})
    res = run_bass_kernel_spmd(nc, in_maps, core_ids=list(range(n)))
    R = res.results
    y_p = np.concatenate([r["yp"].reshape(NP_SEQ, L_P, D) for r in R], axis=0)
    y_s = np.stack([r["ys"] for r in R], axis=0)
    st_o = np.concatenate([r["st_out"] for r in R], axis=0)
    ch_o = np.concatenate([r["cache_out"] for r in R], axis=0)
    return (y_p.astype(np.float32), y_s.astype(np.float32), st_o.astype(np.float32), ch_o.astype(np.float32))
```

```python
import types
import numpy as np
import ml_dtypes
from contextlib import ExitStack
import concourse.bass as bass
import concourse.mybir as mybir
from concourse.bass_utils import run_bass_kernel_spmd

F32 = mybir.dt.float32
BF16 = mybir.dt.bfloat16
AF = mybir.ActivationFunctionType
ALU = mybir.AluOpType

D = 1024
DEPTH = 2
A_W = 512
B_W = 512
C_Q = 384
C_KV = 256
C_ROPE = 32
D_FF = 2816
IN_COLS = 7840
OFF_Q, OFF_FF, OFF_FB, OFF_I, OFF_G = 0, 512, 1024, 1536, 2048
OFF_BB, OFF_BC, OFF_BH = 2560, 3072, 3584
OFF_CQ = 4096
OFF_CKV = 4480
OFF_GATE = 4768
EPS = 1e-6
NP_SEQ, L_P = 4, 256
L_S = 2048
PAST = 256
WSLOT = 2816


class Buf:
    __slots__ = ("name", "w", "r", "excl")

    def __init__(self, name, excl=False):
        self.name = name
        self.w = None
        self.r = []
        self.excl = excl


def _snap(fn):
    if fn.__closure__ is None:
        return fn
    cells = []
    for c in fn.__closure__:
        try:
            cells.append(types.CellType(c.cell_contents))
        except ValueError:
            cells.append(c)
    return types.FunctionType(fn.__code__, fn.__globals__, fn.__name__, fn.__defaults__, tuple(cells))


class Node:
    __slots__ = ("eng", "emit", "deps", "cost", "lat", "idx", "sig", "start", "finish", "prev", "isdma", "pri", "bl")


class Eng:
    def __init__(self, name, h, sems, is_pe=False):
        self.name = name
        self.h = h
        self.sems = sems
        self.si = 0
        self.cnt = 0
        self.seen = {}
        self.is_pe = is_pe
        self.dslots = []
        self.di = 0


class KB:
    ROLL = 16000
    W = 48
    LAT = 0.2

    def __init__(self, nc, es):
        self.nc = nc
        self.es = es
        self.E = {}
        for name, h, n in (("pe", nc.tensor, 6), ("act", nc.scalar, 6), ("dve", nc.vector, 8),
                           ("pool", nc.gpsimd, 3), ("sp", nc.sync, 1)):
            sems = [es.enter_context(nc.semaphore(f"s_{name}{i}")) for i in range(n)]
            self.E[name] = Eng(name, h, sems, is_pe=(name == "pe"))
        for qn, n in (("sp", 12), ("pool", 8)):
            q = self.E[qn]
            q.dslots = [[es.enter_context(nc.semaphore(f"d_{qn}{i}")), 0] for i in range(n)]
        self.pe_nums = set(s.num for s in self.E["pe"].sems)
        self.pe_ins = 0
        self.marks = []
        self.nodes = []
        self.nidx = 0
        self.reorder = True

    def _mk(self, en, emit, r, w, cost, lat=None):
        nd = Node()
        nd.eng = en
        nd.emit = emit
        nd.cost = cost
        nd.lat = cost if lat is None else lat
        nd.idx = self.nidx
        self.nidx += 1
        nd.sig = None
        nd.start = None
        nd.finish = None
        nd.prev = 0
        nd.isdma = False
        nd.pri = 0.4 if (en != "pe" and any(b.excl for b in r)) else 0.0
        deps = set()
        for b in r:
            if b.w is not None:
                deps.add(b.w)
            if b.excl:
                deps.update(b.r)
        for b in w:
            if b.w is not None:
                deps.add(b.w)
            deps.update(b.r)
        nd.deps = deps
        for b in r:
            b.r.append(nd)
        for b in w:
            b.w = nd
            b.r = []
        self.nodes.append(nd)
        return nd

    def op(self, en, fn, r=(), w=(), n=512):
        if en == "act":
            cost = 0.22 + n / 1400.0
        elif en == "dve":
            cost = 0.10 + max(n, 64) / 960.0
        else:
            cost = 0.30 + n / 450.0

        fn2 = _snap(fn)

        def emit(h):
            return fn2(h)
        return self._mk(en, emit, r, w, cost)

    def mm(self, steps, r=(), w=(), start=True, stop=True):
        n = len(steps)
        self.pe_ins += n
        cost = 0.0
        for st in steps:
            fr = 1
            for d_ in st[2].shape[1:]:
                fr *= d_
            c = max(fr, 64) / 2400.0 + 0.05
            if st[1].dtype == F32:
                c *= 4
            cost += c

        def emit(h):
            ins = None
            for i, st in enumerate(steps):
                kw = st[3] if len(st) > 3 else {}
                ins = h.matmul(st[0], st[1], st[2], start=(start and i == 0), stop=(stop and i == n - 1), **kw)
            return ins
        return self._mk("pe", emit, r, w, cost, lat=cost + 0.15)

    def tr(self, outs_ins, ident, r=(), w=()):
        self.pe_ins += len(outs_ins)

        def emit(h):
            ins = None
            for out, in_ in outs_ins:
                ins = h.transpose(out, in_, ident)
            return ins
        return self._mk("pe", emit, r, w, 0.12 * len(outs_ins), lat=0.12 * len(outs_ins) + 0.15)

    def dma(self, qn, out, in_, r=(), w=()):
        nbytes = 1
        for d_ in in_.shape:
            nbytes *= d_
        nbytes *= 4 if in_.dtype == F32 else 2

        def emit(h):
            return h.dma_start(out=out, in_=in_)
        occ = 1.1 if qn == "pool" else 0.1
        nd = self._mk(qn, emit, r, w, occ, lat=2.2 + nbytes / 150e3)
        nd.isdma = True
        return nd

    def flush(self):
        nodes = self.nodes
        self.nodes = []
        if not nodes:
            return
        per = {en: [] for en in self.E}
        for nd in nodes:
            per[nd.eng].append(nd)
        order = {en: [] for en in self.E}
        if not self.reorder:
            for en in per:
                order[en] = per[en]
        else:
            for nd in nodes:
                nd.bl = nd.lat
            for nd in reversed(nodes):
                b_ = nd.bl
                for d_ in nd.deps:
                    if d_.start is None:
                        v_ = b_ + d_.lat
                        if v_ > d_.bl:
                            d_.bl = v_
            head = {en: 0 for en in self.E}
            t_eng = {en: 0.0 for en in self.E}
            nleft = len(nodes)
            W, LAT = self.W, self.LAT
            while nleft:
                best = None
                for en, lst in per.items():
                    hpos = head[en]
                    L_ = len(lst)
                    while hpos < L_ and lst[hpos].start is not None:
                        hpos += 1
                    head[en] = hpos
                    cnt = 0
                    i = hpos
                    te = t_eng[en]
                    while i < L_ and cnt < W:
                        nd = lst[i]
                        i += 1
                        if nd.start is not None:
                            continue
                        cnt += 1
                        rt = te
                        ok = True
                        for d_ in nd.deps:
                            if d_.start is None:
                                ok = False
                                break
                            f = d_.finish if d_.eng == en else d_.finish + LAT
                            if f > rt:
                                rt = f
                        if not ok:
                            continue
                        key = rt - nd.pri
                        if best is None or key < best[3] - 0.05 or (key < best[3] + 0.05 and nd.bl > best[1].bl):
                            best = (rt, nd, en, key)
                        if rt <= te and (en == "pe" or nd.pri > 0):
                            break
                assert best is not None, "scheduler stuck"
                rt, nd, en = best[0], best[1], best[2]
                nd.start = rt
                nd.finish = rt + nd.lat
                t_eng[en] = rt + nd.cost
                order[en].append(nd)
                nleft -= 1
        for en, lst in order.items():
            e = self.E[en]
            for nd in lst:
                if nd.isdma:
                    slot = e.dslots[e.di % len(e.dslots)]
                    e.di += 1
                    nd.prev = (slot[0], slot[1])
                    slot[1] += 16
                    nd.sig = (slot[0], slot[1], 16)
                else:
                    if e.cnt >= self.ROLL:
                        e.si += 1
                        e.cnt = 0
                    e.cnt += 1
                    nd.sig = (e.sems[e.si], e.cnt, 1)
        for en, lst in order.items():
            e = self.E[en]
            for nd in lst:
                best = {}
                for d_ in nd.deps:
                    sem, val = d_.sig[0], d_.sig[1]
                    if e.is_pe and sem.num in self.pe_nums:
                        continue
                    if val > best.get(sem.num, (None, 0))[1]:
                        best[sem.num] = (sem, val)
                if nd.sig[2] == 16 and nd.prev[1] > 0:
                    sem, val = nd.prev
                    if val > best.get(sem.num, (None, 0))[1]:
                        best[sem.num] = (sem, val)
                for num, (sem, val) in best.items():
                    if e.seen.get(num, 0) < val:
                        e.h.wait_ge(sem, val)
                        e.seen[num] = val
                ins = nd.emit(e.h)
                ins.then_inc(nd.sig[0], nd.sig[2])
                nd.emit = None
        for nd in nodes:
            nd.start = 0.0
            nd.finish = 0.0
            nd.deps = ()

    def barrier(self, engines=("pe", "act", "dve", "sp"), include_pool=False):
        self.flush()
        evs = []
        for en, e in self.E.items():
            if en == "pool" and not include_pool:
                continue
            if e.cnt > 0:
                evs.append((e.sems[e.si], e.cnt))
            for sl in e.dslots:
                if sl[1] > 0:
                    evs.append((sl[0], sl[1]))
        for en in engines:
            e = self.E[en]
            for sem, val in evs:
                if e.is_pe and sem.num in self.pe_nums:
                    continue
                if e.seen.get(sem.num, 0) < val:
                    e.h.wait_ge(sem, val)
                    e.seen[sem.num] = val


def host_consts():
    c = {}
    c["identF"] = np.eye(128, dtype=np.float32)
    s = np.arange(128)[:, None]
    t = np.arange(128)[None, :]
    same = (s // 32) == (t // 32)
    c["mLE"] = (same & (s <= t)).astype(np.float32)
    c["mGE"] = (same & (s >= t)).astype(np.float32)
    c["mLT"] = (same & (s < t)).astype(np.float32)
    c["mGT"] = (same & (s > t)).astype(np.float32)
    T = L_S
    rows = np.repeat(np.arange(T // 64, dtype=np.float32), 64)
    col = np.tile(np.arange(64, dtype=np.float32), T // 64)
    nf = 8
    freq = (np.float32(10000.0) ** (-np.arange(nf, dtype=np.float32) / np.float32(nf))).astype(np.float32)
    ang = np.stack([rows[:, None] * freq, col[:, None] * freq], axis=1).astype(np.float32)
    cosv = np.cos(ang).astype(np.float32)
    sinv = np.sin(ang).astype(np.float32)
    C = np.zeros((96, T), np.float32)
    S = np.zeros((96, T), np.float32)
    for a in range(2):
        for hf in range(2):
            for f in range(nf):
                C[64 + a * 16 + hf * 8 + f] = cosv[:, a, f]
                S[64 + a * 16 + hf * 8 + f] = sinv[:, a, f]
    c["ropeC"] = C
    c["ropeS"] = S
    P = np.zeros((96, 96), np.float32)
    for a in range(2):
        for f in range(nf):
            m0 = 64 + a * 16 + f
            m1 = 64 + a * 16 + 8 + f
            P[m1, m0] = -1.0
            P[m0, m1] = 1.0
    c["P96"] = P
    sw = np.zeros((128, 128), np.float32)
    for m_ in range(128):
        sw[(m_ + 64) % 128, m_] = 1.0
    c["swapM"] = sw
    return c


WEIGHTS = ["w_ada", "b_ada", "norm1", "w_in", "hgrn_lb_logits", "hgrn_gnorm", "w_o_hgrn", "conv_w", "w_o_conv",
           "mla_q_norm", "w_q_up", "mla_kv_norm", "w_kv_up", "w_o_mla", "w_out", "norm2", "w_up", "ffn_conv_w",
           "w_down", "final_norm"]
W_SHAPES = {
    "w_ada": [2, 1024, 6144], "b_ada": [2, 6144], "norm1": [2, 1024], "w_in": [2, 1024, IN_COLS],
    "hgrn_lb_logits": [2, 2, 512], "hgrn_gnorm": [2, 128], "w_o_hgrn": [2, 512, 1024], "conv_w": [2, 3, 512],
    "w_o_conv": [2, 512, 1024], "mla_q_norm": [2, 384], "w_q_up": [2, 384, 768], "mla_kv_norm": [2, 256],
    "w_kv_up": [2, 256, 1024], "w_o_mla": [2, 512, 1024], "w_out": [2, 1024, 1024], "norm2": [2, 1024],
    "w_up": [2, 1024, 5632], "ffn_conv_w": [2, 3, 5632], "w_down": [2, 2816, 1024], "final_norm": [1024],
}


_CACHE = {}


class _Stop(Exception):
    pass


def build(debug=None, stop=None):
    nc = bass.Bass("TRN2", target_bir_lowering=False)
    dr = {}

    def din(name, shape, dt=F32):
        dr[name] = nc.dram_tensor(name, list(shape), dt, kind="ExternalInput").ap()
        return dr[name]

    def dout(name, shape):
        dr[name] = nc.dram_tensor(name, list(shape), F32, kind="ExternalOutput").ap()
        return dr[name]

    din("xp", [NP_SEQ * L_P, D])
    din("xs", [L_S, D])
    din("st_in", [2, 2, 4, 128, 128])
    din("cache_in", [2, PAST, C_KV + C_ROPE])
    din("cvec", [2, D])
    for wn in WEIGHTS:
        din(wn, W_SHAPES[wn])
    hc = host_consts()
    for k, v in hc.items():
        din(k, v.shape)
    dout("yp", [NP_SEQ * L_P, D])
    dout("ys", [L_S, D])
    dout("st_out", [NP_SEQ, 2, 2, 4, 128, 128])
    dout("cache_out", [NP_SEQ, 2, L_P, C_KV + C_ROPE])
    if debug:
        dout("dbg", [128, debug])
    ofw_d = nc.dram_tensor("ofw_d", [128, 4, L_S], BF16).ap()
    oA_d = nc.dram_tensor("oA_d", [128, 4, L_S], BF16).ap()

    with ExitStack() as es:
        kb = KB(nc, es)
        _CACHE['kb'] = kb

        def sb(name, shape, dt):
            return es.enter_context(nc.sbuf_tensor("sb_" + name, list(shape), dt))

        xT = sb("xT", [128, 8, L_S], F32)
        xbuf = [Buf(f"x{j}") for j in range(L_S // 512)]
        wsl = [sb(f"wsl{i}", [128, WSLOT], BF16) for i in range(4)]
        wslb = [Buf(f"wsl{i}") for i in range(4)]
        wctr = [0]
        identF = sb("identF", [128, 128], F32)
        identB = sb("identB", [128, 128], BF16)
        onesB = sb("onesB", [128, 128], BF16)
        mLE = sb("mLE", [128, 128], F32)
        mGE = sb("mGE", [128, 128], F32)
        mLT = sb("mLT", [128, 128], F32)
        mGT = sb("mGT", [128, 128], F32)
        swapM = sb("swapM", [128, 128], F32)
        P96 = sb("P96", [96, 96], BF16)
        P96f = sb("P96f", [96, 96], F32)
        epsT = sb("epsT", [128, 1], F32)
        cbuf = Buf("consts")
        fv = sb("fv", [128, 2, 224], F32)
        fvb = Buf("fv")
        modT = sb("modT", [128, 2, 48, 2], F32)
        modb = Buf("mod")
        dmod = sb("dmod", [128, 6, 8], F32)
        dmodb = Buf("dmod")
        lbT = sb("lbT", [128, 2, 2, 512], F32)
        lbb = Buf("lb")
        kvnB = sb("kvnB", [128, 2, 256], F32)
        kvnb = Buf("kvnB")
        ARENA = 53600
        arena = sb("arena", [128, ARENA], BF16)
        psum = [es.enter_context(nc.psum_tensor(f"ps{i}", [128, 512], F32)) for i in range(8)]
        psb = [Buf(f"ps{i}", excl=True) for i in range(8)]

        class Carver:
            def __init__(self):
                self.off = 0

            def reset(self):
                self.off = 0

            def get(self, shape, dt, nbuf=1):
                n = int(np.prod(shape[1:]))
                nb = n * (4 if dt == F32 else 2)
                nb = (nb + 3) // 4 * 4
                o = self.off
                self.off += nb
                assert self.off <= ARENA * 2, f"arena overflow {self.off}"
                ap = arena[0:shape[0], o // 2:(o + nb) // 2]
                if dt == F32:
                    ap = ap.bitcast(F32)
                ap = ap[:, 0:n]
                if len(shape) == 3:
                    ap = ap.rearrange("p (a b) -> p a b", a=shape[1])
                elif len(shape) == 4:
                    ap = ap.rearrange("p (a b c) -> p a b c", a=shape[1], b=shape[2])
                return ap

        cv = Carver()

        def phase():
            kb.barrier()
            cv.reset()

        class Rot:
            def __init__(self, name, shape, dt, n):
                self.t = [cv.get(shape, dt) for _ in range(n)]
                self.b = [Buf(f"{name}{i}") for i in range(n)]
                self.i = 0

            def nxt(self):
                i = self.i % len(self.t)
                self.i += 1
                return self.t[i], self.b[i]

        def wload(src, kc, ncols, extra=None):
            i = wctr[0] % 4
            wctr[0] += 1
            n = kc * ncols * (extra or 1)
            assert n <= WSLOT
            if extra:
                view = wsl[i][:, 0:n].rearrange("p (k e n) -> p k e n", k=kc, e=extra)
            else:
                view = wsl[i][:, 0:n].rearrange("p (k n) -> p k n", k=kc)
            if extra:
                for e_ in range(extra):
                    kb.dma("pool", view[:, :, e_, :], src[:, :, e_, :], w=[wslb[i]])
            else:
                kb.dma("pool", view, src, w=[wslb[i]])
            return view, wslb[i]

        def w3(name, l):
            return dr[name][l].rearrange("(kc p) n -> p kc n", p=128)

        for nm, t_ in (("identF", identF), ("mLE", mLE), ("mGE", mGE), ("mLT", mLT), ("mGT", mGT), ("swapM", swapM)):
            kb.dma("sp", t_[:], dr[nm], w=[cbuf])
        kb.dma("sp", P96f[:], dr["P96"], w=[cbuf])
        kb.op("dve", lambda h: h.tensor_copy(out=identB[:], in_=identF[:]), r=[cbuf], w=[cbuf])
        kb.op("dve", lambda h: h.tensor_copy(out=P96[:], in_=P96f[:]), r=[cbuf], w=[cbuf])
        kb.op("dve", lambda h: h.memset(onesB[:], 1.0), w=[cbuf])
        kb.op("dve", lambda h: h.memset(epsT[:], EPS), w=[cbuf])

        cv.reset()
        stg = cv.get([128, 128], F32)
        stgb = Buf("stg")
        cin = sb("cin", [128, 2, 8], F32)

        def featvec(rows_ap, nrows, dst_ap, dstbuf):
            kb.dma("sp", stg[0:nrows, :], rows_ap, w=[stgb])
            kb.tr([(psum[7][:, 0:nrows], stg[0:nrows, :])], identF[0:nrows, 0:nrows], r=[stgb, cbuf], w=[psb[7]])
            kb.op("dve", lambda h: h.tensor_copy(out=dst_ap, in_=psum[7][:, 0:nrows]), r=[psb[7]], w=[dstbuf])

        for l in range(DEPTH):
            featvec(dr["b_ada"][l].rearrange("(j p) -> j p", p=128), 48, fv[:, l, 0:48], fvb)
            featvec(dr["norm1"][l].rearrange("(j p) -> j p", p=128), 8, fv[:, l, 48:56], fvb)
            featvec(dr["norm2"][l].rearrange("(j p) -> j p", p=128), 8, fv[:, l, 56:64], fvb)
            featvec(dr["hgrn_gnorm"][l].rearrange("(j p) -> j p", p=128), 1, fv[:, l, 64:65], fvb)
            featvec(dr["conv_w"][l].rearrange("j (g p) -> (j g) p", p=128), 12, fv[:, l, 65:77], fvb)
            featvec(dr["mla_q_norm"][l].rearrange("(j p) -> j p", p=128), 3, fv[:, l, 77:80], fvb)
            featvec(dr["mla_kv_norm"][l].rearrange("(j p) -> j p", p=128), 2, fv[:, l, 80:82], fvb)
            fc = dr["ffn_conv_w"][l].rearrange("j (m p) -> (j m) p", p=128)
            featvec(fc[0:88], 88, fv[:, l, 82:170], fvb)
            featvec(fc[88:132], 44, fv[:, l, 170:214], fvb)
        featvec(dr["final_norm"].rearrange("(j p) -> j p", p=128), 8, fv[:, 0, 214:222], fvb)
        featvec(dr["cvec"].rearrange("c (j p) -> (c j) p", p=128), 16, cin[:].rearrange("p c j -> p (c j)"), fvb)

        lg = cv.get([128, 2, 2, 512], F32)
        lgb = Buf("lg")
        for l in range(2):
            for d_ in range(2):
                kb.dma("sp", lg[:, l, d_, :], dr["hgrn_lb_logits"][l, d_].partition_broadcast(128), w=[lgb])
        for d_ in range(2):
            kb.op("dve", lambda h, d_=d_: h.tensor_tensor(out=lbT[:, d_, 0, :], in0=lg[:, 1, d_, :], in1=lg[:, 0, d_, :],
                                                          op=ALU.subtract), r=[lgb], w=[lbb])
            kb.op("act", lambda h, d_=d_: h.activation(out=lbT[:, d_, 0, :], in_=lbT[:, d_, 0, :], func=AF.Sigmoid),
                  r=[lbb], w=[lbb])
            kb.op("dve", lambda h, d_=d_: h.tensor_scalar(out=lbT[:, d_, 1, :], in0=lbT[:, d_, 0, :], scalar1=-1.0,
                                                          scalar2=1.0, op0=ALU.mult, op1=ALU.add), r=[lbb], w=[lbb])
        for l in range(2):
            kb.dma("sp", kvnB[:, l, :], dr["mla_kv_norm"][l].partition_broadcast(128), w=[kvnb])

        silc = cv.get([128, 8, 2], BF16)
        silb = Buf("silc")
        kb.op("act", lambda h: h.activation(out=silc[:].rearrange("p k c -> p c k"), in_=cin[:], func=AF.Silu),
              r=[fvb], w=[silb])
        for l in range(DEPTH):
            wa = w3("w_ada", l)
            for js in range(0, 48, 2):
                wv, wb = wload(wa[:, :, js * 128:(js + 2) * 128], 8, 256)
                for j in range(js, js + 2):
                    kb.mm([(psum[6][:, 2 * j:2 * j + 2], wv[:, k, (j - js) * 128:(j - js + 1) * 128], silc[:, k, :])
                           for k in range(8)], r=[wb, silb], w=[psb[6]])
            kb.op("dve", lambda h, l=l: h.tensor_tensor(
                out=modT[:, l, :, :], in0=psum[6][:, 0:96].rearrange("p (j c) -> p j c", c=2),
                in1=fv[:, l, 0:48].unsqueeze(2).to_broadcast([128, 48, 2]), op=ALU.add), r=[psb[6], fvb], w=[modb])

        def chk(name):
            kb.marks.append((name, kb.pe_ins))
            if stop == name:
                raise _Stop()

        def run_job(job):
            T, L, nseq = job["T"], job["L"], job["nseq"]
            ji = job["idx"]
            nb = T // 512
            n128 = T // 128
            QN = min(512, L)
            Tk = L + (PAST if job["cache"] else 0)
            nkb = Tk // 128

            phase()
            st0 = Rot("xst", [128, 1024], F32, 2)
            for i in range(n128):
                s_t, s_b = st0.nxt()
                kb.dma("sp", s_t, job["xin"][i * 128:(i + 1) * 128, :], w=[s_b])
                for half in range(2):
                    pi = (2 * i + half) % 4
                    kb.tr([(psum[pi][:, q * 128:(q + 1) * 128], s_t[:, (half * 4 + q) * 128:(half * 4 + q + 1) * 128])
                           for q in range(4)], identF[:], r=[s_b, cbuf], w=[psb[pi]])
                    eng = "dve" if half == 0 else "act"
                    if eng == "dve":
                        kb.op("dve", lambda h, pi=pi, half=half, i=i: h.tensor_copy(
                            out=xT[:, half * 4:half * 4 + 4, i * 128:(i + 1) * 128],
                            in_=psum[pi][:].rearrange("p (q t) -> p q t", q=4)), r=[psb[pi]], w=[xbuf[i // 4]])
                    else:
                        kb.op("act", lambda h, pi=pi, half=half, i=i: h.copy(
                            out=xT[:, half * 4:half * 4 + 4, i * 128:(i + 1) * 128],
                            in_=psum[pi][:].rearrange("p (q t) -> p q t", q=4)), r=[psb[pi]], w=[xbuf[i // 4]])

            chk("p0")

            def rstd_block(j, ntok=512):
                sq = Rot_sq
                for k in range(8):
                    s_t, s_b = sq.nxt()
                    kb.op("act", lambda h, k=k, s_t=s_t: h.activation(out=s_t, in_=xT[:, k, j * 512:(j + 1) * 512],
                                                                      func=AF.Square), r=[xbuf[j]], w=[s_b])
                    kb.mm([(psum[5][:], onesB[:], s_t)], r=[s_b, cbuf], w=[psb[5]], start=(k == 0), stop=(k == 7))
                r_t, r_b = Rot_rstd.nxt()
                kb.op("act", lambda h: h.activation(out=r_t, in_=psum[5][:], func=AF.Ln, bias=epsT[:], scale=1.0 / D),
                      r=[psb[5], cbuf], w=[r_b])
                kb.op("act", lambda h: h.activation(out=r_t, in_=r_t, func=AF.Exp, scale=-0.5), r=[r_b], w=[r_b])
                return r_t, r_b

            class VirtXN:
                def __init__(self):
                    self.tiles = {}

                def __getitem__(self, key):
                    p_, k_, ts_ = key
                    j_ = ts_.start // 512
                    return self.tiles[j_][p_, k_, ts_.start - j_ * 512:ts_.stop - j_ * 512]

            def prep_xn(j, gi, si, xnv, xnb, R_xn):
                t_, b_ = R_xn.nxt()
                xnv.tiles[j] = t_
                xnb[j] = b_
                r_t, r_b = rstd_block(j)
                for k in range(8):
                    t_t, t_b = Rot_tmp.nxt()
                    kb.op("dve", lambda h, k=k, t_t=t_t: h.scalar_tensor_tensor(
                        out=t_t, in0=xT[:, k, j * 512:(j + 1) * 512], scalar=dmod[:, gi, k:k + 1], in1=r_t,
                        op0=ALU.mult, op1=ALU.mult), r=[xbuf[j], dmodb, r_b], w=[t_b])
                    kb.op("act", lambda h, k=k, t_t=t_t: h.activation(
                        out=t_[:, k, :], in_=t_t, func=AF.Identity,
                        bias=dmod[:, si, k:k + 1], scale=1.0), r=[t_b, dmodb], w=[b_])

            def norm_all(gi, si, xn, xnb):
                for j in range(nb):
                    r_t, r_b = rstd_block(j)
                    for k in range(8):
                        t_t, t_b = Rot_tmp.nxt()
                        kb.op("dve", lambda h, k=k, t_t=t_t: h.scalar_tensor_tensor(
                            out=t_t, in0=xT[:, k, j * 512:(j + 1) * 512], scalar=dmod[:, gi, k:k + 1], in1=r_t,
                            op0=ALU.mult, op1=ALU.mult), r=[xbuf[j], dmodb, r_b], w=[t_b])
                        kb.op("act", lambda h, k=k, t_t=t_t: h.activation(
                            out=xn[:, k, j * 512:(j + 1) * 512], in_=t_t, func=AF.Identity,
                            bias=dmod[:, si, k:k + 1], scale=1.0), r=[t_b, dmodb], w=[xnb[j]])

            for l in range(DEPTH):
                fvl = lambda a, b: fv[:, l, a:b]
                phase()
                mo = lambda c0: modT[:, l, c0:c0 + 8, ji]
                kb.op("dve", lambda h: h.scalar_tensor_tensor(out=dmod[:, 0, :], in0=mo(8), scalar=1.0, in1=fvl(48, 56),
                                                              op0=ALU.add, op1=ALU.mult), r=[modb, fvb], w=[dmodb])
                kb.op("dve", lambda h: h.tensor_copy(out=dmod[:, 1, :], in_=mo(0)), r=[modb], w=[dmodb])
                kb.op("dve", lambda h: h.tensor_copy(out=dmod[:, 2, :], in_=mo(16)), r=[modb], w=[dmodb])
                kb.op("dve", lambda h: h.scalar_tensor_tensor(out=dmod[:, 3, :], in0=mo(32), scalar=1.0, in1=fvl(56, 64),
                                                              op0=ALU.add, op1=ALU.mult), r=[modb, fvb], w=[dmodb])
                kb.op("dve", lambda h: h.tensor_copy(out=dmod[:, 4, :], in_=mo(24)), r=[modb], w=[dmodb])
                kb.op("dve", lambda h: h.tensor_copy(out=dmod[:, 5, :], in_=mo(40)), r=[modb], w=[dmodb])
                win = w3("w_in", l)

                R_xn = Rot("xnt", [128, 8, 512], BF16, 2)
                xn = VirtXN()
                xnb = [None] * nb
                mark_x = cv.off
                qT = cv.get([128, 4, T], BF16)
                qTb = [Buf(f"qT{j}") for j in range(nb)]
                mark_h = cv.off
                ofwb = [Buf(f"ofw{j}") for j in range(n128)]
                oAb = [Buf(f"oA{j}") for j in range(nb)]
                R_of = Rot("of", [128, 4, 128], BF16, 2)
                R_oA = Rot("oAt", [128, 4, 512], BF16, 1)
                Rot_sq = Rot("sq", [128, 512], BF16, 2)
                Rot_rstd = Rot("rstd", [128, 512], F32, 2)
                Rot_tmp = Rot("tmp", [128, 512], F32, 2)
                S32 = [cv.get([128, 4, 128], F32) for _ in range(2)]
                S32b = [Buf("S32a"), Buf("S32b")]
                Sp = [0]
                Sbf = cv.get([128, 5, 4, 128], BF16)
                Sbfb = Buf("Sbf")
                R_s = Rot("hs_", [128, 512], F32, 2)
                R_lf = Rot("hlf_", [128, 512], F32, 2)
                R_k = Rot("hk_", [128, 512], BF16, 2)
                R_kh = Rot("hkh_", [128, 512], BF16, 2)
                R_v = Rot("hv_", [128, 512], BF16, 2)
                R_eb = Rot("heb_", [128, 4, 128], F32, 2)
                R_enb = Rot("henb_", [128, 4, 128], F32, 2)
                R_es = Rot("hes_", [128, 512], F32, 2)
                R_qt = Rot("hqt_", [128, 4, 128], BF16, 2)
                R_kt = Rot("hkt_", [128, 4, 128], BF16, 2)
                R_A = Rot("hA_", [128, 4, 128], BF16, 2)
                R_osum = Rot("hos", [128, 4, 512], F32, 1)
                R_sg = Rot("hsg", [128, 4, 512], BF16, 1)

                def hgrn_step(dr_, tb, wf, wfb, wi, wib):
                    tok = slice(tb * 128, (tb + 1) * 128)
                    j = tb // 4
                    TRI = mLE if dr_ == 0 else mGE
                    XM = mGT if dr_ == 0 else mLT
                    MSK = mLE if dr_ == 0 else mGE
                    kb.mm([(psum[0][:, 0:256], xn[:, k, tok], wf[0][:, k, :]) for k in range(8)],
                          r=[xnb[j], wfb[0]], w=[psb[0]])
                    kb.mm([(psum[0][:, 256:512], xn[:, k, tok], wf[1][:, k, :]) for k in range(8)],
                          r=[xnb[j], wfb[1]], w=[psb[0]])
                    s_t, s_b = R_s.nxt()
                    kb.op("act", lambda h: h.activation(out=s_t, in_=psum[0][:], func=AF.Sigmoid), r=[psb[0]], w=[s_b])
                    kb.mm([(psum[0][:, 0:256], xn[:, k, tok], wi[0][:, k, :]) for k in range(8)],
                          r=[xnb[j], wib[0]], w=[psb[0]])
                    kb.mm([(psum[0][:, 256:512], xn[:, k, tok], wi[1][:, k, :]) for k in range(8)],
                          r=[xnb[j], wib[1]], w=[psb[0]])
                    v_t, v_b = R_v.nxt()
                    kb.op("act", lambda h: h.copy(out=v_t, in_=psum[0][:]), r=[psb[0]], w=[v_b])
                    if l > 0:
                        kb.op("dve", lambda h: h.tensor_tensor(out=s_t, in0=s_t, in1=lbT[:, dr_, 1, :], op=ALU.mult),
                              r=[s_b, lbb], w=[s_b])
                        kb.op("dve", lambda h: h.tensor_tensor(out=s_t, in0=s_t, in1=lbT[:, dr_, 0, :], op=ALU.add),
                              r=[s_b, lbb], w=[s_b])
                    lf_t, lf_b = R_lf.nxt()
                    kb.op("act", lambda h: h.activation(out=lf_t, in_=s_t, func=AF.Ln), r=[s_b], w=[lf_b])
                    k_t, k_b = R_k.nxt()
                    kb.op("dve", lambda h: h.tensor_scalar(out=k_t, in0=s_t, scalar1=-1.0, scalar2=1.0, op0=ALU.mult,
                                                           op1=ALU.add), r=[s_b], w=[k_b])
                    for hh in range(4):
                        kb.mm([(psum[2][:, hh * 128:(hh + 1) * 128], lf_t[:, hh * 128:(hh + 1) * 128], TRI[:])],
                              r=[lf_b, cbuf], w=[psb[2]])
                    kb.mm([(psum[3][:], XM[:], lf_t)], r=[lf_b, cbuf], w=[psb[3]])
                    eb_t, eb_b = R_eb.nxt()
                    enb_t, enb_b = R_enb.nxt()
                    es_t, es_b = R_es.nxt()
                    p2v = psum[2][:].rearrange("p (h t) -> p h t", h=4)
                    kb.op("act", lambda h: h.activation(out=eb_t, in_=p2v, func=AF.Exp), r=[psb[2]], w=[eb_b])
                    kb.op("act", lambda h: h.activation(out=enb_t, in_=p2v, func=AF.Exp, scale=-1.0), r=[psb[2]],
                          w=[enb_b])
                    kb.op("act", lambda h: h.activation(out=es_t, in_=psum[3][:], func=AF.Exp), r=[psb[3]], w=[es_b])
                    kh_t, kh_b = R_kh.nxt()
                    kb.op("dve", lambda h: h.tensor_tensor(out=kh_t, in0=k_t, in1=es_t, op=ALU.mult), r=[k_b, es_b],
                          w=[kh_b])
                    p4b = psum[4][:].bitcast(BF16)
                    kb.tr([(p4b[:, hh * 128:(hh + 1) * 128], k_t[:, hh * 128:(hh + 1) * 128]) for hh in range(4)],
                          identB[:], r=[k_b, cbuf], w=[psb[4]])
                    kt_t, kt_b = R_kt.nxt()
                    kb.op("dve", lambda h: h.tensor_tensor(out=kt_t, in0=p4b[:, 0:512].rearrange("p (h t) -> p h t", h=4),
                                                           in1=enb_t, op=ALU.mult), r=[psb[4], enb_b], w=[kt_b])
                    qt_t, qt_b = R_qt.nxt()
                    kb.op("dve", lambda h: h.tensor_tensor(out=qt_t, in0=qT[:, :, tok], in1=eb_t, op=ALU.mult),
                          r=[qTb[j], eb_b], w=[qt_b])
                    for hh in range(4):
                        kb.mm([(psum[5][:, hh * 128:(hh + 1) * 128], kt_t[:, hh, :], qt_t[:, hh, :])],
                              r=[kt_b, qt_b], w=[psb[5]])
                    A_t, A_b = R_A.nxt()
                    kb.op("dve", lambda h: h.tensor_tensor(
                        out=A_t, in0=psum[5][:].rearrange("p (h t) -> p h t", h=4),
                        in1=MSK[:].unsqueeze(1).to_broadcast([128, 4, 128]), op=ALU.mult), r=[psb[5], cbuf], w=[A_b])
                    corder = [0, 1, 2, 3] if dr_ == 0 else [3, 2, 1, 0]
                    p_ = Sp[0]
                    kb.op("act", lambda h: h.copy(out=Sbf[:, 0, :, :], in_=S32[p_][:]), r=[S32b[p_]], w=[Sbfb])
                    for ci, c in enumerate(corder):
                        pu = psum[6 + (ci % 2)]
                        pub = psb[6 + (ci % 2)]
                        for hh in range(4):
                            kw = {"tile_position": (96, 0)} if c == 3 else {}
                            kb.mm([(pu[:, hh * 128:(hh + 1) * 128], kh_t[c * 32:(c + 1) * 32, hh * 128:(hh + 1) * 128],
                                    v_t[c * 32:(c + 1) * 32, hh * 128:(hh + 1) * 128], kw)], r=[kh_b, v_b], w=[pub])
                        tcol = c * 32 + (31 if dr_ == 0 else 0)
                        src, dst = Sp[0], 1 - Sp[0]
                        for hh in range(4):
                            kb.op("dve", lambda h, hh=hh, pu=pu, tcol=tcol, src=src, dst=dst: h.scalar_tensor_tensor(
                                out=S32[dst][:, hh, :], in0=S32[src][:, hh, :], scalar=eb_t[:, hh, tcol:tcol + 1],
                                in1=pu[:, hh * 128:(hh + 1) * 128], op0=ALU.mult, op1=ALU.add),
                                r=[S32b[src], eb_b, pub], w=[S32b[dst]], n=128)
                        Sp[0] = dst
                        if ci < 3:
                            kb.op("act", lambda h, ci=ci, dst=dst: h.copy(out=Sbf[:, ci + 1, :, :], in_=S32[dst][:]),
                                  r=[S32b[dst]], w=[Sbfb])
                    for hh in range(4):
                        steps = [(psum[1][:, hh * 128:(hh + 1) * 128], v_t[:, hh * 128:(hh + 1) * 128], A_t[:, hh, :])]
                        for ci, c in enumerate(corder):
                            steps.append((psum[1][:, hh * 128 + c * 32:hh * 128 + (c + 1) * 32], Sbf[:, ci, hh, :],
                                          qt_t[:, hh, c * 32:(c + 1) * 32]))
                        kb.mm(steps, r=[v_b, A_b, Sbfb, qt_b], w=[psb[1]])
                    return psum[1][:].rearrange("p (h t) -> p h t", h=4), psb[1]

                def state_init(dr_, s):
                    p_ = Sp[0]
                    if job["cache"]:
                        kb.dma("sp", S32[p_][:], dr["st_in"][l, dr_].rearrange("h d v -> d h v"), w=[S32b[p_]])
                    else:
                        kb.op("dve", lambda h: h.memset(S32[p_][:], 0.0), w=[S32b[p_]])

                def state_out(dr_, s):
                    p_ = Sp[0]
                    if not job["cache"]:
                        kb.dma("sp", dr["st_out"][s, l, dr_].rearrange("h d v -> d h v"), S32[p_][:], r=[S32b[p_]])

                for j in range(nb):
                    blk = slice(j * 512, (j + 1) * 512)
                    prep_xn(j, 0, 1, xn, xnb, R_xn)
                    for half in range(2):
                        wv, wb = wload(win[:, :, OFF_Q + half * 256:OFF_Q + (half + 1) * 256], 8, 256)
                        for cc in range(2):
                            hh = half * 2 + cc
                            pq = psum[6 + cc]
                            kb.mm([(pq[:], wv[:, k, cc * 128:(cc + 1) * 128], xn[:, k, blk]) for k in range(8)],
                                  r=[wb, xnb[j]], w=[psb[6 + cc]])
                            t_t, t_b = Rot_tmp.nxt()
                            kb.op("act", lambda h, pq=pq, t_t=t_t: h.activation(out=t_t, in_=pq[:], func=AF.Silu),
                                  r=[psb[6 + cc]], w=[t_b])
                            kb.op("dve", lambda h, hh=hh, t_t=t_t: h.tensor_scalar(
                                out=qT[:, hh, blk], in0=t_t, scalar1=128.0 ** -0.5, scalar2=None, op0=ALU.mult),
                                r=[t_b], w=[qTb[j]])
                    wf, wfb, wi, wib = [], [], [], []
                    for half in range(2):
                        a, b_ = wload(win[:, :, OFF_FF + half * 256:OFF_FF + (half + 1) * 256], 8, 256)
                        wf.append(a)
                        wfb.append(b_)
                    for half in range(2):
                        a, b_ = wload(win[:, :, OFF_I + half * 256:OFF_I + (half + 1) * 256], 8, 256)
                        wi.append(a)
                        wib.append(b_)
                    for sub in range(4):
                        tb = j * 4 + sub
                        if (tb * 128) % L == 0:
                            state_init(0, (tb * 128) // L)
                        po, pob = hgrn_step(0, tb, wf, wfb, wi, wib)
                        of_t, of_b = R_of.nxt()
                        kb.op("act", lambda h, po=po, of_t=of_t: h.copy(out=of_t, in_=po), r=[pob], w=[of_b])
                        kb.dma("sp", ofw_d[:, :, tb * 128:(tb + 1) * 128], of_t, r=[of_b], w=[ofwb[tb]])
                        if ((tb + 1) * 128) % L == 0:
                            state_out(0, (tb * 128) // L)
                chk("fwd")
                for j in reversed(range(nb)):
                    blk = slice(j * 512, (j + 1) * 512)
                    prep_xn(j, 0, 1, xn, xnb, R_xn)
                    wf, wfb, wi, wib = [], [], [], []
                    for half in range(2):
                        a, b_ = wload(win[:, :, OFF_FB + half * 256:OFF_FB + (half + 1) * 256], 8, 256)
                        wf.append(a)
                        wfb.append(b_)
                    for half in range(2):
                        a, b_ = wload(win[:, :, OFF_I + half * 256:OFF_I + (half + 1) * 256], 8, 256)
                        wi.append(a)
                        wib.append(b_)
                    os_t, os_b = R_osum.nxt()
                    for sub in reversed(range(4)):
                        tb = j * 4 + sub
                        if ((tb + 1) * 128) % L == 0:
                            state_init(1, (tb * 128) // L)
                        of_t, of_b = R_of.nxt()
                        kb.dma("sp", of_t, ofw_d[:, :, tb * 128:(tb + 1) * 128], r=[ofwb[tb]], w=[of_b])
                        po, pob = hgrn_step(1, tb, wf, wfb, wi, wib)
                        kb.op("dve", lambda h, po=po, sub=sub, of_t=of_t: h.tensor_tensor(
                            out=os_t[:, :, sub * 128:(sub + 1) * 128], in0=po, in1=of_t,
                            op=ALU.add), r=[pob, of_b], w=[os_b])
                        if (tb * 128) % L == 0:
                            state_out(1, (tb * 128) // L)
                    sg_t, sg_b = R_sg.nxt()
                    for half in range(2):
                        wv, wb = wload(win[:, :, OFF_G + half * 256:OFF_G + (half + 1) * 256], 8, 256)
                        for cc in range(2):
                            hh = half * 2 + cc
                            pq = psum[6 + cc]
                            kb.mm([(pq[:], wv[:, k, cc * 128:(cc + 1) * 128], xn[:, k, blk]) for k in range(8)],
                                  r=[wb, xnb[j]], w=[psb[6 + cc]])
                            kb.op("act", lambda h, pq=pq, hh=hh: h.activation(out=sg_t[:, hh, :], in_=pq[:], func=AF.Silu),
                                  r=[psb[6 + cc]], w=[sg_b])
                    oA_t, oA_b = R_oA.nxt()
                    for hh in range(4):
                        q_t, q_b = Rot_sq.nxt()
                        kb.op("act", lambda h, hh=hh, q_t=q_t: h.activation(out=q_t, in_=os_t[:, hh, :], func=AF.Square),
                              r=[os_b], w=[q_b])
                        pn = psum[4 + (hh % 2)]
                        pnb = psb[4 + (hh % 2)]
                        kb.mm([(pn[:], onesB[:], q_t)], r=[q_b, cbuf], w=[pnb])
                        r_t, r_b = Rot_rstd.nxt()
                        kb.op("act", lambda h, pn=pn, r_t=r_t: h.activation(out=r_t, in_=pn[:], func=AF.Ln, bias=epsT[:],
                                                                            scale=1.0 / 128), r=[pnb, cbuf], w=[r_b])
                        kb.op("act", lambda h, r_t=r_t: h.activation(out=r_t, in_=r_t, func=AF.Exp, scale=-0.5), r=[r_b], w=[r_b])
                        kb.op("dve", lambda h, hh=hh, r_t=r_t: h.scalar_tensor_tensor(
                            out=r_t, in0=os_t[:, hh, :], scalar=fv[:, l, 64:65], in1=r_t, op0=ALU.mult, op1=ALU.mult),
                            r=[os_b, fvb, r_b], w=[r_b])
                        kb.op("dve", lambda h, hh=hh, r_t=r_t, oA_t=oA_t: h.tensor_tensor(out=oA_t[:, hh, :], in0=r_t,
                                                                                          in1=sg_t[:, hh, :], op=ALU.mult),
                              r=[r_b, sg_b], w=[oA_b])
                    kb.dma("sp", oA_d[:, :, blk], oA_t, r=[oA_b], w=[oAb[j]])

                chk("bwd")
                kb.barrier()
                cv.off = mark_x
                Rot_sq = Rot("sq", [128, 512], BF16, 2)
                Rot_rstd = Rot("rstd", [128, 512], F32, 2)
                Rot_tmp = Rot("tmp", [128, 512], F32, 2)
                cqn = cv.get([128, 3, T], BF16)
                cqnb = [Buf(f"cqn{j}") for j in range(nb)]
                ckvT = cv.get([128, 2, nseq * Tk], BF16)
                ckvb = Buf("ckvT")
                krT = cv.get([96, nseq * Tk], BF16)
                krb = Buf("krT")
                mark_p3 = cv.off
                R_c = Rot("cqf", [128, 3, 512], F32, 1)
                R_rp = Rot("rp", [96, 512], F32, 2)
                R_xb = Rot("xb", [96, 512], BF16, 2)
                R_co = Rot("co", [128, 288], F32, 2)
                R_ss = Rot("ss", [128, 2], F32, 2)
                R_cs = Rot("cs", [96, 2, 512], F32, 1)
                for j in range(nb):
                    blk = slice(j * 512, (j + 1) * 512)
                    prep_xn(j, 0, 1, xn, xnb, R_xn)
                    s0 = (j * 512) // L
                    nsb = max(1, 512 // L)
                    c_t, c_b = R_c.nxt()
                    wva0, wba0 = wload(win[:, :, OFF_CQ:OFF_CQ + 256], 8, 256)
                    chk("p3w")
                    wva1, wba1 = wload(win[:, :, OFF_CQ + 256:OFF_CQ + 384], 8, 128)
                    chk("p3x")
                    for cc in range(3):
                        if cc == 1:
                            chk("p3y")
                        pq = psum[cc % 2]
                        pqb = psb[cc % 2]
                        wva, wba, co_ = (wva0, wba0, cc) if cc < 2 else (wva1, wba1, 0)
                        kb.mm([(pq[:], wva[:, k, co_ * 128:(co_ + 1) * 128], xn[:, k, blk]) for k in range(8)],
                              r=[wba, xnb[j]], w=[pqb])
                        q_t, q_b = Rot_sq.nxt()
                        kb.op("act", lambda h, pq=pq, q_t=q_t: h.activation(out=q_t, in_=pq[:], func=AF.Square), r=[pqb],
                              w=[q_b])
                        kb.op("dve", lambda h, pq=pq, cc=cc: h.tensor_copy(out=c_t[:, cc, :], in_=pq[:]), r=[pqb], w=[c_b])
                        kb.mm([(psum[2][:], onesB[:], q_t)], r=[q_b, cbuf], w=[psb[2]], start=(cc == 0), stop=(cc == 2))
                    r_t, r_b = Rot_rstd.nxt()
                    kb.op("act", lambda h: h.activation(out=r_t, in_=psum[2][:], func=AF.Ln, bias=epsT[:],
                                                        scale=1.0 / C_Q), r=[psb[2], cbuf], w=[r_b])
                    kb.op("act", lambda h: h.activation(out=r_t, in_=r_t, func=AF.Exp, scale=-0.5), r=[r_b], w=[r_b])
                    for cc in range(3):
                        kb.op("dve", lambda h, cc=cc: h.scalar_tensor_tensor(
                            out=cqn[:, cc, blk], in0=c_t[:, cc, :], scalar=fv[:, l, 77 + cc:78 + cc], in1=r_t,
                            op0=ALU.mult, op1=ALU.mult), r=[c_b, fvb, r_b], w=[cqnb[j]])
                    chk("p3a")
                    c_t, c_b = R_c.nxt()
                    wvk, wbk = wload(win[:, :, OFF_CKV:OFF_CKV + 256], 8, 256)
                    for cc in range(2):
                        pq = psum[cc % 2]
                        pqb = psb[cc % 2]
                        kb.mm([(pq[:], wvk[:, k, cc * 128:(cc + 1) * 128], xn[:, k, blk]) for k in range(8)],
                              r=[wbk, xnb[j]], w=[pqb])
                        q_t, q_b = Rot_sq.nxt()
                        kb.op("act", lambda h, pq=pq, q_t=q_t: h.activation(out=q_t, in_=pq[:], func=AF.Square), r=[pqb],
                              w=[q_b])
                        kb.op("dve", lambda h, pq=pq, cc=cc: h.tensor_copy(out=c_t[:, cc, :], in_=pq[:]), r=[pqb], w=[c_b])
                        kb.mm([(psum[2][:], onesB[:], q_t)], r=[q_b, cbuf], w=[psb[2]], start=(cc == 0), stop=(cc == 1))
                    r_t, r_b = Rot_rstd.nxt()
                    kb.op("act", lambda h: h.activation(out=r_t, in_=psum[2][:], func=AF.Ln, bias=epsT[:],
                                                        scale=1.0 / C_KV), r=[psb[2], cbuf], w=[r_b])
                    kb.op("act", lambda h: h.activation(out=r_t, in_=r_t, func=AF.Exp, scale=-0.5), r=[r_b], w=[r_b])
                    for cc in range(2):
                        for sb_ in range(nsb):
                            ln = 512 // nsb
                            ko = (s0 + sb_) * Tk + ((j * 512 + sb_ * ln) % L)
                            kb.op("dve", lambda h, cc=cc, sb_=sb_, ln=ln, ko=ko: h.scalar_tensor_tensor(
                                out=ckvT[:, cc, ko:ko + ln], in0=c_t[:, cc, sb_ * ln:(sb_ + 1) * ln],
                                scalar=fv[:, l, 80 + cc:81 + cc], in1=r_t[:, sb_ * ln:(sb_ + 1) * ln],
                                op0=ALU.mult, op1=ALU.mult), r=[c_b, fvb, r_b], w=[ckvb])
                    chk("p3b")
                    wvr, wbr = wload(win[:, :, OFF_CKV + 192:OFF_CKV + 288], 8, 96)
                    kb.mm([(psum[3][0:96, :], wvr[:, k, :], xn[:, k, blk]) for k in range(8)], r=[wbr, xnb[j]], w=[psb[3]])
                    if job["rope"]:
                        x_t, x_b = R_xb.nxt()
                        kb.op("act", lambda h: h.copy(out=x_t[64:96, :], in_=psum[3][64:96, :]), r=[psb[3]], w=[x_b])
                        kb.op("dve", lambda h: h.memset(x_t[0:64, :], 0.0), w=[x_b])
                        kb.mm([(psum[4][0:96, :], P96[:], x_t[:])], r=[x_b, cbuf], w=[psb[4]])
                        cs_t, cs_b = R_cs.nxt()
                        kb.dma("sp", cs_t[64:96, 0, :], dr["ropeC"][64:96, blk], w=[cs_b])
                        kb.dma("sp", cs_t[64:96, 1, :], dr["ropeS"][64:96, blk], w=[cs_b])
                        p_t, p_b = R_rp.nxt()
                        kb.op("dve", lambda h: h.tensor_tensor(out=p_t[64:96, :], in0=psum[4][64:96, :],
                                                               in1=cs_t[64:96, 1, :], op=ALU.mult), r=[psb[4], cs_b],
                              w=[p_b])
                        p2_t, p2_b = R_rp.nxt()
                        kb.op("dve", lambda h: h.tensor_tensor(out=p2_t[64:96, :], in0=psum[3][64:96, :],
                                                               in1=cs_t[64:96, 0, :], op=ALU.mult), r=[psb[3], cs_b],
                              w=[p2_b])
                        kb.op("dve", lambda h: h.tensor_tensor(out=krT[64:96, j * 512:(j + 1) * 512], in0=p_t[64:96, :],
                                                               in1=p2_t[64:96, :], op=ALU.add), r=[p_b, p2_b], w=[krb])
                    else:
                        for sb_ in range(nsb):
                            ln = 512 // nsb
                            ko = (s0 + sb_) * Tk + ((j * 512 + sb_ * ln) % L)
                            kb.op("act", lambda h, sb_=sb_, ln=ln, ko=ko: h.copy(
                                out=krT[64:96, ko:ko + ln], in_=psum[3][64:96, sb_ * ln:(sb_ + 1) * ln]),
                                r=[psb[3]], w=[krb])
                    chk("p3c")
                    if not job["cache"]:
                        for sub in range(4):
                            tb = j * 4 + sub
                            tok = slice(tb * 128, (tb + 1) * 128)
                            pc = psum[5 + (sub % 2)]
                            pcb = psb[5 + (sub % 2)]
                            kb.mm([(pc[:, 0:256], xn[:, k, tok], wvk[:, k, :]) for k in range(8)], r=[xnb[j], wbk],
                                  w=[pcb])
                            kb.mm([(pc[:, 256:288], xn[:, k, tok], wvr[:, k, 64:96]) for k in range(8)], r=[xnb[j], wbr],
                                  w=[pcb])
                            co_t, co_b = R_co.nxt()
                            ss_t, ss_b = R_ss.nxt()
                            kb.op("act", lambda h, pc=pc, co_t=co_t, ss_t=ss_t: h.activation(
                                out=co_t[:, 0:256], in_=pc[:, 0:256], func=AF.Square, accum_out=ss_t[:, 0:1]),
                                r=[pcb], w=[co_b, ss_b])
                            kb.op("act", lambda h, ss_t=ss_t: h.activation(out=ss_t[:, 1:2], in_=ss_t[:, 0:1], func=AF.Ln,
                                                                           bias=epsT[:], scale=1.0 / C_KV),
                                  r=[ss_b, cbuf], w=[ss_b])
                            kb.op("act", lambda h, ss_t=ss_t: h.activation(out=ss_t[:, 1:2], in_=ss_t[:, 1:2], func=AF.Exp,
                                                                           scale=-0.5), r=[ss_b], w=[ss_b], n=1)
                            kb.op("dve", lambda h, pc=pc, co_t=co_t, ss_t=ss_t: h.scalar_tensor_tensor(
                                out=co_t[:, 0:256], in0=pc[:, 0:256], scalar=ss_t[:, 1:2], in1=kvnB[:, l, :],
                                op0=ALU.mult, op1=ALU.mult), r=[pcb, ss_b, kvnb, co_b], w=[co_b])
                            kb.op("act", lambda h, pc=pc, co_t=co_t: h.copy(out=co_t[:, 256:288], in_=pc[:, 256:288]),
                                  r=[pcb], w=[co_b])
                            s_i = (tb * 128) // L
                            to = (tb * 128) % L
                            kb.dma("sp", dr["cache_out"][s_i, l, to:to + 128, :], co_t[:], r=[co_b])
                if job["cache"]:
                    cst = cv.get([128, 2, 288], F32)
                    cstb = Buf("cst")
                    kb.dma("sp", cst[:], dr["cache_in"][l].rearrange("(tb p) f -> p tb f", p=128), w=[cstb])
                    for tb in range(2):
                        kb.tr([(psum[0][:, cc * 128:(cc + 1) * 128], cst[:, tb, cc * 128:(cc + 1) * 128]) for cc in range(2)],
                              identF[:], r=[cstb, cbuf], w=[psb[0]])
                        kb.op("dve", lambda h, tb=tb: h.tensor_copy(
                            out=ckvT[:, :, L + tb * 128:L + (tb + 1) * 128],
                            in_=psum[0][:, 0:256].rearrange("p (c t) -> p c t", c=2)), r=[psb[0]], w=[ckvb])
                        kb.tr([(psum[1][0:96, 0:128], cst[:, tb, 192:288])], identF[:], r=[cstb, cbuf], w=[psb[1]])
                        kb.op("dve", lambda h, tb=tb: h.tensor_copy(out=krT[64:96, L + tb * 128:L + (tb + 1) * 128],
                                                                    in_=psum[1][64:96, 0:128]), r=[psb[1]], w=[krb])

                chk("p3")
                kb.barrier(engines=("pe", "act", "dve", "sp", "pool"))
                mark_a = cv.off
                cv.off = 0
                oC = cv.get([128, 4, T], BF16)
                assert cv.off <= mark_x
                oCb = [Buf(f"oC{j}") for j in range(nb)]
                cv.off = mark_p3
                Wkv = cv.get([128, 2, 1024], BF16)
                Wq = cv.get([128, 3, 768], BF16)
                wab = Buf("Wattn")
                kb.dma("pool", Wkv, w3("w_kv_up", l), w=[wab])
                kb.dma("pool", Wq, w3("w_q_up", l), w=[wab])
                vaug = [cv.get([128, nkb, 128], BF16) for _ in range(2)]
                vaugb = [Buf("vaug0"), Buf("vaug1")]
                rden = [cv.get([128, 512], F32) for _ in range(2)]
                rdenb = [Buf("rden0"), Buf("rden1")]
                kb.op("dve", lambda h: h.memset(vaug[0][:, :, 64:128], 1.0), w=[vaugb[0]])
                kb.op("dve", lambda h: h.memset(vaug[1][:, :, 0:64], 1.0), w=[vaugb[1]])
                kb.op("dve", lambda h: h.memset(rden[0][:], 0.0), w=[rdenb[0]])
                kb.op("dve", lambda h: h.memset(rden[1][:], 0.0), w=[rdenb[1]])
                R_kT = Rot("kT", [96, Tk], BF16, 2)
                R_q = Rot("qh", [96, 512], BF16, 2)
                R_pT = Rot("pT", [128, 512], BF16, 3)
                R_rb = Rot("rb", [128, 512], F32, 1)
                R_rp = Rot("rp2", [96, 512], F32, 2)
                R_cs = Rot("cs2", [96, 2, 512], F32, 1)
                scale_qk = 96.0 ** -0.5
                tasks = [(s, hh, qb) for s in range(nseq) for hh in range(8) for qb in range(L // QN)]
                headc, qc = {}, {}

                def prep_head(s, hh):
                    par = hh % 2
                    kbase = s * Tk
                    kT_t, kT_b = R_kT.nxt()
                    kb.op("dve", lambda h: h.tensor_copy(out=kT_t[64:96, :], in_=krT[64:96, kbase:kbase + Tk]),
                          r=[krb], w=[kT_b])
                    for k5 in range(0, Tk, 512):
                        n5 = min(512, Tk - k5)
                        kb.mm([(psum[7][0:64, 0:n5], Wkv[:, c, hh * 128:hh * 128 + 64],
                                ckvT[:, c, kbase + k5:kbase + k5 + n5]) for c in range(2)], r=[ckvb, wab], w=[psb[7]])
                        kb.op("act", lambda h, k5=k5, n5=n5: h.copy(out=kT_t[0:64, k5:k5 + n5], in_=psum[7][0:64, 0:n5]),
                              r=[psb[7]], w=[kT_b])
                    voff = 0 if par == 0 else 64
                    for kb8 in range(0, nkb, 8):
                        n8 = min(8, nkb - kb8)
                        for q8 in range(n8):
                            kblk = kb8 + q8
                            kb.mm([(psum[7][:, q8 * 64:(q8 + 1) * 64],
                                    ckvT[:, c, kbase + kblk * 128:kbase + (kblk + 1) * 128],
                                    Wkv[:, c, hh * 128 + 64:hh * 128 + 128]) for c in range(2)], r=[ckvb, wab],
                                  w=[psb[7]])
                        kb.op("dve", lambda h, kb8=kb8, n8=n8: h.tensor_copy(
                            out=vaug[par][:, kb8:kb8 + n8, voff:voff + 64],
                            in_=psum[7][:, 0:n8 * 64].rearrange("p (a b) -> p a b", a=n8)), r=[psb[7]], w=[vaugb[par]])
                    return kT_t, kT_b

                def prep_q(s, hh, qb):
                    q0 = s * L + qb * QN
                    jq = q0 // 512
                    q_t, q_b = R_q.nxt()
                    kb.mm([(psum[7][0:96, 0:QN], Wq[:, c, hh * 96:(hh + 1) * 96], cqn[:, c, q0:q0 + QN])
                           for c in range(3)], r=[wab, cqnb[jq]], w=[psb[7]])
                    kb.op("act", lambda h: h.activation(out=q_t[:, 0:QN], in_=psum[7][0:96, 0:QN], func=AF.Identity,
                                                        scale=scale_qk), r=[psb[7]], w=[q_b])
                    if job["rope"]:
                        kb.mm([(psum[3][0:96, 0:QN], P96[:], q_t[:, 0:QN])], r=[q_b, cbuf], w=[psb[3]])
                        cs_t, cs_b = R_cs.nxt()
                        kb.dma("sp", cs_t[64:96, 0, 0:QN], dr["ropeC"][64:96, qb * QN:(qb + 1) * QN], w=[cs_b])
                        kb.dma("sp", cs_t[64:96, 1, 0:QN], dr["ropeS"][64:96, qb * QN:(qb + 1) * QN], w=[cs_b])
                        p_t, p_b = R_rp.nxt()
                        kb.op("dve", lambda h: h.tensor_tensor(out=p_t[64:96, 0:QN], in0=psum[3][64:96, 0:QN],
                                                               in1=cs_t[64:96, 1, 0:QN], op=ALU.mult),
                              r=[psb[3], cs_b], w=[p_b])
                        p2_t, p2_b = R_rp.nxt()
                        kb.op("dve", lambda h: h.scalar_tensor_tensor(
                            out=p2_t[64:96, 0:QN], in0=psum[7][64:96, 0:QN], scalar=scale_qk, in1=cs_t[64:96, 0, 0:QN],
                            op0=ALU.mult, op1=ALU.mult), r=[psb[7], cs_b], w=[p2_b])
                        kb.op("dve", lambda h: h.tensor_tensor(out=q_t[64:96, 0:QN], in0=p_t[64:96, 0:QN],
                                                               in1=p2_t[64:96, 0:QN], op=ALU.add), r=[p_b, p2_b], w=[q_b])
                    return q_t, q_b

                def ensure(i):
                    s, hh, qb = tasks[i]
                    if (s, hh) not in headc:
                        headc[(s, hh)] = prep_head(s, hh)
                    if i not in qc:
                        qc[i] = prep_q(s, hh, qb)

                def attn_main(i):
                    s, hh, qb = tasks[i]
                    par = hh % 2
                    hp = hh // 2
                    q0 = s * L + qb * QN
                    jq = q0 // 512
                    kT_t, kT_b = headc[(s, hh)]
                    q_t, q_b = qc.pop(i)
                    acc, accb = psum[i % 2], psb[i % 2]

                    def qk(kblk):
                        pi = 4 + (kblk % 3)
                        kb.mm([(psum[pi][:, 0:QN], kT_t[:, kblk * 128:(kblk + 1) * 128], q_t[:, 0:QN])], r=[kT_b, q_b],
                              w=[psb[pi]])

                    qk(0)
                    for kblk in range(nkb):
                        if kblk + 1 < nkb:
                            qk(kblk + 1)
                        pi = 4 + (kblk % 3)
                        pT_t, pT_b = R_pT.nxt()
                        kb.op("act", lambda h, pi=pi, pT_t=pT_t: h.activation(out=pT_t[:, 0:QN], in_=psum[pi][:, 0:QN],
                                                                             func=AF.Exp), r=[psb[pi]], w=[pT_b])
                        kb.mm([(acc[:, 0:QN], vaug[par][:, kblk, :], pT_t[:, 0:QN])], r=[vaugb[par], pT_b], w=[accb],
                              start=(kblk == 0), stop=(kblk == nkb - 1))
                    nrows = slice(0, 64) if par == 0 else slice(64, 128)
                    drows = slice(64, 128) if par == 0 else slice(0, 64)
                    kb.op("act", lambda h: h.activation(out=rden[par][drows, 0:QN], in_=acc[drows, 0:QN], func=AF.Ln),
                          r=[accb], w=[rdenb[par]])
                    kb.op("act", lambda h: h.activation(out=rden[par][drows, 0:QN], in_=rden[par][drows, 0:QN],
                                                        func=AF.Exp, scale=-1.0), r=[rdenb[par]], w=[rdenb[par]])
                    kb.mm([(psum[2][:, 0:QN], swapM[:], rden[par][:, 0:QN])], r=[cbuf, rdenb[par]], w=[psb[2]])
                    rb_t, rb_b = R_rb.nxt()
                    kb.op("act", lambda h: h.copy(out=rb_t[nrows, 0:QN], in_=psum[2][nrows, 0:QN]), r=[psb[2]], w=[rb_b])
                    kb.op("dve", lambda h: h.tensor_tensor(out=oC[nrows, hp, q0:q0 + QN], in0=acc[nrows, 0:QN],
                                                           in1=rb_t[nrows, 0:QN], op=ALU.mult), r=[accb, rb_b],
                          w=[oCb[jq]])

                ensure(0)
                for i in range(len(tasks)):
                    if i + 1 < len(tasks):
                        ensure(i + 1)
                    attn_main(i)

                chk("p4")
                kb.barrier()
                cv.off = 0
                oC2 = cv.get([128, 4, T], BF16)
                xn = cv.get([128, 8, T], BF16)
                xnb = [Buf(f"xnm{j}") for j in range(nb)]
                Rot_sq = Rot("sq", [128, 512], BF16, 2)
                Rot_rstd = Rot("rstd", [128, 512], F32, 2)
                Rot_tmp = Rot("tmp", [128, 512], F32, 2)
                xh = cv.get([128, 8, 2], BF16)
                xhb = Buf("xh")
                R_e = Rot("e", [128, 514], F32, 2)
                R_acc = Rot("acc", [128, 512], F32, 2)
                oBt = cv.get([128, 4, 512], BF16)
                oBb = Buf("oB")
                R_a2 = Rot("a2", [128, 2, 512], F32, 1)
                R_oAl = Rot("oAl", [128, 4, 512], BF16, 1)
                hB = cv.get([128, 8, 512], BF16)
                hBb = Buf("hB")
                R_g = Rot("g", [128, 512], F32, 2)

                def halo_cols(xsrc, xsb, j):
                    t0 = j * 512
                    if t0 % L == 0:
                        kb.op("dve", lambda h: h.memset(xh[:, :, 0:1], 0.0), w=[xhb])
                    else:
                        kb.op("dve", lambda h: h.tensor_copy(out=xh[:, :, 0:1], in_=xsrc[:, :, t0 - 1:t0]),
                              r=[xsb[j - 1]], w=[xhb])
                    if (t0 + 512) % L == 0:
                        kb.op("dve", lambda h: h.memset(xh[:, :, 1:2], 0.0), w=[xhb])
                    else:
                        kb.op("dve", lambda h: h.tensor_copy(out=xh[:, :, 1:2], in_=xsrc[:, :, t0 + 512:t0 + 513]),
                              r=[xsb[j + 1]], w=[xhb])

                def conv3(e_t, e_b, acc_t, acc_b, w0, w1, w2, wbuf):
                    kb.op("act", lambda h: h.activation(out=acc_t, in_=e_t[:, 1:513], func=AF.Identity, scale=w1),
                          r=[e_b, wbuf], w=[acc_b])
                    seg = min(L, 512)
                    for a in range(0, 512, seg):
                        lo = a if a == 0 else a + 1
                        kb.op("dve", lambda h, lo=lo, a=a: h.scalar_tensor_tensor(
                            out=acc_t[:, lo:a + seg], in0=e_t[:, lo:a + seg], scalar=w0, in1=acc_t[:, lo:a + seg],
                            op0=ALU.mult, op1=ALU.add), r=[e_b, wbuf, acc_b], w=[acc_b])
                        hi = a + seg if a + seg == 512 else a + seg - 1
                        kb.op("dve", lambda h, hi=hi, a=a: h.scalar_tensor_tensor(
                            out=acc_t[:, a:hi], in0=e_t[:, a + 2:hi + 2], scalar=w2, in1=acc_t[:, a:hi],
                            op0=ALU.mult, op1=ALU.add), r=[e_b, wbuf, acc_b], w=[acc_b])

                norm_all(0, 1, xn, xnb)
                for j in range(nb):
                    blk = slice(j * 512, (j + 1) * 512)
                    halo_cols(xn, xnb, j)
                    for half in range(2):
                        wvc, wbc = wload(win[:, :, OFF_BC + half * 256:OFF_BC + (half + 1) * 256], 8, 256)
                        wvh, wbh = wload(win[:, :, OFF_BH + half * 256:OFF_BH + (half + 1) * 256], 8, 256)
                        wvb, wbb = wload(win[:, :, OFF_BB + half * 256:OFF_BB + (half + 1) * 256], 8, 256)
                        for cc in range(2):
                            g = half * 2 + cc
                            cs_ = slice(cc * 128, (cc + 1) * 128)
                            kb.mm([(psum[0][:], wvc[:, k, cs_], xn[:, k, blk]) for k in range(8)], r=[wbc, xnb[j]],
                                  w=[psb[0]])
                            kb.mm([(psum[1][:], wvh[:, k, cs_], xn[:, k, blk]) for k in range(8)], r=[wbh, xnb[j]],
                                  w=[psb[1]])
                            kb.mm([(psum[2][:, 0:2], wvc[:, k, cs_], xh[:, k, :]) for k in range(8)], r=[wbc, xhb],
                                  w=[psb[2]])
                            kb.mm([(psum[2][:, 2:4], wvh[:, k, cs_], xh[:, k, :]) for k in range(8)], r=[wbh, xhb],
                                  w=[psb[2]])
                            kb.mm([(psum[3][:], wvb[:, k, cs_], xn[:, k, blk]) for k in range(8)], r=[wbb, xnb[j]],
                                  w=[psb[3]])
                            t_t, t_b = Rot_tmp.nxt()
                            kb.op("act", lambda h, t_t=t_t: h.copy(out=t_t, in_=psum[0][:]), r=[psb[0]], w=[t_b])
                            e_t, e_b = R_e.nxt()
                            kb.op("dve", lambda h, t_t=t_t, e_t=e_t: h.tensor_tensor(out=e_t[:, 1:513], in0=psum[1][:],
                                                                                    in1=t_t, op=ALU.mult),
                                  r=[psb[1], t_b], w=[e_b])
                            t2_t, t2_b = Rot_tmp.nxt()
                            kb.op("act", lambda h, t2_t=t2_t: h.copy(out=t2_t[:, 0:2], in_=psum[2][:, 0:2]), r=[psb[2]],
                                  w=[t2_b])
                            kb.op("dve", lambda h, t2_t=t2_t, e_t=e_t: h.tensor_tensor(
                                out=e_t[:, 0:514:513], in0=psum[2][:, 2:4], in1=t2_t[:, 0:2], op=ALU.mult),
                                r=[psb[2], t2_b], w=[e_b])
                            acc_t, acc_b = R_acc.nxt()
                            conv3(e_t, e_b, acc_t, acc_b, fv[:, l, 65 + g:66 + g], fv[:, l, 69 + g:70 + g],
                                  fv[:, l, 73 + g:74 + g], fvb)
                            kb.op("dve", lambda h, g=g, acc_t=acc_t: h.tensor_tensor(out=oBt[:, g, :], in0=psum[3][:],
                                                                                    in1=acc_t, op=ALU.mult),
                                  r=[psb[3], acc_b], w=[oBb])
                    oAl_t, oAl_b = R_oAl.nxt()
                    kb.dma("sp", oAl_t, oA_d[:, :, blk], r=[oAb[j]], w=[oAl_b])
                    srcs = (("w_o_hgrn", oAl_t, oAl_b), ("w_o_conv", oBt, oBb), ("w_o_mla", oC2[:, :, blk], oCb[j]))
                    for o2 in range(0, 8, 2):
                        a2_t, a2_b = R_a2.nxt()
                        for br, (wname, osrc, osb) in enumerate(srcs):
                            wvo, wbo = wload(w3(wname, l)[:, :, o2 * 128:(o2 + 2) * 128], 4, 256)
                            gc0 = OFF_GATE + br * 1024 + o2 * 128
                            wvg, wbg = wload(win[:, :, gc0:gc0 + 256], 8, 256)
                            for o1 in range(2):
                                oc = o2 + o1
                                py, pyb = psum[4 + o1], psb[4 + o1]
                                pg, pgb = psum[6 + o1], psb[6 + o1]
                                kb.mm([(py[:], wvo[:, k, o1 * 128:(o1 + 1) * 128], osrc[:, k, :]) for k in range(4)],
                                      r=[wbo, osb], w=[pyb])
                                kb.mm([(pg[:], wvg[:, k, o1 * 128:(o1 + 1) * 128], xn[:, k, blk]) for k in range(8)],
                                      r=[wbg, xnb[j]], w=[pgb])
                                g_t, g_b = R_g.nxt()
                                kb.op("act", lambda h, pg=pg, g_t=g_t: h.activation(out=g_t, in_=pg[:], func=AF.Sigmoid),
                                      r=[pgb], w=[g_b])
                                if br == 0:
                                    kb.op("dve", lambda h, o1=o1, py=py, g_t=g_t, a2_t=a2_t: h.tensor_tensor(
                                        out=a2_t[:, o1, :], in0=py[:], in1=g_t, op=ALU.mult), r=[pyb, g_b], w=[a2_b])
                                else:
                                    kb.op("dve", lambda h, py=py, g_t=g_t: h.tensor_tensor(
                                        out=g_t, in0=py[:], in1=g_t, op=ALU.mult), r=[pyb, g_b], w=[g_b])
                                    if br == 1:
                                        kb.op("dve", lambda h, o1=o1, g_t=g_t, a2_t=a2_t: h.tensor_tensor(
                                            out=a2_t[:, o1, :], in0=a2_t[:, o1, :], in1=g_t, op=ALU.add),
                                            r=[a2_b, g_b], w=[a2_b])
                                    else:
                                        kb.op("dve", lambda h, oc=oc, o1=o1, g_t=g_t, a2_t=a2_t: h.tensor_tensor(
                                            out=hB[:, oc, :], in0=a2_t[:, o1, :], in1=g_t, op=ALU.add),
                                            r=[a2_b, g_b], w=[hBb])
                    wo3 = w3("w_out", l)
                    for o2 in range(0, 8, 2):
                        wvo, wbo = wload(wo3[:, :, o2 * 128:(o2 + 2) * 128], 8, 256)
                        for o1 in range(2):
                            oc = o2 + o1
                            py, pyb = psum[oc % 2], psb[oc % 2]
                            kb.mm([(py[:], wvo[:, k, o1 * 128:(o1 + 1) * 128], hB[:, k, :]) for k in range(8)],
                                  r=[wbo, hBb], w=[pyb])
                            kb.op("dve", lambda h, oc=oc, py=py: h.scalar_tensor_tensor(
                                out=xT[:, oc, blk], in0=py[:], scalar=dmod[:, 2, oc:oc + 1], in1=xT[:, oc, blk],
                                op0=ALU.mult, op1=ALU.add), r=[pyb, dmodb, xbuf[j]], w=[xbuf[j]])

                chk("p5")
                phase()
                xn2 = cv.get([128, 8, T], BF16)
                xn2b = [Buf(f"xn2{j}") for j in range(nb)]
                Rot_sq = Rot("sq", [128, 512], BF16, 2)
                Rot_rstd = Rot("rstd", [128, 512], F32, 2)
                Rot_tmp = Rot("tmp", [128, 512], F32, 2)
                xh = cv.get([128, 8, 2], BF16)
                xhb = Buf("xh")
                R_e = Rot("e", [128, 514], F32, 3)
                R_acc = Rot("acc", [128, 512], F32, 3)
                hid = cv.get([128, 22, 512], BF16)
                hidb = Buf("hid")
                norm_all(3, 4, xn2, xn2b)
                wup = dr["w_up"][l].rearrange("(kc p) (two n) -> p kc two n", p=128, two=2)
                wd3 = w3("w_down", l)
                fo = 82
                for j in range(nb):
                    blk = slice(j * 512, (j + 1) * 512)
                    halo_cols(xn2, xn2b, j)
                    for m in range(22):
                        wv, wb = wload(wup[:, :, :, m * 128:(m + 1) * 128], 8, 128, extra=2)
                        accs = []
                        for ab in range(2):
                            pm, pmb = psum[2 * ab], psb[2 * ab]
                            ph, phb = psum[2 * ab + 1], psb[2 * ab + 1]
                            kb.mm([(pm[:], wv[:, k, ab, :], xn2[:, k, blk]) for k in range(8)], r=[wb, xn2b[j]], w=[pmb])
                            kb.mm([(ph[:, 0:2], wv[:, k, ab, :], xh[:, k, :]) for k in range(8)], r=[wb, xhb], w=[phb])
                            e_t, e_b = R_e.nxt()
                            kb.op("act", lambda h, pm=pm, e_t=e_t: h.copy(out=e_t[:, 1:513], in_=pm[:]), r=[pmb], w=[e_b])
                            kb.op("dve", lambda h, ph=ph, e_t=e_t: h.tensor_copy(out=e_t[:, 0:514:513], in_=ph[:, 0:2]),
                                  r=[phb, e_b], w=[e_b])
                            acc_t, acc_b = R_acc.nxt()
                            mm_ = ab * 22 + m
                            conv3(e_t, e_b, acc_t, acc_b, fv[:, l, fo + mm_:fo + mm_ + 1],
                                  fv[:, l, fo + 44 + mm_:fo + 44 + mm_ + 1], fv[:, l, fo + 88 + mm_:fo + 88 + mm_ + 1], fvb)
                            accs.append((acc_t, acc_b))
                        (a_t, a_b), (b_t, b_b) = accs
                        t_t, t_b = Rot_tmp.nxt()
                        kb.op("act", lambda h, a_t=a_t, t_t=t_t: h.activation(out=t_t, in_=a_t, func=AF.Silu), r=[a_b],
                              w=[t_b])
                        kb.op("dve", lambda h, m=m, t_t=t_t, b_t=b_t: h.tensor_tensor(out=hid[:, m, :], in0=t_t, in1=b_t,
                                                                                     op=ALU.mult), r=[t_b, b_b], w=[hidb])
                    for oc in range(8):
                        wv, wb = wload(wd3[:, :, oc * 128:(oc + 1) * 128], 22, 128)
                        py, pyb = psum[4 + (oc % 2)], psb[4 + (oc % 2)]
                        kb.mm([(py[:], wv[:, k, :], hid[:, k, :]) for k in range(22)], r=[wb, hidb], w=[pyb])
                        kb.op("dve", lambda h, oc=oc, py=py: h.scalar_tensor_tensor(
                            out=xT[:, oc, blk], in0=py[:], scalar=dmod[:, 5, oc:oc + 1], in1=xT[:, oc, blk],
                            op0=ALU.mult, op1=ALU.add), r=[pyb, dmodb, xbuf[j]], w=[xbuf[j]])

            chk("p6")
            phase()
            Rot_sq = Rot("sq", [128, 512], BF16, 2)
            Rot_rstd = Rot("rstd", [128, 512], F32, 2)
            yt = cv.get([128, 8, 512], F32)
            ytb = Buf("yt")
            ost = Rot("ost", [128, 1024], F32, 2)
            for j in range(nb):
                r_t, r_b = rstd_block(j)
                for k in range(8):
                    kb.op("dve", lambda h, k=k: h.scalar_tensor_tensor(
                        out=yt[:, k, :], in0=xT[:, k, j * 512:(j + 1) * 512], scalar=fv[:, 0, 214 + k:215 + k], in1=r_t,
                        op0=ALU.mult, op1=ALU.mult), r=[xbuf[j], fvb, r_b], w=[ytb])
                for sub in range(4):
                    o_t, o_b = ost.nxt()
                    for half in range(2):
                        pi = half
                        kb.tr([(psum[pi][:, q * 128:(q + 1) * 128], yt[:, half * 4 + q, sub * 128:(sub + 1) * 128])
                               for q in range(4)], identF[:], r=[ytb, cbuf], w=[psb[pi]])
                        if half == 0:
                            kb.op("dve", lambda h, o_t=o_t: h.tensor_copy(out=o_t[:, 0:512], in_=psum[0][:]), r=[psb[0]],
                                  w=[o_b])
                        else:
                            kb.op("act", lambda h, o_t=o_t: h.copy(out=o_t[:, 512:1024], in_=psum[1][:]), r=[psb[1]],
                                  w=[o_b])
                    r0 = j * 512 + sub * 128
                    kb.dma("sp", job["yout"][r0:r0 + 128, :], o_t, r=[o_b])

        jobs = [
            dict(idx=0, T=NP_SEQ * L_P, L=L_P, nseq=NP_SEQ, rope=False, cache=False, xin=dr["xp"], yout=dr["yp"]),
            dict(idx=1, T=L_S, L=L_S, nseq=1, rope=True, cache=True, xin=dr["xs"], yout=dr["ys"]),
        ]
        try:
            chk("mod")
            for job in jobs:
                run_job(job)
        except _Stop:
            pass
        kb.barrier(engines=("pe", "act", "dve", "sp", "pool"), include_pool=True)
    return nc, hc


def kernel(**inputs):
    n = 8
    if "nc" not in _CACHE:
        _CACHE["nc"] = build()
    nc, hc = _CACHE["nc"]
    f = lambda a: np.ascontiguousarray(np.asarray(a, dtype=np.float32))
    xp = f(inputs["x_prompt"])
    xs = f(inputs["x_sample"])
    st = f(inputs["state_hgrn"])
    cm = f(inputs["cache_mla"])
    c = f(inputs["c"])
    cctx = f(inputs["c_ctx"])
    in_maps = []
    for i in range(n):
        m = {
            "xp": xp[4 * i:4 * i + 4].reshape(NP_SEQ * L_P, D),
            "xs": xs[i],
            "st_in": st[i],
            "cache_in": cm[i],
            "cvec": np.stack([cctx, c[i]], axis=0),
        }
        for wn in WEIGHTS:
            m[wn] = f(inputs[wn])
        for k, v in hc.items():
            m[k] = v
        in_maps.append({k: np.ascontiguousarray(v) for k, v in m.items()})
    res = run_bass_kernel_spmd(nc, in_maps, core_ids=list(range(n)))
    R = res.results
    y_p = np.concatenate([r["yp"].reshape(NP_SEQ, L_P, D) for r in R], axis=0)
    y_s = np.stack([r["ys"] for r in R], axis=0)
    st_o = np.concatenate([r["st_out"] for r in R], axis=0)
    ch_o = np.concatenate([r["cache_out"] for r in R], axis=0)
    return (y_p.astype(np.float32), y_s.astype(np.float32), st_o.astype(np.float32), ch_o.astype(np.float32))
```

```python
import types
import numpy as np
import ml_dtypes
from contextlib import ExitStack
import concourse.bass as bass
import concourse.mybir as mybir
from concourse.bass_utils import run_bass_kernel_spmd

F32 = mybir.dt.float32
BF16 = mybir.dt.bfloat16
AF = mybir.ActivationFunctionType
ALU = mybir.AluOpType

D = 1024
DEPTH = 2
A_W = 512
B_W = 512
C_Q = 384
C_KV = 256
C_ROPE = 32
D_FF = 2816
IN_COLS = 7840
OFF_Q, OFF_FF, OFF_FB, OFF_I, OFF_G = 0, 512, 1024, 1536, 2048
OFF_BB, OFF_BC, OFF_BH = 2560, 3072, 3584
OFF_CQ = 4096
OFF_CKV = 4480
OFF_GATE = 4768
EPS = 1e-6
NP_SEQ, L_P = 4, 256
L_S = 2048
PAST = 256
WSLOT = 2816


class Buf:
    __slots__ = ("name", "w", "r", "excl")

    def __init__(self, name, excl=False):
        self.name = name
        self.w = None
        self.r = []
        self.excl = excl


def _snap(fn):
    if fn.__closure__ is None:
        return fn
    cells = []
    for c in fn.__closure__:
        try:
            cells.append(types.CellType(c.cell_contents))
        except ValueError:
            cells.append(c)
    return types.FunctionType(fn.__code__, fn.__globals__, fn.__name__, fn.__defaults__, tuple(cells))


class Node:
    __slots__ = ("eng", "emit", "deps", "cost", "lat", "idx", "sig", "start", "finish", "prev", "isdma", "pri", "bl")


class Eng:
    def __init__(self, name, h, sems, is_pe=False):
        self.name = name
        self.h = h
        self.sems = sems
        self.si = 0
        self.cnt = 0
        self.seen = {}
        self.is_pe = is_pe
        self.dslots = []
        self.di = 0


class KB:
    ROLL = 16000
    W = 48
    LAT = 0.2

    def __init__(self, nc, es):
        self.nc = nc
        self.es = es
        self.E = {}
        for name, h, n in (("pe", nc.tensor, 6), ("act", nc.scalar, 6), ("dve", nc.vector, 8),
                           ("pool", nc.gpsimd, 3), ("sp", nc.sync, 1)):
            sems = [es.enter_context(nc.semaphore(f"s_{name}{i}")) for i in range(n)]
            self.E[name] = Eng(name, h, sems, is_pe=(name == "pe"))
        for qn, n in (("sp", 12), ("pool", 8)):
            q = self.E[qn]
            q.dslots = [[es.enter_context(nc.semaphore(f"d_{qn}{i}")), 0] for i in range(n)]
        self.pe_nums = set(s.num for s in self.E["pe"].sems)
        self.pe_ins = 0
        self.marks = []
        self.nodes = []
        self.nidx = 0
        self.reorder = True

    def _mk(self, en, emit, r, w, cost, lat=None):
        nd = Node()
        nd.eng = en
        nd.emit = emit
        nd.cost = cost
        nd.lat = cost if lat is None else lat
        nd.idx = self.nidx
        self.nidx += 1
        nd.sig = None
        nd.start = None
        nd.finish = None
        nd.prev = 0
        nd.isdma = False
        nd.pri = 0.4 if (en != "pe" and any(b.excl for b in r)) else 0.0
        deps = set()
        for b in r:
            if b.w is not None:
                deps.add(b.w)
            if b.excl:
                deps.update(b.r)
        for b in w:
            if b.w is not None:
                deps.add(b.w)
            deps.update(b.r)
        nd.deps = deps
        for b in r:
            b.r.append(nd)
        for b in w:
            b.w = nd
            b.r = []
        self.nodes.append(nd)
        return nd

    def op(self, en, fn, r=(), w=(), n=512):
        if en == "act":
            cost = 0.22 + n / 1400.0
        elif en == "dve":
            cost = 0.10 + max(n, 64) / 960.0
        else:
            cost = 0.30 + n / 450.0

        fn2 = _snap(fn)

        def emit(h):
            return fn2(h)
        return self._mk(en, emit, r, w, cost)

    def mm(self, steps, r=(), w=(), start=True, stop=True):
        n = len(steps)
        self.pe_ins += n
        cost = 0.0
        for st in steps:
            fr = 1
            for d_ in st[2].shape[1:]:
                fr *= d_
            c = max(fr, 64) / 2400.0 + 0.05
            if st[1].dtype == F32:
                c *= 4
            cost += c

        def emit(h):
            ins = None
            for i, st in enumerate(steps):
                kw = st[3] if len(st) > 3 else {}
                ins = h.matmul(st[0], st[1], st[2], start=(start and i == 0), stop=(stop and i == n - 1), **kw)
            return ins
        return self._mk("pe", emit, r, w, cost, lat=cost + 0.15)

    def tr(self, outs_ins, ident, r=(), w=()):
        self.pe_ins += len(outs_ins)

        def emit(h):
            ins = None
            for out, in_ in outs_ins:
                ins = h.transpose(out, in_, ident)
            return ins
        return self._mk("pe", emit, r, w, 0.12 * len(outs_ins), lat=0.12 * len(outs_ins) + 0.15)

    def dma(self, qn, out, in_, r=(), w=()):
        nbytes = 1
        for d_ in in_.shape:
            nbytes *= d_
        nbytes *= 4 if in_.dtype == F32 else 2

        def emit(h):
            return h.dma_start(out=out, in_=in_)
        occ = 1.1 if qn == "pool" else 0.1
        nd = self._mk(qn, emit, r, w, occ, lat=2.2 + nbytes / 150e3)
        nd.isdma = True
        return nd

    def flush(self):
        nodes = self.nodes
        self.nodes = []
        if not nodes:
            return
        per = {en: [] for en in self.E}
        for nd in nodes:
            per[nd.eng].append(nd)
        order = {en: [] for en in self.E}
        if not self.reorder:
            for en in per:
                order[en] = per[en]
        else:
            for nd in nodes:
                nd.bl = nd.lat
            for nd in reversed(nodes):
                b_ = nd.bl
                for d_ in nd.deps:
                    if d_.start is None:
                        v_ = b_ + d_.lat
                        if v_ > d_.bl:
                            d_.bl = v_
            head = {en: 0 for en in self.E}
            t_eng = {en: 0.0 for en in self.E}
            nleft = len(nodes)
            W, LAT = self.W, self.LAT
            while nleft:
                best = None
                for en, lst in per.items():
                    hpos = head[en]
                    L_ = len(lst)
                    while hpos < L_ and lst[hpos].start is not None:
                        hpos += 1
                    head[en] = hpos
                    cnt = 0
                    i = hpos
                    te = t_eng[en]
                    while i < L_ and cnt < W:
                        nd = lst[i]
                        i += 1
                        if nd.start is not None:
                            continue
                        cnt += 1
                        rt = te
                        ok = True
                        for d_ in nd.deps:
                            if d_.start is None:
                                ok = False
                                break
                            f = d_.finish if d_.eng == en else d_.finish + LAT
                            if f > rt:
                                rt = f
                        if not ok:
                            continue
                        key = rt - nd.pri
                        if best is None or key < best[3] - 0.05 or (key < best[3] + 0.05 and nd.bl > best[1].bl):
                            best = (rt, nd, en, key)
                        if rt <= te and nd.pri > 0:
                            break
                assert best is not None, "scheduler stuck"
                rt, nd, en = best[0], best[1], best[2]
                nd.start = rt
                nd.finish = rt + nd.lat
                t_eng[en] = rt + nd.cost
                order[en].append(nd)
                nleft -= 1
        for en, lst in order.items():
            e = self.E[en]
            for nd in lst:
                if nd.isdma:
                    slot = e.dslots[e.di % len(e.dslots)]
                    e.di += 1
                    nd.prev = (slot[0], slot[1])
                    slot[1] += 16
                    nd.sig = (slot[0], slot[1], 16)
                else:
                    if e.cnt >= self.ROLL:
                        e.si += 1
                        e.cnt = 0
                    e.cnt += 1
                    nd.sig = (e.sems[e.si], e.cnt, 1)
        for en, lst in order.items():
            e = self.E[en]
            for nd in lst:
                best = {}
                for d_ in nd.deps:
                    sem, val = d_.sig[0], d_.sig[1]
                    if e.is_pe and sem.num in self.pe_nums:
                        continue
                    if val > best.get(sem.num, (None, 0))[1]:
                        best[sem.num] = (sem, val)
                if nd.sig[2] == 16 and nd.prev[1] > 0:
                    sem, val = nd.prev
                    if val > best.get(sem.num, (None, 0))[1]:
                        best[sem.num] = (sem, val)
                for num, (sem, val) in best.items():
                    if e.seen.get(num, 0) < val:
                        e.h.wait_ge(sem, val)
                        e.seen[num] = val
                ins = nd.emit(e.h)
                ins.then_inc(nd.sig[0], nd.sig[2])
                nd.emit = None
        for nd in nodes:
            nd.start = 0.0
            nd.finish = 0.0
            nd.deps = ()

    def barrier(self, engines=("pe", "act", "dve", "sp"), include_pool=False):
        self.flush()
        evs = []
        for en, e in self.E.items():
            if en == "pool" and not include_pool:
                continue
            if e.cnt > 0:
                evs.append((e.sems[e.si], e.cnt))
            for sl in e.dslots:
                if sl[1] > 0:
                    evs.append((sl[0], sl[1]))
        for en in engines:
            e = self.E[en]
            for sem, val in evs:
                if e.is_pe and sem.num in self.pe_nums:
                    continue
                if e.seen.get(sem.num, 0) < val:
                    e.h.wait_ge(sem, val)
                    e.seen[sem.num] = val


def host_consts():
    c = {}
    c["identF"] = np.eye(128, dtype=np.float32)
    s = np.arange(128)[:, None]
    t = np.arange(128)[None, :]
    same = (s // 32) == (t // 32)
    c["mLE"] = (same & (s <= t)).astype(np.float32)
    c["mGE"] = (same & (s >= t)).astype(np.float32)
    c["mLT"] = (same & (s < t)).astype(np.float32)
    c["mGT"] = (same & (s > t)).astype(np.float32)
    T = L_S
    rows = np.repeat(np.arange(T // 64, dtype=np.float32), 64)
    col = np.tile(np.arange(64, dtype=np.float32), T // 64)
    nf = 8
    freq = (np.float32(10000.0) ** (-np.arange(nf, dtype=np.float32) / np.float32(nf))).astype(np.float32)
    ang = np.stack([rows[:, None] * freq, col[:, None] * freq], axis=1).astype(np.float32)
    cosv = np.cos(ang).astype(np.float32)
    sinv = np.sin(ang).astype(np.float32)
    C = np.zeros((96, T), np.float32)
    S = np.zeros((96, T), np.float32)
    for a in range(2):
        for hf in range(2):
            for f in range(nf):
                C[64 + a * 16 + hf * 8 + f] = cosv[:, a, f]
                S[64 + a * 16 + hf * 8 + f] = sinv[:, a, f]
    c["ropeC"] = C
    c["ropeS"] = S
    P = np.zeros((96, 96), np.float32)
    for a in range(2):
        for f in range(nf):
            m0 = 64 + a * 16 + f
            m1 = 64 + a * 16 + 8 + f
            P[m1, m0] = -1.0
            P[m0, m1] = 1.0
    c["P96"] = P
    sw = np.zeros((128, 128), np.float32)
    for m_ in range(128):
        sw[(m_ + 64) % 128, m_] = 1.0
    c["swapM"] = sw
    return c


WEIGHTS = ["w_ada", "b_ada", "norm1", "w_in", "hgrn_lb_logits", "hgrn_gnorm", "w_o_hgrn", "conv_w", "w_o_conv",
           "mla_q_norm", "w_q_up", "mla_kv_norm", "w_kv_up", "w_o_mla", "w_out", "norm2", "w_up", "ffn_conv_w",
           "w_down", "final_norm"]
W_SHAPES = {
    "w_ada": [2, 1024, 6144], "b_ada": [2, 6144], "norm1": [2, 1024], "w_in": [2, 1024, IN_COLS],
    "hgrn_lb_logits": [2, 2, 512], "hgrn_gnorm": [2, 128], "w_o_hgrn": [2, 512, 1024], "conv_w": [2, 3, 512],
    "w_o_conv": [2, 512, 1024], "mla_q_norm": [2, 384], "w_q_up": [2, 384, 768], "mla_kv_norm": [2, 256],
    "w_kv_up": [2, 256, 1024], "w_o_mla": [2, 512, 1024], "w_out": [2, 1024, 1024], "norm2": [2, 1024],
    "w_up": [2, 1024, 5632], "ffn_conv_w": [2, 3, 5632], "w_down": [2, 2816, 1024], "final_norm": [1024],
}


_CACHE = {}


class _Stop(Exception):
    pass


def build(debug=None, stop=None):
    nc = bass.Bass("TRN2", target_bir_lowering=False)
    dr = {}

    def din(name, shape, dt=F32):
        dr[name] = nc.dram_tensor(name, list(shape), dt, kind="ExternalInput").ap()
        return dr[name]

    def dout(name, shape):
        dr[name] = nc.dram_tensor(name, list(shape), F32, kind="ExternalOutput").ap()
        return dr[name]

    din("xp", [NP_SEQ * L_P, D])
    din("xs", [L_S, D])
    din("st_in", [2, 2, 4, 128, 128])
    din("cache_in", [2, PAST, C_KV + C_ROPE])
    din("cvec", [2, D])
    for wn in WEIGHTS:
        din(wn, W_SHAPES[wn])
    hc = host_consts()
    for k, v in hc.items():
        din(k, v.shape)
    dout("yp", [NP_SEQ * L_P, D])
    dout("ys", [L_S, D])
    dout("st_out", [NP_SEQ, 2, 2, 4, 128, 128])
    dout("cache_out", [NP_SEQ, 2, L_P, C_KV + C_ROPE])
    if debug:
        dout("dbg", [128, debug])
    ofw_d = nc.dram_tensor("ofw_d", [128, 4, L_S], BF16).ap()
    oA_d = nc.dram_tensor("oA_d", [128, 4, L_S], BF16).ap()

    with ExitStack() as es:
        kb = KB(nc, es)
        _CACHE['kb'] = kb

        def sb(name, shape, dt):
            return es.enter_context(nc.sbuf_tensor("sb_" + name, list(shape), dt))

        xT = sb("xT", [128, 8, L_S], F32)
        xbuf = [Buf(f"x{j}") for j in range(L_S // 512)]
        wsl = [sb(f"wsl{i}", [128, WSLOT], BF16) for i in range(4)]
        wslb = [Buf(f"wsl{i}") for i in range(4)]
        wctr = [0]
        identF = sb("identF", [128, 128], F32)
        identB = sb("identB", [128, 128], BF16)
        onesB = sb("onesB", [128, 128], BF16)
        mLE = sb("mLE", [128, 128], F32)
        mGE = sb("mGE", [128, 128], F32)
        mLT = sb("mLT", [128, 128], F32)
        mGT = sb("mGT", [128, 128], F32)
        swapM = sb("swapM", [128, 128], F32)
        P96 = sb("P96", [96, 96], BF16)
        P96f = sb("P96f", [96, 96], F32)
        epsT = sb("epsT", [128, 1], F32)
        cbuf = Buf("consts")
        fv = sb("fv", [128, 2, 224], F32)
        fvb = Buf("fv")
        modT = sb("modT", [128, 2, 48, 2], F32)
        modb = Buf("mod")
        dmod = sb("dmod", [128, 6, 8], F32)
        dmodb = Buf("dmod")
        lbT = sb("lbT", [128, 2, 2, 512], F32)
        lbb = Buf("lb")
        kvnB = sb("kvnB", [128, 2, 256], F32)
        kvnb = Buf("kvnB")
        ARENA = 53600
        arena = sb("arena", [128, ARENA], BF16)
        psum = [es.enter_context(nc.psum_tensor(f"ps{i}", [128, 512], F32)) for i in range(8)]
        psb = [Buf(f"ps{i}", excl=True) for i in range(8)]

        class Carver:
            def __init__(self):
                self.off = 0

            def reset(self):
                self.off = 0

            def get(self, shape, dt, nbuf=1):
                n = int(np.prod(shape[1:]))
                nb = n * (4 if dt == F32 else 2)
                nb = (nb + 3) // 4 * 4
                o = self.off
                self.off += nb
                assert self.off <= ARENA * 2, f"arena overflow {self.off}"
                ap = arena[0:shape[0], o // 2:(o + nb) // 2]
                if dt == F32:
                    ap = ap.bitcast(F32)
                ap = ap[:, 0:n]
                if len(shape) == 3:
                    ap = ap.rearrange("p (a b) -> p a b", a=shape[1])
                elif len(shape) == 4:
                    ap = ap.rearrange("p (a b c) -> p a b c", a=shape[1], b=shape[2])
                return ap

        cv = Carver()

        def phase():
            kb.barrier()
            cv.reset()

        class Rot:
            def __init__(self, name, shape, dt, n):
                self.t = [cv.get(shape, dt) for _ in range(n)]
                self.b = [Buf(f"{name}{i}") for i in range(n)]
                self.i = 0

            def nxt(self):
                i = self.i % len(self.t)
                self.i += 1
                return self.t[i], self.b[i]

        def wload(src, kc, ncols, extra=None):
            i = wctr[0] % 4
            wctr[0] += 1
            n = kc * ncols * (extra or 1)
            assert n <= WSLOT
            if extra:
                view = wsl[i][:, 0:n].rearrange("p (k e n) -> p k e n", k=kc, e=extra)
            else:
                view = wsl[i][:, 0:n].rearrange("p (k n) -> p k n", k=kc)
            if extra:
                for e_ in range(extra):
                    kb.dma("pool", view[:, :, e_, :], src[:, :, e_, :], w=[wslb[i]])
            else:
                kb.dma("pool", view, src, w=[wslb[i]])
            return view, wslb[i]

        def w3(name, l):
            return dr[name][l].rearrange("(kc p) n -> p kc n", p=128)

        for nm, t_ in (("identF", identF), ("mLE", mLE), ("mGE", mGE), ("mLT", mLT), ("mGT", mGT), ("swapM", swapM)):
            kb.dma("sp", t_[:], dr[nm], w=[cbuf])
        kb.dma("sp", P96f[:], dr["P96"], w=[cbuf])
        kb.op("dve", lambda h: h.tensor_copy(out=identB[:], in_=identF[:]), r=[cbuf], w=[cbuf])
        kb.op("dve", lambda h: h.tensor_copy(out=P96[:], in_=P96f[:]), r=[cbuf], w=[cbuf])
        kb.op("dve", lambda h: h.memset(onesB[:], 1.0), w=[cbuf])
        kb.op("dve", lambda h: h.memset(epsT[:], EPS), w=[cbuf])

        cv.reset()
        stg = cv.get([128, 128], F32)
        stgb = Buf("stg")
        cin = sb("cin", [128, 2, 8], F32)

        def featvec(rows_ap, nrows, dst_ap, dstbuf):
            kb.dma("sp", stg[0:nrows, :], rows_ap, w=[stgb])
            kb.tr([(psum[7][:, 0:nrows], stg[0:nrows, :])], identF[0:nrows, 0:nrows], r=[stgb, cbuf], w=[psb[7]])
            kb.op("dve", lambda h: h.tensor_copy(out=dst_ap, in_=psum[7][:, 0:nrows]), r=[psb[7]], w=[dstbuf])

        for l in range(DEPTH):
            featvec(dr["b_ada"][l].rearrange("(j p) -> j p", p=128), 48, fv[:, l, 0:48], fvb)
            featvec(dr["norm1"][l].rearrange("(j p) -> j p", p=128), 8, fv[:, l, 48:56], fvb)
            featvec(dr["norm2"][l].rearrange("(j p) -> j p", p=128), 8, fv[:, l, 56:64], fvb)
            featvec(dr["hgrn_gnorm"][l].rearrange("(j p) -> j p", p=128), 1, fv[:, l, 64:65], fvb)
            featvec(dr["conv_w"][l].rearrange("j (g p) -> (j g) p", p=128), 12, fv[:, l, 65:77], fvb)
            featvec(dr["mla_q_norm"][l].rearrange("(j p) -> j p", p=128), 3, fv[:, l, 77:80], fvb)
            featvec(dr["mla_kv_norm"][l].rearrange("(j p) -> j p", p=128), 2, fv[:, l, 80:82], fvb)
            fc = dr["ffn_conv_w"][l].rearrange("j (m p) -> (j m) p", p=128)
            featvec(fc[0:88], 88, fv[:, l, 82:170], fvb)
            featvec(fc[88:132], 44, fv[:, l, 170:214], fvb)
        featvec(dr["final_norm"].rearrange("(j p) -> j p", p=128), 8, fv[:, 0, 214:222], fvb)
        featvec(dr["cvec"].rearrange("c (j p) -> (c j) p", p=128), 16, cin[:].rearrange("p c j -> p (c j)"), fvb)

        lg = cv.get([128, 2, 2, 512], F32)
        lgb = Buf("lg")
        for l in range(2):
            for d_ in range(2):
                kb.dma("sp", lg[:, l, d_, :], dr["hgrn_lb_logits"][l, d_].partition_broadcast(128), w=[lgb])
        for d_ in range(2):
            kb.op("dve", lambda h, d_=d_: h.tensor_tensor(out=lbT[:, d_, 0, :], in0=lg[:, 1, d_, :], in1=lg[:, 0, d_, :],
                                                          op=ALU.subtract), r=[lgb], w=[lbb])
            kb.op("act", lambda h, d_=d_: h.activation(out=lbT[:, d_, 0, :], in_=lbT[:, d_, 0, :], func=AF.Sigmoid),
                  r=[lbb], w=[lbb])
            kb.op("dve", lambda h, d_=d_: h.tensor_scalar(out=lbT[:, d_, 1, :], in0=lbT[:, d_, 0, :], scalar1=-1.0,
                                                          scalar2=1.0, op0=ALU.mult, op1=ALU.add), r=[lbb], w=[lbb])
        for l in range(2):
            kb.dma("sp", kvnB[:, l, :], dr["mla_kv_norm"][l].partition_broadcast(128), w=[kvnb])

        silc = cv.get([128, 8, 2], BF16)
        silb = Buf("silc")
        kb.op("act", lambda h: h.activation(out=silc[:].rearrange("p k c -> p c k"), in_=cin[:], func=AF.Silu),
              r=[fvb], w=[silb])
        for l in range(DEPTH):
            wa = w3("w_ada", l)
            for js in range(0, 48, 2):
                wv, wb = wload(wa[:, :, js * 128:(js + 2) * 128], 8, 256)
                for j in range(js, js + 2):
                    kb.mm([(psum[6][:, 2 * j:2 * j + 2], wv[:, k, (j - js) * 128:(j - js + 1) * 128], silc[:, k, :])
                           for k in range(8)], r=[wb, silb], w=[psb[6]])
            kb.op("dve", lambda h, l=l: h.tensor_tensor(
                out=modT[:, l, :, :], in0=psum[6][:, 0:96].rearrange("p (j c) -> p j c", c=2),
                in1=fv[:, l, 0:48].unsqueeze(2).to_broadcast([128, 48, 2]), op=ALU.add), r=[psb[6], fvb], w=[modb])

        def chk(name):
            kb.marks.append((name, kb.pe_ins))
            if stop == name:
                raise _Stop()

        def run_job(job):
            T, L, nseq = job["T"], job["L"], job["nseq"]
            ji = job["idx"]
            nb = T // 512
            n128 = T // 128
            QN = min(512, L)
            Tk = L + (PAST if job["cache"] else 0)
            nkb = Tk // 128

            phase()
            st0 = Rot("xst", [128, 1024], F32, 2)
            for i in range(n128):
                s_t, s_b = st0.nxt()
                kb.dma("sp", s_t, job["xin"][i * 128:(i + 1) * 128, :], w=[s_b])
                for half in range(2):
                    pi = (2 * i + half) % 4
                    kb.tr([(psum[pi][:, q * 128:(q + 1) * 128], s_t[:, (half * 4 + q) * 128:(half * 4 + q + 1) * 128])
                           for q in range(4)], identF[:], r=[s_b, cbuf], w=[psb[pi]])
                    eng = "dve" if half == 0 else "act"
                    if eng == "dve":
                        kb.op("dve", lambda h, pi=pi, half=half, i=i: h.tensor_copy(
                            out=xT[:, half * 4:half * 4 + 4, i * 128:(i + 1) * 128],
                            in_=psum[pi][:].rearrange("p (q t) -> p q t", q=4)), r=[psb[pi]], w=[xbuf[i // 4]])
                    else:
                        kb.op("act", lambda h, pi=pi, half=half, i=i: h.copy(
                            out=xT[:, half * 4:half * 4 + 4, i * 128:(i + 1) * 128],
                            in_=psum[pi][:].rearrange("p (q t) -> p q t", q=4)), r=[psb[pi]], w=[xbuf[i // 4]])

            chk("p0")

            def rstd_block(j, ntok=512):
                sq = Rot_sq
                for k in range(8):
                    s_t, s_b = sq.nxt()
                    kb.op("act", lambda h, k=k, s_t=s_t: h.activation(out=s_t, in_=xT[:, k, j * 512:(j + 1) * 512],
                                                                      func=AF.Square), r=[xbuf[j]], w=[s_b])
                    kb.mm([(psum[5][:], onesB[:], s_t)], r=[s_b, cbuf], w=[psb[5]], start=(k == 0), stop=(k == 7))
                r_t, r_b = Rot_rstd.nxt()
                kb.op("act", lambda h: h.activation(out=r_t, in_=psum[5][:], func=AF.Ln, bias=epsT[:], scale=1.0 / D),
                      r=[psb[5], cbuf], w=[r_b])
                kb.op("act", lambda h: h.activation(out=r_t, in_=r_t, func=AF.Exp, scale=-0.5), r=[r_b], w=[r_b])
                return r_t, r_b

            class VirtXN:
                def __init__(self):
                    self.tiles = {}

                def __getitem__(self, key):
                    p_, k_, ts_ = key
                    j_ = ts_.start // 512
                    return self.tiles[j_][p_, k_, ts_.start - j_ * 512:ts_.stop - j_ * 512]

            def prep_xn(j, gi, si, xnv, xnb, R_xn):
                t_, b_ = R_xn.nxt()
                xnv.tiles[j] = t_
                xnb[j] = b_
                r_t, r_b = rstd_block(j)
                for k in range(8):
                    t_t, t_b = Rot_tmp.nxt()
                    kb.op("dve", lambda h, k=k, t_t=t_t: h.scalar_tensor_tensor(
                        out=t_t, in0=xT[:, k, j * 512:(j + 1) * 512], scalar=dmod[:, gi, k:k + 1], in1=r_t,
                        op0=ALU.mult, op1=ALU.mult), r=[xbuf[j], dmodb, r_b], w=[t_b])
                    kb.op("act", lambda h, k=k, t_t=t_t: h.activation(
                        out=t_[:, k, :], in_=t_t, func=AF.Identity,
                        bias=dmod[:, si, k:k + 1], scale=1.0), r=[t_b, dmodb], w=[b_])

            def norm_all(gi, si, xn, xnb):
                for j in range(nb):
                    r_t, r_b = rstd_block(j)
                    for k in range(8):
                        t_t, t_b = Rot_tmp.nxt()
                        kb.op("dve", lambda h, k=k, t_t=t_t: h.scalar_tensor_tensor(
                            out=t_t, in0=xT[:, k, j * 512:(j + 1) * 512], scalar=dmod[:, gi, k:k + 1], in1=r_t,
                            op0=ALU.mult, op1=ALU.mult), r=[xbuf[j], dmodb, r_b], w=[t_b])
                        kb.op("act", lambda h, k=k, t_t=t_t: h.activation(
                            out=xn[:, k, j * 512:(j + 1) * 512], in_=t_t, func=AF.Identity,
                            bias=dmod[:, si, k:k + 1], scale=1.0), r=[t_b, dmodb], w=[xnb[j]])

            for l in range(DEPTH):
                fvl = lambda a, b: fv[:, l, a:b]
                phase()
                mo = lambda c0: modT[:, l, c0:c0 + 8, ji]
                kb.op("dve", lambda h: h.scalar_tensor_tensor(out=dmod[:, 0, :], in0=mo(8), scalar=1.0, in1=fvl(48, 56),
                                                              op0=ALU.add, op1=ALU.mult), r=[modb, fvb], w=[dmodb])
                kb.op("dve", lambda h: h.tensor_copy(out=dmod[:, 1, :], in_=mo(0)), r=[modb], w=[dmodb])
                kb.op("dve", lambda h: h.tensor_copy(out=dmod[:, 2, :], in_=mo(16)), r=[modb], w=[dmodb])
                kb.op("dve", lambda h: h.scalar_tensor_tensor(out=dmod[:, 3, :], in0=mo(32), scalar=1.0, in1=fvl(56, 64),
                                                              op0=ALU.add, op1=ALU.mult), r=[modb, fvb], w=[dmodb])
                kb.op("dve", lambda h: h.tensor_copy(out=dmod[:, 4, :], in_=mo(24)), r=[modb], w=[dmodb])
                kb.op("dve", lambda h: h.tensor_copy(out=dmod[:, 5, :], in_=mo(40)), r=[modb], w=[dmodb])
                win = w3("w_in", l)

                R_xn = Rot("xnt", [128, 8, 512], BF16, 2)
                xn = VirtXN()
                xnb = [None] * nb
                mark_x = cv.off
                qT = cv.get([128, 4, T], BF16)
                qTb = [Buf(f"qT{j}") for j in range(nb)]
                mark_h = cv.off
                ofwb = [Buf(f"ofw{j}") for j in range(n128)]
                oAb = [Buf(f"oA{j}") for j in range(nb)]
                R_of = Rot("of", [128, 4, 128], BF16, 2)
                R_oA = Rot("oAt", [128, 4, 512], BF16, 1)
                Rot_sq = Rot("sq", [128, 512], BF16, 2)
                Rot_rstd = Rot("rstd", [128, 512], F32, 2)
                Rot_tmp = Rot("tmp", [128, 512], F32, 2)
                S32 = [cv.get([128, 4, 128], F32) for _ in range(2)]
                S32b = [Buf("S32a"), Buf("S32b")]
                Sp = [0]
                Sbf = cv.get([128, 5, 4, 128], BF16)
                Sbfb = Buf("Sbf")
                R_s = Rot("hs_", [128, 512], F32, 2)
                R_lf = Rot("hlf_", [128, 512], F32, 2)
                R_k = Rot("hk_", [128, 512], BF16, 2)
                R_kh = Rot("hkh_", [128, 512], BF16, 2)
                R_v = Rot("hv_", [128, 512], BF16, 2)
                R_eb = Rot("heb_", [128, 4, 128], F32, 2)
                R_enb = Rot("henb_", [128, 4, 128], F32, 2)
                R_es = Rot("hes_", [128, 512], F32, 2)
                R_qt = Rot("hqt_", [128, 4, 128], BF16, 2)
                R_kt = Rot("hkt_", [128, 4, 128], BF16, 2)
                R_A = Rot("hA_", [128, 4, 128], BF16, 2)
                R_osum = Rot("hos", [128, 4, 512], F32, 1)
                R_sg = Rot("hsg", [128, 4, 512], BF16, 1)

                def hgrn_step(dr_, tb, wf, wfb, wi, wib):
                    tok = slice(tb * 128, (tb + 1) * 128)
                    j = tb // 4
                    TRI = mLE if dr_ == 0 else mGE
                    XM = mGT if dr_ == 0 else mLT
                    MSK = mLE if dr_ == 0 else mGE
                    kb.mm([(psum[0][:, 0:256], xn[:, k, tok], wf[0][:, k, :]) for k in range(8)],
                          r=[xnb[j], wfb[0]], w=[psb[0]])
                    kb.mm([(psum[0][:, 256:512], xn[:, k, tok], wf[1][:, k, :]) for k in range(8)],
                          r=[xnb[j], wfb[1]], w=[psb[0]])
                    s_t, s_b = R_s.nxt()
                    kb.op("act", lambda h: h.activation(out=s_t, in_=psum[0][:], func=AF.Sigmoid), r=[psb[0]], w=[s_b])
                    kb.mm([(psum[0][:, 0:256], xn[:, k, tok], wi[0][:, k, :]) for k in range(8)],
                          r=[xnb[j], wib[0]], w=[psb[0]])
                    kb.mm([(psum[0][:, 256:512], xn[:, k, tok], wi[1][:, k, :]) for k in range(8)],
                          r=[xnb[j], wib[1]], w=[psb[0]])
                    v_t, v_b = R_v.nxt()
                    kb.op("act", lambda h: h.copy(out=v_t, in_=psum[0][:]), r=[psb[0]], w=[v_b])
                    if l > 0:
                        kb.op("dve", lambda h: h.tensor_tensor(out=s_t, in0=s_t, in1=lbT[:, dr_, 1, :], op=ALU.mult),
                              r=[s_b, lbb], w=[s_b])
                        kb.op("dve", lambda h: h.tensor_tensor(out=s_t, in0=s_t, in1=lbT[:, dr_, 0, :], op=ALU.add),
                              r=[s_b, lbb], w=[s_b])
                    lf_t, lf_b = R_lf.nxt()
                    kb.op("act", lambda h: h.activation(out=lf_t, in_=s_t, func=AF.Ln), r=[s_b], w=[lf_b])
                    k_t, k_b = R_k.nxt()
                    kb.op("dve", lambda h: h.tensor_scalar(out=k_t, in0=s_t, scalar1=-1.0, scalar2=1.0, op0=ALU.mult,
                                                           op1=ALU.add), r=[s_b], w=[k_b])
                    for hh in range(4):
                        kb.mm([(psum[2][:, hh * 128:(hh + 1) * 128], lf_t[:, hh * 128:(hh + 1) * 128], TRI[:])],
                              r=[lf_b, cbuf], w=[psb[2]])
                    kb.mm([(psum[3][:], XM[:], lf_t)], r=[lf_b, cbuf], w=[psb[3]])
                    eb_t, eb_b = R_eb.nxt()
                    enb_t, enb_b = R_enb.nxt()
                    es_t, es_b = R_es.nxt()
                    p2v = psum[2][:].rearrange("p (h t) -> p h t", h=4)
                    kb.op("act", lambda h: h.activation(out=eb_t, in_=p2v, func=AF.Exp), r=[psb[2]], w=[eb_b])
                    kb.op("act", lambda h: h.activation(out=enb_t, in_=p2v, func=AF.Exp, scale=-1.0), r=[psb[2]],
                          w=[enb_b])
                    kb.op("act", lambda h: h.activation(out=es_t, in_=psum[3][:], func=AF.Exp), r=[psb[3]], w=[es_b])
                    kh_t, kh_b = R_kh.nxt()
                    kb.op("dve", lambda h: h.tensor_tensor(out=kh_t, in0=k_t, in1=es_t, op=ALU.mult), r=[k_b, es_b],
                          w=[kh_b])
                    p4b = psum[4][:].bitcast(BF16)
                    kb.tr([(p4b[:, hh * 128:(hh + 1) * 128], k_t[:, hh * 128:(hh + 1) * 128]) for hh in range(4)],
                          identB[:], r=[k_b, cbuf], w=[psb[4]])
                    kt_t, kt_b = R_kt.nxt()
                    kb.op("dve", lambda h: h.tensor_tensor(out=kt_t, in0=p4b[:, 0:512].rearrange("p (h t) -> p h t", h=4),
                                                           in1=enb_t, op=ALU.mult), r=[psb[4], enb_b], w=[kt_b])
                    qt_t, qt_b = R_qt.nxt()
                    kb.op("dve", lambda h: h.tensor_tensor(out=qt_t, in0=qT[:, :, tok], in1=eb_t, op=ALU.mult),
                          r=[qTb[j], eb_b], w=[qt_b])
                    for hh in range(4):
                        kb.mm([(psum[5][:, hh * 128:(hh + 1) * 128], kt_t[:, hh, :], qt_t[:, hh, :])],
                              r=[kt_b, qt_b], w=[psb[5]])
                    A_t, A_b = R_A.nxt()
                    kb.op("dve", lambda h: h.tensor_tensor(
                        out=A_t, in0=psum[5][:].rearrange("p (h t) -> p h t", h=4),
                        in1=MSK[:].unsqueeze(1).to_broadcast([128, 4, 128]), op=ALU.mult), r=[psb[5], cbuf], w=[A_b])
                    corder = [0, 1, 2, 3] if dr_ == 0 else [3, 2, 1, 0]
                    p_ = Sp[0]
                    kb.op("act", lambda h: h.copy(out=Sbf[:, 0, :, :], in_=S32[p_][:]), r=[S32b[p_]], w=[Sbfb])
                    for ci, c in enumerate(corder):
                        pu = psum[6 + (ci % 2)]
                        pub = psb[6 + (ci % 2)]
                        for hh in range(4):
                            kw = {"tile_position": (96, 0)} if c == 3 else {}
                            kb.mm([(pu[:, hh * 128:(hh + 1) * 128], kh_t[c * 32:(c + 1) * 32, hh * 128:(hh + 1) * 128],
                                    v_t[c * 32:(c + 1) * 32, hh * 128:(hh + 1) * 128], kw)], r=[kh_b, v_b], w=[pub])
                        tcol = c * 32 + (31 if dr_ == 0 else 0)
                        src, dst = Sp[0], 1 - Sp[0]
                        for hh in range(4):
                            kb.op("dve", lambda h, hh=hh, pu=pu, tcol=tcol, src=src, dst=dst: h.scalar_tensor_tensor(
                                out=S32[dst][:, hh, :], in0=S32[src][:, hh, :], scalar=eb_t[:, hh, tcol:tcol + 1],
                                in1=pu[:, hh * 128:(hh + 1) * 128], op0=ALU.mult, op1=ALU.add),
                                r=[S32b[src], eb_b, pub], w=[S32b[dst]], n=128)
                        Sp[0] = dst
                        if ci < 3:
                            kb.op("act", lambda h, ci=ci, dst=dst: h.copy(out=Sbf[:, ci + 1, :, :], in_=S32[dst][:]),
                                  r=[S32b[dst]], w=[Sbfb])
                    for hh in range(4):
                        steps = [(psum[1][:, hh * 128:(hh + 1) * 128], v_t[:, hh * 128:(hh + 1) * 128], A_t[:, hh, :])]
                        for ci, c in enumerate(corder):
                            steps.append((psum[1][:, hh * 128 + c * 32:hh * 128 + (c + 1) * 32], Sbf[:, ci, hh, :],
                                          qt_t[:, hh, c * 32:(c + 1) * 32]))
                        kb.mm(steps, r=[v_b, A_b, Sbfb, qt_b], w=[psb[1]])
                    return psum[1][:].rearrange("p (h t) -> p h t", h=4), psb[1]

                def state_init(dr_, s):
                    p_ = Sp[0]
                    if job["cache"]:
                        kb.dma("sp", S32[p_][:], dr["st_in"][l, dr_].rearrange("h d v -> d h v"), w=[S32b[p_]])
                    else:
                        kb.op("dve", lambda h: h.memset(S32[p_][:], 0.0), w=[S32b[p_]])

                def state_out(dr_, s):
                    p_ = Sp[0]
                    if not job["cache"]:
                        kb.dma("sp", dr["st_out"][s, l, dr_].rearrange("h d v -> d h v"), S32[p_][:], r=[S32b[p_]])

                for j in range(nb):
                    blk = slice(j * 512, (j + 1) * 512)
                    prep_xn(j, 0, 1, xn, xnb, R_xn)
                    for half in range(2):
                        wv, wb = wload(win[:, :, OFF_Q + half * 256:OFF_Q + (half + 1) * 256], 8, 256)
                        for cc in range(2):
                            hh = half * 2 + cc
                            pq = psum[6 + cc]
                            kb.mm([(pq[:], wv[:, k, cc * 128:(cc + 1) * 128], xn[:, k, blk]) for k in range(8)],
                                  r=[wb, xnb[j]], w=[psb[6 + cc]])
                            t_t, t_b = Rot_tmp.nxt()
                            kb.op("act", lambda h, pq=pq, t_t=t_t: h.activation(out=t_t, in_=pq[:], func=AF.Silu),
                                  r=[psb[6 + cc]], w=[t_b])
                            kb.op("dve", lambda h, hh=hh, t_t=t_t: h.tensor_scalar(
                                out=qT[:, hh, blk], in0=t_t, scalar1=128.0 ** -0.5, scalar2=None, op0=ALU.mult),
                                r=[t_b], w=[qTb[j]])
                    wf, wfb, wi, wib = [], [], [], []
                    for half in range(2):
                        a, b_ = wload(win[:, :, OFF_FF + half * 256:OFF_FF + (half + 1) * 256], 8, 256)
                        wf.append(a)
                        wfb.append(b_)
                    for half in range(2):
                        a, b_ = wload(win[:, :, OFF_I + half * 256:OFF_I + (half + 1) * 256], 8, 256)
                        wi.append(a)
                        wib.append(b_)
                    for sub in range(4):
                        tb = j * 4 + sub
                        if (tb * 128) % L == 0:
                            state_init(0, (tb * 128) // L)
                        po, pob = hgrn_step(0, tb, wf, wfb, wi, wib)
                        of_t, of_b = R_of.nxt()
                        kb.op("act", lambda h, po=po, of_t=of_t: h.copy(out=of_t, in_=po), r=[pob], w=[of_b])
                        kb.dma("sp", ofw_d[:, :, tb * 128:(tb + 1) * 128], of_t, r=[of_b], w=[ofwb[tb]])
                        if ((tb + 1) * 128) % L == 0:
                            state_out(0, (tb * 128) // L)
                chk("fwd")
                for j in reversed(range(nb)):
                    blk = slice(j * 512, (j + 1) * 512)
                    prep_xn(j, 0, 1, xn, xnb, R_xn)
                    wf, wfb, wi, wib = [], [], [], []
                    for half in range(2):
                        a, b_ = wload(win[:, :, OFF_FB + half * 256:OFF_FB + (half + 1) * 256], 8, 256)
                        wf.append(a)
                        wfb.append(b_)
                    for half in range(2):
                        a, b_ = wload(win[:, :, OFF_I + half * 256:OFF_I + (half + 1) * 256], 8, 256)
                        wi.append(a)
                        wib.append(b_)
                    os_t, os_b = R_osum.nxt()
                    for sub in reversed(range(4)):
                        tb = j * 4 + sub
                        if ((tb + 1) * 128) % L == 0:
                            state_init(1, (tb * 128) // L)
                        of_t, of_b = R_of.nxt()
                        kb.dma("sp", of_t, ofw_d[:, :, tb * 128:(tb + 1) * 128], r=[ofwb[tb]], w=[of_b])
                        po, pob = hgrn_step(1, tb, wf, wfb, wi, wib)
                        kb.op("dve", lambda h, po=po, sub=sub, of_t=of_t: h.tensor_tensor(
                            out=os_t[:, :, sub * 128:(sub + 1) * 128], in0=po, in1=of_t,
                            op=ALU.add), r=[pob, of_b], w=[os_b])
                        if (tb * 128) % L == 0:
                            state_out(1, (tb * 128) // L)
                    sg_t, sg_b = R_sg.nxt()
                    for half in range(2):
                        wv, wb = wload(win[:, :, OFF_G + half * 256:OFF_G + (half + 1) * 256], 8, 256)
                        for cc in range(2):
                            hh = half * 2 + cc
                            pq = psum[6 + cc]
                            kb.mm([(pq[:], wv[:, k, cc * 128:(cc + 1) * 128], xn[:, k, blk]) for k in range(8)],
                                  r=[wb, xnb[j]], w=[psb[6 + cc]])
                            kb.op("act", lambda h, pq=pq, hh=hh: h.activation(out=sg_t[:, hh, :], in_=pq[:], func=AF.Silu),
                                  r=[psb[6 + cc]], w=[sg_b])
                    oA_t, oA_b = R_oA.nxt()
                    for hh in range(4):
                        q_t, q_b = Rot_sq.nxt()
                        kb.op("act", lambda h, hh=hh, q_t=q_t: h.activation(out=q_t, in_=os_t[:, hh, :], func=AF.Square),
                              r=[os_b], w=[q_b])
                        pn = psum[4 + (hh % 2)]
                        pnb = psb[4 + (hh % 2)]
                        kb.mm([(pn[:], onesB[:], q_t)], r=[q_b, cbuf], w=[pnb])
                        r_t, r_b = Rot_rstd.nxt()
                        kb.op("act", lambda h, pn=pn, r_t=r_t: h.activation(out=r_t, in_=pn[:], func=AF.Ln, bias=epsT[:],
                                                                            scale=1.0 / 128), r=[pnb, cbuf], w=[r_b])
                        kb.op("act", lambda h, r_t=r_t: h.activation(out=r_t, in_=r_t, func=AF.Exp, scale=-0.5), r=[r_b], w=[r_b])
                        kb.op("dve", lambda h, hh=hh, r_t=r_t: h.scalar_tensor_tensor(
                            out=r_t, in0=os_t[:, hh, :], scalar=fv[:, l, 64:65], in1=r_t, op0=ALU.mult, op1=ALU.mult),
                            r=[os_b, fvb, r_b], w=[r_b])
                        kb.op("dve", lambda h, hh=hh, r_t=r_t, oA_t=oA_t: h.tensor_tensor(out=oA_t[:, hh, :], in0=r_t,
                                                                                          in1=sg_t[:, hh, :], op=ALU.mult),
                              r=[r_b, sg_b], w=[oA_b])
                    kb.dma("sp", oA_d[:, :, blk], oA_t, r=[oA_b], w=[oAb[j]])

                chk("bwd")
                kb.barrier()
                cv.off = mark_x
                Rot_sq = Rot("sq", [128, 512], BF16, 2)
                Rot_rstd = Rot("rstd", [128, 512], F32, 2)
                Rot_tmp = Rot("tmp", [128, 512], F32, 2)
                cqn = cv.get([128, 3, T], BF16)
                cqnb = [Buf(f"cqn{j}") for j in range(nb)]
                ckvT = cv.get([128, 2, nseq * Tk], BF16)
                ckvb = Buf("ckvT")
                krT = cv.get([96, nseq * Tk], BF16)
                krb = Buf("krT")
                mark_p3 = cv.off
                R_c = Rot("cqf", [128, 3, 512], F32, 1)
                R_rp = Rot("rp", [96, 512], F32, 2)
                R_xb = Rot("xb", [96, 512], BF16, 2)
                R_co = Rot("co", [128, 288], F32, 2)
                R_ss = Rot("ss", [128, 2], F32, 2)
                R_cs = Rot("cs", [96, 2, 512], F32, 1)
                for j in range(nb):
                    blk = slice(j * 512, (j + 1) * 512)
                    prep_xn(j, 0, 1, xn, xnb, R_xn)
                    s0 = (j * 512) // L
                    nsb = max(1, 512 // L)
                    c_t, c_b = R_c.nxt()
                    wva0, wba0 = wload(win[:, :, OFF_CQ:OFF_CQ + 256], 8, 256)
                    chk("p3w")
                    wva1, wba1 = wload(win[:, :, OFF_CQ + 256:OFF_CQ + 384], 8, 128)
                    chk("p3x")
                    for cc in range(3):
                        if cc == 1:
                            chk("p3y")
                        pq = psum[cc % 2]
                        pqb = psb[cc % 2]
                        wva, wba, co_ = (wva0, wba0, cc) if cc < 2 else (wva1, wba1, 0)
                        kb.mm([(pq[:], wva[:, k, co_ * 128:(co_ + 1) * 128], xn[:, k, blk]) for k in range(8)],
                              r=[wba, xnb[j]], w=[pqb])
                        q_t, q_b = Rot_sq.nxt()
                        kb.op("act", lambda h, pq=pq, q_t=q_t: h.activation(out=q_t, in_=pq[:], func=AF.Square), r=[pqb],
                              w=[q_b])
                        kb.op("dve", lambda h, pq=pq, cc=cc: h.tensor_copy(out=c_t[:, cc, :], in_=pq[:]), r=[pqb], w=[c_b])
                        kb.mm([(psum[2][:], onesB[:], q_t)], r=[q_b, cbuf], w=[psb[2]], start=(cc == 0), stop=(cc == 2))
                    r_t, r_b = Rot_rstd.nxt()
                    kb.op("act", lambda h: h.activation(out=r_t, in_=psum[2][:], func=AF.Ln, bias=epsT[:],
                                                        scale=1.0 / C_Q), r=[psb[2], cbuf], w=[r_b])
                    kb.op("act", lambda h: h.activation(out=r_t, in_=r_t, func=AF.Exp, scale=-0.5), r=[r_b], w=[r_b])
                    for cc in range(3):
                        kb.op("dve", lambda h, cc=cc: h.scalar_tensor_tensor(
                            out=cqn[:, cc, blk], in0=c_t[:, cc, :], scalar=fv[:, l, 77 + cc:78 + cc], in1=r_t,
                            op0=ALU.mult, op1=ALU.mult), r=[c_b, fvb, r_b], w=[cqnb[j]])
                    chk("p3a")
                    c_t, c_b = R_c.nxt()
                    wvk, wbk = wload(win[:, :, OFF_CKV:OFF_CKV + 256], 8, 256)
                    for cc in range(2):
                        pq = psum[cc % 2]
                        pqb = psb[cc % 2]
                        kb.mm([(pq[:], wvk[:, k, cc * 128:(cc + 1) * 128], xn[:, k, blk]) for k in range(8)],
                              r=[wbk, xnb[j]], w=[pqb])
                        q_t, q_b = Rot_sq.nxt()
                        kb.op("act", lambda h, pq=pq, q_t=q_t: h.activation(out=q_t, in_=pq[:], func=AF.Square), r=[pqb],
                              w=[q_b])
                        kb.op("dve", lambda h, pq=pq, cc=cc: h.tensor_copy(out=c_t[:, cc, :], in_=pq[:]), r=[pqb], w=[c_b])
                        kb.mm([(psum[2][:], onesB[:], q_t)], r=[q_b, cbuf], w=[psb[2]], start=(cc == 0), stop=(cc == 1))
                    r_t, r_b = Rot_rstd.nxt()
                    kb.op("act", lambda h: h.activation(out=r_t, in_=psum[2][:], func=AF.Ln, bias=epsT[:],
                                                        scale=1.0 / C_KV), r=[psb[2], cbuf], w=[r_b])
                    kb.op("act", lambda h: h.activation(out=r_t, in_=r_t, func=AF.Exp, scale=-0.5), r=[r_b], w=[r_b])
                    for cc in range(2):
                        for sb_ in range(nsb):
                            ln = 512 // nsb
                            ko = (s0 + sb_) * Tk + ((j * 512 + sb_ * ln) % L)
                            kb.op("dve", lambda h, cc=cc, sb_=sb_, ln=ln, ko=ko: h.scalar_tensor_tensor(
                                out=ckvT[:, cc, ko:ko + ln], in0=c_t[:, cc, sb_ * ln:(sb_ + 1) * ln],
                                scalar=fv[:, l, 80 + cc:81 + cc], in1=r_t[:, sb_ * ln:(sb_ + 1) * ln],
                                op0=ALU.mult, op1=ALU.mult), r=[c_b, fvb, r_b], w=[ckvb])
                    chk("p3b")
                    wvr, wbr = wload(win[:, :, OFF_CKV + 192:OFF_CKV + 288], 8, 96)
                    kb.mm([(psum[3][0:96, :], wvr[:, k, :], xn[:, k, blk]) for k in range(8)], r=[wbr, xnb[j]], w=[psb[3]])
                    if job["rope"]:
                        x_t, x_b = R_xb.nxt()
                        kb.op("act", lambda h: h.copy(out=x_t[64:96, :], in_=psum[3][64:96, :]), r=[psb[3]], w=[x_b])
                        kb.op("dve", lambda h: h.memset(x_t[0:64, :], 0.0), w=[x_b])
                        kb.mm([(psum[4][0:96, :], P96[:], x_t[:])], r=[x_b, cbuf], w=[psb[4]])
                        cs_t, cs_b = R_cs.nxt()
                        kb.dma("sp", cs_t[64:96, 0, :], dr["ropeC"][64:96, blk], w=[cs_b])
                        kb.dma("sp", cs_t[64:96, 1, :], dr["ropeS"][64:96, blk], w=[cs_b])
                        p_t, p_b = R_rp.nxt()
                        kb.op("dve", lambda h: h.tensor_tensor(out=p_t[64:96, :], in0=psum[4][64:96, :],
                                                               in1=cs_t[64:96, 1, :], op=ALU.mult), r=[psb[4], cs_b],
                              w=[p_b])
                        p2_t, p2_b = R_rp.nxt()
                        kb.op("dve", lambda h: h.tensor_tensor(out=p2_t[64:96, :], in0=psum[3][64:96, :],
                                                               in1=cs_t[64:96, 0, :], op=ALU.mult), r=[psb[3], cs_b],
                              w=[p2_b])
                        kb.op("dve", lambda h: h.tensor_tensor(out=krT[64:96, j * 512:(j + 1) * 512], in0=p_t[64:96, :],
                                                               in1=p2_t[64:96, :], op=ALU.add), r=[p_b, p2_b], w=[krb])
                    else:
                        for sb_ in range(nsb):
                            ln = 512 // nsb
                            ko = (s0 + sb_) * Tk + ((j * 512 + sb_ * ln) % L)
                            kb.op("act", lambda h, sb_=sb_, ln=ln, ko=ko: h.copy(
                                out=krT[64:96, ko:ko + ln], in_=psum[3][64:96, sb_ * ln:(sb_ + 1) * ln]),
                                r=[psb[3]], w=[krb])
                    chk("p3c")
                    if not job["cache"]:
                        for sub in range(4):
                            tb = j * 4 + sub
                            tok = slice(tb * 128, (tb + 1) * 128)
                            pc = psum[5 + (sub % 2)]
                            pcb = psb[5 + (sub % 2)]
                            kb.mm([(pc[:, 0:256], xn[:, k, tok], wvk[:, k, :]) for k in range(8)], r=[xnb[j], wbk],
                                  w=[pcb])
                            kb.mm([(pc[:, 256:288], xn[:, k, tok], wvr[:, k, 64:96]) for k in range(8)], r=[xnb[j], wbr],
                                  w=[pcb])
                            co_t, co_b = R_co.nxt()
                            ss_t, ss_b = R_ss.nxt()
                            kb.op("act", lambda h, pc=pc, co_t=co_t, ss_t=ss_t: h.activation(
                                out=co_t[:, 0:256], in_=pc[:, 0:256], func=AF.Square, accum_out=ss_t[:, 0:1]),
                                r=[pcb], w=[co_b, ss_b])
                            kb.op("act", lambda h, ss_t=ss_t: h.activation(out=ss_t[:, 1:2], in_=ss_t[:, 0:1], func=AF.Ln,
                                                                           bias=epsT[:], scale=1.0 / C_KV),
                                  r=[ss_b, cbuf], w=[ss_b])
                            kb.op("act", lambda h, ss_t=ss_t: h.activation(out=ss_t[:, 1:2], in_=ss_t[:, 1:2], func=AF.Exp,
                                                                           scale=-0.5), r=[ss_b], w=[ss_b], n=1)
                            kb.op("dve", lambda h, pc=pc, co_t=co_t, ss_t=ss_t: h.scalar_tensor_tensor(
                                out=co_t[:, 0:256], in0=pc[:, 0:256], scalar=ss_t[:, 1:2], in1=kvnB[:, l, :],
                                op0=ALU.mult, op1=ALU.mult), r=[pcb, ss_b, kvnb, co_b], w=[co_b])
                            kb.op("act", lambda h, pc=pc, co_t=co_t: h.copy(out=co_t[:, 256:288], in_=pc[:, 256:288]),
                                  r=[pcb], w=[co_b])
                            s_i = (tb * 128) // L
                            to = (tb * 128) % L
                            kb.dma("sp", dr["cache_out"][s_i, l, to:to + 128, :], co_t[:], r=[co_b])
                if job["cache"]:
                    cst = cv.get([128, 2, 288], F32)
                    cstb = Buf("cst")
                    kb.dma("sp", cst[:], dr["cache_in"][l].rearrange("(tb p) f -> p tb f", p=128), w=[cstb])
                    for tb in range(2):
                        kb.tr([(psum[0][:, cc * 128:(cc + 1) * 128], cst[:, tb, cc * 128:(cc + 1) * 128]) for cc in range(2)],
                              identF[:], r=[cstb, cbuf], w=[psb[0]])
                        kb.op("dve", lambda h, tb=tb: h.tensor_copy(
                            out=ckvT[:, :, L + tb * 128:L + (tb + 1) * 128],
                            in_=psum[0][:, 0:256].rearrange("p (c t) -> p c t", c=2)), r=[psb[0]], w=[ckvb])
                        kb.tr([(psum[1][0:96, 0:128], cst[:, tb, 192:288])], identF[:], r=[cstb, cbuf], w=[psb[1]])
                        kb.op("dve", lambda h, tb=tb: h.tensor_copy(out=krT[64:96, L + tb * 128:L + (tb + 1) * 128],
                                                                    in_=psum[1][64:96, 0:128]), r=[psb[1]], w=[krb])

                chk("p3")
                kb.barrier(engines=("pe", "act", "dve", "sp", "pool"))
                mark_a = cv.off
                cv.off = 0
                oC = cv.get([128, 4, T], BF16)
                assert cv.off <= mark_x
                oCb = [Buf(f"oC{j}") for j in range(nb)]
                cv.off = mark_p3
                Wkv = cv.get([128, 2, 1024], BF16)
                Wq = cv.get([128, 3, 768], BF16)
                wab = Buf("Wattn")
                kb.dma("pool", Wkv, w3("w_kv_up", l), w=[wab])
                kb.dma("pool", Wq, w3("w_q_up", l), w=[wab])
                vaug = [cv.get([128, nkb, 128], BF16) for _ in range(2)]
                vaugb = [Buf("vaug0"), Buf("vaug1")]
                rden = [cv.get([128, 512], F32) for _ in range(2)]
                rdenb = [Buf("rden0"), Buf("rden1")]
                kb.op("dve", lambda h: h.memset(vaug[0][:, :, 64:128], 1.0), w=[vaugb[0]])
                kb.op("dve", lambda h: h.memset(vaug[1][:, :, 0:64], 1.0), w=[vaugb[1]])
                kb.op("dve", lambda h: h.memset(rden[0][:], 0.0), w=[rdenb[0]])
                kb.op("dve", lambda h: h.memset(rden[1][:], 0.0), w=[rdenb[1]])
                R_kT = Rot("kT", [96, Tk], BF16, 2)
                R_q = Rot("qh", [96, 512], BF16, 2)
                R_pT = Rot("pT", [128, 512], BF16, 3)
                R_rb = Rot("rb", [128, 512], F32, 1)
                R_rp = Rot("rp2", [96, 512], F32, 2)
                R_cs = Rot("cs2", [96, 2, 512], F32, 1)
                scale_qk = 96.0 ** -0.5
                tasks = [(s, hh, qb) for s in range(nseq) for hh in range(8) for qb in range(L // QN)]
                headc, qc = {}, {}

                def prep_head(s, hh):
                    par = hh % 2
                    kbase = s * Tk
                    kT_t, kT_b = R_kT.nxt()
                    kb.op("dve", lambda h: h.tensor_copy(out=kT_t[64:96, :], in_=krT[64:96, kbase:kbase + Tk]),
                          r=[krb], w=[kT_b])
                    for k5 in range(0, Tk, 512):
                        n5 = min(512, Tk - k5)
                        kb.mm([(psum[7][0:64, 0:n5], Wkv[:, c, hh * 128:hh * 128 + 64],
                                ckvT[:, c, kbase + k5:kbase + k5 + n5]) for c in range(2)], r=[ckvb, wab], w=[psb[7]])
                        kb.op("act", lambda h, k5=k5, n5=n5: h.copy(out=kT_t[0:64, k5:k5 + n5], in_=psum[7][0:64, 0:n5]),
                              r=[psb[7]], w=[kT_b])
                    voff = 0 if par == 0 else 64
                    for kb8 in range(0, nkb, 8):
                        n8 = min(8, nkb - kb8)
                        for q8 in range(n8):
                            kblk = kb8 + q8
                            kb.mm([(psum[7][:, q8 * 64:(q8 + 1) * 64],
                                    ckvT[:, c, kbase + kblk * 128:kbase + (kblk + 1) * 128],
                                    Wkv[:, c, hh * 128 + 64:hh * 128 + 128]) for c in range(2)], r=[ckvb, wab],
                                  w=[psb[7]])
                        kb.op("dve", lambda h, kb8=kb8, n8=n8: h.tensor_copy(
                            out=vaug[par][:, kb8:kb8 + n8, voff:voff + 64],
                            in_=psum[7][:, 0:n8 * 64].rearrange("p (a b) -> p a b", a=n8)), r=[psb[7]], w=[vaugb[par]])
                    return kT_t, kT_b

                def prep_q(s, hh, qb):
                    q0 = s * L + qb * QN
                    jq = q0 // 512
                    q_t, q_b = R_q.nxt()
                    kb.mm([(psum[7][0:96, 0:QN], Wq[:, c, hh * 96:(hh + 1) * 96], cqn[:, c, q0:q0 + QN])
                           for c in range(3)], r=[wab, cqnb[jq]], w=[psb[7]])
                    kb.op("act", lambda h: h.activation(out=q_t[:, 0:QN], in_=psum[7][0:96, 0:QN], func=AF.Identity,
                                                        scale=scale_qk), r=[psb[7]], w=[q_b])
                    if job["rope"]:
                        kb.mm([(psum[3][0:96, 0:QN], P96[:], q_t[:, 0:QN])], r=[q_b, cbuf], w=[psb[3]])
                        cs_t, cs_b = R_cs.nxt()
                        kb.dma("sp", cs_t[64:96, 0, 0:QN], dr["ropeC"][64:96, qb * QN:(qb + 1) * QN], w=[cs_b])
                        kb.dma("sp", cs_t[64:96, 1, 0:QN], dr["ropeS"][64:96, qb * QN:(qb + 1) * QN], w=[cs_b])
                        p_t, p_b = R_rp.nxt()
                        kb.op("dve", lambda h: h.tensor_tensor(out=p_t[64:96, 0:QN], in0=psum[3][64:96, 0:QN],
                                                               in1=cs_t[64:96, 1, 0:QN], op=ALU.mult),
                              r=[psb[3], cs_b], w=[p_b])
                        p2_t, p2_b = R_rp.nxt()
                        kb.op("dve", lambda h: h.scalar_tensor_tensor(
                            out=p2_t[64:96, 0:QN], in0=psum[7][64:96, 0:QN], scalar=scale_qk, in1=cs_t[64:96, 0, 0:QN],
                            op0=ALU.mult, op1=ALU.mult), r=[psb[7], cs_b], w=[p2_b])
                        kb.op("dve", lambda h: h.tensor_tensor(out=q_t[64:96, 0:QN], in0=p_t[64:96, 0:QN],
                                                               in1=p2_t[64:96, 0:QN], op=ALU.add), r=[p_b, p2_b], w=[q_b])
                    return q_t, q_b

                def ensure(i):
                    s, hh, qb = tasks[i]
                    if (s, hh) not in headc:
                        headc[(s, hh)] = prep_head(s, hh)
                    if i not in qc:
                        qc[i] = prep_q(s, hh, qb)

                def attn_main(i):
                    s, hh, qb = tasks[i]
                    par = hh % 2
                    hp = hh // 2
                    q0 = s * L + qb * QN
                    jq = q0 // 512
                    kT_t, kT_b = headc[(s, hh)]
                    q_t, q_b = qc.pop(i)
                    acc, accb = psum[i % 2], psb[i % 2]

                    def qk(kblk):
                        pi = 4 + (kblk % 3)
                        kb.mm([(psum[pi][:, 0:QN], kT_t[:, kblk * 128:(kblk + 1) * 128], q_t[:, 0:QN])], r=[kT_b, q_b],
                              w=[psb[pi]])

                    qk(0)
                    for kblk in range(nkb):
                        if kblk + 1 < nkb:
                            qk(kblk + 1)
                        pi = 4 + (kblk % 3)
                        pT_t, pT_b = R_pT.nxt()
                        kb.op("act", lambda h, pi=pi, pT_t=pT_t: h.activation(out=pT_t[:, 0:QN], in_=psum[pi][:, 0:QN],
                                                                             func=AF.Exp), r=[psb[pi]], w=[pT_b])
                        kb.mm([(acc[:, 0:QN], vaug[par][:, kblk, :], pT_t[:, 0:QN])], r=[vaugb[par], pT_b], w=[accb],
                              start=(kblk == 0), stop=(kblk == nkb - 1))
                    nrows = slice(0, 64) if par == 0 else slice(64, 128)
                    drows = slice(64, 128) if par == 0 else slice(0, 64)
                    kb.op("act", lambda h: h.activation(out=rden[par][drows, 0:QN], in_=acc[drows, 0:QN], func=AF.Ln),
                          r=[accb], w=[rdenb[par]])
                    kb.op("act", lambda h: h.activation(out=rden[par][drows, 0:QN], in_=rden[par][drows, 0:QN],
                                                        func=AF.Exp, scale=-1.0), r=[rdenb[par]], w=[rdenb[par]])
                    kb.mm([(psum[2][:, 0:QN], swapM[:], rden[par][:, 0:QN])], r=[cbuf, rdenb[par]], w=[psb[2]])
                    rb_t, rb_b = R_rb.nxt()
                    kb.op("act", lambda h: h.copy(out=rb_t[nrows, 0:QN], in_=psum[2][nrows, 0:QN]), r=[psb[2]], w=[rb_b])
                    kb.op("dve", lambda h: h.tensor_tensor(out=oC[nrows, hp, q0:q0 + QN], in0=acc[nrows, 0:QN],
                                                           in1=rb_t[nrows, 0:QN], op=ALU.mult), r=[accb, rb_b],
                          w=[oCb[jq]])

                ensure(0)
                for i in range(len(tasks)):
                    if i + 1 < len(tasks):
                        ensure(i + 1)
                    attn_main(i)

                chk("p4")
                kb.barrier()
                cv.off = 0
                oC2 = cv.get([128, 4, T], BF16)
                xn = cv.get([128, 8, T], BF16)
                xnb = [Buf(f"xnm{j}") for j in range(nb)]
                Rot_sq = Rot("sq", [128, 512], BF16, 2)
                Rot_rstd = Rot("rstd", [128, 512], F32, 2)
                Rot_tmp = Rot("tmp", [128, 512], F32, 2)
                xh = cv.get([128, 8, 2], BF16)
                xhb = Buf("xh")
                R_e = Rot("e", [128, 514], F32, 2)
                R_acc = Rot("acc", [128, 512], F32, 2)
                oBt = cv.get([128, 4, 512], BF16)
                oBb = Buf("oB")
                R_a2 = Rot("a2", [128, 2, 512], F32, 1)
                R_oAl = Rot("oAl", [128, 4, 512], BF16, 1)
                hB = cv.get([128, 8, 512], BF16)
                hBb = Buf("hB")
                R_g = Rot("g", [128, 512], F32, 2)

                def halo_cols(xsrc, xsb, j):
                    t0 = j * 512
                    if t0 % L == 0:
                        kb.op("dve", lambda h: h.memset(xh[:, :, 0:1], 0.0), w=[xhb])
                    else:
                        kb.op("dve", lambda h: h.tensor_copy(out=xh[:, :, 0:1], in_=xsrc[:, :, t0 - 1:t0]),
                              r=[xsb[j - 1]], w=[xhb])
                    if (t0 + 512) % L == 0:
                        kb.op("dve", lambda h: h.memset(xh[:, :, 1:2], 0.0), w=[xhb])
                    else:
                        kb.op("dve", lambda h: h.tensor_copy(out=xh[:, :, 1:2], in_=xsrc[:, :, t0 + 512:t0 + 513]),
                              r=[xsb[j + 1]], w=[xhb])

                def conv3(e_t, e_b, acc_t, acc_b, w0, w1, w2, wbuf):
                    kb.op("act", lambda h: h.activation(out=acc_t, in_=e_t[:, 1:513], func=AF.Identity, scale=w1),
                          r=[e_b, wbuf], w=[acc_b])
                    seg = min(L, 512)
                    for a in range(0, 512, seg):
                        lo = a if a == 0 else a + 1
                        kb.op("dve", lambda h, lo=lo, a=a: h.scalar_tensor_tensor(
                            out=acc_t[:, lo:a + seg], in0=e_t[:, lo:a + seg], scalar=w0, in1=acc_t[:, lo:a + seg],
                            op0=ALU.mult, op1=ALU.add), r=[e_b, wbuf, acc_b], w=[acc_b])
                        hi = a + seg if a + seg == 512 else a + seg - 1
                        kb.op("dve", lambda h, hi=hi, a=a: h.scalar_tensor_tensor(
                            out=acc_t[:, a:hi], in0=e_t[:, a + 2:hi + 2], scalar=w2, in1=acc_t[:, a:hi],
                            op0=ALU.mult, op1=ALU.add), r=[e_b, wbuf, acc_b], w=[acc_b])

                norm_all(0, 1, xn, xnb)
                for j in range(nb):
                    blk = slice(j * 512, (j + 1) * 512)
                    halo_cols(xn, xnb, j)
                    for half in range(2):
                        wvc, wbc = wload(win[:, :, OFF_BC + half * 256:OFF_BC + (half + 1) * 256], 8, 256)
                        wvh, wbh = wload(win[:, :, OFF_BH + half * 256:OFF_BH + (half + 1) * 256], 8, 256)
                        wvb, wbb = wload(win[:, :, OFF_BB + half * 256:OFF_BB + (half + 1) * 256], 8, 256)
                        for cc in range(2):
                            g = half * 2 + cc
                            cs_ = slice(cc * 128, (cc + 1) * 128)
                            kb.mm([(psum[0][:], wvc[:, k, cs_], xn[:, k, blk]) for k in range(8)], r=[wbc, xnb[j]],
                                  w=[psb[0]])
                            kb.mm([(psum[1][:], wvh[:, k, cs_], xn[:, k, blk]) for k in range(8)], r=[wbh, xnb[j]],
                                  w=[psb[1]])
                            kb.mm([(psum[2][:, 0:2], wvc[:, k, cs_], xh[:, k, :]) for k in range(8)], r=[wbc, xhb],
                                  w=[psb[2]])
                            kb.mm([(psum[2][:, 2:4], wvh[:, k, cs_], xh[:, k, :]) for k in range(8)], r=[wbh, xhb],
                                  w=[psb[2]])
                            kb.mm([(psum[3][:], wvb[:, k, cs_], xn[:, k, blk]) for k in range(8)], r=[wbb, xnb[j]],
                                  w=[psb[3]])
                            t_t, t_b = Rot_tmp.nxt()
                            kb.op("act", lambda h, t_t=t_t: h.copy(out=t_t, in_=psum[0][:]), r=[psb[0]], w=[t_b])
                            e_t, e_b = R_e.nxt()
                            kb.op("dve", lambda h, t_t=t_t, e_t=e_t: h.tensor_tensor(out=e_t[:, 1:513], in0=psum[1][:],
                                                                                    in1=t_t, op=ALU.mult),
                                  r=[psb[1], t_b], w=[e_b])
                            t2_t, t2_b = Rot_tmp.nxt()
                            kb.op("act", lambda h, t2_t=t2_t: h.copy(out=t2_t[:, 0:2], in_=psum[2][:, 0:2]), r=[psb[2]],
                                  w=[t2_b])
                            kb.op("dve", lambda h, t2_t=t2_t, e_t=e_t: h.tensor_tensor(
                                out=e_t[:, 0:514:513], in0=psum[2][:, 2:4], in1=t2_t[:, 0:2], op=ALU.mult),
                                r=[psb[2], t2_b], w=[e_b])
                            acc_t, acc_b = R_acc.nxt()
                            conv3(e_t, e_b, acc_t, acc_b, fv[:, l, 65 + g:66 + g], fv[:, l, 69 + g:70 + g],
                                  fv[:, l, 73 + g:74 + g], fvb)
                            kb.op("dve", lambda h, g=g, acc_t=acc_t: h.tensor_tensor(out=oBt[:, g, :], in0=psum[3][:],
                                                                                    in1=acc_t, op=ALU.mult),
                                  r=[psb[3], acc_b], w=[oBb])
                    oAl_t, oAl_b = R_oAl.nxt()
                    kb.dma("sp", oAl_t, oA_d[:, :, blk], r=[oAb[j]], w=[oAl_b])
                    srcs = (("w_o_hgrn", oAl_t, oAl_b), ("w_o_conv", oBt, oBb), ("w_o_mla", oC2[:, :, blk], oCb[j]))
                    for o2 in range(0, 8, 2):
                        a2_t, a2_b = R_a2.nxt()
                        for br, (wname, osrc, osb) in enumerate(srcs):
                            wvo, wbo = wload(w3(wname, l)[:, :, o2 * 128:(o2 + 2) * 128], 4, 256)
                            gc0 = OFF_GATE + br * 1024 + o2 * 128
                            wvg, wbg = wload(win[:, :, gc0:gc0 + 256], 8, 256)
                            for o1 in range(2):
                                oc = o2 + o1
                                py, pyb = psum[4 + o1], psb[4 + o1]
                                pg, pgb = psum[6 + o1], psb[6 + o1]
                                kb.mm([(py[:], wvo[:, k, o1 * 128:(o1 + 1) * 128], osrc[:, k, :]) for k in range(4)],
                                      r=[wbo, osb], w=[pyb])
                                kb.mm([(pg[:], wvg[:, k, o1 * 128:(o1 + 1) * 128], xn[:, k, blk]) for k in range(8)],
                                      r=[wbg, xnb[j]], w=[pgb])
                                g_t, g_b = R_g.nxt()
                                kb.op("act", lambda h, pg=pg, g_t=g_t: h.activation(out=g_t, in_=pg[:], func=AF.Sigmoid),
                                      r=[pgb], w=[g_b])
                                if br == 0:
                                    kb.op("dve", lambda h, o1=o1, py=py, g_t=g_t, a2_t=a2_t: h.tensor_tensor(
                                        out=a2_t[:, o1, :], in0=py[:], in1=g_t, op=ALU.mult), r=[pyb, g_b], w=[a2_b])
                                else:
                                    kb.op("dve", lambda h, py=py, g_t=g_t: h.tensor_tensor(
                                        out=g_t, in0=py[:], in1=g_t, op=ALU.mult), r=[pyb, g_b], w=[g_b])
                                    if br == 1:
                                        kb.op("dve", lambda h, o1=o1, g_t=g_t, a2_t=a2_t: h.tensor_tensor(
                                            out=a2_t[:, o1, :], in0=a2_t[:, o1, :], in1=g_t, op=ALU.add),
                                            r=[a2_b, g_b], w=[a2_b])
                                    else:
                                        kb.op("dve", lambda h, oc=oc, o1=o1, g_t=g_t, a2_t=a2_t: h.tensor_tensor(
                                            out=hB[:, oc, :], in0=a2_t[:, o1, :], in1=g_t, op=ALU.add),
                                            r=[a2_b, g_b], w=[hBb])
                    wo3 = w3("w_out", l)
                    for o2 in range(0, 8, 2):
                        wvo, wbo = wload(wo3[:, :, o2 * 128:(o2 + 2) * 128], 8, 256)
                        for o1 in range(2):
                            oc = o2 + o1
                            py, pyb = psum[oc % 2], psb[oc % 2]
                            kb.mm([(py[:], wvo[:, k, o1 * 128:(o1 + 1) * 128], hB[:, k, :]) for k in range(8)],
                                  r=[wbo, hBb], w=[pyb])
                            kb.op("dve", lambda h, oc=oc, py=py: h.scalar_tensor_tensor(
                                out=xT[:, oc, blk], in0=py[:], scalar=dmod[:, 2, oc:oc + 1], in1=xT[:, oc, blk],
                                op0=ALU.mult, op1=ALU.add), r=[pyb, dmodb, xbuf[j]], w=[xbuf[j]])

                chk("p5")
                phase()
                xn2 = cv.get([128, 8, T], BF16)
                xn2b = [Buf(f"xn2{j}") for j in range(nb)]
                Rot_sq = Rot("sq", [128, 512], BF16, 2)
                Rot_rstd = Rot("rstd", [128, 512], F32, 2)
                Rot_tmp = Rot("tmp", [128, 512], F32, 2)
                xh = cv.get([128, 8, 2], BF16)
                xhb = Buf("xh")
                R_e = Rot("e", [128, 514], F32, 3)
                R_acc = Rot("acc", [128, 512], F32, 3)
                hid = cv.get([128, 22, 512], BF16)
                hidb = Buf("hid")
                norm_all(3, 4, xn2, xn2b)
                wup = dr["w_up"][l].rearrange("(kc p) (two n) -> p kc two n", p=128, two=2)
                wd3 = w3("w_down", l)
                fo = 82
                for j in range(nb):
                    blk = slice(j * 512, (j + 1) * 512)
                    halo_cols(xn2, xn2b, j)
                    for m in range(22):
                        wv, wb = wload(wup[:, :, :, m * 128:(m + 1) * 128], 8, 128, extra=2)
                        accs = []
                        for ab in range(2):
                            pm, pmb = psum[2 * ab], psb[2 * ab]
                            ph, phb = psum[2 * ab + 1], psb[2 * ab + 1]
                            kb.mm([(pm[:], wv[:, k, ab, :], xn2[:, k, blk]) for k in range(8)], r=[wb, xn2b[j]], w=[pmb])
                            kb.mm([(ph[:, 0:2], wv[:, k, ab, :], xh[:, k, :]) for k in range(8)], r=[wb, xhb], w=[phb])
                            e_t, e_b = R_e.nxt()
                            kb.op("act", lambda h, pm=pm, e_t=e_t: h.copy(out=e_t[:, 1:513], in_=pm[:]), r=[pmb], w=[e_b])
                            kb.op("dve", lambda h, ph=ph, e_t=e_t: h.tensor_copy(out=e_t[:, 0:514:513], in_=ph[:, 0:2]),
                                  r=[phb, e_b], w=[e_b])
                            acc_t, acc_b = R_acc.nxt()
                            mm_ = ab * 22 + m
                            conv3(e_t, e_b, acc_t, acc_b, fv[:, l, fo + mm_:fo + mm_ + 1],
                                  fv[:, l, fo + 44 + mm_:fo + 44 + mm_ + 1], fv[:, l, fo + 88 + mm_:fo + 88 + mm_ + 1], fvb)
                            accs.append((acc_t, acc_b))
                        (a_t, a_b), (b_t, b_b) = accs
                        t_t, t_b = Rot_tmp.nxt()
                        kb.op("act", lambda h, a_t=a_t, t_t=t_t: h.activation(out=t_t, in_=a_t, func=AF.Silu), r=[a_b],
                              w=[t_b])
                        kb.op("dve", lambda h, m=m, t_t=t_t, b_t=b_t: h.tensor_tensor(out=hid[:, m, :], in0=t_t, in1=b_t,
                                                                                     op=ALU.mult), r=[t_b, b_b], w=[hidb])
                    for oc in range(8):
                        wv, wb = wload(wd3[:, :, oc * 128:(oc + 1) * 128], 22, 128)
                        py, pyb = psum[4 + (oc % 2)], psb[4 + (oc % 2)]
                        kb.mm([(py[:], wv[:, k, :], hid[:, k, :]) for k in range(22)], r=[wb, hidb], w=[pyb])
                        kb.op("dve", lambda h, oc=oc, py=py: h.scalar_tensor_tensor(
                            out=xT[:, oc, blk], in0=py[:], scalar=dmod[:, 5, oc:oc + 1], in1=xT[:, oc, blk],
                            op0=ALU.mult, op1=ALU.add), r=[pyb, dmodb, xbuf[j]], w=[xbuf[j]])

            chk("p6")
            phase()
            Rot_sq = Rot("sq", [128, 512], BF16, 2)
            Rot_rstd = Rot("rstd", [128, 512], F32, 2)
            yt = cv.get([128, 8, 512], F32)
            ytb = Buf("yt")
            ost = Rot("ost", [128, 1024], F32, 2)
            for j in range(nb):
                r_t, r_b = rstd_block(j)
                for k in range(8):
                    kb.op("dve", lambda h, k=k: h.scalar_tensor_tensor(
                        out=yt[:, k, :], in0=xT[:, k, j * 512:(j + 1) * 512], scalar=fv[:, 0, 214 + k:215 + k], in1=r_t,
                        op0=ALU.mult, op1=ALU.mult), r=[xbuf[j], fvb, r_b], w=[ytb])
                for sub in range(4):
                    o_t, o_b = ost.nxt()
                    for half in range(2):
                        pi = half
                        kb.tr([(psum[pi][:, q * 128:(q + 1) * 128], yt[:, half * 4 + q, sub * 128:(sub + 1) * 128])
                               for q in range(4)], identF[:], r=[ytb, cbuf], w=[psb[pi]])
                        if half == 0:
                            kb.op("dve", lambda h, o_t=o_t: h.tensor_copy(out=o_t[:, 0:512], in_=psum[0][:]), r=[psb[0]],
                                  w=[o_b])
                        else:
                            kb.op("act", lambda h, o_t=o_t: h.copy(out=o_t[:, 512:1024], in_=psum[1][:]), r=[psb[1]],
                                  w=[o_b])
                    r0 = j * 512 + sub * 128
                    kb.dma("sp", job["yout"][r0:r0 + 128, :], o_t, r=[o_b])

        jobs = [
            dict(idx=0, T=NP_SEQ * L_P, L=L_P, nseq=NP_SEQ, rope=False, cache=False, xin=dr["xp"], yout=dr["yp"]),
            dict(idx=1, T=L_S, L=L_S, nseq=1, rope=True, cache=True, xin=dr["xs"], yout=dr["ys"]),
        ]
        try:
            chk("mod")
            for job in jobs:
                run_job(job)
        except _Stop:
            pass
        kb.barrier(engines=("pe", "act", "dve", "sp", "pool"), include_pool=True)
    return nc, hc


def kernel(**inputs):
    n = 8
    if "nc" not in _CACHE:
        _CACHE["nc"] = build()
    nc, hc = _CACHE["nc"]
    f = lambda a: np.ascontiguousarray(np.asarray(a, dtype=np.float32))
    xp = f(inputs["x_prompt"])
    xs = f(inputs["x_sample"])
    st = f(inputs["state_hgrn"])
    cm = f(inputs["cache_mla"])
    c = f(inputs["c"])
    cctx = f(inputs["c_ctx"])
    in_maps = []
    for i in range(n):
        m = {
            "xp": xp[4 * i:4 * i + 4].reshape(NP_SEQ * L_P, D),
            "xs": xs[i],
            "st_in": st[i],
            "cache_in": cm[i],
            "cvec": np.stack([cctx, c[i]], axis=0),
        }
        for wn in WEIGHTS:
            m[wn] = f(inputs[wn])
        for k, v in hc.items():
            m[k] = v
        in_maps.append({k: np.ascontiguousarray(v) for k, v in m.items()})
    res = run_bass_kernel_spmd(nc, in_maps, core_ids=list(range(n)))
    R = res.results
    y_p = np.concatenate([r["yp"].reshape(NP_SEQ, L_P, D) for r in R], axis=0)
    y_s = np.stack([r["ys"] for r in R], axis=0)
    st_o = np.concatenate([r["st_out"] for r in R], axis=0)
    ch_o = np.concatenate([r["cache_out"] for r in R], axis=0)
    return (y_p.astype(np.float32), y_s.astype(np.float32), st_o.astype(np.float32), ch_o.astype(np.float32))
```

```python
import types
import numpy as np
import ml_dtypes
from contextlib import ExitStack
import concourse.bass as bass
import concourse.mybir as mybir
from concourse.bass_utils import run_bass_kernel_spmd

F32 = mybir.dt.float32
BF16 = mybir.dt.bfloat16
AF = mybir.ActivationFunctionType
ALU = mybir.AluOpType

D = 1024
DEPTH = 2
A_W = 512
B_W = 512
C_Q = 384
C_KV = 256
C_ROPE = 32
D_FF = 2816
IN_COLS = 7840
OFF_Q, OFF_FF, OFF_FB, OFF_I, OFF_G = 0, 512, 1024, 1536, 2048
OFF_BB, OFF_BC, OFF_BH = 2560, 3072, 3584
OFF_CQ = 4096
OFF_CKV = 4480
OFF_GATE = 4768
EPS = 1e-6
NP_SEQ, L_P = 4, 256
L_S = 2048
PAST = 256
WSLOT = 2816


class Buf:
    __slots__ = ("name", "w", "r", "excl")

    def __init__(self, name, excl=False):
        self.name = name
        self.w = None
        self.r = []
        self.excl = excl


def _snap(fn):
    if fn.__closure__ is None:
        return fn
    cells = []
    for c in fn.__closure__:
        try:
            cells.append(types.CellType(c.cell_contents))
        except ValueError:
            cells.append(c)
    return types.FunctionType(fn.__code__, fn.__globals__, fn.__name__, fn.__defaults__, tuple(cells))


class Node:
    __slots__ = ("eng", "emit", "deps", "cost", "lat", "idx", "sig", "start", "finish", "prev", "isdma", "pri", "bl")


class Eng:
    def __init__(self, name, h, sems, is_pe=False):
        self.name = name
        self.h = h
        self.sems = sems
        self.si = 0
        self.cnt = 0
        self.seen = {}
        self.is_pe = is_pe
        self.dslots = []
        self.di = 0


class KB:
    ROLL = 16000
    W = 48
    LAT = 0.2

    def __init__(self, nc, es):
        self.nc = nc
        self.es = es
        self.E = {}
        for name, h, n in (("pe", nc.tensor, 6), ("act", nc.scalar, 6), ("dve", nc.vector, 8),
                           ("pool", nc.gpsimd, 3), ("sp", nc.sync, 1)):
            sems = [es.enter_context(nc.semaphore(f"s_{name}{i}")) for i in range(n)]
            self.E[name] = Eng(name, h, sems, is_pe=(name == "pe"))
        for qn, n in (("sp", 12), ("pool", 8)):
            q = self.E[qn]
            q.dslots = [[es.enter_context(nc.semaphore(f"d_{qn}{i}")), 0] for i in range(n)]
        self.pe_nums = set(s.num for s in self.E["pe"].sems)
        self.pe_ins = 0
        self.marks = []
        self.nodes = []
        self.nidx = 0
        self.reorder = True

    def _mk(self, en, emit, r, w, cost, lat=None):
        nd = Node()
        nd.eng = en
        nd.emit = emit
        nd.cost = cost
        nd.lat = cost if lat is None else lat
        nd.idx = self.nidx
        self.nidx += 1
        nd.sig = None
        nd.start = None
        nd.finish = None
        nd.prev = 0
        nd.isdma = False
        nd.pri = 0.4 if (en != "pe" and any(b.excl for b in r)) else 0.0
        deps = set()
        for b in r:
            if b.w is not None:
                deps.add(b.w)
            if b.excl:
                deps.update(b.r)
        for b in w:
            if b.w is not None:
                deps.add(b.w)
            deps.update(b.r)
        nd.deps = deps
        for b in r:
            b.r.append(nd)
        for b in w:
            b.w = nd
            b.r = []
        self.nodes.append(nd)
        return nd

    def op(self, en, fn, r=(), w=(), n=512):
        if en == "act":
            cost = 0.22 + n / 1400.0
        elif en == "dve":
            cost = 0.10 + max(n, 64) / 960.0
        else:
            cost = 0.30 + n / 450.0

        fn2 = _snap(fn)

        def emit(h):
            return fn2(h)
        return self._mk(en, emit, r, w, cost)

    def mm(self, steps, r=(), w=(), start=True, stop=True):
        n = len(steps)
        self.pe_ins += n
        cost = 0.0
        for st in steps:
            fr = 1
            for d_ in st[2].shape[1:]:
                fr *= d_
            c = max(fr, 64) / 2400.0 + 0.05
            if st[1].dtype == F32:
                c *= 4
            cost += c

        def emit(h):
            ins = None
            for i, st in enumerate(steps):
                kw = st[3] if len(st) > 3 else {}
                ins = h.matmul(st[0], st[1], st[2], start=(start and i == 0), stop=(stop and i == n - 1), **kw)
            return ins
        return self._mk("pe", emit, r, w, cost, lat=cost + 0.15)

    def tr(self, outs_ins, ident, r=(), w=()):
        self.pe_ins += len(outs_ins)

        def emit(h):
            ins = None
            for out, in_ in outs_ins:
                ins = h.transpose(out, in_, ident)
            return ins
        return self._mk("pe", emit, r, w, 0.12 * len(outs_ins), lat=0.12 * len(outs_ins) + 0.15)

    def dma(self, qn, out, in_, r=(), w=()):
        nbytes = 1
        for d_ in in_.shape:
            nbytes *= d_
        nbytes *= 4 if in_.dtype == F32 else 2

        def emit(h):
            return h.dma_start(out=out, in_=in_)
        occ = 1.1 if qn == "pool" else 0.1
        nd = self._mk(qn, emit, r, w, occ, lat=2.2 + nbytes / 150e3)
        nd.isdma = True
        return nd

    def flush(self):
        nodes = self.nodes
        self.nodes = []
        if not nodes:
            return
        per = {en: [] for en in self.E}
        for nd in nodes:
            per[nd.eng].append(nd)
        order = {en: [] for en in self.E}
        if not self.reorder:
            for en in per:
                order[en] = per[en]
        else:
            for nd in nodes:
                nd.bl = nd.lat
            for nd in reversed(nodes):
                b_ = nd.bl
                for d_ in nd.deps:
                    if d_.start is None:
                        v_ = b_ + d_.lat
                        if v_ > d_.bl:
                            d_.bl = v_
            head = {en: 0 for en in self.E}
            t_eng = {en: 0.0 for en in self.E}
            nleft = len(nodes)
            W, LAT = self.W, self.LAT
            while nleft:
                best = None
                for en, lst in per.items():
                    hpos = head[en]
                    L_ = len(lst)
                    while hpos < L_ and lst[hpos].start is not None:
                        hpos += 1
                    head[en] = hpos
                    cnt = 0
                    i = hpos
                    te = t_eng[en]
                    while i < L_ and cnt < W:
                        nd = lst[i]
                        i += 1
                        if nd.start is not None:
                            continue
                        cnt += 1
                        rt = te
                        ok = True
                        for d_ in nd.deps:
                            if d_.start is None:
                                ok = False
                                break
                            f = d_.finish if d_.eng == en else d_.finish + LAT
                            if f > rt:
                                rt = f
                        if not ok:
                            continue
                        key = rt - nd.pri
                        if best is None or key < best[3] - 0.15 or (key < best[3] + 0.15 and nd.bl > best[1].bl):
                            best = (rt, nd, en, key)
                        if rt <= te and (en == "pe" or nd.pri > 0):
                            break
                assert best is not None, "scheduler stuck"
                rt, nd, en = best[0], best[1], best[2]
                nd.start = rt
                nd.finish = rt + nd.lat
                t_eng[en] = rt + nd.cost
                order[en].append(nd)
                nleft -= 1
        for en, lst in order.items():
            e = self.E[en]
            for nd in lst:
                if nd.isdma:
                    slot = e.dslots[e.di % len(e.dslots)]
                    e.di += 1
                    nd.prev = (slot[0], slot[1])
                    slot[1] += 16
                    nd.sig = (slot[0], slot[1], 16)
                else:
                    if e.cnt >= self.ROLL:
                        e.si += 1
                        e.cnt = 0
                    e.cnt += 1
                    nd.sig = (e.sems[e.si], e.cnt, 1)
        for en, lst in order.items():
            e = self.E[en]
            for nd in lst:
                best = {}
                for d_ in nd.deps:
                    sem, val = d_.sig[0], d_.sig[1]
                    if e.is_pe and sem.num in self.pe_nums:
                        continue
                    if val > best.get(sem.num, (None, 0))[1]:
                        best[sem.num] = (sem, val)
                if nd.sig[2] == 16 and nd.prev[1] > 0:
                    sem, val = nd.prev
                    if val > best.get(sem.num, (None, 0))[1]:
                        best[sem.num] = (sem, val)
                for num, (sem, val) in best.items():
                    if e.seen.get(num, 0) < val:
                        e.h.wait_ge(sem, val)
                        e.seen[num] = val
                ins = nd.emit(e.h)
                ins.then_inc(nd.sig[0], nd.sig[2])
                nd.emit = None
        for nd in nodes:
            nd.start = 0.0
            nd.finish = 0.0
            nd.deps = ()

    def barrier(self, engines=("pe", "act", "dve", "sp"), include_pool=False):
        self.flush()
        evs = []
        for en, e in self.E.items():
            if en == "pool" and not include_pool:
                continue
            if e.cnt > 0:
                evs.append((e.sems[e.si], e.cnt))
            for sl in e.dslots:
                if sl[1] > 0:
                    evs.append((sl[0], sl[1]))
        for en in engines:
            e = self.E[en]
            for sem, val in evs:
                if e.is_pe and sem.num in self.pe_nums:
                    continue
                if e.seen.get(sem.num, 0) < val:
                    e.h.wait_ge(sem, val)
                    e.seen[sem.num] = val


def host_consts():
    c = {}
    c["identF"] = np.eye(128, dtype=np.float32)
    s = np.arange(128)[:, None]
    t = np.arange(128)[None, :]
    same = (s // 32) == (t // 32)
    c["mLE"] = (same & (s <= t)).astype(np.float32)
    c["mGE"] = (same & (s >= t)).astype(np.float32)
    c["mLT"] = (same & (s < t)).astype(np.float32)
    c["mGT"] = (same & (s > t)).astype(np.float32)
    T = L_S
    rows = np.repeat(np.arange(T // 64, dtype=np.float32), 64)
    col = np.tile(np.arange(64, dtype=np.float32), T // 64)
    nf = 8
    freq = (np.float32(10000.0) ** (-np.arange(nf, dtype=np.float32) / np.float32(nf))).astype(np.float32)
    ang = np.stack([rows[:, None] * freq, col[:, None] * freq], axis=1).astype(np.float32)
    cosv = np.cos(ang).astype(np.float32)
    sinv = np.sin(ang).astype(np.float32)
    C = np.zeros((96, T), np.float32)
    S = np.zeros((96, T), np.float32)
    for a in range(2):
        for hf in range(2):
            for f in range(nf):
                C[64 + a * 16 + hf * 8 + f] = cosv[:, a, f]
                S[64 + a * 16 + hf * 8 + f] = sinv[:, a, f]
    c["ropeC"] = C
    c["ropeS"] = S
    P = np.zeros((96, 96), np.float32)
    for a in range(2):
        for f in range(nf):
            m0 = 64 + a * 16 + f
            m1 = 64 + a * 16 + 8 + f
            P[m1, m0] = -1.0
            P[m0, m1] = 1.0
    c["P96"] = P
    sw = np.zeros((128, 128), np.float32)
    for m_ in range(128):
        sw[(m_ + 64) % 128, m_] = 1.0
    c["swapM"] = sw
    return c


WEIGHTS = ["w_ada", "b_ada", "norm1", "w_in", "hgrn_lb_logits", "hgrn_gnorm", "w_o_hgrn", "conv_w", "w_o_conv",
           "mla_q_norm", "w_q_up", "mla_kv_norm", "w_kv_up", "w_o_mla", "w_out", "norm2", "w_up", "ffn_conv_w",
           "w_down", "final_norm"]
W_SHAPES = {
    "w_ada": [2, 1024, 6144], "b_ada": [2, 6144], "norm1": [2, 1024], "w_in": [2, 1024, IN_COLS],
    "hgrn_lb_logits": [2, 2, 512], "hgrn_gnorm": [2, 128], "w_o_hgrn": [2, 512, 1024], "conv_w": [2, 3, 512],
    "w_o_conv": [2, 512, 1024], "mla_q_norm": [2, 384], "w_q_up": [2, 384, 768], "mla_kv_norm": [2, 256],
    "w_kv_up": [2, 256, 1024], "w_o_mla": [2, 512, 1024], "w_out": [2, 1024, 1024], "norm2": [2, 1024],
    "w_up": [2, 1024, 5632], "ffn_conv_w": [2, 3, 5632], "w_down": [2, 2816, 1024], "final_norm": [1024],
}


_CACHE = {}


class _Stop(Exception):
    pass


def build(debug=None, stop=None):
    nc = bass.Bass("TRN2", target_bir_lowering=False)
    dr = {}

    def din(name, shape, dt=F32):
        dr[name] = nc.dram_tensor(name, list(shape), dt, kind="ExternalInput").ap()
        return dr[name]

    def dout(name, shape):
        dr[name] = nc.dram_tensor(name, list(shape), F32, kind="ExternalOutput").ap()
        return dr[name]

    din("xp", [NP_SEQ * L_P, D])
    din("xs", [L_S, D])
    din("st_in", [2, 2, 4, 128, 128])
    din("cache_in", [2, PAST, C_KV + C_ROPE])
    din("cvec", [2, D])
    for wn in WEIGHTS:
        din(wn, W_SHAPES[wn])
    hc = host_consts()
    for k, v in hc.items():
        din(k, v.shape)
    dout("yp", [NP_SEQ * L_P, D])
    dout("ys", [L_S, D])
    dout("st_out", [NP_SEQ, 2, 2, 4, 128, 128])
    dout("cache_out", [NP_SEQ, 2, L_P, C_KV + C_ROPE])
    if debug:
        dout("dbg", [128, debug])
    ofw_d = nc.dram_tensor("ofw_d", [128, 4, L_S], BF16).ap()
    oA_d = nc.dram_tensor("oA_d", [128, 4, L_S], BF16).ap()

    with ExitStack() as es:
        kb = KB(nc, es)
        _CACHE['kb'] = kb

        def sb(name, shape, dt):
            return es.enter_context(nc.sbuf_tensor("sb_" + name, list(shape), dt))

        xT = sb("xT", [128, 8, L_S], F32)
        xbuf = [Buf(f"x{j}") for j in range(L_S // 512)]
        wsl = [sb(f"wsl{i}", [128, WSLOT], BF16) for i in range(4)]
        wslb = [Buf(f"wsl{i}") for i in range(4)]
        wctr = [0]
        identF = sb("identF", [128, 128], F32)
        identB = sb("identB", [128, 128], BF16)
        onesB = sb("onesB", [128, 128], BF16)
        mLE = sb("mLE", [128, 128], F32)
        mGE = sb("mGE", [128, 128], F32)
        mLT = sb("mLT", [128, 128], F32)
        mGT = sb("mGT", [128, 128], F32)
        swapM = sb("swapM", [128, 128], F32)
        P96 = sb("P96", [96, 96], BF16)
        P96f = sb("P96f", [96, 96], F32)
        epsT = sb("epsT", [128, 1], F32)
        cbuf = Buf("consts")
        fv = sb("fv", [128, 2, 224], F32)
        fvb = Buf("fv")
        modT = sb("modT", [128, 2, 48, 2], F32)
        modb = Buf("mod")
        dmod = sb("dmod", [128, 6, 8], F32)
        dmodb = Buf("dmod")
        lbT = sb("lbT", [128, 2, 2, 512], F32)
        lbb = Buf("lb")
        kvnB = sb("kvnB", [128, 2, 256], F32)
        kvnb = Buf("kvnB")
        ARENA = 53600
        arena = sb("arena", [128, ARENA], BF16)
        psum = [es.enter_context(nc.psum_tensor(f"ps{i}", [128, 512], F32)) for i in range(8)]
        psb = [Buf(f"ps{i}", excl=True) for i in range(8)]

        class Carver:
            def __init__(self):
                self.off = 0

            def reset(self):
                self.off = 0

            def get(self, shape, dt, nbuf=1):
                n = int(np.prod(shape[1:]))
                nb = n * (4 if dt == F32 else 2)
                nb = (nb + 3) // 4 * 4
                o = self.off
                self.off += nb
                assert self.off <= ARENA * 2, f"arena overflow {self.off}"
                ap = arena[0:shape[0], o // 2:(o + nb) // 2]
                if dt == F32:
                    ap = ap.bitcast(F32)
                ap = ap[:, 0:n]
                if len(shape) == 3:
                    ap = ap.rearrange("p (a b) -> p a b", a=shape[1])
                elif len(shape) == 4:
                    ap = ap.rearrange("p (a b c) -> p a b c", a=shape[1], b=shape[2])
                return ap

        cv = Carver()

        def phase():
            kb.barrier()
            cv.reset()

        class Rot:
            def __init__(self, name, shape, dt, n):
                self.t = [cv.get(shape, dt) for _ in range(n)]
                self.b = [Buf(f"{name}{i}") for i in range(n)]
                self.i = 0

            def nxt(self):
                i = self.i % len(self.t)
                self.i += 1
                return self.t[i], self.b[i]

        def wload(src, kc, ncols, extra=None):
            i = wctr[0] % 4
            wctr[0] += 1
            n = kc * ncols * (extra or 1)
            assert n <= WSLOT
            if extra:
                view = wsl[i][:, 0:n].rearrange("p (k e n) -> p k e n", k=kc, e=extra)
            else:
                view = wsl[i][:, 0:n].rearrange("p (k n) -> p k n", k=kc)
            if extra:
                for e_ in range(extra):
                    kb.dma("pool", view[:, :, e_, :], src[:, :, e_, :], w=[wslb[i]])
            else:
                kb.dma("pool", view, src, w=[wslb[i]])
            return view, wslb[i]

        def w3(name, l):
            return dr[name][l].rearrange("(kc p) n -> p kc n", p=128)

        for nm, t_ in (("identF", identF), ("mLE", mLE), ("mGE", mGE), ("mLT", mLT), ("mGT", mGT), ("swapM", swapM)):
            kb.dma("sp", t_[:], dr[nm], w=[cbuf])
        kb.dma("sp", P96f[:], dr["P96"], w=[cbuf])
        kb.op("dve", lambda h: h.tensor_copy(out=identB[:], in_=identF[:]), r=[cbuf], w=[cbuf])
        kb.op("dve", lambda h: h.tensor_copy(out=P96[:], in_=P96f[:]), r=[cbuf], w=[cbuf])
        kb.op("dve", lambda h: h.memset(onesB[:], 1.0), w=[cbuf])
        kb.op("dve", lambda h: h.memset(epsT[:], EPS), w=[cbuf])

        cv.reset()
        stg = cv.get([128, 128], F32)
        stgb = Buf("stg")
        cin = sb("cin", [128, 2, 8], F32)

        def featvec(rows_ap, nrows, dst_ap, dstbuf):
            kb.dma("sp", stg[0:nrows, :], rows_ap, w=[stgb])
            kb.tr([(psum[7][:, 0:nrows], stg[0:nrows, :])], identF[0:nrows, 0:nrows], r=[stgb, cbuf], w=[psb[7]])
            kb.op("dve", lambda h: h.tensor_copy(out=dst_ap, in_=psum[7][:, 0:nrows]), r=[psb[7]], w=[dstbuf])

        for l in range(DEPTH):
            featvec(dr["b_ada"][l].rearrange("(j p) -> j p", p=128), 48, fv[:, l, 0:48], fvb)
            featvec(dr["norm1"][l].rearrange("(j p) -> j p", p=128), 8, fv[:, l, 48:56], fvb)
            featvec(dr["norm2"][l].rearrange("(j p) -> j p", p=128), 8, fv[:, l, 56:64], fvb)
            featvec(dr["hgrn_gnorm"][l].rearrange("(j p) -> j p", p=128), 1, fv[:, l, 64:65], fvb)
            featvec(dr["conv_w"][l].rearrange("j (g p) -> (j g) p", p=128), 12, fv[:, l, 65:77], fvb)
            featvec(dr["mla_q_norm"][l].rearrange("(j p) -> j p", p=128), 3, fv[:, l, 77:80], fvb)
            featvec(dr["mla_kv_norm"][l].rearrange("(j p) -> j p", p=128), 2, fv[:, l, 80:82], fvb)
            fc = dr["ffn_conv_w"][l].rearrange("j (m p) -> (j m) p", p=128)
            featvec(fc[0:88], 88, fv[:, l, 82:170], fvb)
            featvec(fc[88:132], 44, fv[:, l, 170:214], fvb)
        featvec(dr["final_norm"].rearrange("(j p) -> j p", p=128), 8, fv[:, 0, 214:222], fvb)
        featvec(dr["cvec"].rearrange("c (j p) -> (c j) p", p=128), 16, cin[:].rearrange("p c j -> p (c j)"), fvb)

        lg = cv.get([128, 2, 2, 512], F32)
        lgb = Buf("lg")
        for l in range(2):
            for d_ in range(2):
                kb.dma("sp", lg[:, l, d_, :], dr["hgrn_lb_logits"][l, d_].partition_broadcast(128), w=[lgb])
        for d_ in range(2):
            kb.op("dve", lambda h, d_=d_: h.tensor_tensor(out=lbT[:, d_, 0, :], in0=lg[:, 1, d_, :], in1=lg[:, 0, d_, :],
                                                          op=ALU.subtract), r=[lgb], w=[lbb])
            kb.op("act", lambda h, d_=d_: h.activation(out=lbT[:, d_, 0, :], in_=lbT[:, d_, 0, :], func=AF.Sigmoid),
                  r=[lbb], w=[lbb])
            kb.op("dve", lambda h, d_=d_: h.tensor_scalar(out=lbT[:, d_, 1, :], in0=lbT[:, d_, 0, :], scalar1=-1.0,
                                                          scalar2=1.0, op0=ALU.mult, op1=ALU.add), r=[lbb], w=[lbb])
        for l in range(2):
            kb.dma("sp", kvnB[:, l, :], dr["mla_kv_norm"][l].partition_broadcast(128), w=[kvnb])

        silc = cv.get([128, 8, 2], BF16)
        silb = Buf("silc")
        kb.op("act", lambda h: h.activation(out=silc[:].rearrange("p k c -> p c k"), in_=cin[:], func=AF.Silu),
              r=[fvb], w=[silb])
        for l in range(DEPTH):
            wa = w3("w_ada", l)
            for js in range(0, 48, 2):
                wv, wb = wload(wa[:, :, js * 128:(js + 2) * 128], 8, 256)
                for j in range(js, js + 2):
                    kb.mm([(psum[6][:, 2 * j:2 * j + 2], wv[:, k, (j - js) * 128:(j - js + 1) * 128], silc[:, k, :])
                           for k in range(8)], r=[wb, silb], w=[psb[6]])
            kb.op("dve", lambda h, l=l: h.tensor_tensor(
                out=modT[:, l, :, :], in0=psum[6][:, 0:96].rearrange("p (j c) -> p j c", c=2),
                in1=fv[:, l, 0:48].unsqueeze(2).to_broadcast([128, 48, 2]), op=ALU.add), r=[psb[6], fvb], w=[modb])

        def chk(name):
            kb.marks.append((name, kb.pe_ins))
            if stop == name:
                raise _Stop()

        def run_job(job):
            T, L, nseq = job["T"], job["L"], job["nseq"]
            ji = job["idx"]
            nb = T // 512
            n128 = T // 128
            QN = min(512, L)
            Tk = L + (PAST if job["cache"] else 0)
            nkb = Tk // 128

            phase()
            st0 = Rot("xst", [128, 1024], F32, 2)
            for i in range(n128):
                s_t, s_b = st0.nxt()
                kb.dma("sp", s_t, job["xin"][i * 128:(i + 1) * 128, :], w=[s_b])
                for half in range(2):
                    pi = (2 * i + half) % 4
                    kb.tr([(psum[pi][:, q * 128:(q + 1) * 128], s_t[:, (half * 4 + q) * 128:(half * 4 + q + 1) * 128])
                           for q in range(4)], identF[:], r=[s_b, cbuf], w=[psb[pi]])
                    eng = "dve" if half == 0 else "act"
                    if eng == "dve":
                        kb.op("dve", lambda h, pi=pi, half=half, i=i: h.tensor_copy(
                            out=xT[:, half * 4:half * 4 + 4, i * 128:(i + 1) * 128],
                            in_=psum[pi][:].rearrange("p (q t) -> p q t", q=4)), r=[psb[pi]], w=[xbuf[i // 4]])
                    else:
                        kb.op("act", lambda h, pi=pi, half=half, i=i: h.copy(
                            out=xT[:, half * 4:half * 4 + 4, i * 128:(i + 1) * 128],
                            in_=psum[pi][:].rearrange("p (q t) -> p q t", q=4)), r=[psb[pi]], w=[xbuf[i // 4]])

            chk("p0")

            def rstd_block(j, ntok=512):
                sq = Rot_sq
                for k in range(8):
                    s_t, s_b = sq.nxt()
                    kb.op("act", lambda h, k=k, s_t=s_t: h.activation(out=s_t, in_=xT[:, k, j * 512:(j + 1) * 512],
                                                                      func=AF.Square), r=[xbuf[j]], w=[s_b])
                    kb.mm([(psum[5][:], onesB[:], s_t)], r=[s_b, cbuf], w=[psb[5]], start=(k == 0), stop=(k == 7))
                r_t, r_b = Rot_rstd.nxt()
                kb.op("act", lambda h: h.activation(out=r_t, in_=psum[5][:], func=AF.Ln, bias=epsT[:], scale=1.0 / D),
                      r=[psb[5], cbuf], w=[r_b])
                kb.op("act", lambda h: h.activation(out=r_t, in_=r_t, func=AF.Exp, scale=-0.5), r=[r_b], w=[r_b])
                return r_t, r_b

            class VirtXN:
                def __init__(self):
                    self.tiles = {}

                def __getitem__(self, key):
                    p_, k_, ts_ = key
                    j_ = ts_.start // 512
                    return self.tiles[j_][p_, k_, ts_.start - j_ * 512:ts_.stop - j_ * 512]

            def prep_xn(j, gi, si, xnv, xnb, R_xn):
                t_, b_ = R_xn.nxt()
                xnv.tiles[j] = t_
                xnb[j] = b_
                r_t, r_b = rstd_block(j)
                for k in range(8):
                    t_t, t_b = Rot_tmp.nxt()
                    kb.op("dve", lambda h, k=k, t_t=t_t: h.scalar_tensor_tensor(
                        out=t_t, in0=xT[:, k, j * 512:(j + 1) * 512], scalar=dmod[:, gi, k:k + 1], in1=r_t,
                        op0=ALU.mult, op1=ALU.mult), r=[xbuf[j], dmodb, r_b], w=[t_b])
                    kb.op("act", lambda h, k=k, t_t=t_t: h.activation(
                        out=t_[:, k, :], in_=t_t, func=AF.Identity,
                        bias=dmod[:, si, k:k + 1], scale=1.0), r=[t_b, dmodb], w=[b_])

            def norm_all(gi, si, xn, xnb):
                for j in range(nb):
                    r_t, r_b = rstd_block(j)
                    for k in range(8):
                        t_t, t_b = Rot_tmp.nxt()
                        kb.op("dve", lambda h, k=k, t_t=t_t: h.scalar_tensor_tensor(
                            out=t_t, in0=xT[:, k, j * 512:(j + 1) * 512], scalar=dmod[:, gi, k:k + 1], in1=r_t,
                            op0=ALU.mult, op1=ALU.mult), r=[xbuf[j], dmodb, r_b], w=[t_b])
                        kb.op("act", lambda h, k=k, t_t=t_t: h.activation(
                            out=xn[:, k, j * 512:(j + 1) * 512], in_=t_t, func=AF.Identity,
                            bias=dmod[:, si, k:k + 1], scale=1.0), r=[t_b, dmodb], w=[xnb[j]])

            for l in range(DEPTH):
                fvl = lambda a, b: fv[:, l, a:b]
                phase()
                mo = lambda c0: modT[:, l, c0:c0 + 8, ji]
                kb.op("dve", lambda h: h.scalar_tensor_tensor(out=dmod[:, 0, :], in0=mo(8), scalar=1.0, in1=fvl(48, 56),
                                                              op0=ALU.add, op1=ALU.mult), r=[modb, fvb], w=[dmodb])
                kb.op("dve", lambda h: h.tensor_copy(out=dmod[:, 1, :], in_=mo(0)), r=[modb], w=[dmodb])
                kb.op("dve", lambda h: h.tensor_copy(out=dmod[:, 2, :], in_=mo(16)), r=[modb], w=[dmodb])
                kb.op("dve", lambda h: h.scalar_tensor_tensor(out=dmod[:, 3, :], in0=mo(32), scalar=1.0, in1=fvl(56, 64),
                                                              op0=ALU.add, op1=ALU.mult), r=[modb, fvb], w=[dmodb])
                kb.op("dve", lambda h: h.tensor_copy(out=dmod[:, 4, :], in_=mo(24)), r=[modb], w=[dmodb])
                kb.op("dve", lambda h: h.tensor_copy(out=dmod[:, 5, :], in_=mo(40)), r=[modb], w=[dmodb])
                win = w3("w_in", l)

                R_xn = Rot("xnt", [128, 8, 512], BF16, 2)
                xn = VirtXN()
                xnb = [None] * nb
                mark_x = cv.off
                qT = cv.get([128, 4, T], BF16)
                qTb = [Buf(f"qT{j}") for j in range(nb)]
                mark_h = cv.off
                ofwb = [Buf(f"ofw{j}") for j in range(n128)]
                oAb = [Buf(f"oA{j}") for j in range(nb)]
                R_of = Rot("of", [128, 4, 128], BF16, 2)
                R_oA = Rot("oAt", [128, 4, 512], BF16, 1)
                Rot_sq = Rot("sq", [128, 512], BF16, 2)
                Rot_rstd = Rot("rstd", [128, 512], F32, 2)
                Rot_tmp = Rot("tmp", [128, 512], F32, 2)
                S32 = [cv.get([128, 4, 128], F32) for _ in range(2)]
                S32b = [Buf("S32a"), Buf("S32b")]
                Sp = [0]
                Sbf = cv.get([128, 5, 4, 128], BF16)
                Sbfb = Buf("Sbf")
                R_s = Rot("hs_", [128, 512], F32, 2)
                R_lf = Rot("hlf_", [128, 512], F32, 2)
                R_k = Rot("hk_", [128, 512], BF16, 2)
                R_kh = Rot("hkh_", [128, 512], BF16, 2)
                R_v = Rot("hv_", [128, 512], BF16, 2)
                R_eb = Rot("heb_", [128, 4, 128], F32, 2)
                R_enb = Rot("henb_", [128, 4, 128], F32, 2)
                R_es = Rot("hes_", [128, 512], F32, 2)
                R_qt = Rot("hqt_", [128, 4, 128], BF16, 2)
                R_kt = Rot("hkt_", [128, 4, 128], BF16, 2)
                R_A = Rot("hA_", [128, 4, 128], BF16, 2)
                R_osum = Rot("hos", [128, 4, 512], F32, 1)
                R_sg = Rot("hsg", [128, 4, 512], BF16, 1)

                def hgrn_step(dr_, tb, wf, wfb, wi, wib):
                    tok = slice(tb * 128, (tb + 1) * 128)
                    j = tb // 4
                    TRI = mLE if dr_ == 0 else mGE
                    XM = mGT if dr_ == 0 else mLT
                    MSK = mLE if dr_ == 0 else mGE
                    kb.mm([(psum[0][:, 0:256], xn[:, k, tok], wf[0][:, k, :]) for k in range(8)],
                          r=[xnb[j], wfb[0]], w=[psb[0]])
                    kb.mm([(psum[0][:, 256:512], xn[:, k, tok], wf[1][:, k, :]) for k in range(8)],
                          r=[xnb[j], wfb[1]], w=[psb[0]])
                    s_t, s_b = R_s.nxt()
                    kb.op("act", lambda h: h.activation(out=s_t, in_=psum[0][:], func=AF.Sigmoid), r=[psb[0]], w=[s_b])
                    kb.mm([(psum[0][:, 0:256], xn[:, k, tok], wi[0][:, k, :]) for k in range(8)],
                          r=[xnb[j], wib[0]], w=[psb[0]])
                    kb.mm([(psum[0][:, 256:512], xn[:, k, tok], wi[1][:, k, :]) for k in range(8)],
                          r=[xnb[j], wib[1]], w=[psb[0]])
                    v_t, v_b = R_v.nxt()
                    kb.op("act", lambda h: h.copy(out=v_t, in_=psum[0][:]), r=[psb[0]], w=[v_b])
                    if l > 0:
                        kb.op("dve", lambda h: h.tensor_tensor(out=s_t, in0=s_t, in1=lbT[:, dr_, 1, :], op=ALU.mult),
                              r=[s_b, lbb], w=[s_b])
                        kb.op("dve", lambda h: h.tensor_tensor(out=s_t, in0=s_t, in1=lbT[:, dr_, 0, :], op=ALU.add),
                              r=[s_b, lbb], w=[s_b])
                    lf_t, lf_b = R_lf.nxt()
                    kb.op("act", lambda h: h.activation(out=lf_t, in_=s_t, func=AF.Ln), r=[s_b], w=[lf_b])
                    k_t, k_b = R_k.nxt()
                    kb.op("dve", lambda h: h.tensor_scalar(out=k_t, in0=s_t, scalar1=-1.0, scalar2=1.0, op0=ALU.mult,
                                                           op1=ALU.add), r=[s_b], w=[k_b])
                    for hh in range(4):
                        kb.mm([(psum[2][:, hh * 128:(hh + 1) * 128], lf_t[:, hh * 128:(hh + 1) * 128], TRI[:])],
                              r=[lf_b, cbuf], w=[psb[2]])
                    kb.mm([(psum[3][:], XM[:], lf_t)], r=[lf_b, cbuf], w=[psb[3]])
                    eb_t, eb_b = R_eb.nxt()
                    enb_t, enb_b = R_enb.nxt()
                    es_t, es_b = R_es.nxt()
                    p2v = psum[2][:].rearrange("p (h t) -> p h t", h=4)
                    kb.op("act", lambda h: h.activation(out=eb_t, in_=p2v, func=AF.Exp), r=[psb[2]], w=[eb_b])
                    kb.op("act", lambda h: h.activation(out=enb_t, in_=p2v, func=AF.Exp, scale=-1.0), r=[psb[2]],
                          w=[enb_b])
                    kb.op("act", lambda h: h.activation(out=es_t, in_=psum[3][:], func=AF.Exp), r=[psb[3]], w=[es_b])
                    kh_t, kh_b = R_kh.nxt()
                    kb.op("dve", lambda h: h.tensor_tensor(out=kh_t, in0=k_t, in1=es_t, op=ALU.mult), r=[k_b, es_b],
                          w=[kh_b])
                    p4b = psum[4][:].bitcast(BF16)
                    kb.tr([(p4b[:, hh * 128:(hh + 1) * 128], k_t[:, hh * 128:(hh + 1) * 128]) for hh in range(4)],
                          identB[:], r=[k_b, cbuf], w=[psb[4]])
                    kt_t, kt_b = R_kt.nxt()
                    kb.op("dve", lambda h: h.tensor_tensor(out=kt_t, in0=p4b[:, 0:512].rearrange("p (h t) -> p h t", h=4),
                                                           in1=enb_t, op=ALU.mult), r=[psb[4], enb_b], w=[kt_b])
                    qt_t, qt_b = R_qt.nxt()
                    kb.op("dve", lambda h: h.tensor_tensor(out=qt_t, in0=qT[:, :, tok], in1=eb_t, op=ALU.mult),
                          r=[qTb[j], eb_b], w=[qt_b])
                    for hh in range(4):
                        kb.mm([(psum[5][:, hh * 128:(hh + 1) * 128], kt_t[:, hh, :], qt_t[:, hh, :])],
                              r=[kt_b, qt_b], w=[psb[5]])
                    A_t, A_b = R_A.nxt()
                    kb.op("dve", lambda h: h.tensor_tensor(
                        out=A_t, in0=psum[5][:].rearrange("p (h t) -> p h t", h=4),
                        in1=MSK[:].unsqueeze(1).to_broadcast([128, 4, 128]), op=ALU.mult), r=[psb[5], cbuf], w=[A_b])
                    corder = [0, 1, 2, 3] if dr_ == 0 else [3, 2, 1, 0]
                    p_ = Sp[0]
                    kb.op("act", lambda h: h.copy(out=Sbf[:, 0, :, :], in_=S32[p_][:]), r=[S32b[p_]], w=[Sbfb])
                    for ci, c in enumerate(corder):
                        pu = psum[6 + (ci % 2)]
                        pub = psb[6 + (ci % 2)]
                        for hh in range(4):
                            kw = {"tile_position": (96, 0)} if c == 3 else {}
                            kb.mm([(pu[:, hh * 128:(hh + 1) * 128], kh_t[c * 32:(c + 1) * 32, hh * 128:(hh + 1) * 128],
                                    v_t[c * 32:(c + 1) * 32, hh * 128:(hh + 1) * 128], kw)], r=[kh_b, v_b], w=[pub])
                        tcol = c * 32 + (31 if dr_ == 0 else 0)
                        src, dst = Sp[0], 1 - Sp[0]
                        for hh in range(4):
                            kb.op("dve", lambda h, hh=hh, pu=pu, tcol=tcol, src=src, dst=dst: h.scalar_tensor_tensor(
                                out=S32[dst][:, hh, :], in0=S32[src][:, hh, :], scalar=eb_t[:, hh, tcol:tcol + 1],
                                in1=pu[:, hh * 128:(hh + 1) * 128], op0=ALU.mult, op1=ALU.add),
                                r=[S32b[src], eb_b, pub], w=[S32b[dst]], n=128)
                        Sp[0] = dst
                        if ci < 3:
                            kb.op("act", lambda h, ci=ci, dst=dst: h.copy(out=Sbf[:, ci + 1, :, :], in_=S32[dst][:]),
                                  r=[S32b[dst]], w=[Sbfb])
                    for hh in range(4):
                        steps = [(psum[1][:, hh * 128:(hh + 1) * 128], v_t[:, hh * 128:(hh + 1) * 128], A_t[:, hh, :])]
                        for ci, c in enumerate(corder):
                            steps.append((psum[1][:, hh * 128 + c * 32:hh * 128 + (c + 1) * 32], Sbf[:, ci, hh, :],
                                          qt_t[:, hh, c * 32:(c + 1) * 32]))
                        kb.mm(steps, r=[v_b, A_b, Sbfb, qt_b], w=[psb[1]])
                    return psum[1][:].rearrange("p (h t) -> p h t", h=4), psb[1]

                def state_init(dr_, s):
                    p_ = Sp[0]
                    if job["cache"]:
                        kb.dma("sp", S32[p_][:], dr["st_in"][l, dr_].rearrange("h d v -> d h v"), w=[S32b[p_]])
                    else:
                        kb.op("dve", lambda h: h.memset(S32[p_][:], 0.0), w=[S32b[p_]])

                def state_out(dr_, s):
                    p_ = Sp[0]
                    if not job["cache"]:
                        kb.dma("sp", dr["st_out"][s, l, dr_].rearrange("h d v -> d h v"), S32[p_][:], r=[S32b[p_]])

                for j in range(nb):
                    blk = slice(j * 512, (j + 1) * 512)
                    prep_xn(j, 0, 1, xn, xnb, R_xn)
                    for half in range(2):
                        wv, wb = wload(win[:, :, OFF_Q + half * 256:OFF_Q + (half + 1) * 256], 8, 256)
                        for cc in range(2):
                            hh = half * 2 + cc
                            pq = psum[6 + cc]
                            kb.mm([(pq[:], wv[:, k, cc * 128:(cc + 1) * 128], xn[:, k, blk]) for k in range(8)],
                                  r=[wb, xnb[j]], w=[psb[6 + cc]])
                            t_t, t_b = Rot_tmp.nxt()
                            kb.op("act", lambda h, pq=pq, t_t=t_t: h.activation(out=t_t, in_=pq[:], func=AF.Silu),
                                  r=[psb[6 + cc]], w=[t_b])
                            kb.op("dve", lambda h, hh=hh, t_t=t_t: h.tensor_scalar(
                                out=qT[:, hh, blk], in0=t_t, scalar1=128.0 ** -0.5, scalar2=None, op0=ALU.mult),
                                r=[t_b], w=[qTb[j]])
                    wf, wfb, wi, wib = [], [], [], []
                    for half in range(2):
                        a, b_ = wload(win[:, :, OFF_FF + half * 256:OFF_FF + (half + 1) * 256], 8, 256)
                        wf.append(a)
                        wfb.append(b_)
                    for half in range(2):
                        a, b_ = wload(win[:, :, OFF_I + half * 256:OFF_I + (half + 1) * 256], 8, 256)
                        wi.append(a)
                        wib.append(b_)
                    for sub in range(4):
                        tb = j * 4 + sub
                        if (tb * 128) % L == 0:
                            state_init(0, (tb * 128) // L)
                        po, pob = hgrn_step(0, tb, wf, wfb, wi, wib)
                        of_t, of_b = R_of.nxt()
                        kb.op("act", lambda h, po=po, of_t=of_t: h.copy(out=of_t, in_=po), r=[pob], w=[of_b])
                        kb.dma("sp", ofw_d[:, :, tb * 128:(tb + 1) * 128], of_t, r=[of_b], w=[ofwb[tb]])
                        if ((tb + 1) * 128) % L == 0:
                            state_out(0, (tb * 128) // L)
                chk("fwd")
                for j in reversed(range(nb)):
                    blk = slice(j * 512, (j + 1) * 512)
                    prep_xn(j, 0, 1, xn, xnb, R_xn)
                    wf, wfb, wi, wib = [], [], [], []
                    for half in range(2):
                        a, b_ = wload(win[:, :, OFF_FB + half * 256:OFF_FB + (half + 1) * 256], 8, 256)
                        wf.append(a)
                        wfb.append(b_)
                    for half in range(2):
                        a, b_ = wload(win[:, :, OFF_I + half * 256:OFF_I + (half + 1) * 256], 8, 256)
                        wi.append(a)
                        wib.append(b_)
                    os_t, os_b = R_osum.nxt()
                    for sub in reversed(range(4)):
                        tb = j * 4 + sub
                        if ((tb + 1) * 128) % L == 0:
                            state_init(1, (tb * 128) // L)
                        of_t, of_b = R_of.nxt()
                        kb.dma("sp", of_t, ofw_d[:, :, tb * 128:(tb + 1) * 128], r=[ofwb[tb]], w=[of_b])
                        po, pob = hgrn_step(1, tb, wf, wfb, wi, wib)
                        kb.op("dve", lambda h, po=po, sub=sub, of_t=of_t: h.tensor_tensor(
                            out=os_t[:, :, sub * 128:(sub + 1) * 128], in0=po, in1=of_t,
                            op=ALU.add), r=[pob, of_b], w=[os_b])
                        if (tb * 128) % L == 0:
                            state_out(1, (tb * 128) // L)
                    sg_t, sg_b = R_sg.nxt()
                    for half in range(2):
                        wv, wb = wload(win[:, :, OFF_G + half * 256:OFF_G + (half + 1) * 256], 8, 256)
                        for cc in range(2):
                            hh = half * 2 + cc
                            pq = psum[6 + cc]
                            kb.mm([(pq[:], wv[:, k, cc * 128:(cc + 1) * 128], xn[:, k, blk]) for k in range(8)],
                                  r=[wb, xnb[j]], w=[psb[6 + cc]])
                            kb.op("act", lambda h, pq=pq, hh=hh: h.activation(out=sg_t[:, hh, :], in_=pq[:], func=AF.Silu),
                                  r=[psb[6 + cc]], w=[sg_b])
                    oA_t, oA_b = R_oA.nxt()
                    for hh in range(4):
                        q_t, q_b = Rot_sq.nxt()
                        kb.op("act", lambda h, hh=hh, q_t=q_t: h.activation(out=q_t, in_=os_t[:, hh, :], func=AF.Square),
                              r=[os_b], w=[q_b])
                        pn = psum[4 + (hh % 2)]
                        pnb = psb[4 + (hh % 2)]
                        kb.mm([(pn[:], onesB[:], q_t)], r=[q_b, cbuf], w=[pnb])
                        r_t, r_b = Rot_rstd.nxt()
                        kb.op("act", lambda h, pn=pn, r_t=r_t: h.activation(out=r_t, in_=pn[:], func=AF.Ln, bias=epsT[:],
                                                                            scale=1.0 / 128), r=[pnb, cbuf], w=[r_b])
                        kb.op("act", lambda h, r_t=r_t: h.activation(out=r_t, in_=r_t, func=AF.Exp, scale=-0.5), r=[r_b], w=[r_b])
                        kb.op("dve", lambda h, hh=hh, r_t=r_t: h.scalar_tensor_tensor(
                            out=r_t, in0=os_t[:, hh, :], scalar=fv[:, l, 64:65], in1=r_t, op0=ALU.mult, op1=ALU.mult),
                            r=[os_b, fvb, r_b], w=[r_b])
                        kb.op("dve", lambda h, hh=hh, r_t=r_t, oA_t=oA_t: h.tensor_tensor(out=oA_t[:, hh, :], in0=r_t,
                                                                                          in1=sg_t[:, hh, :], op=ALU.mult),
                              r=[r_b, sg_b], w=[oA_b])
                    kb.dma("sp", oA_d[:, :, blk], oA_t, r=[oA_b], w=[oAb[j]])

                chk("bwd")
                kb.barrier()
                cv.off = mark_x
                Rot_sq = Rot("sq", [128, 512], BF16, 2)
                Rot_rstd = Rot("rstd", [128, 512], F32, 2)
                Rot_tmp = Rot("tmp", [128, 512], F32, 2)
                cqn = cv.get([128, 3, T], BF16)
                cqnb = [Buf(f"cqn{j}") for j in range(nb)]
                ckvT = cv.get([128, 2, nseq * Tk], BF16)
                ckvb = Buf("ckvT")
                krT = cv.get([96, nseq * Tk], BF16)
                krb = Buf("krT")
                mark_p3 = cv.off
                R_c = Rot("cqf", [128, 3, 512], F32, 1)
                R_rp = Rot("rp", [96, 512], F32, 2)
                R_xb = Rot("xb", [96, 512], BF16, 2)
                R_co = Rot("co", [128, 288], F32, 2)
                R_ss = Rot("ss", [128, 2], F32, 2)
                R_cs = Rot("cs", [96, 2, 512], F32, 1)
                for j in range(nb):
                    blk = slice(j * 512, (j + 1) * 512)
                    prep_xn(j, 0, 1, xn, xnb, R_xn)
                    s0 = (j * 512) // L
                    nsb = max(1, 512 // L)
                    c_t, c_b = R_c.nxt()
                    wva0, wba0 = wload(win[:, :, OFF_CQ:OFF_CQ + 256], 8, 256)
                    chk("p3w")
                    wva1, wba1 = wload(win[:, :, OFF_CQ + 256:OFF_CQ + 384], 8, 128)
                    chk("p3x")
                    for cc in range(3):
                        if cc == 1:
                            chk("p3y")
                        pq = psum[cc % 2]
                        pqb = psb[cc % 2]
                        wva, wba, co_ = (wva0, wba0, cc) if cc < 2 else (wva1, wba1, 0)
                        kb.mm([(pq[:], wva[:, k, co_ * 128:(co_ + 1) * 128], xn[:, k, blk]) for k in range(8)],
                              r=[wba, xnb[j]], w=[pqb])
                        q_t, q_b = Rot_sq.nxt()
                        kb.op("act", lambda h, pq=pq, q_t=q_t: h.activation(out=q_t, in_=pq[:], func=AF.Square), r=[pqb],
                              w=[q_b])
                        kb.op("dve", lambda h, pq=pq, cc=cc: h.tensor_copy(out=c_t[:, cc, :], in_=pq[:]), r=[pqb], w=[c_b])
                        kb.mm([(psum[2][:], onesB[:], q_t)], r=[q_b, cbuf], w=[psb[2]], start=(cc == 0), stop=(cc == 2))
                    r_t, r_b = Rot_rstd.nxt()
                    kb.op("act", lambda h: h.activation(out=r_t, in_=psum[2][:], func=AF.Ln, bias=epsT[:],
                                                        scale=1.0 / C_Q), r=[psb[2], cbuf], w=[r_b])
                    kb.op("act", lambda h: h.activation(out=r_t, in_=r_t, func=AF.Exp, scale=-0.5), r=[r_b], w=[r_b])
                    for cc in range(3):
                        kb.op("dve", lambda h, cc=cc: h.scalar_tensor_tensor(
                            out=cqn[:, cc, blk], in0=c_t[:, cc, :], scalar=fv[:, l, 77 + cc:78 + cc], in1=r_t,
                            op0=ALU.mult, op1=ALU.mult), r=[c_b, fvb, r_b], w=[cqnb[j]])
                    chk("p3a")
                    c_t, c_b = R_c.nxt()
                    wvk, wbk = wload(win[:, :, OFF_CKV:OFF_CKV + 256], 8, 256)
                    for cc in range(2):
                        pq = psum[cc % 2]
                        pqb = psb[cc % 2]
                        kb.mm([(pq[:], wvk[:, k, cc * 128:(cc + 1) * 128], xn[:, k, blk]) for k in range(8)],
                              r=[wbk, xnb[j]], w=[pqb])
                        q_t, q_b = Rot_sq.nxt()
                        kb.op("act", lambda h, pq=pq, q_t=q_t: h.activation(out=q_t, in_=pq[:], func=AF.Square), r=[pqb],
                              w=[q_b])
                        kb.op("dve", lambda h, pq=pq, cc=cc: h.tensor_copy(out=c_t[:, cc, :], in_=pq[:]), r=[pqb], w=[c_b])
                        kb.mm([(psum[2][:], onesB[:], q_t)], r=[q_b, cbuf], w=[psb[2]], start=(cc == 0), stop=(cc == 1))
                    r_t, r_b = Rot_rstd.nxt()
                    kb.op("act", lambda h: h.activation(out=r_t, in_=psum[2][:], func=AF.Ln, bias=epsT[:],
                                                        scale=1.0 / C_KV), r=[psb[2], cbuf], w=[r_b])
                    kb.op("act", lambda h: h.activation(out=r_t, in_=r_t, func=AF.Exp, scale=-0.5), r=[r_b], w=[r_b])
                    for cc in range(2):
                        for sb_ in range(nsb):
                            ln = 512 // nsb
                            ko = (s0 + sb_) * Tk + ((j * 512 + sb_ * ln) % L)
                            kb.op("dve", lambda h, cc=cc, sb_=sb_, ln=ln, ko=ko: h.scalar_tensor_tensor(
                                out=ckvT[:, cc, ko:ko + ln], in0=c_t[:, cc, sb_ * ln:(sb_ + 1) * ln],
                                scalar=fv[:, l, 80 + cc:81 + cc], in1=r_t[:, sb_ * ln:(sb_ + 1) * ln],
                                op0=ALU.mult, op1=ALU.mult), r=[c_b, fvb, r_b], w=[ckvb])
                    chk("p3b")
                    wvr, wbr = wload(win[:, :, OFF_CKV + 192:OFF_CKV + 288], 8, 96)
                    kb.mm([(psum[3][0:96, :], wvr[:, k, :], xn[:, k, blk]) for k in range(8)], r=[wbr, xnb[j]], w=[psb[3]])
                    if job["rope"]:
                        x_t, x_b = R_xb.nxt()
                        kb.op("act", lambda h: h.copy(out=x_t[64:96, :], in_=psum[3][64:96, :]), r=[psb[3]], w=[x_b])
                        kb.op("dve", lambda h: h.memset(x_t[0:64, :], 0.0), w=[x_b])
                        kb.mm([(psum[4][0:96, :], P96[:], x_t[:])], r=[x_b, cbuf], w=[psb[4]])
                        cs_t, cs_b = R_cs.nxt()
                        kb.dma("sp", cs_t[64:96, 0, :], dr["ropeC"][64:96, blk], w=[cs_b])
                        kb.dma("sp", cs_t[64:96, 1, :], dr["ropeS"][64:96, blk], w=[cs_b])
                        p_t, p_b = R_rp.nxt()
                        kb.op("dve", lambda h: h.tensor_tensor(out=p_t[64:96, :], in0=psum[4][64:96, :],
                                                               in1=cs_t[64:96, 1, :], op=ALU.mult), r=[psb[4], cs_b],
                              w=[p_b])
                        p2_t, p2_b = R_rp.nxt()
                        kb.op("dve", lambda h: h.tensor_tensor(out=p2_t[64:96, :], in0=psum[3][64:96, :],
                                                               in1=cs_t[64:96, 0, :], op=ALU.mult), r=[psb[3], cs_b],
                              w=[p2_b])
                        kb.op("dve", lambda h: h.tensor_tensor(out=krT[64:96, j * 512:(j + 1) * 512], in0=p_t[64:96, :],
                                                               in1=p2_t[64:96, :], op=ALU.add), r=[p_b, p2_b], w=[krb])
                    else:
                        for sb_ in range(nsb):
                            ln = 512 // nsb
                            ko = (s0 + sb_) * Tk + ((j * 512 + sb_ * ln) % L)
                            kb.op("act", lambda h, sb_=sb_, ln=ln, ko=ko: h.copy(
                                out=krT[64:96, ko:ko + ln], in_=psum[3][64:96, sb_ * ln:(sb_ + 1) * ln]),
                                r=[psb[3]], w=[krb])
                    chk("p3c")
                    if not job["cache"]:
                        for sub in range(4):
                            tb = j * 4 + sub
                            tok = slice(tb * 128, (tb + 1) * 128)
                            pc = psum[5 + (sub % 2)]
                            pcb = psb[5 + (sub % 2)]
                            kb.mm([(pc[:, 0:256], xn[:, k, tok], wvk[:, k, :]) for k in range(8)], r=[xnb[j], wbk],
                                  w=[pcb])
                            kb.mm([(pc[:, 256:288], xn[:, k, tok], wvr[:, k, 64:96]) for k in range(8)], r=[xnb[j], wbr],
                                  w=[pcb])
                            co_t, co_b = R_co.nxt()
                            ss_t, ss_b = R_ss.nxt()
                            kb.op("act", lambda h, pc=pc, co_t=co_t, ss_t=ss_t: h.activation(
                                out=co_t[:, 0:256], in_=pc[:, 0:256], func=AF.Square, accum_out=ss_t[:, 0:1]),
                                r=[pcb], w=[co_b, ss_b])
                            kb.op("act", lambda h, ss_t=ss_t: h.activation(out=ss_t[:, 1:2], in_=ss_t[:, 0:1], func=AF.Ln,
                                                                           bias=epsT[:], scale=1.0 / C_KV),
                                  r=[ss_b, cbuf], w=[ss_b])
                            kb.op("act", lambda h, ss_t=ss_t: h.activation(out=ss_t[:, 1:2], in_=ss_t[:, 1:2], func=AF.Exp,
                                                                           scale=-0.5), r=[ss_b], w=[ss_b], n=1)
                            kb.op("dve", lambda h, pc=pc, co_t=co_t, ss_t=ss_t: h.scalar_tensor_tensor(
                                out=co_t[:, 0:256], in0=pc[:, 0:256], scalar=ss_t[:, 1:2], in1=kvnB[:, l, :],
                                op0=ALU.mult, op1=ALU.mult), r=[pcb, ss_b, kvnb, co_b], w=[co_b])
                            kb.op("act", lambda h, pc=pc, co_t=co_t: h.copy(out=co_t[:, 256:288], in_=pc[:, 256:288]),
                                  r=[pcb], w=[co_b])
                            s_i = (tb * 128) // L
                            to = (tb * 128) % L
                            kb.dma("sp", dr["cache_out"][s_i, l, to:to + 128, :], co_t[:], r=[co_b])
                if job["cache"]:
                    cst = cv.get([128, 2, 288], F32)
                    cstb = Buf("cst")
                    kb.dma("sp", cst[:], dr["cache_in"][l].rearrange("(tb p) f -> p tb f", p=128), w=[cstb])
                    for tb in range(2):
                        kb.tr([(psum[0][:, cc * 128:(cc + 1) * 128], cst[:, tb, cc * 128:(cc + 1) * 128]) for cc in range(2)],
                              identF[:], r=[cstb, cbuf], w=[psb[0]])
                        kb.op("dve", lambda h, tb=tb: h.tensor_copy(
                            out=ckvT[:, :, L + tb * 128:L + (tb + 1) * 128],
                            in_=psum[0][:, 0:256].rearrange("p (c t) -> p c t", c=2)), r=[psb[0]], w=[ckvb])
                        kb.tr([(psum[1][0:96, 0:128], cst[:, tb, 192:288])], identF[:], r=[cstb, cbuf], w=[psb[1]])
                        kb.op("dve", lambda h, tb=tb: h.tensor_copy(out=krT[64:96, L + tb * 128:L + (tb + 1) * 128],
                                                                    in_=psum[1][64:96, 0:128]), r=[psb[1]], w=[krb])

                chk("p3")
                kb.barrier(engines=("pe", "act", "dve", "sp", "pool"))
                mark_a = cv.off
                cv.off = 0
                oC = cv.get([128, 4, T], BF16)
                assert cv.off <= mark_x
                oCb = [Buf(f"oC{j}") for j in range(nb)]
                cv.off = mark_p3
                Wkv = cv.get([128, 2, 1024], BF16)
                Wq = cv.get([128, 3, 768], BF16)
                wab = Buf("Wattn")
                kb.dma("pool", Wkv, w3("w_kv_up", l), w=[wab])
                kb.dma("pool", Wq, w3("w_q_up", l), w=[wab])
                vaug = [cv.get([128, nkb, 128], BF16) for _ in range(2)]
                vaugb = [Buf("vaug0"), Buf("vaug1")]
                rden = [cv.get([128, 512], F32) for _ in range(2)]
                rdenb = [Buf("rden0"), Buf("rden1")]
                kb.op("dve", lambda h: h.memset(vaug[0][:, :, 64:128], 1.0), w=[vaugb[0]])
                kb.op("dve", lambda h: h.memset(vaug[1][:, :, 0:64], 1.0), w=[vaugb[1]])
                kb.op("dve", lambda h: h.memset(rden[0][:], 0.0), w=[rdenb[0]])
                kb.op("dve", lambda h: h.memset(rden[1][:], 0.0), w=[rdenb[1]])
                R_kT = Rot("kT", [96, Tk], BF16, 2)
                R_q = Rot("qh", [96, 512], BF16, 2)
                R_pT = Rot("pT", [128, 512], BF16, 3)
                R_rb = Rot("rb", [128, 512], F32, 1)
                R_rp = Rot("rp2", [96, 512], F32, 2)
                R_cs = Rot("cs2", [96, 2, 512], F32, 1)
                scale_qk = 96.0 ** -0.5
                tasks = [(s, hh, qb) for s in range(nseq) for hh in range(8) for qb in range(L // QN)]
                headc, qc = {}, {}

                def prep_head(s, hh):
                    par = hh % 2
                    kbase = s * Tk
                    kT_t, kT_b = R_kT.nxt()
                    kb.op("dve", lambda h: h.tensor_copy(out=kT_t[64:96, :], in_=krT[64:96, kbase:kbase + Tk]),
                          r=[krb], w=[kT_b])
                    for k5 in range(0, Tk, 512):
                        n5 = min(512, Tk - k5)
                        kb.mm([(psum[7][0:64, 0:n5], Wkv[:, c, hh * 128:hh * 128 + 64],
                                ckvT[:, c, kbase + k5:kbase + k5 + n5]) for c in range(2)], r=[ckvb, wab], w=[psb[7]])
                        kb.op("act", lambda h, k5=k5, n5=n5: h.copy(out=kT_t[0:64, k5:k5 + n5], in_=psum[7][0:64, 0:n5]),
                              r=[psb[7]], w=[kT_b])
                    voff = 0 if par == 0 else 64
                    for kb8 in range(0, nkb, 8):
                        n8 = min(8, nkb - kb8)
                        for q8 in range(n8):
                            kblk = kb8 + q8
                            kb.mm([(psum[7][:, q8 * 64:(q8 + 1) * 64],
                                    ckvT[:, c, kbase + kblk * 128:kbase + (kblk + 1) * 128],
                                    Wkv[:, c, hh * 128 + 64:hh * 128 + 128]) for c in range(2)], r=[ckvb, wab],
                                  w=[psb[7]])
                        kb.op("dve", lambda h, kb8=kb8, n8=n8: h.tensor_copy(
                            out=vaug[par][:, kb8:kb8 + n8, voff:voff + 64],
                            in_=psum[7][:, 0:n8 * 64].rearrange("p (a b) -> p a b", a=n8)), r=[psb[7]], w=[vaugb[par]])
                    return kT_t, kT_b

                def prep_q(s, hh, qb):
                    q0 = s * L + qb * QN
                    jq = q0 // 512
                    q_t, q_b = R_q.nxt()
                    kb.mm([(psum[7][0:96, 0:QN], Wq[:, c, hh * 96:(hh + 1) * 96], cqn[:, c, q0:q0 + QN])
                           for c in range(3)], r=[wab, cqnb[jq]], w=[psb[7]])
                    kb.op("act", lambda h: h.activation(out=q_t[:, 0:QN], in_=psum[7][0:96, 0:QN], func=AF.Identity,
                                                        scale=scale_qk), r=[psb[7]], w=[q_b])
                    if job["rope"]:
                        kb.mm([(psum[3][0:96, 0:QN], P96[:], q_t[:, 0:QN])], r=[q_b, cbuf], w=[psb[3]])
                        cs_t, cs_b = R_cs.nxt()
                        kb.dma("sp", cs_t[64:96, 0, 0:QN], dr["ropeC"][64:96, qb * QN:(qb + 1) * QN], w=[cs_b])
                        kb.dma("sp", cs_t[64:96, 1, 0:QN], dr["ropeS"][64:96, qb * QN:(qb + 1) * QN], w=[cs_b])
                        p_t, p_b = R_rp.nxt()
                        kb.op("dve", lambda h: h.tensor_tensor(out=p_t[64:96, 0:QN], in0=psum[3][64:96, 0:QN],
                                                               in1=cs_t[64:96, 1, 0:QN], op=ALU.mult),
                              r=[psb[3], cs_b], w=[p_b])
                        p2_t, p2_b = R_rp.nxt()
                        kb.op("dve", lambda h: h.scalar_tensor_tensor(
                            out=p2_t[64:96, 0:QN], in0=psum[7][64:96, 0:QN], scalar=scale_qk, in1=cs_t[64:96, 0, 0:QN],
                            op0=ALU.mult, op1=ALU.mult), r=[psb[7], cs_b], w=[p2_b])
                        kb.op("dve", lambda h: h.tensor_tensor(out=q_t[64:96, 0:QN], in0=p_t[64:96, 0:QN],
                                                               in1=p2_t[64:96, 0:QN], op=ALU.add), r=[p_b, p2_b], w=[q_b])
                    return q_t, q_b

                def ensure(i):
                    s, hh, qb = tasks[i]
                    if (s, hh) not in headc:
                        headc[(s, hh)] = prep_head(s, hh)
                    if i not in qc:
                        qc[i] = prep_q(s, hh, qb)

                def attn_main(i):
                    s, hh, qb = tasks[i]
                    par = hh % 2
                    hp = hh // 2
                    q0 = s * L + qb * QN
                    jq = q0 // 512
                    kT_t, kT_b = headc[(s, hh)]
                    q_t, q_b = qc.pop(i)
                    acc, accb = psum[i % 2], psb[i % 2]

                    def qk(kblk):
                        pi = 4 + (kblk % 3)
                        kb.mm([(psum[pi][:, 0:QN], kT_t[:, kblk * 128:(kblk + 1) * 128], q_t[:, 0:QN])], r=[kT_b, q_b],
                              w=[psb[pi]])

                    qk(0)
                    for kblk in range(nkb):
                        if kblk + 1 < nkb:
                            qk(kblk + 1)
                        pi = 4 + (kblk % 3)
                        pT_t, pT_b = R_pT.nxt()
                        kb.op("act", lambda h, pi=pi, pT_t=pT_t: h.activation(out=pT_t[:, 0:QN], in_=psum[pi][:, 0:QN],
                                                                             func=AF.Exp), r=[psb[pi]], w=[pT_b])
                        kb.mm([(acc[:, 0:QN], vaug[par][:, kblk, :], pT_t[:, 0:QN])], r=[vaugb[par], pT_b], w=[accb],
                              start=(kblk == 0), stop=(kblk == nkb - 1))
                    nrows = slice(0, 64) if par == 0 else slice(64, 128)
                    drows = slice(64, 128) if par == 0 else slice(0, 64)
                    kb.op("act", lambda h: h.activation(out=rden[par][drows, 0:QN], in_=acc[drows, 0:QN], func=AF.Ln),
                          r=[accb], w=[rdenb[par]])
                    kb.op("act", lambda h: h.activation(out=rden[par][drows, 0:QN], in_=rden[par][drows, 0:QN],
                                                        func=AF.Exp, scale=-1.0), r=[rdenb[par]], w=[rdenb[par]])
                    kb.mm([(psum[2][:, 0:QN], swapM[:], rden[par][:, 0:QN])], r=[cbuf, rdenb[par]], w=[psb[2]])
                    rb_t, rb_b = R_rb.nxt()
                    kb.op("act", lambda h: h.copy(out=rb_t[nrows, 0:QN], in_=psum[2][nrows, 0:QN]), r=[psb[2]], w=[rb_b])
                    kb.op("dve", lambda h: h.tensor_tensor(out=oC[nrows, hp, q0:q0 + QN], in0=acc[nrows, 0:QN],
                                                           in1=rb_t[nrows, 0:QN], op=ALU.mult), r=[accb, rb_b],
                          w=[oCb[jq]])

                ensure(0)
                for i in range(len(tasks)):
                    if i + 1 < len(tasks):
                        ensure(i + 1)
                    attn_main(i)

                chk("p4")
                kb.barrier()
                cv.off = 0
                oC2 = cv.get([128, 4, T], BF16)
                xn = cv.get([128, 8, T], BF16)
                xnb = [Buf(f"xnm{j}") for j in range(nb)]
                Rot_sq = Rot("sq", [128, 512], BF16, 2)
                Rot_rstd = Rot("rstd", [128, 512], F32, 2)
                Rot_tmp = Rot("tmp", [128, 512], F32, 2)
                xh = cv.get([128, 8, 2], BF16)
                xhb = Buf("xh")
                R_e = Rot("e", [128, 514], F32, 2)
                R_acc = Rot("acc", [128, 512], F32, 2)
                oBt = cv.get([128, 4, 512], BF16)
                oBb = Buf("oB")
                R_a2 = Rot("a2", [128, 2, 512], F32, 1)
                R_oAl = Rot("oAl", [128, 4, 512], BF16, 1)
                hB = cv.get([128, 8, 512], BF16)
                hBb = Buf("hB")
                R_g = Rot("g", [128, 512], F32, 2)

                def halo_cols(xsrc, xsb, j):
                    t0 = j * 512
                    if t0 % L == 0:
                        kb.op("dve", lambda h: h.memset(xh[:, :, 0:1], 0.0), w=[xhb])
                    else:
                        kb.op("dve", lambda h: h.tensor_copy(out=xh[:, :, 0:1], in_=xsrc[:, :, t0 - 1:t0]),
                              r=[xsb[j - 1]], w=[xhb])
                    if (t0 + 512) % L == 0:
                        kb.op("dve", lambda h: h.memset(xh[:, :, 1:2], 0.0), w=[xhb])
                    else:
                        kb.op("dve", lambda h: h.tensor_copy(out=xh[:, :, 1:2], in_=xsrc[:, :, t0 + 512:t0 + 513]),
                              r=[xsb[j + 1]], w=[xhb])

                def conv3(e_t, e_b, acc_t, acc_b, w0, w1, w2, wbuf):
                    kb.op("act", lambda h: h.activation(out=acc_t, in_=e_t[:, 1:513], func=AF.Identity, scale=w1),
                          r=[e_b, wbuf], w=[acc_b])
                    seg = min(L, 512)
                    for a in range(0, 512, seg):
                        lo = a if a == 0 else a + 1
                        kb.op("dve", lambda h, lo=lo, a=a: h.scalar_tensor_tensor(
                            out=acc_t[:, lo:a + seg], in0=e_t[:, lo:a + seg], scalar=w0, in1=acc_t[:, lo:a + seg],
                            op0=ALU.mult, op1=ALU.add), r=[e_b, wbuf, acc_b], w=[acc_b])
                        hi = a + seg if a + seg == 512 else a + seg - 1
                        kb.op("dve", lambda h, hi=hi, a=a: h.scalar_tensor_tensor(
                            out=acc_t[:, a:hi], in0=e_t[:, a + 2:hi + 2], scalar=w2, in1=acc_t[:, a:hi],
                            op0=ALU.mult, op1=ALU.add), r=[e_b, wbuf, acc_b], w=[acc_b])

                norm_all(0, 1, xn, xnb)
                for j in range(nb):
                    blk = slice(j * 512, (j + 1) * 512)
                    halo_cols(xn, xnb, j)
                    for half in range(2):
                        wvc, wbc = wload(win[:, :, OFF_BC + half * 256:OFF_BC + (half + 1) * 256], 8, 256)
                        wvh, wbh = wload(win[:, :, OFF_BH + half * 256:OFF_BH + (half + 1) * 256], 8, 256)
                        wvb, wbb = wload(win[:, :, OFF_BB + half * 256:OFF_BB + (half + 1) * 256], 8, 256)
                        for cc in range(2):
                            g = half * 2 + cc
                            cs_ = slice(cc * 128, (cc + 1) * 128)
                            kb.mm([(psum[0][:], wvc[:, k, cs_], xn[:, k, blk]) for k in range(8)], r=[wbc, xnb[j]],
                                  w=[psb[0]])
                            kb.mm([(psum[1][:], wvh[:, k, cs_], xn[:, k, blk]) for k in range(8)], r=[wbh, xnb[j]],
                                  w=[psb[1]])
                            kb.mm([(psum[2][:, 0:2], wvc[:, k, cs_], xh[:, k, :]) for k in range(8)], r=[wbc, xhb],
                                  w=[psb[2]])
                            kb.mm([(psum[2][:, 2:4], wvh[:, k, cs_], xh[:, k, :]) for k in range(8)], r=[wbh, xhb],
                                  w=[psb[2]])
                            kb.mm([(psum[3][:], wvb[:, k, cs_], xn[:, k, blk]) for k in range(8)], r=[wbb, xnb[j]],
                                  w=[psb[3]])
                            t_t, t_b = Rot_tmp.nxt()
                            kb.op("act", lambda h, t_t=t_t: h.copy(out=t_t, in_=psum[0][:]), r=[psb[0]], w=[t_b])
                            e_t, e_b = R_e.nxt()
                            kb.op("dve", lambda h, t_t=t_t, e_t=e_t: h.tensor_tensor(out=e_t[:, 1:513], in0=psum[1][:],
                                                                                    in1=t_t, op=ALU.mult),
                                  r=[psb[1], t_b], w=[e_b])
                            t2_t, t2_b = Rot_tmp.nxt()
                            kb.op("act", lambda h, t2_t=t2_t: h.copy(out=t2_t[:, 0:2], in_=psum[2][:, 0:2]), r=[psb[2]],
                                  w=[t2_b])
                            kb.op("dve", lambda h, t2_t=t2_t, e_t=e_t: h.tensor_tensor(
                                out=e_t[:, 0:514:513], in0=psum[2][:, 2:4], in1=t2_t[:, 0:2], op=ALU.mult),
                                r=[psb[2], t2_b], w=[e_b])
                            acc_t, acc_b = R_acc.nxt()
                            conv3(e_t, e_b, acc_t, acc_b, fv[:, l, 65 + g:66 + g], fv[:, l, 69 + g:70 + g],
                                  fv[:, l, 73 + g:74 + g], fvb)
                            kb.op("dve", lambda h, g=g, acc_t=acc_t: h.tensor_tensor(out=oBt[:, g, :], in0=psum[3][:],
                                                                                    in1=acc_t, op=ALU.mult),
                                  r=[psb[3], acc_b], w=[oBb])
                    oAl_t, oAl_b = R_oAl.nxt()
                    kb.dma("sp", oAl_t, oA_d[:, :, blk], r=[oAb[j]], w=[oAl_b])
                    srcs = (("w_o_hgrn", oAl_t, oAl_b), ("w_o_conv", oBt, oBb), ("w_o_mla", oC2[:, :, blk], oCb[j]))
                    for o2 in range(0, 8, 2):
                        a2_t, a2_b = R_a2.nxt()
                        for br, (wname, osrc, osb) in enumerate(srcs):
                            wvo, wbo = wload(w3(wname, l)[:, :, o2 * 128:(o2 + 2) * 128], 4, 256)
                            gc0 = OFF_GATE + br * 1024 + o2 * 128
                            wvg, wbg = wload(win[:, :, gc0:gc0 + 256], 8, 256)
                            for o1 in range(2):
                                oc = o2 + o1
                                py, pyb = psum[4 + o1], psb[4 + o1]
                                pg, pgb = psum[6 + o1], psb[6 + o1]
                                kb.mm([(py[:], wvo[:, k, o1 * 128:(o1 + 1) * 128], osrc[:, k, :]) for k in range(4)],
                                      r=[wbo, osb], w=[pyb])
                                kb.mm([(pg[:], wvg[:, k, o1 * 128:(o1 + 1) * 128], xn[:, k, blk]) for k in range(8)],
                                      r=[wbg, xnb[j]], w=[pgb])
                                g_t, g_b = R_g.nxt()
                                kb.op("act", lambda h, pg=pg, g_t=g_t: h.activation(out=g_t, in_=pg[:], func=AF.Sigmoid),
                                      r=[pgb], w=[g_b])
                                if br == 0:
                                    kb.op("dve", lambda h, o1=o1, py=py, g_t=g_t, a2_t=a2_t: h.tensor_tensor(
                                        out=a2_t[:, o1, :], in0=py[:], in1=g_t, op=ALU.mult), r=[pyb, g_b], w=[a2_b])
                                else:
                                    kb.op("dve", lambda h, py=py, g_t=g_t: h.tensor_tensor(
                                        out=g_t, in0=py[:], in1=g_t, op=ALU.mult), r=[pyb, g_b], w=[g_b])
                                    if br == 1:
                                        kb.op("dve", lambda h, o1=o1, g_t=g_t, a2_t=a2_t: h.tensor_tensor(
                                            out=a2_t[:, o1, :], in0=a2_t[:, o1, :], in1=g_t, op=ALU.add),
                                            r=[a2_b, g_b], w=[a2_b])
                                    else:
                                        kb.op("dve", lambda h, oc=oc, o1=o1, g_t=g_t, a2_t=a2_t: h.tensor_tensor(
                                            out=hB[:, oc, :], in0=a2_t[:, o1, :], in1=g_t, op=ALU.add),
                                            r=[a2_b, g_b], w=[hBb])
                    wo3 = w3("w_out", l)
                    for o2 in range(0, 8, 2):
                        wvo, wbo = wload(wo3[:, :, o2 * 128:(o2 + 2) * 128], 8, 256)
                        for o1 in range(2):
                            oc = o2 + o1
                            py, pyb = psum[oc % 2], psb[oc % 2]
                            kb.mm([(py[:], wvo[:, k, o1 * 128:(o1 + 1) * 128], hB[:, k, :]) for k in range(8)],
                                  r=[wbo, hBb], w=[pyb])
                            kb.op("dve", lambda h, oc=oc, py=py: h.scalar_tensor_tensor(
                                out=xT[:, oc, blk], in0=py[:], scalar=dmod[:, 2, oc:oc + 1], in1=xT[:, oc, blk],
                                op0=ALU.mult, op1=ALU.add), r=[pyb, dmodb, xbuf[j]], w=[xbuf[j]])

                chk("p5")
                phase()
                xn2 = cv.get([128, 8, T], BF16)
                xn2b = [Buf(f"xn2{j}") for j in range(nb)]
                Rot_sq = Rot("sq", [128, 512], BF16, 2)
                Rot_rstd = Rot("rstd", [128, 512], F32, 2)
                Rot_tmp = Rot("tmp", [128, 512], F32, 2)
                xh = cv.get([128, 8, 2], BF16)
                xhb = Buf("xh")
                R_e = Rot("e", [128, 514], F32, 3)
                R_acc = Rot("acc", [128, 512], F32, 3)
                hid = cv.get([128, 22, 512], BF16)
                hidb = Buf("hid")
                norm_all(3, 4, xn2, xn2b)
                wup = dr["w_up"][l].rearrange("(kc p) (two n) -> p kc two n", p=128, two=2)
                wd3 = w3("w_down", l)
                fo = 82
                for j in range(nb):
                    blk = slice(j * 512, (j + 1) * 512)
                    halo_cols(xn2, xn2b, j)
                    for m in range(22):
                        wv, wb = wload(wup[:, :, :, m * 128:(m + 1) * 128], 8, 128, extra=2)
                        accs = []
                        for ab in range(2):
                            pm, pmb = psum[2 * ab], psb[2 * ab]
                            ph, phb = psum[2 * ab + 1], psb[2 * ab + 1]
                            kb.mm([(pm[:], wv[:, k, ab, :], xn2[:, k, blk]) for k in range(8)], r=[wb, xn2b[j]], w=[pmb])
                            kb.mm([(ph[:, 0:2], wv[:, k, ab, :], xh[:, k, :]) for k in range(8)], r=[wb, xhb], w=[phb])
                            e_t, e_b = R_e.nxt()
                            kb.op("act", lambda h, pm=pm, e_t=e_t: h.copy(out=e_t[:, 1:513], in_=pm[:]), r=[pmb], w=[e_b])
                            kb.op("dve", lambda h, ph=ph, e_t=e_t: h.tensor_copy(out=e_t[:, 0:514:513], in_=ph[:, 0:2]),
                                  r=[phb, e_b], w=[e_b])
                            acc_t, acc_b = R_acc.nxt()
                            mm_ = ab * 22 + m
                            conv3(e_t, e_b, acc_t, acc_b, fv[:, l, fo + mm_:fo + mm_ + 1],
                                  fv[:, l, fo + 44 + mm_:fo + 44 + mm_ + 1], fv[:, l, fo + 88 + mm_:fo + 88 + mm_ + 1], fvb)
                            accs.append((acc_t, acc_b))
                        (a_t, a_b), (b_t, b_b) = accs
                        t_t, t_b = Rot_tmp.nxt()
                        kb.op("act", lambda h, a_t=a_t, t_t=t_t: h.activation(out=t_t, in_=a_t, func=AF.Silu), r=[a_b],
                              w=[t_b])
                        kb.op("dve", lambda h, m=m, t_t=t_t, b_t=b_t: h.tensor_tensor(out=hid[:, m, :], in0=t_t, in1=b_t,
                                                                                     op=ALU.mult), r=[t_b, b_b], w=[hidb])
                    for oc in range(8):
                        wv, wb = wload(wd3[:, :, oc * 128:(oc + 1) * 128], 22, 128)
                        py, pyb = psum[4 + (oc % 2)], psb[4 + (oc % 2)]
                        kb.mm([(py[:], wv[:, k, :], hid[:, k, :]) for k in range(22)], r=[wb, hidb], w=[pyb])
                        kb.op("dve", lambda h, oc=oc, py=py: h.scalar_tensor_tensor(
                            out=xT[:, oc, blk], in0=py[:], scalar=dmod[:, 5, oc:oc + 1], in1=xT[:, oc, blk],
                            op0=ALU.mult, op1=ALU.add), r=[pyb, dmodb, xbuf[j]], w=[xbuf[j]])

            chk("p6")
            phase()
            Rot_sq = Rot("sq", [128, 512], BF16, 2)
            Rot_rstd = Rot("rstd", [128, 512], F32, 2)
            yt = cv.get([128, 8, 512], F32)
            ytb = Buf("yt")
            ost = Rot("ost", [128, 1024], F32, 2)
            for j in range(nb):
                r_t, r_b = rstd_block(j)
                for k in range(8):
                    kb.op("dve", lambda h, k=k: h.scalar_tensor_tensor(
                        out=yt[:, k, :], in0=xT[:, k, j * 512:(j + 1) * 512], scalar=fv[:, 0, 214 + k:215 + k], in1=r_t,
                        op0=ALU.mult, op1=ALU.mult), r=[xbuf[j], fvb, r_b], w=[ytb])
                for sub in range(4):
                    o_t, o_b = ost.nxt()
                    for half in range(2):
                        pi = half
                        kb.tr([(psum[pi][:, q * 128:(q + 1) * 128], yt[:, half * 4 + q, sub * 128:(sub + 1) * 128])
                               for q in range(4)], identF[:], r=[ytb, cbuf], w=[psb[pi]])
                        if half == 0:
                            kb.op("dve", lambda h, o_t=o_t: h.tensor_copy(out=o_t[:, 0:512], in_=psum[0][:]), r=[psb[0]],
                                  w=[o_b])
                        else:
                            kb.op("act", lambda h, o_t=o_t: h.copy(out=o_t[:, 512:1024], in_=psum[1][:]), r=[psb[1]],
                                  w=[o_b])
                    r0 = j * 512 + sub * 128
                    kb.dma("sp", job["yout"][r0:r0 + 128, :], o_t, r=[o_b])

        jobs = [
            dict(idx=0, T=NP_SEQ * L_P, L=L_P, nseq=NP_SEQ, rope=False, cache=False, xin=dr["xp"], yout=dr["yp"]),
            dict(idx=1, T=L_S, L=L_S, nseq=1, rope=True, cache=True, xin=dr["xs"], yout=dr["ys"]),
        ]
        try:
            chk("mod")
            for job in jobs:
                run_job(job)
        except _Stop:
            pass
        kb.barrier(engines=("pe", "act", "dve", "sp", "pool"), include_pool=True)
    return nc, hc


def kernel(**inputs):
    n = 8
    if "nc" not in _CACHE:
        _CACHE["nc"] = build()
    nc, hc = _CACHE["nc"]
    f = lambda a: np.ascontiguousarray(np.asarray(a, dtype=np.float32))
    xp = f(inputs["x_prompt"])
    xs = f(inputs["x_sample"])
    st = f(inputs["state_hgrn"])
    cm = f(inputs["cache_mla"])
    c = f(inputs["c"])
    cctx = f(inputs["c_ctx"])
    in_maps = []
    for i in range(n):
        m = {
            "xp": xp[4 * i:4 * i + 4].reshape(NP_SEQ * L_P, D),
            "xs": xs[i],
            "st_in": st[i],
            "cache_in": cm[i],
            "cvec": np.stack([cctx, c[i]], axis=0),
        }
        for wn in WEIGHTS:
            m[wn] = f(inputs[wn])
        for k, v in hc.items():
            m[k] = v
        in_maps.append({k: np.ascontiguousarray(v) for k, v in m.items()})
    res = run_bass_kernel_spmd(nc, in_maps, core_ids=list(range(n)))
    R = res.results
    y_p = np.concatenate([r["yp"].reshape(NP_SEQ, L_P, D) for r in R], axis=0)
    y_s = np.stack([r["ys"] for r in R], axis=0)
    st_o = np.concatenate([r["st_out"] for r in R], axis=0)
    ch_o = np.concatenate([r["cache_out"] for r in R], axis=0)
    return (y_p.astype(np.float32), y_s.astype(np.float32), st_o.astype(np.float32), ch_o.astype(np.float32))
```
